# Optimizing a Trainium2 kernel written in Bass

```python
import jax, jax.numpy as jnp
from jax import lax
import numpy as np

D_MODEL = 1024
BATCH = 32
SEQ = 2048
DEPTH = 2

F32 = jnp.float32
N_MEM = 256
HEAD_DIM = 64
ROPE_THETA = 10000.0
EPS = 1e-6
NEG_INF = -1e30
POS_INF = 1e30
Q_BLOCK = 128

A_HEADS = 8
A_KV_HEADS = 2
A_WINDOW = 128
A_WIDTH = A_HEADS * HEAD_DIM
A_KV_WIDTH = A_KV_HEADS * HEAD_DIM

B_WIDTH = 512
B_BLOCKS = 8
B_BLOCK_DIM = B_WIDTH // B_BLOCKS
B_CONV = 4
B_C = 8.0

M_HEADS = 4
M_WIDTH = M_HEADS * HEAD_DIM

C_HEADS = 4
C_HEAD_DIM = 128
C_WIDTH = C_HEADS * C_HEAD_DIM
C_CHUNK = 64

D_HEADS = 8
D_KV_HEADS = 2
D_WIDTH = D_HEADS * HEAD_DIM
D_KV_WIDTH = D_KV_HEADS * HEAD_DIM
CMP_LEN = 32
CMP_STRIDE = 16
CMP_HIDDEN = 128
SEL_LEN = 64
SEL_TOPK = 4
SEL_Q_BLOCK = 64
D_WINDOW = 512
D_BRANCHES = 3

EVEN_SPLITS = [A_WIDTH, A_KV_WIDTH, A_KV_WIDTH, A_WIDTH, B_WIDTH, B_WIDTH, M_WIDTH, M_WIDTH]
ODD_SPLITS = [C_WIDTH, C_WIDTH, C_WIDTH, C_WIDTH, D_WIDTH] + [D_KV_WIDTH] * 6 + [D_BRANCHES * D_HEADS, D_WIDTH, M_WIDTH, M_WIDTH]
EVEN_IN = sum(EVEN_SPLITS)
ODD_IN = sum(ODD_SPLITS)
EVEN_MIX = A_WIDTH + B_WIDTH + M_WIDTH
ODD_MIX = C_WIDTH + D_WIDTH + M_WIDTH
N_EVEN = (DEPTH + 1) // 2
N_ODD = DEPTH // 2

kernel_name = "hybrid_swa_rglru_hgrn2_nsa_trunk"


def rms_norm(x, g):
    xf = x.astype(F32)
    y = xf * lax.rsqrt(jnp.mean(xf * xf, axis=-1, keepdims=True) + EPS)
    return (y * g.astype(F32)).astype(x.dtype)


def rope(x, pos):
    half = x.shape[-1] // 2
    inv = ROPE_THETA ** (-jnp.arange(half, dtype=F32) / half)
    ang = pos.astype(F32)[:, None] * inv[None, :]
    cos = jnp.cos(ang)[:, None, :]
    sin = jnp.sin(ang)[:, None, :]
    x1 = x[..., :half].astype(F32)
    x2 = x[..., half:].astype(F32)
    return jnp.concatenate([x1 * cos - x2 * sin, x2 * cos + x1 * sin], axis=-1).astype(x.dtype)


def split_cols(z, widths):
    return jnp.split(z, [int(c) for c in np.cumsum(widths)[:-1]], axis=-1)


def banded_attention(q, k, v, window, sinks=None):
    B_, S_, Hkv, G, hd = q.shape
    pad = -(-(window - 1) // Q_BLOCK) * Q_BLOCK
    span = pad + Q_BLOCK
    kp = jnp.pad(k, ((0, 0), (pad, 0), (0, 0), (0, 0)))
    vp = jnp.pad(v, ((0, 0), (pad, 0), (0, 0), (0, 0)))
    scale = hd ** -0.5

    def one_block(j):
        start = j * Q_BLOCK
        qb = lax.dynamic_slice_in_dim(q, start, Q_BLOCK, axis=1)
        kb = lax.dynamic_slice_in_dim(kp, start, span, axis=1)
        vb = lax.dynamic_slice_in_dim(vp, start, span, axis=1)
        s = jnp.einsum('bqkgd,bskd->bkgqs', qb, kb).astype(F32) * scale
        tq = start + jnp.arange(Q_BLOCK)
        ts = start - pad + jnp.arange(span)
        rel = tq[:, None] - ts[None, :]
        mask = (rel >= 0) & (rel < window) & (ts[None, :] >= 0)
        s = jnp.where(mask, s, NEG_INF)
        if sinks is None:
            p = jax.nn.softmax(s, axis=-1)
        else:
            sink = sinks.astype(F32).reshape(Hkv, G)[None, :, :, None, None]
            m = jnp.maximum(jnp.max(s, axis=-1, keepdims=True), sink)
            e = jnp.exp(s - m)
            p = e / (jnp.sum(e, axis=-1, keepdims=True) + jnp.exp(sink - m))
        return jnp.einsum('bkgqs,bskd->bqkgd', p.astype(vb.dtype), vb)

    out = lax.map(one_block, jnp.arange(S_ // Q_BLOCK))
    return jnp.moveaxis(out, 0, 1).reshape(B_, S_, Hkv, G, hd)


def swa_sink_attention(q, k, v, qn, kn, sinks, pos):
    B_, S_, _ = q.shape
    G = A_HEADS // A_KV_HEADS
    q = rope(rms_norm(q.reshape(B_, S_, A_HEADS, HEAD_DIM), qn), pos).reshape(B_, S_, A_KV_HEADS, G, HEAD_DIM)
    k = rope(rms_norm(k.reshape(B_, S_, A_KV_HEADS, HEAD_DIM), kn), pos)
    v = v.reshape(B_, S_, A_KV_HEADS, HEAD_DIM)
    o = banded_attention(q, k, v, A_WINDOW, sinks)
    return o.reshape(B_, S_, A_WIDTH)


def rglru(xb, conv_w, conv_b, w_r, b_r, w_i, b_i, lam):
    B_, S_, W = xb.shape
    xc = lax.conv_general_dilated(xb, conv_w[:, None, :], window_strides=(1,), padding=[(B_CONV - 1, 0)],
                                  dimension_numbers=('NWC', 'WIO', 'NWC'), feature_group_count=W) + conv_b
    xh = xc.reshape(B_, S_, B_BLOCKS, B_BLOCK_DIM)
    r = jax.nn.sigmoid(jnp.einsum('bshi,hij->bshj', xh, w_r).reshape(B_, S_, W) + b_r).astype(F32)
    i = jax.nn.sigmoid(jnp.einsum('bshi,hij->bshj', xh, w_i).reshape(B_, S_, W) + b_i)
    log_a = -B_C * r * jax.nn.softplus(-lam.astype(F32))
    a = jnp.exp(log_a)
    u = jnp.sqrt(-jnp.expm1(2.0 * log_a)) * (i * xc).astype(F32)

    def combine(c1, c2):
        a1, b1 = c1
        a2, b2 = c2
        return a1 * a2, a2 * b1 + b2

    _, h = lax.associative_scan(combine, (a, u), axis=1)
    return h.astype(xb.dtype)


def memory_cross_attention(qm, mem_n, w_mem_kv, qn, kn):
    B_, S_, _ = qm.shape
    N_ = mem_n.shape[1]
    q = rms_norm(qm.reshape(B_, S_, M_HEADS, HEAD_DIM), qn)
    km, vm = jnp.split(jnp.einsum('bnd,de->bne', mem_n, w_mem_kv), 2, axis=-1)
    km = rms_norm(km.reshape(B_, N_, M_HEADS, HEAD_DIM), kn)
    vm = vm.reshape(B_, N_, M_HEADS, HEAD_DIM)
    s = jnp.einsum('bshd,bnhd->bhsn', q, km).astype(F32) * HEAD_DIM ** -0.5
    p = jax.nn.softmax(s, axis=-1)
    o = jnp.einsum('bhsn,bnhd->bshd', p.astype(vm.dtype), vm)
    return o.reshape(B_, S_, M_WIDTH)


def hgrn_lower_bounds(p):
    c = jnp.cumsum(jax.nn.softmax(p.astype(F32), axis=0), axis=0)
    return c - c[0:1]


def hgrn2(q, f_logit, i_in, lb, o_gain):
    B_, S_, _ = q.shape
    nc = S_ // C_CHUNK
    f = lb + (1.0 - lb) * jax.nn.sigmoid(f_logit.astype(F32))
    log_f = jnp.log(f)
    k = 1.0 - f
    qf = jax.nn.silu(q.astype(F32))
    v = i_in.astype(F32)

    def to_chunks(t):
        return t.reshape(B_, nc, C_CHUNK, C_HEADS, C_HEAD_DIM).transpose(1, 0, 3, 2, 4)

    causal = jnp.tril(jnp.ones((C_CHUNK, C_CHUNK), dtype=bool))

    def step(state, inp):
        qc, kc, vc, gc = inp
        b = jnp.cumsum(gc, axis=2)
        o_inter = jnp.einsum('bhtk,bhkv->bhtv', qc * jnp.exp(b), state)
        diff = b[:, :, :, None, :] - b[:, :, None, :, :]
        decay = jnp.exp(jnp.where(causal[:, :, None], diff, NEG_INF))
        attn = jnp.einsum('bhtk,bhsk,bhtsk->bhts', qc, kc, decay)
        o_intra = jnp.einsum('bhts,bhsv->bhtv', attn, vc)
        b_last = b[:, :, -1:, :]
        state = jnp.exp(b_last[:, :, 0, :])[..., None] * state + jnp.einsum('bhsk,bhsv->bhkv', kc * jnp.exp(b_last - b), vc)
        return state, o_inter + o_intra

    init = jnp.zeros((B_, C_HEADS, C_HEAD_DIM, C_HEAD_DIM), F32)
    _, o = lax.scan(step, init, (to_chunks(qf), to_chunks(k), to_chunks(v), to_chunks(log_f)))
    o = o.transpose(1, 0, 3, 2, 4).reshape(B_, S_, C_HEADS, C_HEAD_DIM)
    o = rms_norm(o, o_gain)
    return o.reshape(B_, S_, C_WIDTH).astype(q.dtype)


def nsa(q, k_cmp, v_cmp, k_slc, v_slc, k_win, v_win, gate_logits, qn, kn_cmp, kn_slc, kn_win,
        pe_k, pe_v, w1k, w2k, w1v, w2v, pos):
    B_, S_, _ = q.shape
    Hkv = D_KV_HEADS
    G = D_HEADS // D_KV_HEADS
    scale = HEAD_DIM ** -0.5
    q = rope(rms_norm(q.reshape(B_, S_, D_HEADS, HEAD_DIM), qn), pos).reshape(B_, S_, Hkv, G, HEAD_DIM)
    t_pos = jnp.arange(S_)

    n_cmp = (S_ - CMP_LEN) // CMP_STRIDE + 1
    cmp_start = jnp.arange(n_cmp) * CMP_STRIDE
    blk_idx = cmp_start[:, None] + jnp.arange(CMP_LEN)[None, :]

    def compress(t, pe, w1, w2):
        tb = t.reshape(B_, S_, Hkv, HEAD_DIM)[:, blk_idx] + pe[None, None, :, None, :]
        flat = tb.transpose(0, 1, 3, 2, 4).reshape(B_, n_cmp, Hkv, CMP_LEN * HEAD_DIM)
        return jax.nn.silu(flat @ w1) @ w2

    cmp_end = cmp_start + CMP_LEN - 1
    kc = rope(rms_norm(compress(k_cmp, pe_k, w1k, w2k), kn_cmp), cmp_end)
    vc = compress(v_cmp, pe_v, w1v, w2v)
    s_c = jnp.einsum('bskgd,bnkd->bkgsn', q, kc).astype(F32) * scale
    mask_c = cmp_end[None, :] <= t_pos[:, None]
    p_c = jax.nn.softmax(jnp.where(mask_c, s_c, NEG_INF), axis=-1)
    p_c = jnp.where(jnp.any(mask_c, axis=-1)[:, None], p_c, 0.0)
    o_cmp = jnp.einsum('bkgsn,bnkd->bskgd', p_c.astype(vc.dtype), vc)

    n_sel = S_ // SEL_LEN
    sel_start = jnp.arange(n_sel) * SEL_LEN
    overlap = ((cmp_start[:, None] < sel_start[None, :] + SEL_LEN) &
               (cmp_start[:, None] + CMP_LEN > sel_start[None, :])).astype(F32)
    imp = jnp.einsum('bkgsn,nj->bksj', p_c, overlap)
    cur = t_pos // SEL_LEN
    jj = jnp.arange(n_sel)
    forced = (jj[None, :] == 0) | (jj[None, :] == cur[:, None])
    valid = jj[None, :] <= cur[:, None]
    score = jnp.where(forced, POS_INF, jnp.where(valid, imp, NEG_INF))
    k_top = min(SEL_TOPK, n_sel)
    _, sel_idx = lax.top_k(score, k_top)
    ks = rope(rms_norm(k_slc.reshape(B_, S_, Hkv, HEAD_DIM), kn_slc), pos)
    ksb = ks.reshape(B_, n_sel, SEL_LEN, Hkv, HEAD_DIM).transpose(0, 3, 1, 2, 4)
    vsb = v_slc.reshape(B_, n_sel, SEL_LEN, Hkv, HEAD_DIM).transpose(0, 3, 1, 2, 4)
    bi = jnp.arange(B_)[:, None, None, None]
    hi = jnp.arange(Hkv)[None, :, None, None]

    def sel_block(j):
        start = j * SEL_Q_BLOCK
        qb = lax.dynamic_slice_in_dim(q, start, SEL_Q_BLOCK, axis=1)
        idx = lax.dynamic_slice_in_dim(sel_idx, start, SEL_Q_BLOCK, axis=2)
        kg = ksb[bi, hi, idx]
        vg = vsb[bi, hi, idx]
        s = jnp.einsum('bqkgd,bkqnld->bkgqnl', qb, kg).astype(F32) * scale
        tq = start + jnp.arange(SEL_Q_BLOCK)
        key_pos = idx[..., None] * SEL_LEN + jnp.arange(SEL_LEN)
        mask = (key_pos <= tq[None, None, :, None, None])[:, :, None]
        s = jnp.where(mask, s, NEG_INF)
        p = jax.nn.softmax(s.reshape(B_, Hkv, G, SEL_Q_BLOCK, k_top * SEL_LEN), axis=-1).reshape(s.shape)
        return jnp.einsum('bkgqnl,bkqnld->bqkgd', p.astype(vg.dtype), vg)

    o_sel = lax.map(sel_block, jnp.arange(S_ // SEL_Q_BLOCK))
    o_sel = jnp.moveaxis(o_sel, 0, 1).reshape(B_, S_, Hkv, G, HEAD_DIM)

    kw = rope(rms_norm(k_win.reshape(B_, S_, Hkv, HEAD_DIM), kn_win), pos)
    o_win = banded_attention(q, kw, v_win.reshape(B_, S_, Hkv, HEAD_DIM), D_WINDOW)

    g = jax.nn.sigmoid(gate_logits).reshape(B_, S_, Hkv, G, D_BRANCHES, 1)
    o = g[..., 0, :] * o_cmp + g[..., 1, :] * o_sel + g[..., 2, :] * o_win
    return o.reshape(B_, S_, D_WIDTH)


def even_layer(h, mem, g, mem_g, w_mem_kv, m_qn, m_kn, w_in, w_out, a_qn, a_kn, a_sinks,
               conv_w, conv_b, w_r, b_r, w_i, b_i, lam, pos):
    xn = rms_norm(h, g)
    z = jnp.einsum('bsd,de->bse', xn, w_in)
    qa, ka, va, ga, xb, gb, qm, gm = split_cols(z, EVEN_SPLITS)
    oa = swa_sink_attention(qa, ka, va, a_qn, a_kn, a_sinks, pos) * jax.nn.silu(ga)
    ob = rglru(xb, conv_w, conv_b, w_r, b_r, w_i, b_i, lam) * jax.nn.silu(gb)
    om = memory_cross_attention(qm, rms_norm(mem, mem_g), w_mem_kv, m_qn, m_kn) * jax.nn.silu(gm)
    return h + jnp.einsum('bse,ed->bsd', jnp.concatenate([oa, ob, om], axis=-1), w_out)


def odd_layer(h, mem, g, mem_g, w_mem_kv, m_qn, m_kn, w_in, w_out, lb, c_og,
              d_qn, d_kn_cmp, d_kn_slc, d_kn_win, pe_k, pe_v, w1k, w2k, w1v, w2v, pos):
    xn = rms_norm(h, g)
    z = jnp.einsum('bsd,de->bse', xn, w_in)
    (qc, fc, ic, gc, qd, kcd, vcd, ksd, vsd, kwd, vwd, gate_d, gd, qm, gm) = split_cols(z, ODD_SPLITS)
    oc = hgrn2(qc, fc, ic, lb, c_og) * jax.nn.silu(gc)
    od = nsa(qd, kcd, vcd, ksd, vsd, kwd, vwd, gate_d, d_qn, d_kn_cmp, d_kn_slc, d_kn_win,
             pe_k, pe_v, w1k, w2k, w1v, w2v, pos) * jax.nn.silu(gd)
    om = memory_cross_attention(qm, rms_norm(mem, mem_g), w_mem_kv, m_qn, m_kn) * jax.nn.silu(gm)
    return h + jnp.einsum('bse,ed->bsd', jnp.concatenate([oc, od, om], axis=-1), w_out)


def setup_inputs(seed: int = 0) -> dict:
    key = jax.random.key(seed)
    keys = iter(jax.random.split(key, 48))

    def nrm(shape, scale):
        return jax.random.normal(next(keys), shape, F32) * scale

    def gain(shape):
        return 1.0 + nrm(shape, 0.02)

    u = jax.random.uniform(next(keys), (N_EVEN, B_WIDTH), F32, minval=0.9, maxval=0.999)
    s = u ** (1.0 / B_C)
    lam = jnp.log(s) - jnp.log1p(-s)
    return {
        "x": nrm((BATCH, SEQ, D_MODEL), 1.0),
        "mem": nrm((BATCH, N_MEM, D_MODEL), 1.0),
        "norm_g": gain((DEPTH, D_MODEL)),
        "mem_norm_g": gain((DEPTH, D_MODEL)),
        "mem_w_kv": nrm((DEPTH, D_MODEL, 2 * M_WIDTH), D_MODEL ** -0.5),
        "mem_qn": gain((DEPTH, HEAD_DIM)),
        "mem_kn": gain((DEPTH, HEAD_DIM)),
        "ev_w_in": nrm((N_EVEN, D_MODEL, EVEN_IN), D_MODEL ** -0.5),
        "ev_w_out": nrm((N_EVEN, EVEN_MIX, D_MODEL), EVEN_MIX ** -0.5),
        "a_qn": gain((N_EVEN, HEAD_DIM)),
        "a_kn": gain((N_EVEN, HEAD_DIM)),
        "a_sinks": nrm((N_EVEN, A_HEADS), 0.5),
        "b_conv_w": nrm((N_EVEN, B_CONV, B_WIDTH), B_CONV ** -0.5),
        "b_conv_b": nrm((N_EVEN, B_WIDTH), 0.01),
        "b_w_r": nrm((N_EVEN, B_BLOCKS, B_BLOCK_DIM, B_BLOCK_DIM), B_BLOCK_DIM ** -0.5),
        "b_b_r": nrm((N_EVEN, B_WIDTH), 0.01),
        "b_w_i": nrm((N_EVEN, B_BLOCKS, B_BLOCK_DIM, B_BLOCK_DIM), B_BLOCK_DIM ** -0.5),
        "b_b_i": nrm((N_EVEN, B_WIDTH), 0.01),
        "b_lambda": lam,
        "od_w_in": nrm((N_ODD, D_MODEL, ODD_IN), D_MODEL ** -0.5),
        "od_w_out": nrm((N_ODD, ODD_MIX, D_MODEL), ODD_MIX ** -0.5),
        "c_lb": nrm((DEPTH, C_WIDTH), 0.5),
        "c_onorm": gain((N_ODD, C_HEAD_DIM)),
        "d_qn": gain((N_ODD, HEAD_DIM)),
        "d_kn_cmp": gain((N_ODD, HEAD_DIM)),
        "d_kn_slc": gain((N_ODD, HEAD_DIM)),
        "d_kn_win": gain((N_ODD, HEAD_DIM)),
        "d_pe_k": nrm((N_ODD, CMP_LEN, HEAD_DIM), 0.02),
        "d_pe_v": nrm((N_ODD, CMP_LEN, HEAD_DIM), 0.02),
        "d_w1k": nrm((N_ODD, CMP_LEN * HEAD_DIM, CMP_HIDDEN), (CMP_LEN * HEAD_DIM) ** -0.5),
        "d_w2k": nrm((N_ODD, CMP_HIDDEN, HEAD_DIM), CMP_HIDDEN ** -0.5),
        "d_w1v": nrm((N_ODD, CMP_LEN * HEAD_DIM, CMP_HIDDEN), (CMP_LEN * HEAD_DIM) ** -0.5),
        "d_w2v": nrm((N_ODD, CMP_HIDDEN, HEAD_DIM), CMP_HIDDEN ** -0.5),
    }


def reference(x, mem, norm_g, mem_norm_g, mem_w_kv, mem_qn, mem_kn, ev_w_in, ev_w_out, a_qn, a_kn, a_sinks,
              b_conv_w, b_conv_b, b_w_r, b_b_r, b_w_i, b_b_i, b_lambda, od_w_in, od_w_out, c_lb, c_onorm,
              d_qn, d_kn_cmp, d_kn_slc, d_kn_win, d_pe_k, d_pe_v, d_w1k, d_w2k, d_w1v, d_w2v):
    pos = jnp.arange(x.shape[1])
    lbs = hgrn_lower_bounds(c_lb)
    h = x
    for l in range(DEPTH):
        e = l // 2
        if l % 2 == 0:
            h = even_layer(h, mem, norm_g[l], mem_norm_g[l], mem_w_kv[l], mem_qn[l], mem_kn[l],
                           ev_w_in[e], ev_w_out[e], a_qn[e], a_kn[e], a_sinks[e],
                           b_conv_w[e], b_conv_b[e], b_w_r[e], b_b_r[e], b_w_i[e], b_b_i[e], b_lambda[e], pos)
        else:
            h = odd_layer(h, mem, norm_g[l], mem_norm_g[l], mem_w_kv[l], mem_qn[l], mem_kn[l],
                          od_w_in[e], od_w_out[e], lbs[l], c_onorm[e],
                          d_qn[e], d_kn_cmp[e], d_kn_slc[e], d_kn_win[e], d_pe_k[e], d_pe_v[e],
                          d_w1k[e], d_w2k[e], d_w1v[e], d_w2v[e], pos)
    return h
```

```python
from contextlib import ExitStack
import numpy as np
import concourse.bass as bass
import concourse.mybir as mybir
from concourse.bass_utils import run_bass_kernel_spmd

F32 = mybir.dt.float32
BF16 = mybir.dt.bfloat16
AF = mybir.ActivationFunctionType
ALU = mybir.AluOpType
AX = mybir.AxisListType

ENGINES = ["pe", "act", "dve", "pool", "sp"]
EPOCH = 30000
NDMASEM = 8
import os
NO_POOL = os.environ.get("K_NO_POOL", "0") == "1"
DBG = int(os.environ.get("K_DBG", "99"))

D_MODEL = 1024
SEQ = 2048
NT = SEQ // 128
N_MEM = 256
EVEN_IN = 2816
ODD_IN = 4376
EPS = 1e-6
NEG = -1024.0
N_CMP = 127


class Buf:
    __slots__ = ("name", "lw", "rd")

    def __init__(self, name):
        self.name = name
        self.lw = None
        self.rd = {}


class Op:
    __slots__ = ("eng", "fn", "deps", "is_dma", "needs_inc", "token", "waits", "dsem", "idx")

    def __init__(self, eng, fn, is_dma=False):
        self.eng = eng
        self.fn = fn
        self.deps = []
        self.is_dma = is_dma
        self.needs_inc = is_dma
        self.token = None
        self.waits = []
        self.dsem = None
        self.idx = None


class T:
    def __init__(self, h, name):
        self.h = h
        self.name = name
        self.whole = Buf(name)
        self.subs = {}

    def __getitem__(self, idx):
        return self.h[idx]

    def b(self, key=None):
        if key is None:
            return self.whole
        s = self.subs.get(key)
        if s is None:
            s = Buf(f"{self.name}[{key}]")
            self.subs[key] = s
        return s


class Sched:
    def __init__(self, nc):
        self.nc = nc
        self.es = ExitStack()
        self.ops = {e: [] for e in ENGINES}
        self.all_dma = []
        self.dma_since_bar = []
        self.pending = {e: [] for e in ENGINES}
        self.arena = None
        self.arena_words = 0
        self.arena_off = 0

    def sb(self, name, shape, dt):
        h = self.es.enter_context(self.nc.sbuf_tensor("sb_" + name, list(shape), dt))
        return T(h, name)

    def ps(self, name, shape, dt):
        h = self.es.enter_context(self.nc.psum_tensor("ps_" + name, list(shape), dt))
        return T(h, name)

    def make_arena(self, words):
        self.arena = self.es.enter_context(self.nc.sbuf_tensor("arena", [128, words], F32))
        self.arena_words = words
        self.arena_off = 0

    def carve(self, name, shape, dt):
        n = 1
        for s in shape[1:]:
            n *= s
        words = (n + 1) // 2 if dt == BF16 else n
        words = (words + 7) // 8 * 8
        assert self.arena_off + words <= self.arena_words, (name, self.arena_off, words, self.arena_words)
        ap = self.arena[:, self.arena_off:self.arena_off + words]
        if dt == BF16:
            ap = ap.bitcast(BF16)[:, 0:n]
        else:
            ap = ap[:, 0:n]
        self.arena_off += words
        if len(shape) == 3:
            ap = ap.rearrange("p (a b) -> p a b", a=shape[1])
        elif len(shape) == 4:
            ap = ap.rearrange("p (a b c) -> p a b c", a=shape[1], b=shape[2])
        return T(ap, name)

    def phase_reset(self, to=0):
        self.barrier()
        self.arena_off = to

    def _bufs(self, xs):
        out = []
        for x in xs or []:
            out.append(x.whole if isinstance(x, T) else x)
        return out

    def op(self, eng, fn, r=None, w=None, is_dma=False):
        if eng == "pool" and NO_POOL and not is_dma:
            eng = "dve"
        o = Op(eng, fn, is_dma)
        skey = ("dma", len(self.all_dma)) if is_dma else eng
        deps = []
        rb = self._bufs(r)
        wb = self._bufs(w)
        for b in rb:
            if b.lw is not None:
                deps.append(b.lw)
        for b in wb:
            if b.lw is not None:
                deps.append(b.lw)
            deps.extend(b.rd.values())
        if self.pending[eng]:
            deps.extend(self.pending[eng])
            self.pending[eng] = []
        for b in rb:
            b.rd[skey] = o
        for b in wb:
            b.lw = o
            b.rd = {}
        seen = set()
        for d in deps:
            if id(d) in seen or d is o:
                continue
            seen.add(id(d))
            if (not d.is_dma) and (not is_dma) and d.eng == eng and eng == "pe":
                continue
            o.deps.append(d)
        o.idx = len(self.ops[eng])
        self.ops[eng].append(o)
        if is_dma:
            self.all_dma.append(o)
            self.dma_since_bar.append(o)
        return o

    def dma(self, out, in_, r=None, w=None, q="sp", **kw):
        return self.op(q, lambda e: e.dma_start(out=out, in_=in_, **kw), r=r, w=w, is_dma=True)

    def barrier(self):
        lasts = []
        for e in ENGINES:
            for o in reversed(self.ops[e]):
                if not o.is_dma:
                    lasts.append(o)
                    break
        lasts.extend(self.dma_since_bar)
        self.dma_since_bar = []
        for e in ENGINES:
            self.pending[e] = list(self.pending[e]) + lasts

    def emit(self):
        nc = self.nc
        for e in ENGINES:
            for o in self.ops[e]:
                for d in o.deps:
                    d.needs_inc = True
        nsem_eng = {}
        for e in ENGINES:
            c = 0
            k = 0
            for o in self.ops[e]:
                if o.is_dma:
                    o.dsem = (e, k % NDMASEM)
                    k += 1
                elif o.needs_inc:
                    c += 1
                    o.token = (("e", e, (c - 1) // EPOCH), (c - 1) % EPOCH + 1)
            nsem_eng[e] = (c + EPOCH - 1) // EPOCH if c else 0
        dcount = {}
        prev_dma = {}
        for e in ENGINES:
            for o in self.ops[e]:
                if o.is_dma:
                    key = ("d",) + o.dsem
                    v = dcount.get(key, 0) + 16
                    dcount[key] = v
                    o.token = (key, v)
                    if key in prev_dma:
                        o.deps.append(prev_dma[key])
                    prev_dma[key] = o
        sems = {}
        for e in ENGINES:
            for ep in range(nsem_eng[e]):
                sems[("e", e, ep)] = self.es.enter_context(nc.semaphore(f"s_{e}_{ep}"))
        for key in dcount:
            sems[key] = self.es.enter_context(nc.semaphore(f"d_{key[1]}_{key[2]}"))
        for e in ENGINES:
            seen = {}
            for o in self.ops[e]:
                need = {}
                for d in o.deps:
                    k, v = d.token
                    if seen.get(k, 0) >= v:
                        continue
                    if need.get(k, 0) < v:
                        need[k] = v
                for k, v in need.items():
                    seen[k] = v
                o.waits = list(need.items())
        final_waits = list(dcount.items())
        self.nsems = len(sems)
        self.ninst = {e: len(self.ops[e]) for e in ENGINES}
        engmap = {"pe": "tensor", "act": "scalar", "dve": "vector", "pool": "gpsimd", "sp": "sync"}
        with nc.Block() as block:
            for e in ENGINES:
                ops = self.ops[e]

                def body(eng, ops=ops, e=e):
                    for o in ops:
                        for k, v in o.waits:
                            eng.wait_ge(sems[k], v)
                        ins = o.fn(eng)
                        if o.is_dma:
                            ins.then_inc(sems[o.token[0]], 16)
                        elif o.needs_inc:
                            ins.then_inc(sems[o.token[0]], 1)
                    if e == "sp":
                        for k, v in final_waits:
                            eng.wait_ge(sems[k], v)

                getattr(block, engmap[e])(body)
        self.es.close()


def host_consts():
    c = {}
    c["ident"] = np.eye(128, dtype=np.float32)
    half = 32
    inv = 10000.0 ** (-np.arange(half, dtype=np.float32) / half)
    pos = np.arange(SEQ, dtype=np.float32)
    ang = pos[:, None] * inv[None, :]
    c["rope_cs"] = np.concatenate([np.cos(ang), np.sin(ang), -np.sin(ang)], axis=1).astype(np.float32)
    cend = (np.arange(N_CMP) * 16 + 31).astype(np.float32)
    angc = cend[:, None] * inv[None, :]
    c["rope_cmp"] = np.concatenate([np.cos(angc), np.sin(angc), -np.sin(angc)], axis=1).astype(np.float32)
    a = np.arange(128)[:, None]
    b = np.arange(128)[None, :]
    c["mdiag"] = np.where(a <= b, 0.0, NEG).astype(np.float32)
    c["mprev"] = np.where(a > b, 0.0, NEG).astype(np.float32)
    j = np.arange(N_CMP)[:, None, None]
    i = np.arange(NT)[None, :, None]
    bb = np.arange(128)[None, None, :]
    c["cmpmask"] = ((16 * j + 31) <= (128 * i + bb)).astype(np.float32)
    s = np.arange(SEQ)[None, :]
    js = np.arange(32)[:, None]
    c["selE"] = ((s // 64) == js).astype(np.float32)
    n = np.arange(N_CMP)[:, None]
    jj = np.arange(32)[None, :]
    c["ovl"] = ((16 * n < 64 * jj + 64) & (16 * n + 32 > 64 * jj)).astype(np.float32)
    t = np.arange(SEQ)[:, None]
    cur = t // 64
    forced = (jj == 0) | (jj == cur)
    valid = jj <= cur
    c["seladj"] = np.where(forced, 1e4, np.where(valid, 0.0, -1e4)).astype(np.float32)
    tri = (np.arange(64)[:, None] <= np.arange(64)[None, :]).astype(np.float32)
    c["tri64"] = np.concatenate([tri, tri], axis=0)
    return c


CONST_SHAPES = {"ident": [128, 128], "rope_cs": [SEQ, 96], "rope_cmp": [N_CMP, 96], "mdiag": [128, 128],
                "mprev": [128, 128], "cmpmask": [N_CMP, NT, 128], "selE": [32, SEQ], "ovl": [N_CMP, 32],
                "seladj": [SEQ, 32], "tri64": [128, 64]}

PARAM_SHAPES = {
    "norm_g": [2, 1024], "mem_norm_g": [2, 1024], "mem_w_kv": [2, 1024, 512], "mem_qn": [2, 64], "mem_kn": [2, 64],
    "ev_w_in": [1, 1024, 2816], "ev_w_out": [1, 1280, 1024], "a_qn": [1, 64], "a_kn": [1, 64], "a_sinks": [1, 8],
    "b_conv_w": [1, 4, 512], "b_conv_b": [1, 512], "b_w_r": [1, 8, 64, 64], "b_b_r": [1, 512],
    "b_w_i": [1, 8, 64, 64], "b_b_i": [1, 512], "b_lambda": [1, 512], "od_w_in": [1, 1024, 4376],
    "od_w_out": [1, 1280, 1024], "c_lb": [2, 512], "c_onorm": [1, 128], "d_qn": [1, 64], "d_kn_cmp": [1, 64],
    "d_kn_slc": [1, 64], "d_kn_win": [1, 64], "d_pe_k": [1, 32, 64], "d_pe_v": [1, 32, 64],
    "d_w1k": [1, 2048, 128], "d_w2k": [1, 128, 64], "d_w1v": [1, 2048, 128], "d_w2v": [1, 128, 64],
}


def build_program(nseq, layers=(0, 1), mix0=("A", "B", "M"), mix1=("C", "D", "M")):
    nc = bass.Bass("TRN2", target_bir_lowering=False)
    D = {}
    D["x"] = nc.dram_tensor("x", [nseq, SEQ, D_MODEL], F32, kind="ExternalInput").ap()
    D["mem"] = nc.dram_tensor("mem", [nseq, N_MEM, D_MODEL], F32, kind="ExternalInput").ap()
    for k, shp in PARAM_SHAPES.items():
        D[k] = nc.dram_tensor(k, shp, F32, kind="ExternalInput").ap()
    for k, shp in CONST_SHAPES.items():
        D[k] = nc.dram_tensor(k, shp, F32, kind="ExternalInput").ap()
    Y = nc.dram_tensor("y", [nseq, SEQ, D_MODEL], F32, kind="ExternalOutput").ap()
    W_IN = [nc.dram_tensor("w_in0s", [1024, EVEN_IN], BF16, kind="Internal").ap(),
            nc.dram_tensor("w_in1s", [1024, ODD_IN], BF16, kind="Internal").ap()]
    W_OUT = [nc.dram_tensor("w_out0s", [1280, 1024], BF16, kind="Internal").ap(),
             nc.dram_tensor("w_out1s", [1280, 1024], BF16, kind="Internal").ap()]
    W_KV = [nc.dram_tensor("w_kv0s", [1024, 512], BF16, kind="Internal").ap(),
            nc.dram_tensor("w_kv1s", [1024, 512], BF16, kind="Internal").ap()]
    W_1 = {"k": nc.dram_tensor("w1ks", [2048, 128], BF16, kind="Internal").ap(),
           "v": nc.dram_tensor("w1vs", [2048, 128], BF16, kind="Internal").ap()}

    S = Sched(nc)
    op = S.op

    H = S.sb("H", [128, NT, 1024], F32)
    xnT = S.sb("xnT", [128, 8, SEQ], BF16)
    ident = S.sb("ident", [128, 128], BF16)
    ROPE = S.sb("ROPE", [128, NT, 96], F32)
    mdiag = S.sb("mdiag", [128, 128], BF16)
    mprev = S.sb("mprev", [128, 128], BF16)
    normg = S.sb("normg", [128, 2, 8], F32)
    memg = S.sb("memg", [128, 2, 8], F32)
    GN = S.sb("GN", [128, 10, 64], F32)
    GI = {"mem_qn0": 0, "mem_qn1": 1, "mem_kn0": 2, "mem_kn1": 3, "a_qn": 4, "a_kn": 5, "d_qn": 6, "d_kn_slc": 7,
          "d_kn_win": 8, "d_kn_cmp": 9}
    COG = S.sb("COG", [128, 128], F32)
    LB = S.sb("LB", [128, 2, 4], F32)
    ones512 = S.sb("ones512", [128, 512], F32)
    esink = S.sb("esink", [128, 8], F32)
    ss16 = S.sb("ss16", [128, NT], F32)
    rstd16 = S.sb("rstd16", [128, NT], F32)
    tA = S.sb("tA", [128, 1024], F32)
    tB = S.sb("tB", [128, 512], F32)
    tC = S.sb("tC", [128, 512], F32)
    tD = S.sb("tD", [128, 512], F32)
    tE = S.sb("tE", [128, 512], F32)
    tF = S.sb("tF", [128, 512], F32)
    xnb = S.sb("xnb", [128, 1024], BF16)
    sm8 = S.sb("sm8", [128, 8, 8], F32)
    Pb = S.sb("Pb", [128, 2, 512], BF16)
    ob16 = S.sb("ob16", [128, 1280], BF16)
    oT = S.sb("oT", [128, 10, 128], BF16)
    qrot = S.sb("qrot", [128, 512], BF16)
    qz = S.sb("qz", [128, 2, 4, 128], BF16)
    krot = S.sb("krot", [128, 256], BF16)
    pZ = S.ps("pZ", [128, 512], F32)
    pS0 = S.ps("pS0", [128, 512], F32)
    pS1 = S.ps("pS1", [128, 512], F32)
    pSb = [pS0, pS1]
    pO0 = S.ps("pO0", [128, 4, 128], F32)
    pO1 = S.ps("pO1", [128, 4, 128], F32)
    pOb = [pO0, pO1]
    pT = S.ps("pT", [128, 8, 128], BF16)
    pY0 = S.ps("pY0", [128, 512], F32)
    pY1 = S.ps("pY1", [128, 512], F32)
    pYb = [pY0, pY1]

    ARENA_WORDS = 18 * 1024
    S.make_arena(ARENA_WORDS)

    def mm(out, lhsT, rhs, start, stop, r, w, skip=False):
        if skip:
            return op("pe", lambda e: e.matmul(out=out, lhsT=lhsT, rhs=rhs, start=start, stop=stop,
                                               skip_group_check=True), r=r, w=w)
        return op("pe", lambda e: e.matmul(out=out, lhsT=lhsT, rhs=rhs, start=start, stop=stop), r=r, w=w)

    def tr(out, in_, npart, r, w):
        return op("pe", lambda e: e.transpose(out=out, in_=in_, identity=ident[0:npart, 0:npart]), r=list(r) + [ident], w=w)

    def act(out, in_, func, r, w, scale=1.0, bias=0.0, accum=None):
        if accum is None:
            return op("act", lambda e: e.activation(out=out, in_=in_, func=func, scale=scale, bias=bias), r=r, w=w)
        return op("act", lambda e: e.activation(out=out, in_=in_, func=func, scale=scale, bias=bias, accum_out=accum), r=r, w=w)

    def tt(eng, out, in0, in1, o, r, w):
        return op(eng, lambda e: e.tensor_tensor(out=out, in0=in0, in1=in1, op=o), r=r, w=w)

    def ts(eng, out, in0, s1, s2, o0, o1, r, w):
        if s2 is None:
            return op(eng, lambda e: e.tensor_scalar(out=out, in0=in0, scalar1=s1, scalar2=None, op0=o0), r=r, w=w)
        return op(eng, lambda e: e.tensor_scalar(out=out, in0=in0, scalar1=s1, scalar2=s2, op0=o0, op1=o1), r=r, w=w)

    def stt(eng, out, in0, sc, in1, o0, o1, r, w):
        return op(eng, lambda e: e.scalar_tensor_tensor(out=out, in0=in0, scalar=sc, in1=in1, op0=o0, op1=o1), r=r, w=w)

    def cp(eng, out, in_, r, w):
        if eng == "act":
            return act(out, in_, AF.Copy, r, w)
        return op(eng, lambda e: e.tensor_copy(out=out, in_=in_), r=r, w=w)

    def recip(out, in_, r, w):
        return op("dve", lambda e: e.reciprocal(out=out, in_=in_), r=r, w=w)

    def memset(eng, ap, val, w):
        return op(eng, lambda e: e.memset(ap, val), w=w)

    def rstd_from_ss(ap, n_mean, r, w):
        act(ap, ap, AF.Ln, r, w, scale=1.0 / n_mean, bias=EPS)
        act(ap, ap, AF.Exp, w, w, scale=-0.5)

    def bc3(ap2, n):
        return ap2.unsqueeze(2).broadcast_to([ap2.shape[0], ap2.shape[1], n])

    def bch(ap2, nh):
        return ap2.unsqueeze(1).broadcast_to([ap2.shape[0], nh, ap2.shape[1]])

    stage = S.carve("stage0", [128, 2048], F32)
    stage1 = S.carve("stage1", [128, 2048], F32)
    stb0 = S.carve("stb0", [128, 2048], BF16)
    stb1 = S.carve("stb1", [128, 2048], BF16)
    stages = [(stage, stb0), (stage1, stb1)]

    S.dma(stage[:, 0:128], D["ident"], w=[stage])
    cp("dve", ident[:], stage[:, 0:128], [stage], [ident])
    S.dma(stage[:, 0:128], D["mdiag"], w=[stage])
    cp("dve", mdiag[:], stage[:, 0:128], [stage], [mdiag])
    S.dma(stage[:, 0:128], D["mprev"], w=[stage])
    cp("dve", mprev[:], stage[:, 0:128], [stage], [mprev])
    memset("dve", qz[:], 0.0, [qz.b(0), qz.b(1)])
    S.dma(ROPE[:], D["rope_cs"].rearrange("(i p) f -> p i f", p=128), w=[ROPE])
    S.dma(normg[:], D["norm_g"].rearrange("l (c p) -> p l c", p=128), w=[normg], allow_slow_non_contiguous=True)
    S.dma(memg[:], D["mem_norm_g"].rearrange("l (c p) -> p l c", p=128), w=[memg], allow_slow_non_contiguous=True)
    for nm, gi in GI.items():
        if nm.startswith("mem_"):
            src = D[nm[:-1]][int(nm[-1])]
        else:
            src = D[nm][0]
        S.dma(GN[:, gi, :], src.partition_broadcast(128), w=[GN.b(gi)])
    S.dma(esink[:], D["a_sinks"][0].partition_broadcast(128), w=[esink])
    act(esink[:], esink[:], AF.Exp, [esink], [esink])
    S.dma(COG[:], D["c_onorm"][0].partition_broadcast(128), w=[COG])
    memset("dve", ones512[:], 1.0, [ones512])
    S.dma(LB[:, 0, :], D["c_lb"][0].rearrange("(h p) -> p h", p=128), w=[LB.b(0)], allow_slow_non_contiguous=True)
    S.dma(LB[:, 1, :], D["c_lb"][1].rearrange("(h p) -> p h", p=128), w=[LB.b(1)], allow_slow_non_contiguous=True)
    tt("dve", LB[:, 0, :], LB[:, 0, :], LB[:, 1, :], ALU.subtract, [LB.b(0), LB.b(1)], [LB.b(0)])
    act(LB[:, 0, :], LB[:, 0, :], AF.Exp, [LB.b(0)], [LB.b(0)])
    ts("dve", LB[:, 0, :], LB[:, 0, :], 1.0, None, ALU.add, None, [LB.b(0)], [LB.b(0)])
    recip(LB[:, 0, :], LB[:, 0, :], [LB.b(0)], [LB.b(0)])
    ts("dve", LB[:, 1, :], LB[:, 0, :], -1.0, 1.0, ALU.mult, ALU.add, [LB.b(0)], [LB.b(1)])

    cnt = [0]

    def conv_weight(src, dst, R, C, gt=None, l=0, perm0=None):
        for rc in range(R // 128):
            for c0 in range(0, C, 2048):
                cw = min(2048, C - c0)
                sf, sbf = stages[cnt[0] % 2]
                eng = ["dve", "pool"][cnt[0] % 2]
                cnt[0] += 1
                S.dma(sf[:, 0:cw], src[rc * 128:(rc + 1) * 128, c0:c0 + cw], w=[sf])
                if gt is not None:
                    ts(eng, sbf[:, 0:cw], sf[:, 0:cw], gt[:, l, rc:rc + 1], None, ALU.mult, None, [sf, gt], [sbf])
                else:
                    cp(eng, sbf[:, 0:cw], sf[:, 0:cw], [sf], [sbf])
                rows = slice(rc * 128, (rc + 1) * 128)
                if perm0 is not None and c0 <= perm0 < c0 + cw:
                    p0 = perm0 - c0
                    if p0 > 0:
                        S.dma(dst[rows, c0:c0 + p0], sbf[:, 0:p0], r=[sbf])
                    for w_ in range(2):
                        S.dma(dst[rows, perm0:perm0 + 512].rearrange("r (pr w d) -> r w pr d", pr=4, w=2)[:, w_],
                              sbf[:, p0 + w_ * 256:p0 + (w_ + 1) * 256].rearrange("p (pr d) -> p pr d", pr=4), r=[sbf])
                    if p0 + 512 < cw:
                        S.dma(dst[rows, perm0 + 512:c0 + cw], sbf[:, p0 + 512:cw], r=[sbf])
                else:
                    S.dma(dst[rows, c0:c0 + cw], sbf[:, 0:cw], r=[sbf])

    if 0 in layers:
        conv_weight(D["ev_w_in"][0], W_IN[0], 1024, EVEN_IN, normg, 0, perm0=0)
        conv_weight(D["ev_w_out"][0], W_OUT[0], 1280, 1024)
        conv_weight(D["mem_w_kv"][0], W_KV[0], 1024, 512, memg, 0)
    if 1 in layers:
        conv_weight(D["od_w_in"][0], W_IN[1], 1024, ODD_IN, normg, 1, perm0=2048)
        conv_weight(D["od_w_out"][0], W_OUT[1], 1280, 1024)
        conv_weight(D["mem_w_kv"][1], W_KV[1], 1024, 512, memg, 1)
        conv_weight(D["d_w1k"][0], W_1["k"], 2048, 128)
        conv_weight(D["d_w1v"][0], W_1["v"], 2048, 128)
    S.phase_reset()

    def load_slab(dst, src_w, c0, ncols, key=None):
        S.dma(dst[:, :, 0:ncols], src_w.rearrange("(c p) n -> p c n", p=128)[:, :, c0:c0 + ncols],
              w=[dst.b(key)])

    def proj_tok(ps_ap, slab, col0, ncols, i, w, skey=None):
        for c in range(8):
            mm(ps_ap, xnT[:, c, i * 128:(i + 1) * 128], slab[:, c, col0:col0 + ncols], c == 0, c == 7,
               [xnT.b(i), slab.b(skey)], w)

    def proj_feat(ps_ap, slab, col0, g, w, skey=None):
        for c in range(8):
            mm(ps_ap, slab[:, c, col0:col0 + 128], xnT[:, c, g * 512:(g + 1) * 512], c == 0, c == 7,
               [xnT.b(4 * g), xnT.b(4 * g + 1), xnT.b(4 * g + 2), xnT.b(4 * g + 3), slab.b(skey)], w)

    def silu_psum(out_ap, zp, n, rz, wout, t1, t2, np_=128):
        act(t1[0:np_, 0:n], zp, AF.Exp, rz, [t1], scale=-1.0)
        act(t2[0:np_, 0:n], zp, AF.Copy, rz, [t2])
        ts("pool", t1[0:np_, 0:n], t1[0:np_, 0:n], 1.0, None, ALU.add, None, [t1], [t1])
        recip(t1[0:np_, 0:n], t1[0:np_, 0:n], [t1], [t1])
        tt("dve", out_ap, t2[0:np_, 0:n], t1[0:np_, 0:n], ALU.mult, [t1, t2], wout)

    def norm_rope(zp, nh, gi, rope_ap, out_ap, rz, wout, slot, np_=128):
        n = nh * 64
        z3 = zp.rearrange("p (h d) -> p h d", h=nh)
        ssq = sm8[0:np_, slot, 0:nh]
        act(tA[0:np_, 0:n], zp, AF.Square, rz, [tA])
        op("dve", lambda e: e.tensor_reduce(out=ssq, in_=tA[0:np_, 0:n].rearrange("p (h d) -> p h d", h=nh),
                                            axis=AX.X, op=ALU.add), r=[tA], w=[sm8.b(slot)])
        rstd_from_ss(ssq, 64.0, [sm8.b(slot)], [sm8.b(slot)])
        zg = tB[0:np_, 0:n].rearrange("p (h d) -> p h d", h=nh)
        tt("dve", zg, z3, bch(GN[0:np_, gi, :], nh), ALU.mult, list(rz) + [GN.b(gi)], [tB])
        if rope_ap is None:
            tt("dve", out_ap, zg, bc3(ssq, 64), ALU.mult, [tB, sm8.b(slot)], wout)
            return
        zg4 = tB[0:np_, 0:n].rearrange("p (h a f) -> p h a f", h=nh, a=2)
        a4 = tC[0:np_, 0:n].rearrange("p (h a f) -> p h a f", h=nh, a=2)
        b4 = tD[0:np_, 0:n].rearrange("p (h a f) -> p h a f", h=nh, a=2)
        cos4 = rope_ap[:, 0:32].unsqueeze(1).unsqueeze(1).broadcast_to([np_, nh, 2, 32])
        sin3 = bch(rope_ap[:, 32:64], nh)
        nsin3 = bch(rope_ap[:, 64:96], nh)
        tt("dve", a4, zg4, cos4, ALU.mult, [tB], [tC])
        tt("dve", b4[:, :, 0, :], zg4[:, :, 1, :], nsin3, ALU.mult, [tB], [tD.b(0)])
        tt("dve", b4[:, :, 1, :], zg4[:, :, 0, :], sin3, ALU.mult, [tB], [tD.b(1)])
        tt("dve", tC[0:np_, 0:n], tC[0:np_, 0:n], tD[0:np_, 0:n], ALU.add, [tC, tD.b(0), tD.b(1)], [tC])
        tt("dve", out_ap, tC[0:np_, 0:n].rearrange("p (h d) -> p h d", h=nh), bc3(ssq, 64), ALU.mult,
           [tC, sm8.b(slot)], wout)

    def h_update(i, nchunks, wo, wkey=None):
        for half in range(2):
            for c in range(nchunks):
                mm(pYb[half][:], oT[:, c, :], wo[:, c, half * 512:(half + 1) * 512], c == 0, c == nchunks - 1,
                   [oT, wo.b(wkey)], [pYb[half]])
            tt("dve", H[:, i, half * 512:(half + 1) * 512], H[:, i, half * 512:(half + 1) * 512], pYb[half][:], ALU.add,
               [H.b(i), pYb[half]], [H.b(i)])

    def transposes_to_oT(nch, src_cols0=0, dst0=0):
        for c in range(nch):
            tr(pT[:, c, :], ob16[:, src_cols0 + c * 128: src_cols0 + (c + 1) * 128], 128, [ob16], [pT])
        cp("act", oT[:, dst0:dst0 + nch, :], pT[:, 0:nch, :], [pT], [oT])

    def rmsnorm_to_xnT():
        memset("pool", ss16[:], 0.0, [ss16])
        for i in range(NT):
            act(tA[:], H[:, i, :], AF.Square, [H.b(i)], [tA, ss16], accum=ss16[:, i:i + 1])
        cp("dve", rstd16[:], ss16[:], [ss16], [rstd16])
        rstd_from_ss(rstd16[:], 1024.0, [rstd16], [rstd16])
        for i in range(NT):
            ts("dve", xnb[:], H[:, i, :], rstd16[:, i:i + 1], None, ALU.mult, None, [H.b(i), rstd16], [xnb])
            for c in range(8):
                tr(pT[:, c, :], xnb[:, c * 128:(c + 1) * 128], 128, [xnb], [pT])
            cp("act", xnT[:, :, i * 128:(i + 1) * 128], pT[:], [pT], [xnT.b(i)])

    def mem_kv(s, l, kmT, VM1, memf, memT, wkv):
        load_slab(wkv, W_KV[l], 0, 512)
        memset("dve", VM1[:, :, :, 64:65], 1.0, [VM1.b("ones")])
        for nt in range(2):
            S.dma(memf[:], D["mem"][s, nt * 128:(nt + 1) * 128, :], w=[memf])
            memset("dve", sm8[:, 7, 0:1], 0.0, [sm8.b(7)])
            act(tA[:], memf[:], AF.Square, [memf], [tA, sm8.b(7)], accum=sm8[:, 7, 0:1])
            rstd_from_ss(sm8[:, 7, 0:1], 1024.0, [sm8.b(7)], [sm8.b(7)])
            ts("dve", xnb[:], memf[:], sm8[:, 7, 0:1], None, ALU.mult, None, [memf, sm8.b(7)], [xnb])
            for c in range(8):
                tr(pT[:, c, :], xnb[:, c * 128:(c + 1) * 128], 128, [xnb], [pT])
            cp("act", memT[:, :, nt * 128:(nt + 1) * 128], pT[:], [pT], [memT.b(nt)])
            for c in range(8):
                mm(pZ[:], memT[:, c, nt * 128:(nt + 1) * 128], wkv[:, c, 0:512], c == 0, c == 7, [memT.b(nt), wkv], [pZ])
            norm_rope(pZ[:, 0:256], 4, GI["mem_kn%d" % l], None, krot[:].rearrange("p (h d) -> p h d", h=4),
                      [pZ], [krot], 6)
            cp("act", VM1[:, nt, :, 0:64], pZ[:, 256:512].rearrange("p (h d) -> p h d", h=4), [pZ], [VM1.b(nt)])
            for pr in range(2):
                tr(pT[:, pr, :], krot[:, pr * 128:(pr + 1) * 128], 128, [krot], [pT])
            cp("act", kmT[:, :, nt * 128:(nt + 1) * 128], pT[:, 0:2, :], [pT], [kmT.b(nt)])

    def mem_attn_tile(i, qz_ap, qz_r, l, kmT, VM1, gate_ap, gate_r, out_cols0):
        norm_rope(qz_ap, 4, GI["mem_qn%d" % l], None, qrot[:, 0:256].rearrange("p (h d) -> p h d", h=4),
                  qz_r, [qrot], 5)
        for pr in range(2):
            tr(pT[:, pr, :], qrot[:, pr * 128:(pr + 1) * 128], 128, [qrot], [pT])
        cp("act", qz[0:64, 0, 0:2, :], pT[0:64, 0:2, :], [pT], [qz.b(0)])
        cp("act", qz[64:128, 1, 0:2, :], pT[64:128, 0:2, :], [pT], [qz.b(1)])
        if DBG < 2:
            return
        for h in range(4):
            pr, hf = h // 2, h % 2
            for nt in range(2):
                mm(pSb[nt][:, h * 128:(h + 1) * 128], kmT[:, pr, nt * 128:(nt + 1) * 128],
                   qz[:, hf, pr, :], True, True, [kmT.b(nt), qz.b(hf)], [pSb[nt]])
        for nt in range(2):
            act(Pb[:, nt, :], pSb[nt][:], AF.Exp, [pSb[nt]], [Pb.b(nt)], scale=0.125)
        if DBG < 3:
            return
        for h in range(4):
            for nt in range(2):
                mm(pO0[:, h, 0:65], Pb[:, nt, h * 128:(h + 1) * 128], VM1[:, nt, h, :], nt == 0, nt == 1,
                   [Pb.b(nt), VM1.b(nt), VM1.b("ones")], [pO0])
        cp("dve", sm8[:, 4, 0:4], pO0[:, :, 64], [pO0], [sm8.b(4)])
        recip(sm8[:, 4, 0:4], sm8[:, 4, 0:4], [sm8.b(4)], [sm8.b(4)])
        tt("dve", tE[:, 0:256].rearrange("p (h d) -> p h d", h=4), pO0[:, :, 0:64], bc3(sm8[:, 4, 0:4], 64), ALU.mult,
           [pO0, sm8.b(4)], [tE])
        tt("dve", ob16[:, out_cols0:out_cols0 + 256], tE[:, 0:256], gate_ap, ALU.mult, [tE] + list(gate_r), [ob16])

    def layer0(s):
        l = 0
        rmsnorm_to_xnT()
        S.phase_reset()
        if "B" in mix0:
            PB = S.carve("PB", [128, 4, 8], F32)
            PD = S.carve("PD", [128, 4, 4], F32)
            BDf = S.carve("BDf", [128, 2, 4, 128], F32)
            BD = S.carve("BD", [128, 2, 4, 128], BF16)
            XB = S.carve("XB", [128, 3 + SEQ], F32)
            hB = S.carve("hB", [128, SEQ], F32)
            mixB = S.carve("mixB", [128, 4, SEQ], BF16)
            wsl = [S.carve("wslB0", [128, 8, 256], BF16), S.carve("wslB1", [128, 8, 256], BF16)]
            woB = S.carve("woB", [128, 4, 1024], BF16)
            xcb = S.carve("xcb", [128, 512], BF16)
            for j in range(4):
                S.dma(PB[:, :, j], D["b_conv_w"][0, j].rearrange("(c p) -> p c", p=128), w=[PB.b(j)],
                      allow_slow_non_contiguous=True)
            for j, nm in enumerate(["b_conv_b", "b_b_r", "b_b_i", "b_lambda"]):
                S.dma(PB[:, :, 4 + j], D[nm][0].rearrange("(c p) -> p c", p=128), w=[PB.b(4 + j)],
                      allow_slow_non_contiguous=True)
            ts("dve", PD[:, :, 0], PB[:, :, 5], -1.0, None, ALU.mult, None, [PB.b(5)], [PD.b(0)])
            ts("dve", PD[:, :, 1], PB[:, :, 6], -1.0, None, ALU.mult, None, [PB.b(6)], [PD.b(1)])
            act(PD[:, :, 2], PB[:, :, 7], AF.Exp, [PB.b(7)], [PD.b(2)], scale=-1.0)
            act(PD[:, :, 2], PD[:, :, 2], AF.Ln, [PD.b(2)], [PD.b(2)], bias=1.0)
            ts("dve", PD[:, :, 2], PD[:, :, 2], -8.0, None, ALU.mult, None, [PD.b(2)], [PD.b(2)])
            bdkeys = [BDf.b((a_, b_, c_)) for a_ in range(2) for b_ in range(4) for c_ in range(2)]
            memset("pool", BDf[:], 0.0, bdkeys)
            for gi_, nm in enumerate(["b_w_r", "b_w_i"]):
                for cb in range(4):
                    for hb in range(2):
                        S.dma(BDf[64 * hb:64 * hb + 64, gi_, cb, 64 * hb:64 * hb + 64], D[nm][0, 2 * cb + hb],
                              w=[BDf.b((gi_, cb, hb))])
            cp("dve", BD[:], BDf[:], bdkeys, [BD])
            memset("pool", XB[:, 0:3], 0.0, [XB.b("pad")])
            S.dma(woB[:], W_OUT[l].rearrange("(c p) n -> p c n", p=128)[:, 4:8, :], w=[woB])
            for cb in range(4):
                wS = wsl[cb % 2]
                S.dma(wS[:, :, 0:128], W_IN[l].rearrange("(c p) n -> p c n", p=128)[:, :, 1280 + cb * 128:1280 + (cb + 1) * 128],
                      w=[wS.b("x")])
                S.dma(wS[:, :, 128:256], W_IN[l].rearrange("(c p) n -> p c n", p=128)[:, :, 1792 + cb * 128:1792 + (cb + 1) * 128],
                      w=[wS.b("g")])
                for g in range(4):
                    sl = slice(3 + g * 512, 3 + (g + 1) * 512)
                    proj_feat(pZ[:], wS, 0, g, [pZ], skey="x")
                    cp("act", XB[:, sl], pZ[:], [pZ], [XB.b(g)])
                    rd = [XB.b(g), XB.b(g - 1) if g > 0 else XB.b("pad"), PB.b(0), PB.b(1), PB.b(2), PB.b(3), PB.b(4)]
                    xc = tB
                    ts("dve", xc[:], XB[:, g * 512:g * 512 + 512], PB[:, cb, 0:1], PB[:, cb, 4:5], ALU.mult, ALU.add, rd, [tB])
                    for j in (1, 2, 3):
                        stt("dve", xc[:], XB[:, g * 512 + j:g * 512 + j + 512], PB[:, cb, j:j + 1], xc[:], ALU.mult, ALU.add,
                            rd + [tB], [tB])
                    cp("pool", xcb[:], xc[:], [tB], [xcb])
                    mm(pS0[:], BD[:, 0, cb, :], xcb[:], True, True, [BD, xcb], [pS0])
                    mm(pS1[:], BD[:, 1, cb, :], xcb[:], True, True, [BD, xcb], [pS1])
                    act(tC[:], pS0[:], AF.Exp, [pS0, PD.b(0)], [tC], scale=-1.0, bias=PD[:, cb, 0:1])
                    ts("pool", tC[:], tC[:], 1.0, None, ALU.add, None, [tC], [tC])
                    recip(tC[:], tC[:], [tC], [tC])
                    act(tC[:], tC[:], AF.Exp, [tC, PD.b(2)], [tC], scale=PD[:, cb, 2:3])
                    act(tD[:], pS1[:], AF.Exp, [pS1, PD.b(1)], [tD], scale=-1.0, bias=PD[:, cb, 1:2])
                    ts("pool", tD[:], tD[:], 1.0, None, ALU.add, None, [tD], [tD])
                    recip(tD[:], tD[:], [tD], [tD])
                    act(tE[:], tC[:], AF.Square, [tC], [tE])
                    act(tE[:], tE[:], AF.Ln, [tE], [tE], scale=-1.0, bias=1.0)
                    act(tE[:], tE[:], AF.Exp, [tE], [tE], scale=0.5)
                    tt("pool", tD[:], tD[:], xc[:], ALU.mult, [tD, tB], [tD])
                    tt("pool", tD[:], tD[:], tE[:], ALU.mult, [tD, tE], [tD])
                    init = 0.0 if g == 0 else hB[:, g * 512 - 1:g * 512]
                    op("dve", lambda e, g=g, init=init: e.tensor_tensor_scan(
                        out=hB[:, g * 512:(g + 1) * 512], data0=tC[:], data1=tD[:], initial=init,
                        op0=ALU.mult, op1=ALU.add), r=[tC, tD] + ([hB.b(g - 1)] if g > 0 else []), w=[hB.b(g)])
                    proj_feat(pZ[:], wS, 128, g, [pZ], skey="g")
                    silu_psum(tF[:], pZ[:], 512, [pZ], [tF], tE, tF)
                    tt("dve", mixB[:, cb, g * 512:(g + 1) * 512], tF[:], hB[:, g * 512:(g + 1) * 512], ALU.mult,
                       [tF, hB.b(g)], [mixB.b((cb, g))])
            for i in range(NT):
                for half in range(2):
                    for cb in range(4):
                        mm(pYb[half][:], mixB[:, cb, i * 128:(i + 1) * 128], woB[:, cb, half * 512:(half + 1) * 512],
                           cb == 0, cb == 3, [mixB.b((cb, i // 4)), woB], [pYb[half]])
                    tt("dve", H[:, i, half * 512:(half + 1) * 512], H[:, i, half * 512:(half + 1) * 512], pYb[half][:],
                       ALU.add, [H.b(i), pYb[half]], [H.b(i)])
            S.phase_reset()
        doA, doM = "A" in mix0, "M" in mix0
        if doA or doM:
            kT = S.carve("kT", [128, SEQ], BF16)
            V1 = S.carve("V1", [128, NT, 2, 65], BF16)
            wq = S.carve("wq", [128, 8, 512], BF16)
            wga = S.carve("wga", [128, 8, 512], BF16)
            wqgm = S.carve("wqgm", [128, 8, 512], BF16)
            woAM = S.carve("woAM", [128, 6, 1024], BF16)
            kmT = S.carve("kmT", [128, 2, N_MEM], BF16)
            VM1 = S.carve("VM1", [128, 2, 4, 65], BF16)
            memT = S.carve("memT", [128, 8, N_MEM], BF16)
            memf = S.carve("memf", [128, 1024], F32)
            gat = S.carve("gat", [128, 768], BF16)
            if doM:
                mem_kv(s, l, kmT, VM1, memf, memT, wq)
            if doA:
                load_slab(wga, W_IN[l], 512, 256)
                memset("dve", V1[:, :, :, 64:65], 1.0, [V1.b("ones")])
                for i in range(NT):
                    proj_tok(pZ[:, 0:256], wga, 0, 256, i, [pZ])
                    norm_rope(pZ[:, 0:128], 2, GI["a_kn"], ROPE[:, i, :], krot[:, 0:128].rearrange("p (h d) -> p h d", h=2),
                              [pZ], [krot], 0)
                    cp("act", V1[:, i, :, 0:64], pZ[:, 128:256].rearrange("p (h d) -> p h d", h=2), [pZ], [V1.b(i)])
                    tr(pT[:, 0, :], krot[:, 0:128], 128, [krot], [pT])
                    cp("act", kT[:, i * 128:(i + 1) * 128], pT[:, 0, :], [pT], [kT.b(i)])
            if doA:
                load_slab(wq, W_IN[l], 0, 512)
                load_slab(wga, W_IN[l], 768, 512)
                S.dma(woAM[:, 0:4, :], W_OUT[l].rearrange("(c p) n -> p c n", p=128)[:, 0:4, :], w=[woAM.b("a")])
            if doM:
                load_slab(wqgm, W_IN[l], 2304, 512)
                S.dma(woAM[:, 4:6, :], W_OUT[l].rearrange("(c p) n -> p c n", p=128)[:, 8:10, :], w=[woAM.b("m")])
            for i in range(NT):
                nch = 0
                if doA:
                    proj_tok(pZ[:], wga, 0, 512, i, [pZ])
                    silu_psum(gat[:, 0:512], pZ[:], 512, [pZ], [gat.b("a")], tE, tF)
                    proj_tok(pZ[:], wq, 0, 512, i, [pZ])
                    norm_rope(pZ[:], 8, GI["a_qn"], ROPE[:, i, :], qrot[:].rearrange("p (h d) -> p h d", h=8), [pZ], [qrot], 1)
                    for pr in range(4):
                        tr(pT[:, pr, :], qrot[:, pr * 128:(pr + 1) * 128], 128, [qrot], [pT])
                    cp("act", qz[0:64, 0, :, :], pT[0:64, 0:4, :], [pT], [qz.b(0)])
                    cp("act", qz[64:128, 1, :, :], pT[64:128, 0:4, :], [pT], [qz.b(1)])
                    kts = [i - 1, i] if i > 0 else [i]
                    for k in range(2):
                        for n_, kt in enumerate(kts):
                            ps = pSb[n_]
                            mm(ps[:].rearrange("p (g t) -> p g t", g=4), kT[:, kt * 128:(kt + 1) * 128],
                               qz[:, k, :, :], True, False, [kT.b(kt), qz.b(k)], [ps])
                            msk = mdiag if kt == i else mprev
                            mm(ps[:].rearrange("p (g t) -> p g t", g=4), ident[:], bch(msk[:], 4), False, True,
                               [ident, msk], [ps])
                            act(Pb[:, n_, :], ps[:], AF.Exp, [ps], [Pb.b(n_)], scale=0.125)
                        for g in range(4):
                            for n_, kt in enumerate(kts):
                                mm(pOb[k][:, g, 0:65], Pb[:, n_, g * 128:(g + 1) * 128], V1[:, kt, k, :],
                                   n_ == 0, n_ == len(kts) - 1, [Pb.b(n_), V1.b(kt), V1.b("ones")], [pOb[k]])
                    for k in range(2):
                        tt("dve", sm8[:, 2, 4 * k:4 * k + 4], pOb[k][:, :, 64], esink[:, 4 * k:4 * k + 4], ALU.add,
                           [pOb[k], esink], [sm8.b((2, k))])
                        recip(sm8[:, 2, 4 * k:4 * k + 4], sm8[:, 2, 4 * k:4 * k + 4], [sm8.b((2, k))], [sm8.b((2, k))])
                        tt("dve", tE[:, 256 * k:256 * k + 256].rearrange("p (h d) -> p h d", h=4), pOb[k][:, :, 0:64],
                           bc3(sm8[:, 2, 4 * k:4 * k + 4], 64), ALU.mult, [pOb[k], sm8.b((2, k))], [tE.b(k)])
                    tt("dve", ob16[:, 0:512], tE[:], gat[:, 0:512], ALU.mult, [tE.b(0), tE.b(1), gat.b("a")], [ob16])
                    transposes_to_oT(4, 0, 0)
                    nch = 4
                if doM:
                    proj_tok(pZ[:], wqgm, 0, 512, i, [pZ])
                    silu_psum(gat[:, 512:768], pZ[:, 256:512], 256, [pZ], [gat.b("m")], tE, tF)
                    if DBG >= 1:
                        mem_attn_tile(i, pZ[:, 0:256], [pZ], l, kmT, VM1, gat[:, 512:768], [gat.b("m")], 512)
                    if DBG < 4:
                        continue
                    for c in range(2):
                        tr(pT[:, 4 + c, :], ob16[:, 512 + c * 128:512 + (c + 1) * 128], 128, [ob16], [pT])
                    cp("act", oT[:, 4:6, :], pT[:, 4:6, :], [pT], [oT])
                chunks = ([0, 1, 2, 3] if doA else []) + ([4, 5] if doM else [])
                for half in range(2):
                    for n_, c in enumerate(chunks):
                        mm(pYb[half][:], oT[:, c, :], woAM[:, c, half * 512:(half + 1) * 512], n_ == 0, n_ == len(chunks) - 1,
                           [oT, woAM.b("a"), woAM.b("m")], [pYb[half]])
                    tt("dve", H[:, i, half * 512:(half + 1) * 512], H[:, i, half * 512:(half + 1) * 512], pYb[half][:],
                       ALU.add, [H.b(i), pYb[half]], [H.b(i)])
            S.phase_reset()

    def separator():
        mm(pY1[:, 0:1], ident[:], ident[:, 0:1], True, True, [ident], [pY1])

    def hgrn2_phase(s):
        l = 1
        QT = S.carve("QT", [128, 4, 512], BF16)
        KT = S.carve("KT", [128, 4, 512], BF16)
        KH = S.carve("KH", [128, 4, 512], BF16)
        Vc = S.carve("Vc", [128, 4, 512], BF16)
        wsm = [S.carve("wsmC0", [128, 8, 256], BF16), S.carve("wsmC1", [128, 8, 256], BF16)]
        wic = S.carve("wic", [128, 8, 512], BF16)
        wgc = S.carve("wgc", [128, 8, 512], BF16)
        woC = S.carve("woC", [128, 4, 1024], BF16)
        St = S.carve("St", [128, 4, 128], F32)
        Sb = S.carve("Sb", [128, 4, 128], BF16)
        EBL = S.carve("EBL", [128, 4, 32], F32)
        ATb = S.carve("ATb", [128, 4, 128], BF16)
        KHz = S.carve("KHz", [128, 4, 2, 128], BF16)
        tri = S.carve("tri", [128, 64], F32)
        Bp = S.carve("Bp", [128, 8], F32)
        tG = S.carve("tG", [128, 512], F32)
        tH = S.carve("tH", [128, 512], F32)
        gcb = S.carve("gcb", [128, 512], BF16)
        S.dma(tri[:], D["tri64"], w=[tri])
        memset("dve", St[:], 0.0, [St])
        memset("dve", Sb[:], 0.0, [Sb])
        memset("dve", ATb[:], 0.0, [ATb.b(0), ATb.b(1)])
        memset("dve", KHz[:], 0.0, [KHz.b(0), KHz.b(1)])
        memset("dve", Bp[:], 0.0, [Bp])
        load_slab(wic, W_IN[l], 1024, 512)
        load_slab(wgc, W_IN[l], 1536, 512)
        S.dma(woC[:], W_OUT[l].rearrange("(c p) n -> p c n", p=128)[:, 0:4, :], w=[woC])
        pS0v = pS0[:].rearrange("p (h t) -> p h t", h=4)
        pS1v = pS1[:].rearrange("p (h t) -> p h t", h=4)
        for g in range(4):
            for hd in range(4):
                wS = wsm[(g * 4 + hd) % 2]
                wv = W_IN[l].rearrange("(c p) n -> p c n", p=128)
                S.dma(wS[:, :, 0:128], wv[:, :, hd * 128:(hd + 1) * 128], w=[wS.b("q")])
                S.dma(wS[:, :, 128:256], wv[:, :, 512 + hd * 128:512 + (hd + 1) * 128], w=[wS.b("f")])
                proj_feat(pZ[:], wS, 128, g, [pZ], skey="f")
                act(tB[:], pZ[:], AF.Exp, [pZ], [tB], scale=-1.0)
                ts("pool", tB[:], tB[:], 1.0, None, ALU.add, None, [tB], [tB])
                recip(tB[:], tB[:], [tB], [tB])
                ts("dve", tB[:], tB[:], LB[:, 1, hd:hd + 1], LB[:, 0, hd:hd + 1], ALU.mult, ALU.add,
                   [tB, LB.b(0), LB.b(1)], [tB])
                act(tC[:], tB[:], AF.Ln, [tB], [tC])
                ts("pool", tB[:], tB[:], -1.0, 1.0, ALU.mult, ALU.add, [tB], [tB])
                op("dve", lambda e: e.tensor_tensor_scan(out=tD[:], data0=ones512[:], data1=tC[:], initial=0.0,
                                                         op0=ALU.mult, op1=ALU.add), r=[ones512, tC], w=[tD])
                tD3 = tD[:].rearrange("p (c j) -> p c j", j=64)
                cp("dve", Bp[:, 1:8], tD3[:, 0:7, 63], [tD], [Bp])
                tt("dve", tD3, tD3, bc3(Bp[:, 0:8], 64), ALU.subtract, [tD, Bp], [tD])
                act(tE[:], tD[:], AF.Exp, [tD], [tE])
                cp("dve", EBL[:, hd, 8 * g:8 * g + 8], tE[:].rearrange("p (c j) -> p c j", j=64)[:, :, 63], [tE],
                   [EBL.b((hd, g))])
                act(tF[:], tD[:], AF.Exp, [tD], [tF], scale=-1.0)
                tt("dve", KT[:, hd, :], tB[:], tF[:], ALU.mult, [tB, tF], [KT.b(hd)])
                tt("dve", tG[:].rearrange("p (c j) -> p c j", j=64), tD3, bc3(tD3[:, :, 63], 64), ALU.subtract,
                   [tD], [tG])
                act(tG[:], tG[:], AF.Exp, [tG], [tG], scale=-1.0)
                tt("dve", KH[:, hd, :], tB[:], tG[:], ALU.mult, [tB, tG], [KH.b(hd)])
                proj_feat(pZ[:], wS, 0, g, [pZ], skey="q")
                silu_psum(tH[:], pZ[:], 512, [pZ], [tH], tF, tH)
                tt("dve", QT[:, hd, :], tH[:], tE[:], ALU.mult, [tH, tE], [QT.b(hd)])
            for tl in range(4):
                i = 4 * g + tl
                proj_tok(pZ[:], wic, 0, 512, i, [pZ])
                cp("act", Vc[:, tl, :], pZ[:], [pZ], [Vc.b(tl)])
            for tl in range(4):
                i = 4 * g + tl
                cs = tl * 128
                allh = [QT.b(h_) for h_ in range(4)]
                for hd in range(4):
                    tr(pT[:, hd, :], KH[:, hd, cs:cs + 128], 128, [KH.b(hd)], [pT])
                cp("act", KHz[0:64, :, 0, :], pT[0:64, 0:4, :], [pT], [KHz.b(0)])
                cp("act", KHz[64:128, :, 1, :], pT[64:128, 0:4, :], [pT], [KHz.b(1)])
                for hd in range(4):
                    for c in range(2):
                        mm(pS0v[64 * c:64 * c + 64, hd, 64 * c:64 * c + 64], KT[:, hd, cs + 64 * c:cs + 64 * c + 64],
                           QT[:, hd, cs + 64 * c:cs + 64 * c + 64], True, True, [KT.b(hd), QT.b(hd)], [pS0])
                for c in range(2):
                    tt("dve", ATb[64 * c:64 * c + 64, :, 64 * c:64 * c + 64], pS0v[64 * c:64 * c + 64, :, 64 * c:64 * c + 64],
                       bch(tri[64 * c:64 * c + 64, :], 4), ALU.mult, [pS0, tri], [ATb.b(c)])
                for hd in range(4):
                    mm(pO0[:, hd, :], ATb[:, hd, :], Vc[:, tl, hd * 128:(hd + 1) * 128], True, True,
                       [ATb.b(0), ATb.b(1), Vc.b(tl)], [pO0])
                for c in range(2):
                    ch = 8 * g + 2 * tl + c
                    for hd in range(4):
                        mm(pO1[64 * c:64 * c + 64, hd, :], QT[:, hd, cs + 64 * c:cs + 64 * c + 64], Sb[:, hd, :], True, True,
                           [QT.b(hd), Sb], [pO1])
                    for hd in range(4):
                        mm(pS1v[:, hd, :], KHz[:, hd, c, :], Vc[:, tl, hd * 128:(hd + 1) * 128], True, True,
                           [KHz.b(c), Vc.b(tl)], [pS1])
                    tt("dve", St[:], St[:], bc3(EBL[:, :, ch], 128), ALU.mult, [St] + [EBL.b((h_, g)) for h_ in range(4)], [St])
                    tt("dve", St[:], St[:], pS1v, ALU.add, [St, pS1], [St])
                    cp("act", Sb[:], St[:], [St], [Sb])
                cp("act", tG[:], pO0[:].rearrange("p h v -> p (h v)"), [pO0], [tG])
                tt("dve", tG[:], tG[:], pO1[:].rearrange("p h v -> p (h v)"), ALU.add, [tG, pO1], [tG])
                act(tA[:, 0:512], tG[:], AF.Square, [tG], [tA])
                op("dve", lambda e: e.tensor_reduce(out=sm8[:, 3, 0:4], in_=tA[:, 0:512].rearrange("p (h v) -> p h v", h=4),
                                                    axis=AX.X, op=ALU.add), r=[tA], w=[sm8.b(3)])
                rstd_from_ss(sm8[:, 3, 0:4], 128.0, [sm8.b(3)], [sm8.b(3)])
                tG3 = tG[:].rearrange("p (h v) -> p h v", h=4)
                tt("dve", tG3, tG3, bc3(sm8[:, 3, 0:4], 128), ALU.mult, [tG, sm8.b(3)], [tG])
                tt("dve", tG3, tG3, bch(COG[:], 4), ALU.mult, [tG, COG], [tG])
                proj_tok(pZ[:], wgc, 0, 512, i, [pZ])
                silu_psum(gcb[:], pZ[:], 512, [pZ], [gcb], tH, tF)
                tt("dve", ob16[:, 0:512], tG[:], gcb[:], ALU.mult, [tG, gcb], [ob16])
                transposes_to_oT(4, 0, 0)
                h_update(i, 4, woC)

    def nsa_phase(s, doD, doM):
        l = 1
        ksT = S.carve("ksT", [128, SEQ], BF16)
        kwT = S.carve("kwT", [128, SEQ], BF16)
        Vs1 = S.carve("Vs1", [128, NT, 2, 65], BF16)
        Vw1 = S.carve("Vw1", [128, NT, 2, 65], BF16)
        kcT = S.carve("kcT", [128, 128], BF16)
        VC1 = S.carve("VC1", [128, 2, 97], BF16)
        cmpneg = S.carve("cmpneg", [128, NT, 128], BF16)
        selE = S.carve("selE", [128, SEQ], BF16)
        seladj = S.carve("seladj", [128, NT, 32], F32)
        ROPEC = S.carve("ROPEC", [128, 96], F32)
        kmT = S.carve("kmT1", [128, 2, N_MEM], BF16)
        VM1 = S.carve("VM11", [128, 2, 4, 65], BF16)
        mark = S.arena_off
        if doM:
            wkv = S.carve("wkv1", [128, 8, 512], BF16)
            memT = S.carve("memT1", [128, 8, N_MEM], BF16)
            memf = S.carve("memf1", [128, 1024], F32)
            mem_kv(s, l, kmT, VM1, memf, memT, wkv)
            S.phase_reset(mark)
        if doD:
            tmpf = S.carve("tmpf", [128, 2048], F32)
            w512 = S.carve("w512", [128, 8, 512], BF16)
            wkc = S.carve("wkc", [128, 8, 256], BF16)
            kcdT = S.carve("kcdT", [128, SEQ], BF16)
            vcdT = S.carve("vcdT", [128, SEQ], BF16)
            W1r = S.carve("W1r", [128, 32, 128], BF16)
            w2f = S.carve("w2f", [128, 2, 64], F32)
            w2b = S.carve("w2b", [128, 2, 64], BF16)
            pef = S.carve("pef", [128, 32], F32)
            peT = S.carve("peT", [128, 32], BF16)
            cb = S.carve("cb", [128, 2], F32)
            hid = S.carve("hid", [128, 2, 128], BF16)
            S.dma(tmpf[0:N_CMP, :], D["cmpmask"].rearrange("j i b -> j (i b)"), w=[tmpf])
            ts("dve", cmpneg[0:N_CMP, :, :].rearrange("p i b -> p (i b)"), tmpf[0:N_CMP, :], -1.0, -NEG, ALU.add, ALU.mult,
               [tmpf], [cmpneg])
            S.dma(tmpf[0:32, :], D["selE"], w=[tmpf])
            cp("dve", selE[0:32, :], tmpf[0:32, :], [tmpf], [selE])
            S.dma(seladj[:], D["seladj"].rearrange("(i p) j -> p i j", p=128), w=[seladj])
            S.dma(ROPEC[0:N_CMP, :], D["rope_cmp"], w=[ROPEC])
            S.dma(tmpf[0:N_CMP, 0:32], D["ovl"], w=[tmpf])
            for k in range(2):
                cp("dve", VC1[0:N_CMP, k, 65:97], tmpf[0:N_CMP, 0:32], [tmpf], [VC1.b(("o", k))])
            memset("dve", VC1[:, :, 64:65], 1.0, [VC1.b("ones")])
            memset("dve", Vs1[:, :, :, 64:65], 1.0, [Vs1.b("ones")])
            memset("dve", Vw1[:, :, :, 64:65], 1.0, [Vw1.b("ones")])
            S.dma(w2f[:, 0, :], D["d_w2k"][0], w=[w2f.b(0)])
            S.dma(w2f[:, 1, :], D["d_w2v"][0], w=[w2f.b(1)])
            cp("dve", w2b[:], w2f[:], [w2f.b(0), w2f.b(1)], [w2b])
            load_slab(w512, W_IN[l], 2816, 512)
            load_slab(wkc, W_IN[l], 2560, 256)
            for i in range(NT):
                proj_tok(pZ[:], w512, 0, 512, i, [pZ])
                norm_rope(pZ[:, 0:128], 2, GI["d_kn_slc"], ROPE[:, i, :], krot[:, 0:128].rearrange("p (h d) -> p h d", h=2),
                          [pZ], [krot.b(0)], 0)
                norm_rope(pZ[:, 256:384], 2, GI["d_kn_win"], ROPE[:, i, :], krot[:, 128:256].rearrange("p (h d) -> p h d", h=2),
                          [pZ], [krot.b(1)], 1)
                cp("act", Vs1[:, i, :, 0:64], pZ[:, 128:256].rearrange("p (h d) -> p h d", h=2), [pZ], [Vs1.b(i)])
                cp("act", Vw1[:, i, :, 0:64], pZ[:, 384:512].rearrange("p (h d) -> p h d", h=2), [pZ], [Vw1.b(i)])
                tr(pT[:, 0, :], krot[:, 0:128], 128, [krot.b(0)], [pT])
                tr(pT[:, 1, :], krot[:, 128:256], 128, [krot.b(1)], [pT])
                cp("act", ksT[:, i * 128:(i + 1) * 128], pT[:, 0, :], [pT], [ksT.b(i)])
                cp("act", kwT[:, i * 128:(i + 1) * 128], pT[:, 1, :], [pT], [kwT.b(i)])
            for g in range(4):
                proj_feat(pZ[:], wkc, 0, g, [pZ])
                cp("act", kcdT[:, g * 512:(g + 1) * 512], pZ[:], [pZ], [kcdT.b(g)])
                proj_feat(pZ[:], wkc, 128, g, [pZ])
                cp("act", vcdT[:, g * 512:(g + 1) * 512], pZ[:], [pZ], [vcdT.b(g)])
            for kind_i, kind in enumerate(["k", "v"]):
                srcT = kcdT if kind == "k" else vcdT
                w1v = W_1[kind].rearrange("(l d) m -> d l m", d=64)
                S.dma(W1r[0:64, :, :], w1v, w=[W1r.b(0)])
                S.dma(W1r[64:128, :, :], w1v, w=[W1r.b(1)])
                pesrc = D["d_pe_k" if kind == "k" else "d_pe_v"][0].rearrange("l d -> d l")
                S.dma(pef[0:64, :], pesrc, w=[pef], allow_slow_non_contiguous=True)
                cp("dve", peT[0:64, :], pef[0:64, :], [pef], [peT])
                for l_ in range(32):
                    mm(pY0[:, 0:1], W1r[0:64, l_, :], peT[0:64, l_:l_ + 1], l_ == 0, l_ == 31, [W1r.b(0), peT], [pY0])
                cp("dve", cb[:, 0:1], pY0[:, 0:1], [pY0], [cb])
                ts("dve", cb[:, 1:2], cb[:, 0:1], -1.0, None, ALU.mult, None, [cb], [cb])
                s3 = srcT[:].rearrange("p (j s) -> p j s", s=16)
                srcb = [srcT.b(g_) for g_ in range(4)]
                for k in range(2):
                    separator()
                    for l_ in range(32):
                        rhs = s3[64 * k:64 * k + 64, 0:N_CMP, l_] if l_ < 16 else s3[64 * k:64 * k + 64, 1:N_CMP + 1, l_ - 16]
                        mm(pSb[k][:, 0:N_CMP], W1r[64 * k:64 * k + 64, l_, :], rhs, l_ == 0, l_ == 31,
                           [W1r.b(k)] + srcb, [pSb[k]])
                    separator()
                    act(tB[:, 0:N_CMP], pSb[k][:, 0:N_CMP], AF.Exp, [pSb[k], cb], [tB], scale=-1.0, bias=cb[:, 1:2])
                    act(tC[:, 0:N_CMP], pSb[k][:, 0:N_CMP], AF.Identity, [pSb[k], cb], [tC], bias=cb[:, 0:1])
                    ts("pool", tB[:, 0:N_CMP], tB[:, 0:N_CMP], 1.0, None, ALU.add, None, [tB], [tB])
                    recip(tB[:, 0:N_CMP], tB[:, 0:N_CMP], [tB], [tB])
                    tt("dve", hid[:, k, 0:N_CMP], tC[:, 0:N_CMP], tB[:, 0:N_CMP], ALU.mult, [tB, tC], [hid.b(k)])
                for k in range(2):
                    c0 = kind_i * 128 + k * 64
                    mm(pZ[0:N_CMP, c0:c0 + 64], hid[:, k, 0:N_CMP], w2b[:, kind_i, :], True, True, [hid.b(k), w2b], [pZ])
            norm_rope(pZ[0:N_CMP, 0:128], 2, GI["d_kn_cmp"], ROPEC[0:N_CMP, :],
                      krot[0:N_CMP, 0:128].rearrange("p (h d) -> p h d", h=2), [pZ], [krot.b(0)], 0, np_=N_CMP)
            cp("act", VC1[0:N_CMP, :, 0:64], pZ[0:N_CMP, 128:256].rearrange("p (h d) -> p h d", h=2), [pZ], [VC1.b("v")])
            tr(pT[:, 0, 0:N_CMP], krot[0:N_CMP, 0:128], N_CMP, [krot.b(0)], [pT])
            cp("act", kcT[:, 0:N_CMP], pT[:, 0, 0:N_CMP], [pT], [kcT])
            S.phase_reset(mark)
        wq = S.carve("wq1", [128, 8, 512], BF16)
        wgd = S.carve("wgd", [128, 8, 512], BF16)
        wqgm = S.carve("wqgm1", [128, 8, 512], BF16)
        wgt = S.carve("wgt", [128, 8, 24], BF16)
        woDM = S.carve("woDM", [128, 6, 1024], BF16)
        acc = S.carve("acc", [128, 512], F32)
        negT = S.carve("negT", [128, 2, 128], BF16)
        nb = S.carve("nb", [128, 2, 32], BF16)
        gts = S.carve("gts", [128, 24], F32)
        gat = S.carve("gat1", [128, 768], BF16)
        impk = S.carve("impk", [128, 64], F32)
        wv = W_IN[l].rearrange("(c p) n -> p c n", p=128)
        if doD:
            load_slab(wq, W_IN[l], 2048, 512)
            load_slab(wgd, W_IN[l], 3352, 512)
            S.dma(wgt[:], wv[:, :, 3328:3352], w=[wgt])
            S.dma(woDM[:, 0:4, :], W_OUT[l].rearrange("(c p) n -> p c n", p=128)[:, 4:8, :], w=[woDM.b("a")])
        if doM:
            load_slab(wqgm, W_IN[l], 3864, 512)
            S.dma(woDM[:, 4:6, :], W_OUT[l].rearrange("(c p) n -> p c n", p=128)[:, 8:10, :], w=[woDM.b("m")])
        gts3 = gts[:].rearrange("p (h b) -> p h b", b=3)
        vones = {id(Vs1): Vs1.b("ones"), id(Vw1): Vw1.b("ones")}

        def evac_branch(k, br, first):
            ts("dve", sm8[:, 2, 4 * k:4 * k + 4], pOb[k][:, :, 64], 1e-30, None, ALU.max, None, [pOb[k]], [sm8.b((2, k))])
            recip(sm8[:, 2, 4 * k:4 * k + 4], sm8[:, 2, 4 * k:4 * k + 4], [sm8.b((2, k))], [sm8.b((2, k))])
            tt("dve", sm8[:, 3, 4 * k:4 * k + 4], sm8[:, 2, 4 * k:4 * k + 4], gts3[:, 4 * k:4 * k + 4, br], ALU.mult,
               [sm8.b((2, k)), gts], [sm8.b((3, k))])
            a3 = acc[:, 256 * k:256 * k + 256].rearrange("p (h d) -> p h d", h=4)
            if first:
                tt("dve", a3, pOb[k][:, :, 0:64], bc3(sm8[:, 3, 4 * k:4 * k + 4], 64), ALU.mult, [pOb[k], sm8.b((3, k))],
                   [acc.b(k)])
            else:
                t3 = tB[:, 256 * k:256 * k + 256].rearrange("p (h d) -> p h d", h=4)
                tt("dve", t3, pOb[k][:, :, 0:64], bc3(sm8[:, 3, 4 * k:4 * k + 4], 64), ALU.mult, [pOb[k], sm8.b((3, k))],
                   [tB.b(k)])
                tt("dve", a3, a3, t3, ALU.add, [acc.b(k), tB.b(k)], [acc.b(k)])

        def attn_branch(i, kts, kT_, V_, br, sel):
            for k in range(2):
                memset("dve", pOb[k][:], 0.0, [pOb[k]])
                for n_, kt in enumerate(kts):
                    ps = pSb[n_ % 2]
                    psv = ps[:].rearrange("p (g t) -> p g t", g=4)
                    extra = []
                    if sel:
                        extra.append((selE[0:32, kt * 128:(kt + 1) * 128], bch(negT[0:32, k, :], 4), [selE, negT]))
                    if kt == i:
                        extra.append((ident[:], bch(mdiag[:], 4), [ident, mdiag]))
                    elif (not sel) and kt == i - 4:
                        extra.append((ident[:], bch(mprev[:], 4), [ident, mprev]))
                    mm(psv, kT_[:, kt * 128:(kt + 1) * 128], qz[:, k, :, :], True, len(extra) == 0, [kT_.b(kt), qz.b(k)], [ps])
                    for j_, (lt, rh, rd_) in enumerate(extra):
                        mm(psv, lt, rh, False, j_ == len(extra) - 1, rd_, [ps])
                    act(Pb[:, n_ % 2, :], ps[:], AF.Exp, [ps], [Pb.b(n_ % 2)], scale=0.125)
                    for g in range(4):
                        mm(pOb[k][:, g, 0:65], Pb[:, n_ % 2, g * 128:(g + 1) * 128], V_[:, kt, k, :], False, False,
                           [Pb.b(n_ % 2), V_.b(kt), vones[id(V_)]], [pOb[k]], skip=True)
                evac_branch(k, br, False)

        for i in range(NT):
            if doD:
                proj_tok(pZ[:], wgd, 0, 512, i, [pZ])
                silu_psum(gat[:, 0:512], pZ[:], 512, [pZ], [gat.b("a")], tE, tF)
                proj_tok(pZ[:, 0:24], wgt, 0, 24, i, [pZ])
                act(gts[:], pZ[:, 0:24], AF.Exp, [pZ], [gts], scale=-1.0)
                ts("dve", gts[:], gts[:], 1.0, None, ALU.add, None, [gts], [gts])
                recip(gts[:], gts[:], [gts], [gts])
                proj_tok(pZ[:], wq, 0, 512, i, [pZ])
                norm_rope(pZ[:], 8, GI["d_qn"], ROPE[:, i, :], qrot[:].rearrange("p (h d) -> p h d", h=8), [pZ], [qrot], 1)
                for pr in range(4):
                    tr(pT[:, pr, :], qrot[:, pr * 128:(pr + 1) * 128], 128, [qrot], [pT])
                cp("act", qz[0:64, 0, :, :], pT[0:64, 0:4, :], [pT], [qz.b(0)])
                cp("act", qz[64:128, 1, :, :], pT[64:128, 0:4, :], [pT], [qz.b(1)])
                for k in range(2):
                    ps = pSb[k]
                    psv = ps[0:N_CMP, :].rearrange("p (g t) -> p g t", g=4)
                    mm(psv, kcT[:, 0:N_CMP], qz[:, k, :, :], True, False, [kcT, qz.b(k)], [ps])
                    mm(psv, ident[0:N_CMP, 0:N_CMP], bch(cmpneg[0:N_CMP, i, :], 4), False, True, [ident, cmpneg], [ps])
                    act(Pb[0:N_CMP, k, :], ps[0:N_CMP, :], AF.Exp, [ps], [Pb.b(k)], scale=0.125)
                for k in range(2):
                    memset("dve", pOb[k][:], 0.0, [pOb[k]])
                    for g in range(4):
                        mm(pOb[k][:, g, 0:97], Pb[0:N_CMP, k, g * 128:(g + 1) * 128], VC1[0:N_CMP, k, :], False, False,
                           [Pb.b(k), VC1.b("ones"), VC1.b("v"), VC1.b(("o", k))], [pOb[k]], skip=True)
                for k in range(2):
                    evac_branch(k, 0, True)
                    t3 = tC[:, 0:128].rearrange("p (g j) -> p g j", g=4)
                    tt("dve", t3, pOb[k][:, :, 65:97], bc3(sm8[:, 2, 4 * k:4 * k + 4], 32), ALU.mult,
                       [pOb[k], sm8.b((2, k))], [tC])
                    op("dve", lambda e, k=k: e.tensor_reduce(out=impk[:, 32 * k:32 * k + 32],
                                                             in_=tC[:, 0:128].rearrange("p (g j) -> p j g", g=4),
                                                             axis=AX.X, op=ALU.add), r=[tC], w=[impk.b(k)])
                    tt("dve", impk[:, 32 * k:32 * k + 32], impk[:, 32 * k:32 * k + 32], seladj[:, i, :], ALU.add,
                       [impk.b(k), seladj], [impk.b(k)])
                    op("dve", lambda e, k=k: e.max(out=sm8[:, 6, 0:8], in_=impk[:, 32 * k:32 * k + 32]), r=[impk.b(k)],
                       w=[sm8.b(6)])
                    ts("dve", tD[:, 0:32], impk[:, 32 * k:32 * k + 32], sm8[:, 6, 3:4], None, ALU.is_ge, None,
                       [impk.b(k), sm8.b(6)], [tD])
                    ts("dve", nb[:, k, :], tD[:, 0:32], -1.0, -NEG, ALU.add, ALU.mult, [tD], [nb.b(k)])
                    tr(pT[0:32, 4 + k, :], nb[:, k, :], 128, [nb.b(k)], [pT])
                cp("act", negT[0:32, :, :], pT[0:32, 4:6, :], [pT], [negT])
                attn_branch(i, list(range(0, i + 1)), ksT, Vs1, 1, True)
                attn_branch(i, list(range(max(0, i - 4), i + 1)), kwT, Vw1, 2, False)
                tt("dve", ob16[:, 0:512], acc[:], gat[:, 0:512], ALU.mult, [acc.b(0), acc.b(1), gat.b("a")], [ob16])
                transposes_to_oT(4, 0, 0)
            if doM:
                proj_tok(pZ[:], wqgm, 0, 512, i, [pZ])
                silu_psum(gat[:, 512:768], pZ[:, 256:512], 256, [pZ], [gat.b("m")], tE, tF)
                mem_attn_tile(i, pZ[:, 0:256], [pZ], l, kmT, VM1, gat[:, 512:768], [gat.b("m")], 512)
                for c in range(2):
                    tr(pT[:, 4 + c, :], ob16[:, 512 + c * 128:512 + (c + 1) * 128], 128, [ob16], [pT])
                cp("act", oT[:, 4:6, :], pT[:, 4:6, :], [pT], [oT])
            chunks = ([0, 1, 2, 3] if doD else []) + ([4, 5] if doM else [])
            for half in range(2):
                for n_, c in enumerate(chunks):
                    mm(pYb[half][:], oT[:, c, :], woDM[:, c, half * 512:(half + 1) * 512], n_ == 0, n_ == len(chunks) - 1,
                       [oT, woDM.b("a"), woDM.b("m")], [pYb[half]])
                tt("dve", H[:, i, half * 512:(half + 1) * 512], H[:, i, half * 512:(half + 1) * 512], pYb[half][:],
                   ALU.add, [H.b(i), pYb[half]], [H.b(i)])

    def layer1(s):
        rmsnorm_to_xnT()
        S.phase_reset()
        if "C" in mix1:
            hgrn2_phase(s)
            S.phase_reset()
        doD, doM = "D" in mix1, "M" in mix1
        if doD or doM:
            nsa_phase(s, doD, doM)
            S.phase_reset()

    for s in range(nseq):
        for i in range(NT):
            S.dma(H[:, i, :], D["x"][s, i * 128:(i + 1) * 128, :], w=[H.b(i)])
        if 0 in layers:
            layer0(s)
        if 1 in layers:
            layer1(s)
        for i in range(NT):
            S.dma(Y[s, i * 128:(i + 1) * 128, :], H[:, i, :], r=[H.b(i)])
        S.phase_reset()
    S.emit()
    return nc, S


N_CORES = 8
_PROG = {}


def kernel(**inputs):
    x = np.ascontiguousarray(inputs["x"], dtype=np.float32)
    mem = np.ascontiguousarray(inputs["mem"], dtype=np.float32)
    B = x.shape[0]
    per = B // N_CORES
    if per not in _PROG:
        _PROG[per] = build_program(per)[0]
    nc = _PROG[per]
    consts = host_consts()
    params = {k: np.ascontiguousarray(inputs[k], dtype=np.float32) for k in PARAM_SHAPES}
    in_maps = []
    for c in range(N_CORES):
        m = {"x": x[c * per:(c + 1) * per], "mem": mem[c * per:(c + 1) * per]}
        m.update(params)
        m.update(consts)
        in_maps.append(m)
    res = run_bass_kernel_spmd(nc, in_maps, core_ids=list(range(N_CORES)))
    return np.concatenate([r["y"] for r in res.results], axis=0)
```

```python
from contextlib import ExitStack
import numpy as np
import concourse.bass as bass
import concourse.mybir as mybir
from concourse.bass_utils import run_bass_kernel_spmd

F32 = mybir.dt.float32
BF16 = mybir.dt.bfloat16
AF = mybir.ActivationFunctionType
ALU = mybir.AluOpType
AX = mybir.AxisListType

ENGINES = ["pe", "act", "dve", "pool", "sp"]
EPOCH = 30000
NDMASEM = 8
import os
NO_POOL = os.environ.get("K_NO_POOL", "1") == "1"
DBG = int(os.environ.get("K_DBG", "99"))

D_MODEL = 1024
SEQ = 2048
NT = SEQ // 128
N_MEM = 256
EVEN_IN = 2816
ODD_IN = 4376
EPS = 1e-6
NEG = -1024.0
N_CMP = 127


class Buf:
    __slots__ = ("name", "lw", "rd")

    def __init__(self, name):
        self.name = name
        self.lw = None
        self.rd = {}


class Op:
    __slots__ = ("eng", "fn", "deps", "is_dma", "needs_inc", "token", "waits", "dsem", "idx")

    def __init__(self, eng, fn, is_dma=False):
        self.eng = eng
        self.fn = fn
        self.deps = []
        self.is_dma = is_dma
        self.needs_inc = is_dma
        self.token = None
        self.waits = []
        self.dsem = None
        self.idx = None


class T:
    def __init__(self, h, name):
        self.h = h
        self.name = name
        self.whole = Buf(name)
        self.subs = {}

    def __getitem__(self, idx):
        return self.h[idx]

    def b(self, key=None):
        if key is None:
            return self.whole
        s = self.subs.get(key)
        if s is None:
            s = Buf(f"{self.name}[{key}]")
            self.subs[key] = s
        return s


class Sched:
    def __init__(self, nc):
        self.nc = nc
        self.es = ExitStack()
        self.ops = {e: [] for e in ENGINES}
        self.all_dma = []
        self.dma_since_bar = []
        self.pending = {e: [] for e in ENGINES}
        self.arena = None
        self.arena_words = 0
        self.arena_off = 0

    def sb(self, name, shape, dt):
        h = self.es.enter_context(self.nc.sbuf_tensor("sb_" + name, list(shape), dt))
        return T(h, name)

    def ps(self, name, shape, dt):
        h = self.es.enter_context(self.nc.psum_tensor("ps_" + name, list(shape), dt))
        return T(h, name)

    def make_arena(self, words):
        self.arena = self.es.enter_context(self.nc.sbuf_tensor("arena", [128, words], F32))
        self.arena_words = words
        self.arena_off = 0

    def carve(self, name, shape, dt):
        n = 1
        for s in shape[1:]:
            n *= s
        words = (n + 1) // 2 if dt == BF16 else n
        words = (words + 7) // 8 * 8
        assert self.arena_off + words <= self.arena_words, (name, self.arena_off, words, self.arena_words)
        ap = self.arena[:, self.arena_off:self.arena_off + words]
        if dt == BF16:
            ap = ap.bitcast(BF16)[:, 0:n]
        else:
            ap = ap[:, 0:n]
        self.arena_off += words
        if len(shape) == 3:
            ap = ap.rearrange("p (a b) -> p a b", a=shape[1])
        elif len(shape) == 4:
            ap = ap.rearrange("p (a b c) -> p a b c", a=shape[1], b=shape[2])
        return T(ap, name)

    def phase_reset(self, to=0):
        self.barrier()
        self.arena_off = to

    def _bufs(self, xs):
        out = []
        for x in xs or []:
            out.append(x.whole if isinstance(x, T) else x)
        return out

    def op(self, eng, fn, r=None, w=None, is_dma=False):
        if eng == "pool" and NO_POOL and not is_dma:
            eng = "dve"
        o = Op(eng, fn, is_dma)
        skey = ("dma", len(self.all_dma)) if is_dma else eng
        deps = []
        rb = self._bufs(r)
        wb = self._bufs(w)
        for b in rb:
            if b.lw is not None:
                deps.append(b.lw)
        for b in wb:
            if b.lw is not None:
                deps.append(b.lw)
            deps.extend(b.rd.values())
        if self.pending[eng]:
            deps.extend(self.pending[eng])
            self.pending[eng] = []
        for b in rb:
            b.rd[skey] = o
        for b in wb:
            b.lw = o
            b.rd = {}
        seen = set()
        for d in deps:
            if id(d) in seen or d is o:
                continue
            seen.add(id(d))
            if (not d.is_dma) and (not is_dma) and d.eng == eng and eng == "pe":
                continue
            o.deps.append(d)
        o.idx = len(self.ops[eng])
        self.ops[eng].append(o)
        if is_dma:
            self.all_dma.append(o)
            self.dma_since_bar.append(o)
        return o

    def dma(self, out, in_, r=None, w=None, q="sp", **kw):
        return self.op(q, lambda e: e.dma_start(out=out, in_=in_, **kw), r=r, w=w, is_dma=True)

    def barrier(self):
        lasts = []
        for e in ENGINES:
            for o in reversed(self.ops[e]):
                if not o.is_dma:
                    lasts.append(o)
                    break
        lasts.extend(self.dma_since_bar)
        self.dma_since_bar = []
        for e in ENGINES:
            self.pending[e] = list(self.pending[e]) + lasts

    def emit(self):
        nc = self.nc
        for e in ENGINES:
            for o in self.ops[e]:
                for d in o.deps:
                    d.needs_inc = True
        nsem_eng = {}
        for e in ENGINES:
            c = 0
            k = 0
            for o in self.ops[e]:
                if o.is_dma:
                    o.dsem = (e, k % NDMASEM)
                    k += 1
                elif o.needs_inc:
                    c += 1
                    o.token = (("e", e, (c - 1) // EPOCH), (c - 1) % EPOCH + 1)
            nsem_eng[e] = (c + EPOCH - 1) // EPOCH if c else 0
        dcount = {}
        prev_dma = {}
        for e in ENGINES:
            for o in self.ops[e]:
                if o.is_dma:
                    key = ("d",) + o.dsem
                    v = dcount.get(key, 0) + 16
                    dcount[key] = v
                    o.token = (key, v)
                    if key in prev_dma:
                        o.deps.append(prev_dma[key])
                    prev_dma[key] = o
        sems = {}
        for e in ENGINES:
            for ep in range(nsem_eng[e]):
                sems[("e", e, ep)] = self.es.enter_context(nc.semaphore(f"s_{e}_{ep}"))
        for key in dcount:
            sems[key] = self.es.enter_context(nc.semaphore(f"d_{key[1]}_{key[2]}"))
        for e in ENGINES:
            seen = {}
            for o in self.ops[e]:
                need = {}
                for d in o.deps:
                    k, v = d.token
                    if seen.get(k, 0) >= v:
                        continue
                    if need.get(k, 0) < v:
                        need[k] = v
                for k, v in need.items():
                    seen[k] = v
                o.waits = list(need.items())
        final_waits = list(dcount.items())
        self.nsems = len(sems)
        self.ninst = {e: len(self.ops[e]) for e in ENGINES}
        engmap = {"pe": "tensor", "act": "scalar", "dve": "vector", "pool": "gpsimd", "sp": "sync"}
        with nc.Block() as block:
            for e in ENGINES:
                ops = self.ops[e]

                def body(eng, ops=ops, e=e):
                    for o in ops:
                        for k, v in o.waits:
                            eng.wait_ge(sems[k], v)
                        ins = o.fn(eng)
                        if o.is_dma:
                            ins.then_inc(sems[o.token[0]], 16)
                        elif o.needs_inc:
                            ins.then_inc(sems[o.token[0]], 1)
                    if e == "sp":
                        for k, v in final_waits:
                            eng.wait_ge(sems[k], v)

                getattr(block, engmap[e])(body)
        self.es.close()


def host_consts():
    c = {}
    c["ident"] = np.eye(128, dtype=np.float32)
    half = 32
    inv = 10000.0 ** (-np.arange(half, dtype=np.float32) / half)
    pos = np.arange(SEQ, dtype=np.float32)
    ang = pos[:, None] * inv[None, :]
    c["rope_cs"] = np.concatenate([np.cos(ang), np.sin(ang), -np.sin(ang)], axis=1).astype(np.float32)
    cend = (np.arange(N_CMP) * 16 + 31).astype(np.float32)
    angc = cend[:, None] * inv[None, :]
    c["rope_cmp"] = np.concatenate([np.cos(angc), np.sin(angc), -np.sin(angc)], axis=1).astype(np.float32)
    a = np.arange(128)[:, None]
    b = np.arange(128)[None, :]
    c["mdiag"] = np.where(a <= b, 0.0, NEG).astype(np.float32)
    c["mprev"] = np.where(a > b, 0.0, NEG).astype(np.float32)
    j = np.arange(N_CMP)[:, None, None]
    i = np.arange(NT)[None, :, None]
    bb = np.arange(128)[None, None, :]
    c["cmpmask"] = ((16 * j + 31) <= (128 * i + bb)).astype(np.float32)
    s = np.arange(SEQ)[None, :]
    js = np.arange(32)[:, None]
    c["selE"] = ((s // 64) == js).astype(np.float32)
    n = np.arange(N_CMP)[:, None]
    jj = np.arange(32)[None, :]
    c["ovl"] = ((16 * n < 64 * jj + 64) & (16 * n + 32 > 64 * jj)).astype(np.float32)
    t = np.arange(SEQ)[:, None]
    cur = t // 64
    forced = (jj == 0) | (jj == cur)
    valid = jj <= cur
    c["seladj"] = np.where(forced, 1e4, np.where(valid, 0.0, -1e4)).astype(np.float32)
    tri = (np.arange(64)[:, None] <= np.arange(64)[None, :]).astype(np.float32)
    c["tri64"] = np.concatenate([tri, tri], axis=0)
    return c


CONST_SHAPES = {"ident": [128, 128], "rope_cs": [SEQ, 96], "rope_cmp": [N_CMP, 96], "mdiag": [128, 128],
                "mprev": [128, 128], "cmpmask": [N_CMP, NT, 128], "selE": [32, SEQ], "ovl": [N_CMP, 32],
                "seladj": [SEQ, 32], "tri64": [128, 64]}

PARAM_SHAPES = {
    "norm_g": [2, 1024], "mem_norm_g": [2, 1024], "mem_w_kv": [2, 1024, 512], "mem_qn": [2, 64], "mem_kn": [2, 64],
    "ev_w_in": [1, 1024, 2816], "ev_w_out": [1, 1280, 1024], "a_qn": [1, 64], "a_kn": [1, 64], "a_sinks": [1, 8],
    "b_conv_w": [1, 4, 512], "b_conv_b": [1, 512], "b_w_r": [1, 8, 64, 64], "b_b_r": [1, 512],
    "b_w_i": [1, 8, 64, 64], "b_b_i": [1, 512], "b_lambda": [1, 512], "od_w_in": [1, 1024, 4376],
    "od_w_out": [1, 1280, 1024], "c_lb": [2, 512], "c_onorm": [1, 128], "d_qn": [1, 64], "d_kn_cmp": [1, 64],
    "d_kn_slc": [1, 64], "d_kn_win": [1, 64], "d_pe_k": [1, 32, 64], "d_pe_v": [1, 32, 64],
    "d_w1k": [1, 2048, 128], "d_w2k": [1, 128, 64], "d_w1v": [1, 2048, 128], "d_w2v": [1, 128, 64],
}


def build_program(nseq, layers=(0, 1), mix0=("A", "B", "M"), mix1=("C", "D", "M")):
    nc = bass.Bass("TRN2", target_bir_lowering=False)
    D = {}
    D["x"] = nc.dram_tensor("x", [nseq, SEQ, D_MODEL], F32, kind="ExternalInput").ap()
    D["mem"] = nc.dram_tensor("mem", [nseq, N_MEM, D_MODEL], F32, kind="ExternalInput").ap()
    for k, shp in PARAM_SHAPES.items():
        D[k] = nc.dram_tensor(k, shp, F32, kind="ExternalInput").ap()
    for k, shp in CONST_SHAPES.items():
        D[k] = nc.dram_tensor(k, shp, F32, kind="ExternalInput").ap()
    Y = nc.dram_tensor("y", [nseq, SEQ, D_MODEL], F32, kind="ExternalOutput").ap()
    W_IN = [nc.dram_tensor("w_in0s", [1024, EVEN_IN], BF16, kind="Internal").ap(),
            nc.dram_tensor("w_in1s", [1024, ODD_IN], BF16, kind="Internal").ap()]
    W_OUT = [nc.dram_tensor("w_out0s", [1280, 1024], BF16, kind="Internal").ap(),
             nc.dram_tensor("w_out1s", [1280, 1024], BF16, kind="Internal").ap()]
    W_KV = [nc.dram_tensor("w_kv0s", [1024, 512], BF16, kind="Internal").ap(),
            nc.dram_tensor("w_kv1s", [1024, 512], BF16, kind="Internal").ap()]
    W_1 = {"k": nc.dram_tensor("w1ks", [2048, 128], BF16, kind="Internal").ap(),
           "v": nc.dram_tensor("w1vs", [2048, 128], BF16, kind="Internal").ap()}

    S = Sched(nc)
    op = S.op

    H = S.sb("H", [128, NT, 1024], F32)
    xnT = S.sb("xnT", [128, 8, SEQ], BF16)
    ident = S.sb("ident", [128, 128], BF16)
    ROPE = S.sb("ROPE", [128, NT, 96], F32)
    mdiag = S.sb("mdiag", [128, 128], BF16)
    mprev = S.sb("mprev", [128, 128], BF16)
    normg = S.sb("normg", [128, 2, 8], F32)
    memg = S.sb("memg", [128, 2, 8], F32)
    GN = S.sb("GN", [128, 10, 64], F32)
    GI = {"mem_qn0": 0, "mem_qn1": 1, "mem_kn0": 2, "mem_kn1": 3, "a_qn": 4, "a_kn": 5, "d_qn": 6, "d_kn_slc": 7,
          "d_kn_win": 8, "d_kn_cmp": 9}
    COG = S.sb("COG", [128, 128], F32)
    LB = S.sb("LB", [128, 2, 4], F32)
    ones512 = S.sb("ones512", [128, 512], F32)
    esink = S.sb("esink", [128, 8], F32)
    ss16 = S.sb("ss16", [128, NT], F32)
    rstd16 = S.sb("rstd16", [128, NT], F32)
    tA = S.sb("tA", [128, 1024], F32)
    tB = S.sb("tB", [128, 512], F32)
    tC = S.sb("tC", [128, 512], F32)
    tD = S.sb("tD", [128, 512], F32)
    tE = S.sb("tE", [128, 512], F32)
    tF = S.sb("tF", [128, 512], F32)
    xnb = S.sb("xnb", [128, 1024], BF16)
    sm8 = S.sb("sm8", [128, 8, 8], F32)
    Pb = S.sb("Pb", [128, 2, 512], BF16)
    ob16 = S.sb("ob16", [128, 1280], BF16)
    oT = S.sb("oT", [128, 10, 128], BF16)
    qrot = S.sb("qrot", [128, 512], BF16)
    qz = S.sb("qz", [128, 2, 4, 128], BF16)
    krot = S.sb("krot", [128, 256], BF16)
    pZ = S.ps("pZ", [128, 512], F32)
    pS0 = S.ps("pS0", [128, 512], F32)
    pS1 = S.ps("pS1", [128, 512], F32)
    pSb = [pS0, pS1]
    pO0 = S.ps("pO0", [128, 4, 128], F32)
    pO1 = S.ps("pO1", [128, 4, 128], F32)
    pOb = [pO0, pO1]
    pT = S.ps("pT", [128, 8, 128], BF16)
    pY0 = S.ps("pY0", [128, 512], F32)
    pY1 = S.ps("pY1", [128, 512], F32)
    pYb = [pY0, pY1]

    ARENA_WORDS = 18 * 1024
    S.make_arena(ARENA_WORDS)

    def mm(out, lhsT, rhs, start, stop, r, w, skip=False):
        if skip:
            return op("pe", lambda e: e.matmul(out=out, lhsT=lhsT, rhs=rhs, start=start, stop=stop,
                                               skip_group_check=True), r=r, w=w)
        return op("pe", lambda e: e.matmul(out=out, lhsT=lhsT, rhs=rhs, start=start, stop=stop), r=r, w=w)

    def tr(out, in_, npart, r, w):
        return op("pe", lambda e: e.transpose(out=out, in_=in_, identity=ident[0:npart, 0:npart]), r=list(r) + [ident], w=w)

    def act(out, in_, func, r, w, scale=1.0, bias=0.0, accum=None):
        if accum is None:
            return op("act", lambda e: e.activation(out=out, in_=in_, func=func, scale=scale, bias=bias), r=r, w=w)
        return op("act", lambda e: e.activation(out=out, in_=in_, func=func, scale=scale, bias=bias, accum_out=accum), r=r, w=w)

    def tt(eng, out, in0, in1, o, r, w):
        return op(eng, lambda e: e.tensor_tensor(out=out, in0=in0, in1=in1, op=o), r=r, w=w)

    def ts(eng, out, in0, s1, s2, o0, o1, r, w):
        if s2 is None:
            return op(eng, lambda e: e.tensor_scalar(out=out, in0=in0, scalar1=s1, scalar2=None, op0=o0), r=r, w=w)
        return op(eng, lambda e: e.tensor_scalar(out=out, in0=in0, scalar1=s1, scalar2=s2, op0=o0, op1=o1), r=r, w=w)

    def stt(eng, out, in0, sc, in1, o0, o1, r, w):
        return op(eng, lambda e: e.scalar_tensor_tensor(out=out, in0=in0, scalar=sc, in1=in1, op0=o0, op1=o1), r=r, w=w)

    def cp(eng, out, in_, r, w):
        if eng == "act":
            return act(out, in_, AF.Copy, r, w)
        return op(eng, lambda e: e.tensor_copy(out=out, in_=in_), r=r, w=w)

    def recip(out, in_, r, w):
        return op("dve", lambda e: e.reciprocal(out=out, in_=in_), r=r, w=w)

    def memset(eng, ap, val, w):
        return op(eng, lambda e: e.memset(ap, val), w=w)

    def rstd_from_ss(ap, n_mean, r, w):
        act(ap, ap, AF.Ln, r, w, scale=1.0 / n_mean, bias=EPS)
        act(ap, ap, AF.Exp, w, w, scale=-0.5)

    def bc3(ap2, n):
        return ap2.unsqueeze(2).broadcast_to([ap2.shape[0], ap2.shape[1], n])

    def bch(ap2, nh):
        return ap2.unsqueeze(1).broadcast_to([ap2.shape[0], nh, ap2.shape[1]])

    stage = S.carve("stage0", [128, 2048], F32)
    stage1 = S.carve("stage1", [128, 2048], F32)
    stb0 = S.carve("stb0", [128, 2048], BF16)
    stb1 = S.carve("stb1", [128, 2048], BF16)
    stages = [(stage, stb0), (stage1, stb1)]

    S.dma(stage[:, 0:128], D["ident"], w=[stage])
    cp("dve", ident[:], stage[:, 0:128], [stage], [ident])
    S.dma(stage[:, 0:128], D["mdiag"], w=[stage])
    cp("dve", mdiag[:], stage[:, 0:128], [stage], [mdiag])
    S.dma(stage[:, 0:128], D["mprev"], w=[stage])
    cp("dve", mprev[:], stage[:, 0:128], [stage], [mprev])
    memset("dve", qz[:], 0.0, [qz.b(0), qz.b(1)])
    S.dma(ROPE[:], D["rope_cs"].rearrange("(i p) f -> p i f", p=128), w=[ROPE])
    S.dma(normg[:], D["norm_g"].rearrange("l (c p) -> p l c", p=128), w=[normg], allow_slow_non_contiguous=True)
    S.dma(memg[:], D["mem_norm_g"].rearrange("l (c p) -> p l c", p=128), w=[memg], allow_slow_non_contiguous=True)
    for nm, gi in GI.items():
        if nm.startswith("mem_"):
            src = D[nm[:-1]][int(nm[-1])]
        else:
            src = D[nm][0]
        S.dma(GN[:, gi, :], src.partition_broadcast(128), w=[GN.b(gi)])
    S.dma(esink[:], D["a_sinks"][0].partition_broadcast(128), w=[esink])
    act(esink[:], esink[:], AF.Exp, [esink], [esink])
    S.dma(COG[:], D["c_onorm"][0].partition_broadcast(128), w=[COG])
    memset("dve", ones512[:], 1.0, [ones512])
    S.dma(LB[:, 0, :], D["c_lb"][0].rearrange("(h p) -> p h", p=128), w=[LB.b(0)], allow_slow_non_contiguous=True)
    S.dma(LB[:, 1, :], D["c_lb"][1].rearrange("(h p) -> p h", p=128), w=[LB.b(1)], allow_slow_non_contiguous=True)
    tt("dve", LB[:, 0, :], LB[:, 0, :], LB[:, 1, :], ALU.subtract, [LB.b(0), LB.b(1)], [LB.b(0)])
    act(LB[:, 0, :], LB[:, 0, :], AF.Exp, [LB.b(0)], [LB.b(0)])
    ts("dve", LB[:, 0, :], LB[:, 0, :], 1.0, None, ALU.add, None, [LB.b(0)], [LB.b(0)])
    recip(LB[:, 0, :], LB[:, 0, :], [LB.b(0)], [LB.b(0)])
    ts("dve", LB[:, 1, :], LB[:, 0, :], -1.0, 1.0, ALU.mult, ALU.add, [LB.b(0)], [LB.b(1)])

    cnt = [0]

    def conv_weight(src, dst, R, C, gt=None, l=0, perm0=None):
        for rc in range(R // 128):
            for c0 in range(0, C, 2048):
                cw = min(2048, C - c0)
                sf, sbf = stages[cnt[0] % 2]
                eng = "dve"
                cnt[0] += 1
                S.dma(sf[:, 0:cw], src[rc * 128:(rc + 1) * 128, c0:c0 + cw], w=[sf])
                if gt is not None:
                    ts(eng, sbf[:, 0:cw], sf[:, 0:cw], gt[:, l, rc:rc + 1], None, ALU.mult, None, [sf, gt], [sbf])
                else:
                    cp(eng, sbf[:, 0:cw], sf[:, 0:cw], [sf], [sbf])
                rows = slice(rc * 128, (rc + 1) * 128)
                if perm0 is not None and c0 <= perm0 < c0 + cw:
                    p0 = perm0 - c0
                    if p0 > 0:
                        S.dma(dst[rows, c0:c0 + p0], sbf[:, 0:p0], r=[sbf])
                    for w_ in range(2):
                        S.dma(dst[rows, perm0:perm0 + 512].rearrange("r (pr w d) -> r w pr d", pr=4, w=2)[:, w_],
                              sbf[:, p0 + w_ * 256:p0 + (w_ + 1) * 256].rearrange("p (pr d) -> p pr d", pr=4), r=[sbf])
                    if p0 + 512 < cw:
                        S.dma(dst[rows, perm0 + 512:c0 + cw], sbf[:, p0 + 512:cw], r=[sbf])
                else:
                    S.dma(dst[rows, c0:c0 + cw], sbf[:, 0:cw], r=[sbf])

    if 0 in layers:
        conv_weight(D["ev_w_in"][0], W_IN[0], 1024, EVEN_IN, normg, 0, perm0=0)
        conv_weight(D["ev_w_out"][0], W_OUT[0], 1280, 1024)
        conv_weight(D["mem_w_kv"][0], W_KV[0], 1024, 512, memg, 0)
    if 1 in layers:
        conv_weight(D["od_w_in"][0], W_IN[1], 1024, ODD_IN, normg, 1, perm0=2048)
        conv_weight(D["od_w_out"][0], W_OUT[1], 1280, 1024)
        conv_weight(D["mem_w_kv"][1], W_KV[1], 1024, 512, memg, 1)
        conv_weight(D["d_w1k"][0], W_1["k"], 2048, 128)
        conv_weight(D["d_w1v"][0], W_1["v"], 2048, 128)
    S.phase_reset()

    def load_slab(dst, src_w, c0, ncols, key=None):
        S.dma(dst[:, :, 0:ncols], src_w.rearrange("(c p) n -> p c n", p=128)[:, :, c0:c0 + ncols],
              w=[dst.b(key)])

    def proj_tok(ps_ap, slab, col0, ncols, i, w, skey=None):
        for c in range(8):
            mm(ps_ap, xnT[:, c, i * 128:(i + 1) * 128], slab[:, c, col0:col0 + ncols], c == 0, c == 7,
               [xnT.b(i), slab.b(skey)], w)

    def proj_feat(ps_ap, slab, col0, g, w, skey=None):
        for c in range(8):
            mm(ps_ap, slab[:, c, col0:col0 + 128], xnT[:, c, g * 512:(g + 1) * 512], c == 0, c == 7,
               [xnT.b(4 * g), xnT.b(4 * g + 1), xnT.b(4 * g + 2), xnT.b(4 * g + 3), slab.b(skey)], w)

    def silu_psum(out_ap, zp, n, rz, wout, t1, t2, np_=128):
        a1 = t1[0:np_, 0:n]
        act(a1, zp, AF.Exp, rz, [t1], scale=-1.0)
        act(a1, a1, AF.Ln, [t1], [t1], bias=1.0)
        act(a1, a1, AF.Exp, [t1], [t1], scale=-1.0)
        tt("dve", out_ap, zp, a1, ALU.mult, list(rz) + [t1], wout)

    def sigmoid_act(ap, r, w):
        act(ap, ap, AF.Ln, r, w, bias=1.0)
        act(ap, ap, AF.Exp, w, w, scale=-1.0)

    def norm_rope(zp, nh, gi, rope_ap, out_ap, rz, wout, slot, np_=128):
        n = nh * 64
        z3 = zp.rearrange("p (h d) -> p h d", h=nh)
        ssq = sm8[0:np_, slot, 0:nh]
        act(tA[0:np_, 0:n], zp, AF.Square, rz, [tA])
        op("dve", lambda e: e.tensor_reduce(out=ssq, in_=tA[0:np_, 0:n].rearrange("p (h d) -> p h d", h=nh),
                                            axis=AX.X, op=ALU.add), r=[tA], w=[sm8.b(slot)])
        rstd_from_ss(ssq, 64.0, [sm8.b(slot)], [sm8.b(slot)])
        zg = tB[0:np_, 0:n].rearrange("p (h d) -> p h d", h=nh)
        tt("dve", zg, z3, bch(GN[0:np_, gi, :], nh), ALU.mult, list(rz) + [GN.b(gi)], [tB])
        if rope_ap is None:
            tt("dve", out_ap, zg, bc3(ssq, 64), ALU.mult, [tB, sm8.b(slot)], wout)
            return
        zg4 = tB[0:np_, 0:n].rearrange("p (h a f) -> p h a f", h=nh, a=2)
        a4 = tC[0:np_, 0:n].rearrange("p (h a f) -> p h a f", h=nh, a=2)
        b4 = tD[0:np_, 0:n].rearrange("p (h a f) -> p h a f", h=nh, a=2)
        cos4 = rope_ap[:, 0:32].unsqueeze(1).unsqueeze(1).broadcast_to([np_, nh, 2, 32])
        sin3 = bch(rope_ap[:, 32:64], nh)
        nsin3 = bch(rope_ap[:, 64:96], nh)
        tt("dve", a4, zg4, cos4, ALU.mult, [tB], [tC])
        tt("dve", b4[:, :, 0, :], zg4[:, :, 1, :], nsin3, ALU.mult, [tB], [tD.b(0)])
        tt("dve", b4[:, :, 1, :], zg4[:, :, 0, :], sin3, ALU.mult, [tB], [tD.b(1)])
        tt("dve", tC[0:np_, 0:n], tC[0:np_, 0:n], tD[0:np_, 0:n], ALU.add, [tC, tD.b(0), tD.b(1)], [tC])
        tt("dve", out_ap, tC[0:np_, 0:n].rearrange("p (h d) -> p h d", h=nh), bc3(ssq, 64), ALU.mult,
           [tC, sm8.b(slot)], wout)

    def h_update(i, nchunks, wo, wkey=None):
        for half in range(2):
            for c in range(nchunks):
                mm(pYb[half][:], oT[:, c, :], wo[:, c, half * 512:(half + 1) * 512], c == 0, c == nchunks - 1,
                   [oT, wo.b(wkey)], [pYb[half]])
            tt("dve", H[:, i, half * 512:(half + 1) * 512], H[:, i, half * 512:(half + 1) * 512], pYb[half][:], ALU.add,
               [H.b(i), pYb[half]], [H.b(i)])

    def transposes_to_oT(nch, src_cols0=0, dst0=0):
        for c in range(nch):
            tr(pT[:, c, :], ob16[:, src_cols0 + c * 128: src_cols0 + (c + 1) * 128], 128, [ob16], [pT])
        cp("act", oT[:, dst0:dst0 + nch, :], pT[:, 0:nch, :], [pT], [oT])

    def rmsnorm_to_xnT():
        memset("pool", ss16[:], 0.0, [ss16])
        for i in range(NT):
            act(tA[:], H[:, i, :], AF.Square, [H.b(i)], [tA, ss16], accum=ss16[:, i:i + 1])
        cp("dve", rstd16[:], ss16[:], [ss16], [rstd16])
        rstd_from_ss(rstd16[:], 1024.0, [rstd16], [rstd16])
        for i in range(NT):
            ts("dve", xnb[:], H[:, i, :], rstd16[:, i:i + 1], None, ALU.mult, None, [H.b(i), rstd16], [xnb])
            for c in range(8):
                tr(pT[:, c, :], xnb[:, c * 128:(c + 1) * 128], 128, [xnb], [pT])
            cp("act", xnT[:, :, i * 128:(i + 1) * 128], pT[:], [pT], [xnT.b(i)])

    def mem_kv(s, l, kmT, VM1, memf, memT, wkv):
        load_slab(wkv, W_KV[l], 0, 512)
        memset("dve", VM1[:, :, :, 64:65], 1.0, [VM1.b("ones")])
        for nt in range(2):
            S.dma(memf[:], D["mem"][s, nt * 128:(nt + 1) * 128, :], w=[memf])
            memset("dve", sm8[:, 7, 0:1], 0.0, [sm8.b(7)])
            act(tA[:], memf[:], AF.Square, [memf], [tA, sm8.b(7)], accum=sm8[:, 7, 0:1])
            rstd_from_ss(sm8[:, 7, 0:1], 1024.0, [sm8.b(7)], [sm8.b(7)])
            ts("dve", xnb[:], memf[:], sm8[:, 7, 0:1], None, ALU.mult, None, [memf, sm8.b(7)], [xnb])
            for c in range(8):
                tr(pT[:, c, :], xnb[:, c * 128:(c + 1) * 128], 128, [xnb], [pT])
            cp("act", memT[:, :, nt * 128:(nt + 1) * 128], pT[:], [pT], [memT.b(nt)])
            for c in range(8):
                mm(pZ[:], memT[:, c, nt * 128:(nt + 1) * 128], wkv[:, c, 0:512], c == 0, c == 7, [memT.b(nt), wkv], [pZ])
            norm_rope(pZ[:, 0:256], 4, GI["mem_kn%d" % l], None, krot[:].rearrange("p (h d) -> p h d", h=4),
                      [pZ], [krot], 6)
            cp("act", VM1[:, nt, :, 0:64], pZ[:, 256:512].rearrange("p (h d) -> p h d", h=4), [pZ], [VM1.b(nt)])
            for pr in range(2):
                tr(pT[:, pr, :], krot[:, pr * 128:(pr + 1) * 128], 128, [krot], [pT])
            cp("act", kmT[:, :, nt * 128:(nt + 1) * 128], pT[:, 0:2, :], [pT], [kmT.b(nt)])

    def mem_attn_tile(i, qz_ap, qz_r, l, kmT, VM1, gate_ap, gate_r, out_cols0):
        norm_rope(qz_ap, 4, GI["mem_qn%d" % l], None, qrot[:, 0:256].rearrange("p (h d) -> p h d", h=4),
                  qz_r, [qrot], 5)
        for pr in range(2):
            tr(pT[:, pr, :], qrot[:, pr * 128:(pr + 1) * 128], 128, [qrot], [pT])
        cp("act", qz[0:64, 0, 0:2, :], pT[0:64, 0:2, :], [pT], [qz.b(0)])
        cp("act", qz[64:128, 1, 0:2, :], pT[64:128, 0:2, :], [pT], [qz.b(1)])
        if DBG < 2:
            return
        for h in range(4):
            pr, hf = h // 2, h % 2
            for nt in range(2):
                mm(pSb[nt][:, h * 128:(h + 1) * 128], kmT[:, pr, nt * 128:(nt + 1) * 128],
                   qz[:, hf, pr, :], True, True, [kmT.b(nt), qz.b(hf)], [pSb[nt]])
        for nt in range(2):
            act(Pb[:, nt, :], pSb[nt][:], AF.Exp, [pSb[nt]], [Pb.b(nt)], scale=0.125)
        if DBG < 3:
            return
        for h in range(4):
            for nt in range(2):
                mm(pO0[:, h, 0:65], Pb[:, nt, h * 128:(h + 1) * 128], VM1[:, nt, h, :], nt == 0, nt == 1,
                   [Pb.b(nt), VM1.b(nt), VM1.b("ones")], [pO0])
        cp("dve", sm8[:, 4, 0:4], pO0[:, :, 64], [pO0], [sm8.b(4)])
        recip(sm8[:, 4, 0:4], sm8[:, 4, 0:4], [sm8.b(4)], [sm8.b(4)])
        tt("dve", tE[:, 0:256].rearrange("p (h d) -> p h d", h=4), pO0[:, :, 0:64], bc3(sm8[:, 4, 0:4], 64), ALU.mult,
           [pO0, sm8.b(4)], [tE])
        tt("dve", ob16[:, out_cols0:out_cols0 + 256], tE[:, 0:256], gate_ap, ALU.mult, [tE] + list(gate_r), [ob16])

    def layer0(s):
        l = 0
        rmsnorm_to_xnT()
        S.phase_reset()
        if "B" in mix0:
            PB = S.carve("PB", [128, 4, 8], F32)
            PD = S.carve("PD", [128, 4, 4], F32)
            BDf = S.carve("BDf", [128, 2, 4, 128], F32)
            BD = S.carve("BD", [128, 2, 4, 128], BF16)
            XB = S.carve("XB", [128, 3 + SEQ], F32)
            hB = S.carve("hB", [128, SEQ], F32)
            mixB = S.carve("mixB", [128, 4, SEQ], BF16)
            wsl = [S.carve("wslB0", [128, 8, 256], BF16), S.carve("wslB1", [128, 8, 256], BF16)]
            woB = S.carve("woB", [128, 4, 1024], BF16)
            xcb = S.carve("xcb", [128, 512], BF16)
            for j in range(4):
                S.dma(PB[:, :, j], D["b_conv_w"][0, j].rearrange("(c p) -> p c", p=128), w=[PB.b(j)],
                      allow_slow_non_contiguous=True)
            for j, nm in enumerate(["b_conv_b", "b_b_r", "b_b_i", "b_lambda"]):
                S.dma(PB[:, :, 4 + j], D[nm][0].rearrange("(c p) -> p c", p=128), w=[PB.b(4 + j)],
                      allow_slow_non_contiguous=True)
            ts("dve", PD[:, :, 0], PB[:, :, 5], -1.0, None, ALU.mult, None, [PB.b(5)], [PD.b(0)])
            ts("dve", PD[:, :, 1], PB[:, :, 6], -1.0, None, ALU.mult, None, [PB.b(6)], [PD.b(1)])
            act(PD[:, :, 2], PB[:, :, 7], AF.Exp, [PB.b(7)], [PD.b(2)], scale=-1.0)
            act(PD[:, :, 2], PD[:, :, 2], AF.Ln, [PD.b(2)], [PD.b(2)], bias=1.0)
            ts("dve", PD[:, :, 2], PD[:, :, 2], -8.0, None, ALU.mult, None, [PD.b(2)], [PD.b(2)])
            bdkeys = [BDf.b((a_, b_, c_)) for a_ in range(2) for b_ in range(4) for c_ in range(2)]
            memset("pool", BDf[:], 0.0, bdkeys)
            for gi_, nm in enumerate(["b_w_r", "b_w_i"]):
                for cb in range(4):
                    for hb in range(2):
                        S.dma(BDf[64 * hb:64 * hb + 64, gi_, cb, 64 * hb:64 * hb + 64], D[nm][0, 2 * cb + hb],
                              w=[BDf.b((gi_, cb, hb))])
            cp("dve", BD[:], BDf[:], bdkeys, [BD])
            memset("pool", XB[:, 0:3], 0.0, [XB.b("pad")])
            S.dma(woB[:], W_OUT[l].rearrange("(c p) n -> p c n", p=128)[:, 4:8, :], w=[woB])
            for cb in range(4):
                wS = wsl[cb % 2]
                S.dma(wS[:, :, 0:128], W_IN[l].rearrange("(c p) n -> p c n", p=128)[:, :, 1280 + cb * 128:1280 + (cb + 1) * 128],
                      w=[wS.b("x")])
                S.dma(wS[:, :, 128:256], W_IN[l].rearrange("(c p) n -> p c n", p=128)[:, :, 1792 + cb * 128:1792 + (cb + 1) * 128],
                      w=[wS.b("g")])
                for g in range(4):
                    sl = slice(3 + g * 512, 3 + (g + 1) * 512)
                    proj_feat(pZ[:], wS, 0, g, [pZ], skey="x")
                    cp("act", XB[:, sl], pZ[:], [pZ], [XB.b(g)])
                    rd = [XB.b(g), XB.b(g - 1) if g > 0 else XB.b("pad"), PB.b(0), PB.b(1), PB.b(2), PB.b(3), PB.b(4)]
                    xc = tB
                    ts("dve", xc[:], XB[:, g * 512:g * 512 + 512], PB[:, cb, 0:1], PB[:, cb, 4:5], ALU.mult, ALU.add, rd, [tB])
                    for j in (1, 2, 3):
                        stt("dve", xc[:], XB[:, g * 512 + j:g * 512 + j + 512], PB[:, cb, j:j + 1], xc[:], ALU.mult, ALU.add,
                            rd + [tB], [tB])
                    cp("pool", xcb[:], xc[:], [tB], [xcb])
                    mm(pS0[:], BD[:, 0, cb, :], xcb[:], True, True, [BD, xcb], [pS0])
                    mm(pS1[:], BD[:, 1, cb, :], xcb[:], True, True, [BD, xcb], [pS1])
                    act(tC[:], pS0[:], AF.Exp, [pS0, PD.b(0)], [tC], scale=-1.0, bias=PD[:, cb, 0:1])
                    sigmoid_act(tC[:], [tC], [tC])
                    act(tC[:], tC[:], AF.Exp, [tC, PD.b(2)], [tC], scale=PD[:, cb, 2:3])
                    act(tD[:], pS1[:], AF.Exp, [pS1, PD.b(1)], [tD], scale=-1.0, bias=PD[:, cb, 1:2])
                    sigmoid_act(tD[:], [tD], [tD])
                    act(tE[:], tC[:], AF.Square, [tC], [tE])
                    act(tE[:], tE[:], AF.Ln, [tE], [tE], scale=-1.0, bias=1.0)
                    act(tE[:], tE[:], AF.Exp, [tE], [tE], scale=0.5)
                    tt("pool", tD[:], tD[:], xc[:], ALU.mult, [tD, tB], [tD])
                    tt("pool", tD[:], tD[:], tE[:], ALU.mult, [tD, tE], [tD])
                    init = 0.0 if g == 0 else hB[:, g * 512 - 1:g * 512]
                    op("dve", lambda e, g=g, init=init: e.tensor_tensor_scan(
                        out=hB[:, g * 512:(g + 1) * 512], data0=tC[:], data1=tD[:], initial=init,
                        op0=ALU.mult, op1=ALU.add), r=[tC, tD] + ([hB.b(g - 1)] if g > 0 else []), w=[hB.b(g)])
                    proj_feat(pZ[:], wS, 128, g, [pZ], skey="g")
                    silu_psum(tF[:], pZ[:], 512, [pZ], [tF], tE, tF)
                    tt("dve", mixB[:, cb, g * 512:(g + 1) * 512], tF[:], hB[:, g * 512:(g + 1) * 512], ALU.mult,
                       [tF, hB.b(g)], [mixB.b((cb, g))])
            for i in range(NT):
                for half in range(2):
                    for cb in range(4):
                        mm(pYb[half][:], mixB[:, cb, i * 128:(i + 1) * 128], woB[:, cb, half * 512:(half + 1) * 512],
                           cb == 0, cb == 3, [mixB.b((cb, i // 4)), woB], [pYb[half]])
                    tt("dve", H[:, i, half * 512:(half + 1) * 512], H[:, i, half * 512:(half + 1) * 512], pYb[half][:],
                       ALU.add, [H.b(i), pYb[half]], [H.b(i)])
            S.phase_reset()
        doA, doM = "A" in mix0, "M" in mix0
        if doA or doM:
            kT = S.carve("kT", [128, SEQ], BF16)
            V1 = S.carve("V1", [128, NT, 2, 65], BF16)
            wq = S.carve("wq", [128, 8, 512], BF16)
            wga = S.carve("wga", [128, 8, 512], BF16)
            wqgm = S.carve("wqgm", [128, 8, 512], BF16)
            woAM = S.carve("woAM", [128, 6, 1024], BF16)
            kmT = S.carve("kmT", [128, 2, N_MEM], BF16)
            VM1 = S.carve("VM1", [128, 2, 4, 65], BF16)
            memT = S.carve("memT", [128, 8, N_MEM], BF16)
            memf = S.carve("memf", [128, 1024], F32)
            gat = S.carve("gat", [128, 768], BF16)
            if doM:
                mem_kv(s, l, kmT, VM1, memf, memT, wq)
            if doA:
                load_slab(wga, W_IN[l], 512, 256)
                memset("dve", V1[:, :, :, 64:65], 1.0, [V1.b("ones")])
                for i in range(NT):
                    proj_tok(pZ[:, 0:256], wga, 0, 256, i, [pZ])
                    norm_rope(pZ[:, 0:128], 2, GI["a_kn"], ROPE[:, i, :], krot[:, 0:128].rearrange("p (h d) -> p h d", h=2),
                              [pZ], [krot], 0)
                    cp("act", V1[:, i, :, 0:64], pZ[:, 128:256].rearrange("p (h d) -> p h d", h=2), [pZ], [V1.b(i)])
                    tr(pT[:, 0, :], krot[:, 0:128], 128, [krot], [pT])
                    cp("act", kT[:, i * 128:(i + 1) * 128], pT[:, 0, :], [pT], [kT.b(i)])
            if doA:
                load_slab(wq, W_IN[l], 0, 512)
                load_slab(wga, W_IN[l], 768, 512)
                S.dma(woAM[:, 0:4, :], W_OUT[l].rearrange("(c p) n -> p c n", p=128)[:, 0:4, :], w=[woAM.b("a")])
            if doM:
                load_slab(wqgm, W_IN[l], 2304, 512)
                S.dma(woAM[:, 4:6, :], W_OUT[l].rearrange("(c p) n -> p c n", p=128)[:, 8:10, :], w=[woAM.b("m")])
            for i in range(NT):
                nch = 0
                if doA:
                    proj_tok(pZ[:], wga, 0, 512, i, [pZ])
                    silu_psum(gat[:, 0:512], pZ[:], 512, [pZ], [gat.b("a")], tE, tF)
                    proj_tok(pZ[:], wq, 0, 512, i, [pZ])
                    norm_rope(pZ[:], 8, GI["a_qn"], ROPE[:, i, :], qrot[:].rearrange("p (h d) -> p h d", h=8), [pZ], [qrot], 1)
                    for pr in range(4):
                        tr(pT[:, pr, :], qrot[:, pr * 128:(pr + 1) * 128], 128, [qrot], [pT])
                    cp("act", qz[0:64, 0, :, :], pT[0:64, 0:4, :], [pT], [qz.b(0)])
                    cp("act", qz[64:128, 1, :, :], pT[64:128, 0:4, :], [pT], [qz.b(1)])
                    kts = [i - 1, i] if i > 0 else [i]
                    for k in range(2):
                        for n_, kt in enumerate(kts):
                            ps = pSb[n_]
                            mm(ps[:].rearrange("p (g t) -> p g t", g=4), kT[:, kt * 128:(kt + 1) * 128],
                               qz[:, k, :, :], True, False, [kT.b(kt), qz.b(k)], [ps])
                            msk = mdiag if kt == i else mprev
                            mm(ps[:].rearrange("p (g t) -> p g t", g=4), ident[:], bch(msk[:], 4), False, True,
                               [ident, msk], [ps])
                            act(Pb[:, n_, :], ps[:], AF.Exp, [ps], [Pb.b(n_)], scale=0.125)
                        for g in range(4):
                            for n_, kt in enumerate(kts):
                                mm(pOb[k][:, g, 0:65], Pb[:, n_, g * 128:(g + 1) * 128], V1[:, kt, k, :],
                                   n_ == 0, n_ == len(kts) - 1, [Pb.b(n_), V1.b(kt), V1.b("ones")], [pOb[k]])
                    for k in range(2):
                        tt("dve", sm8[:, 2, 4 * k:4 * k + 4], pOb[k][:, :, 64], esink[:, 4 * k:4 * k + 4], ALU.add,
                           [pOb[k], esink], [sm8.b((2, k))])
                        recip(sm8[:, 2, 4 * k:4 * k + 4], sm8[:, 2, 4 * k:4 * k + 4], [sm8.b((2, k))], [sm8.b((2, k))])
                        tt("dve", tE[:, 256 * k:256 * k + 256].rearrange("p (h d) -> p h d", h=4), pOb[k][:, :, 0:64],
                           bc3(sm8[:, 2, 4 * k:4 * k + 4], 64), ALU.mult, [pOb[k], sm8.b((2, k))], [tE.b(k)])
                    tt("dve", ob16[:, 0:512], tE[:], gat[:, 0:512], ALU.mult, [tE.b(0), tE.b(1), gat.b("a")], [ob16])
                    transposes_to_oT(4, 0, 0)
                    nch = 4
                if doM:
                    proj_tok(pZ[:], wqgm, 0, 512, i, [pZ])
                    silu_psum(gat[:, 512:768], pZ[:, 256:512], 256, [pZ], [gat.b("m")], tE, tF)
                    if DBG >= 1:
                        mem_attn_tile(i, pZ[:, 0:256], [pZ], l, kmT, VM1, gat[:, 512:768], [gat.b("m")], 512)
                    if DBG < 4:
                        continue
                    for c in range(2):
                        tr(pT[:, 4 + c, :], ob16[:, 512 + c * 128:512 + (c + 1) * 128], 128, [ob16], [pT])
                    cp("act", oT[:, 4:6, :], pT[:, 4:6, :], [pT], [oT])
                chunks = ([0, 1, 2, 3] if doA else []) + ([4, 5] if doM else [])
                for half in range(2):
                    for n_, c in enumerate(chunks):
                        mm(pYb[half][:], oT[:, c, :], woAM[:, c, half * 512:(half + 1) * 512], n_ == 0, n_ == len(chunks) - 1,
                           [oT, woAM.b("a"), woAM.b("m")], [pYb[half]])
                    tt("dve", H[:, i, half * 512:(half + 1) * 512], H[:, i, half * 512:(half + 1) * 512], pYb[half][:],
                       ALU.add, [H.b(i), pYb[half]], [H.b(i)])
            S.phase_reset()

    def separator():
        mm(pY1[:, 0:1], ident[:], ident[:, 0:1], True, True, [ident], [pY1])

    def hgrn2_phase(s):
        l = 1
        QT = S.carve("QT", [128, 4, 512], BF16)
        KT = S.carve("KT", [128, 4, 512], BF16)
        KH = S.carve("KH", [128, 4, 512], BF16)
        Vc = S.carve("Vc", [128, 4, 512], BF16)
        wsm = [S.carve("wsmC0", [128, 8, 256], BF16), S.carve("wsmC1", [128, 8, 256], BF16)]
        wic = S.carve("wic", [128, 8, 512], BF16)
        wgc = S.carve("wgc", [128, 8, 512], BF16)
        woC = S.carve("woC", [128, 4, 1024], BF16)
        St = S.carve("St", [128, 4, 128], F32)
        Sb = S.carve("Sb", [128, 4, 128], BF16)
        EBL = S.carve("EBL", [128, 4, 32], F32)
        ATb = S.carve("ATb", [128, 4, 128], BF16)
        KHz = S.carve("KHz", [128, 4, 2, 128], BF16)
        tri = S.carve("tri", [128, 64], F32)
        Bp = S.carve("Bp", [128, 8], F32)
        tG = S.carve("tG", [128, 512], F32)
        tH = S.carve("tH", [128, 512], F32)
        gcb = S.carve("gcb", [128, 512], BF16)
        S.dma(tri[:], D["tri64"], w=[tri])
        memset("dve", St[:], 0.0, [St])
        memset("dve", Sb[:], 0.0, [Sb])
        memset("dve", ATb[:], 0.0, [ATb.b(0), ATb.b(1)])
        memset("dve", KHz[:], 0.0, [KHz.b(0), KHz.b(1)])
        memset("dve", Bp[:], 0.0, [Bp])
        load_slab(wic, W_IN[l], 1024, 512)
        load_slab(wgc, W_IN[l], 1536, 512)
        S.dma(woC[:], W_OUT[l].rearrange("(c p) n -> p c n", p=128)[:, 0:4, :], w=[woC])
        pS0v = pS0[:].rearrange("p (h t) -> p h t", h=4)
        pS1v = pS1[:].rearrange("p (h t) -> p h t", h=4)
        for g in range(4):
            for hd in range(4):
                wS = wsm[(g * 4 + hd) % 2]
                wv = W_IN[l].rearrange("(c p) n -> p c n", p=128)
                S.dma(wS[:, :, 0:128], wv[:, :, hd * 128:(hd + 1) * 128], w=[wS.b("q")])
                S.dma(wS[:, :, 128:256], wv[:, :, 512 + hd * 128:512 + (hd + 1) * 128], w=[wS.b("f")])
                proj_feat(pZ[:], wS, 128, g, [pZ], skey="f")
                act(tB[:], pZ[:], AF.Exp, [pZ], [tB], scale=-1.0)
                sigmoid_act(tB[:], [tB], [tB])
                ts("dve", tB[:], tB[:], LB[:, 1, hd:hd + 1], LB[:, 0, hd:hd + 1], ALU.mult, ALU.add,
                   [tB, LB.b(0), LB.b(1)], [tB])
                act(tC[:], tB[:], AF.Ln, [tB], [tC])
                ts("pool", tB[:], tB[:], -1.0, 1.0, ALU.mult, ALU.add, [tB], [tB])
                op("dve", lambda e: e.tensor_tensor_scan(out=tD[:], data0=ones512[:], data1=tC[:], initial=0.0,
                                                         op0=ALU.mult, op1=ALU.add), r=[ones512, tC], w=[tD])
                tD3 = tD[:].rearrange("p (c j) -> p c j", j=64)
                cp("dve", Bp[:, 1:8], tD3[:, 0:7, 63], [tD], [Bp])
                tt("dve", tD3, tD3, bc3(Bp[:, 0:8], 64), ALU.subtract, [tD, Bp], [tD])
                act(tE[:], tD[:], AF.Exp, [tD], [tE])
                cp("dve", EBL[:, hd, 8 * g:8 * g + 8], tE[:].rearrange("p (c j) -> p c j", j=64)[:, :, 63], [tE],
                   [EBL.b((hd, g))])
                act(tF[:], tD[:], AF.Exp, [tD], [tF], scale=-1.0)
                tt("dve", KT[:, hd, :], tB[:], tF[:], ALU.mult, [tB, tF], [KT.b(hd)])
                tt("dve", tG[:].rearrange("p (c j) -> p c j", j=64), tD3, bc3(tD3[:, :, 63], 64), ALU.subtract,
                   [tD], [tG])
                act(tG[:], tG[:], AF.Exp, [tG], [tG], scale=-1.0)
                tt("dve", KH[:, hd, :], tB[:], tG[:], ALU.mult, [tB, tG], [KH.b(hd)])
                proj_feat(pZ[:], wS, 0, g, [pZ], skey="q")
                silu_psum(tH[:], pZ[:], 512, [pZ], [tH], tF, tH)
                tt("dve", QT[:, hd, :], tH[:], tE[:], ALU.mult, [tH, tE], [QT.b(hd)])
            for tl in range(4):
                i = 4 * g + tl
                proj_tok(pZ[:], wic, 0, 512, i, [pZ])
                cp("act", Vc[:, tl, :], pZ[:], [pZ], [Vc.b(tl)])
            for tl in range(4):
                i = 4 * g + tl
                cs = tl * 128
                allh = [QT.b(h_) for h_ in range(4)]
                for hd in range(4):
                    tr(pT[:, hd, :], KH[:, hd, cs:cs + 128], 128, [KH.b(hd)], [pT])
                cp("act", KHz[0:64, :, 0, :], pT[0:64, 0:4, :], [pT], [KHz.b(0)])
                cp("act", KHz[64:128, :, 1, :], pT[64:128, 0:4, :], [pT], [KHz.b(1)])
                for hd in range(4):
                    for c in range(2):
                        mm(pS0v[64 * c:64 * c + 64, hd, 64 * c:64 * c + 64], KT[:, hd, cs + 64 * c:cs + 64 * c + 64],
                           QT[:, hd, cs + 64 * c:cs + 64 * c + 64], True, True, [KT.b(hd), QT.b(hd)], [pS0])
                for c in range(2):
                    tt("dve", ATb[64 * c:64 * c + 64, :, 64 * c:64 * c + 64], pS0v[64 * c:64 * c + 64, :, 64 * c:64 * c + 64],
                       bch(tri[64 * c:64 * c + 64, :], 4), ALU.mult, [pS0, tri], [ATb.b(c)])
                for hd in range(4):
                    mm(pO0[:, hd, :], ATb[:, hd, :], Vc[:, tl, hd * 128:(hd + 1) * 128], True, True,
                       [ATb.b(0), ATb.b(1), Vc.b(tl)], [pO0])
                for c in range(2):
                    ch = 8 * g + 2 * tl + c
                    for hd in range(4):
                        mm(pO1[64 * c:64 * c + 64, hd, :], QT[:, hd, cs + 64 * c:cs + 64 * c + 64], Sb[:, hd, :], True, True,
                           [QT.b(hd), Sb], [pO1])
                    for hd in range(4):
                        mm(pS1v[:, hd, :], KHz[:, hd, c, :], Vc[:, tl, hd * 128:(hd + 1) * 128], True, True,
                           [KHz.b(c), Vc.b(tl)], [pS1])
                    tt("dve", St[:], St[:], bc3(EBL[:, :, ch], 128), ALU.mult, [St] + [EBL.b((h_, g)) for h_ in range(4)], [St])
                    tt("dve", St[:], St[:], pS1v, ALU.add, [St, pS1], [St])
                    cp("act", Sb[:], St[:], [St], [Sb])
                cp("act", tG[:], pO0[:].rearrange("p h v -> p (h v)"), [pO0], [tG])
                tt("dve", tG[:], tG[:], pO1[:].rearrange("p h v -> p (h v)"), ALU.add, [tG, pO1], [tG])
                act(tA[:, 0:512], tG[:], AF.Square, [tG], [tA])
                op("dve", lambda e: e.tensor_reduce(out=sm8[:, 3, 0:4], in_=tA[:, 0:512].rearrange("p (h v) -> p h v", h=4),
                                                    axis=AX.X, op=ALU.add), r=[tA], w=[sm8.b(3)])
                rstd_from_ss(sm8[:, 3, 0:4], 128.0, [sm8.b(3)], [sm8.b(3)])
                tG3 = tG[:].rearrange("p (h v) -> p h v", h=4)
                tt("dve", tG3, tG3, bc3(sm8[:, 3, 0:4], 128), ALU.mult, [tG, sm8.b(3)], [tG])
                tt("dve", tG3, tG3, bch(COG[:], 4), ALU.mult, [tG, COG], [tG])
                proj_tok(pZ[:], wgc, 0, 512, i, [pZ])
                silu_psum(gcb[:], pZ[:], 512, [pZ], [gcb], tH, tF)
                tt("dve", ob16[:, 0:512], tG[:], gcb[:], ALU.mult, [tG, gcb], [ob16])
                transposes_to_oT(4, 0, 0)
                h_update(i, 4, woC)

    def nsa_phase(s, doD, doM):
        l = 1
        ksT = S.carve("ksT", [128, SEQ], BF16)
        kwT = S.carve("kwT", [128, SEQ], BF16)
        Vs1 = S.carve("Vs1", [128, NT, 2, 65], BF16)
        Vw1 = S.carve("Vw1", [128, NT, 2, 65], BF16)
        kcT = S.carve("kcT", [128, 128], BF16)
        VC1 = S.carve("VC1", [128, 2, 97], BF16)
        cmpneg = S.carve("cmpneg", [128, NT, 128], BF16)
        selE = S.carve("selE", [128, SEQ], BF16)
        seladj = S.carve("seladj", [128, NT, 32], F32)
        ROPEC = S.carve("ROPEC", [128, 96], F32)
        kmT = S.carve("kmT1", [128, 2, N_MEM], BF16)
        VM1 = S.carve("VM11", [128, 2, 4, 65], BF16)
        mark = S.arena_off
        if doM:
            wkv = S.carve("wkv1", [128, 8, 512], BF16)
            memT = S.carve("memT1", [128, 8, N_MEM], BF16)
            memf = S.carve("memf1", [128, 1024], F32)
            mem_kv(s, l, kmT, VM1, memf, memT, wkv)
            S.phase_reset(mark)
        if doD:
            tmpf = S.carve("tmpf", [128, 2048], F32)
            w512 = S.carve("w512", [128, 8, 512], BF16)
            wkc = S.carve("wkc", [128, 8, 256], BF16)
            kcdT = S.carve("kcdT", [128, SEQ], BF16)
            vcdT = S.carve("vcdT", [128, SEQ], BF16)
            W1r = S.carve("W1r", [128, 32, 128], BF16)
            w2f = S.carve("w2f", [128, 2, 64], F32)
            w2b = S.carve("w2b", [128, 2, 64], BF16)
            pef = S.carve("pef", [128, 32], F32)
            peT = S.carve("peT", [128, 32], BF16)
            cb = S.carve("cb", [128, 2], F32)
            hid = S.carve("hid", [128, 2, 128], BF16)
            S.dma(tmpf[0:N_CMP, :], D["cmpmask"].rearrange("j i b -> j (i b)"), w=[tmpf])
            ts("dve", cmpneg[0:N_CMP, :, :].rearrange("p i b -> p (i b)"), tmpf[0:N_CMP, :], -1.0, -NEG, ALU.add, ALU.mult,
               [tmpf], [cmpneg])
            S.dma(tmpf[0:32, :], D["selE"], w=[tmpf])
            cp("dve", selE[0:32, :], tmpf[0:32, :], [tmpf], [selE])
            S.dma(seladj[:], D["seladj"].rearrange("(i p) j -> p i j", p=128), w=[seladj])
            S.dma(ROPEC[0:N_CMP, :], D["rope_cmp"], w=[ROPEC])
            S.dma(tmpf[0:N_CMP, 0:32], D["ovl"], w=[tmpf])
            for k in range(2):
                cp("dve", VC1[0:N_CMP, k, 65:97], tmpf[0:N_CMP, 0:32], [tmpf], [VC1.b(("o", k))])
            memset("dve", VC1[:, :, 64:65], 1.0, [VC1.b("ones")])
            memset("dve", Vs1[:, :, :, 64:65], 1.0, [Vs1.b("ones")])
            memset("dve", Vw1[:, :, :, 64:65], 1.0, [Vw1.b("ones")])
            S.dma(w2f[:, 0, :], D["d_w2k"][0], w=[w2f.b(0)])
            S.dma(w2f[:, 1, :], D["d_w2v"][0], w=[w2f.b(1)])
            cp("dve", w2b[:], w2f[:], [w2f.b(0), w2f.b(1)], [w2b])
            load_slab(w512, W_IN[l], 2816, 512)
            load_slab(wkc, W_IN[l], 2560, 256)
            for i in range(NT):
                proj_tok(pZ[:], w512, 0, 512, i, [pZ])
                norm_rope(pZ[:, 0:128], 2, GI["d_kn_slc"], ROPE[:, i, :], krot[:, 0:128].rearrange("p (h d) -> p h d", h=2),
                          [pZ], [krot.b(0)], 0)
                norm_rope(pZ[:, 256:384], 2, GI["d_kn_win"], ROPE[:, i, :], krot[:, 128:256].rearrange("p (h d) -> p h d", h=2),
                          [pZ], [krot.b(1)], 1)
                cp("act", Vs1[:, i, :, 0:64], pZ[:, 128:256].rearrange("p (h d) -> p h d", h=2), [pZ], [Vs1.b(i)])
                cp("act", Vw1[:, i, :, 0:64], pZ[:, 384:512].rearrange("p (h d) -> p h d", h=2), [pZ], [Vw1.b(i)])
                tr(pT[:, 0, :], krot[:, 0:128], 128, [krot.b(0)], [pT])
                tr(pT[:, 1, :], krot[:, 128:256], 128, [krot.b(1)], [pT])
                cp("act", ksT[:, i * 128:(i + 1) * 128], pT[:, 0, :], [pT], [ksT.b(i)])
                cp("act", kwT[:, i * 128:(i + 1) * 128], pT[:, 1, :], [pT], [kwT.b(i)])
            for g in range(4):
                proj_feat(pZ[:], wkc, 0, g, [pZ])
                cp("act", kcdT[:, g * 512:(g + 1) * 512], pZ[:], [pZ], [kcdT.b(g)])
                proj_feat(pZ[:], wkc, 128, g, [pZ])
                cp("act", vcdT[:, g * 512:(g + 1) * 512], pZ[:], [pZ], [vcdT.b(g)])
            for kind_i, kind in enumerate(["k", "v"]):
                srcT = kcdT if kind == "k" else vcdT
                w1v = W_1[kind].rearrange("(l d) m -> d l m", d=64)
                S.dma(W1r[0:64, :, :], w1v, w=[W1r.b(0)])
                S.dma(W1r[64:128, :, :], w1v, w=[W1r.b(1)])
                pesrc = D["d_pe_k" if kind == "k" else "d_pe_v"][0].rearrange("l d -> d l")
                S.dma(pef[0:64, :], pesrc, w=[pef], allow_slow_non_contiguous=True)
                cp("dve", peT[0:64, :], pef[0:64, :], [pef], [peT])
                for l_ in range(32):
                    mm(pY0[:, 0:1], W1r[0:64, l_, :], peT[0:64, l_:l_ + 1], l_ == 0, l_ == 31, [W1r.b(0), peT], [pY0])
                cp("dve", cb[:, 0:1], pY0[:, 0:1], [pY0], [cb])
                ts("dve", cb[:, 1:2], cb[:, 0:1], -1.0, None, ALU.mult, None, [cb], [cb])
                s3 = srcT[:].rearrange("p (j s) -> p j s", s=16)
                srcb = [srcT.b(g_) for g_ in range(4)]
                for k in range(2):
                    separator()
                    for l_ in range(32):
                        rhs = s3[64 * k:64 * k + 64, 0:N_CMP, l_] if l_ < 16 else s3[64 * k:64 * k + 64, 1:N_CMP + 1, l_ - 16]
                        mm(pSb[k][:, 0:N_CMP], W1r[64 * k:64 * k + 64, l_, :], rhs, l_ == 0, l_ == 31,
                           [W1r.b(k)] + srcb, [pSb[k]])
                    separator()
                    act(tB[:, 0:N_CMP], pSb[k][:, 0:N_CMP], AF.Exp, [pSb[k], cb], [tB], scale=-1.0, bias=cb[:, 1:2])
                    act(tC[:, 0:N_CMP], pSb[k][:, 0:N_CMP], AF.Identity, [pSb[k], cb], [tC], bias=cb[:, 0:1])
                    sigmoid_act(tB[:, 0:N_CMP], [tB], [tB])
                    tt("dve", hid[:, k, 0:N_CMP], tC[:, 0:N_CMP], tB[:, 0:N_CMP], ALU.mult, [tB, tC], [hid.b(k)])
                for k in range(2):
                    c0 = kind_i * 128 + k * 64
                    mm(pZ[0:N_CMP, c0:c0 + 64], hid[:, k, 0:N_CMP], w2b[:, kind_i, :], True, True, [hid.b(k), w2b], [pZ])
            norm_rope(pZ[0:N_CMP, 0:128], 2, GI["d_kn_cmp"], ROPEC[0:N_CMP, :],
                      krot[0:N_CMP, 0:128].rearrange("p (h d) -> p h d", h=2), [pZ], [krot.b(0)], 0, np_=N_CMP)
            cp("act", VC1[0:N_CMP, :, 0:64], pZ[0:N_CMP, 128:256].rearrange("p (h d) -> p h d", h=2), [pZ], [VC1.b("v")])
            tr(pT[:, 0, 0:N_CMP], krot[0:N_CMP, 0:128], N_CMP, [krot.b(0)], [pT])
            cp("act", kcT[:, 0:N_CMP], pT[:, 0, 0:N_CMP], [pT], [kcT])
            S.phase_reset(mark)
        wq = S.carve("wq1", [128, 8, 512], BF16)
        wgd = S.carve("wgd", [128, 8, 512], BF16)
        wqgm = S.carve("wqgm1", [128, 8, 512], BF16)
        wgt = S.carve("wgt", [128, 8, 24], BF16)
        woDM = S.carve("woDM", [128, 6, 1024], BF16)
        acc = S.carve("acc", [128, 512], F32)
        negT = S.carve("negT", [128, 2, 128], BF16)
        nb = S.carve("nb", [128, 2, 32], BF16)
        gts = S.carve("gts", [128, 24], F32)
        gat = S.carve("gat1", [128, 768], BF16)
        impk = S.carve("impk", [128, 64], F32)
        wv = W_IN[l].rearrange("(c p) n -> p c n", p=128)
        if doD:
            load_slab(wq, W_IN[l], 2048, 512)
            load_slab(wgd, W_IN[l], 3352, 512)
            S.dma(wgt[:], wv[:, :, 3328:3352], w=[wgt])
            S.dma(woDM[:, 0:4, :], W_OUT[l].rearrange("(c p) n -> p c n", p=128)[:, 4:8, :], w=[woDM.b("a")])
        if doM:
            load_slab(wqgm, W_IN[l], 3864, 512)
            S.dma(woDM[:, 4:6, :], W_OUT[l].rearrange("(c p) n -> p c n", p=128)[:, 8:10, :], w=[woDM.b("m")])
        gts3 = gts[:].rearrange("p (h b) -> p h b", b=3)
        vones = {id(Vs1): Vs1.b("ones"), id(Vw1): Vw1.b("ones")}

        def evac_branch(k, br, first):
            ts("dve", sm8[:, 2, 4 * k:4 * k + 4], pOb[k][:, :, 64], 1e-30, None, ALU.max, None, [pOb[k]], [sm8.b((2, k))])
            recip(sm8[:, 2, 4 * k:4 * k + 4], sm8[:, 2, 4 * k:4 * k + 4], [sm8.b((2, k))], [sm8.b((2, k))])
            tt("dve", sm8[:, 3, 4 * k:4 * k + 4], sm8[:, 2, 4 * k:4 * k + 4], gts3[:, 4 * k:4 * k + 4, br], ALU.mult,
               [sm8.b((2, k)), gts], [sm8.b((3, k))])
            a3 = acc[:, 256 * k:256 * k + 256].rearrange("p (h d) -> p h d", h=4)
            if first:
                tt("dve", a3, pOb[k][:, :, 0:64], bc3(sm8[:, 3, 4 * k:4 * k + 4], 64), ALU.mult, [pOb[k], sm8.b((3, k))],
                   [acc.b(k)])
            else:
                t3 = tB[:, 256 * k:256 * k + 256].rearrange("p (h d) -> p h d", h=4)
                tt("dve", t3, pOb[k][:, :, 0:64], bc3(sm8[:, 3, 4 * k:4 * k + 4], 64), ALU.mult, [pOb[k], sm8.b((3, k))],
                   [tB.b(k)])
                tt("dve", a3, a3, t3, ALU.add, [acc.b(k), tB.b(k)], [acc.b(k)])

        def attn_branch(i, kts, kT_, V_, br, sel):
            for k in range(2):
                memset("dve", pOb[k][:], 0.0, [pOb[k]])
                for n_, kt in enumerate(kts):
                    ps = pSb[n_ % 2]
                    psv = ps[:].rearrange("p (g t) -> p g t", g=4)
                    extra = []
                    if sel:
                        extra.append((selE[0:32, kt * 128:(kt + 1) * 128], bch(negT[0:32, k, :], 4), [selE, negT]))
                    if kt == i:
                        extra.append((ident[:], bch(mdiag[:], 4), [ident, mdiag]))
                    elif (not sel) and kt == i - 4:
                        extra.append((ident[:], bch(mprev[:], 4), [ident, mprev]))
                    mm(psv, kT_[:, kt * 128:(kt + 1) * 128], qz[:, k, :, :], True, len(extra) == 0, [kT_.b(kt), qz.b(k)], [ps])
                    for j_, (lt, rh, rd_) in enumerate(extra):
                        mm(psv, lt, rh, False, j_ == len(extra) - 1, rd_, [ps])
                    act(Pb[:, n_ % 2, :], ps[:], AF.Exp, [ps], [Pb.b(n_ % 2)], scale=0.125)
                    for g in range(4):
                        mm(pOb[k][:, g, 0:65], Pb[:, n_ % 2, g * 128:(g + 1) * 128], V_[:, kt, k, :], False, False,
                           [Pb.b(n_ % 2), V_.b(kt), vones[id(V_)]], [pOb[k]], skip=True)
                evac_branch(k, br, False)

        for i in range(NT):
            if doD:
                proj_tok(pZ[:], wgd, 0, 512, i, [pZ])
                silu_psum(gat[:, 0:512], pZ[:], 512, [pZ], [gat.b("a")], tE, tF)
                proj_tok(pZ[:, 0:24], wgt, 0, 24, i, [pZ])
                act(gts[:], pZ[:, 0:24], AF.Exp, [pZ], [gts], scale=-1.0)
                sigmoid_act(gts[:], [gts], [gts])
                proj_tok(pZ[:], wq, 0, 512, i, [pZ])
                norm_rope(pZ[:], 8, GI["d_qn"], ROPE[:, i, :], qrot[:].rearrange("p (h d) -> p h d", h=8), [pZ], [qrot], 1)
                for pr in range(4):
                    tr(pT[:, pr, :], qrot[:, pr * 128:(pr + 1) * 128], 128, [qrot], [pT])
                cp("act", qz[0:64, 0, :, :], pT[0:64, 0:4, :], [pT], [qz.b(0)])
                cp("act", qz[64:128, 1, :, :], pT[64:128, 0:4, :], [pT], [qz.b(1)])
                for k in range(2):
                    ps = pSb[k]
                    psv = ps[0:N_CMP, :].rearrange("p (g t) -> p g t", g=4)
                    mm(psv, kcT[:, 0:N_CMP], qz[:, k, :, :], True, False, [kcT, qz.b(k)], [ps])
                    mm(psv, ident[0:N_CMP, 0:N_CMP], bch(cmpneg[0:N_CMP, i, :], 4), False, True, [ident, cmpneg], [ps])
                    act(Pb[0:N_CMP, k, :], ps[0:N_CMP, :], AF.Exp, [ps], [Pb.b(k)], scale=0.125)
                for k in range(2):
                    memset("dve", pOb[k][:], 0.0, [pOb[k]])
                    for g in range(4):
                        mm(pOb[k][:, g, 0:97], Pb[0:N_CMP, k, g * 128:(g + 1) * 128], VC1[0:N_CMP, k, :], False, False,
                           [Pb.b(k), VC1.b("ones"), VC1.b("v"), VC1.b(("o", k))], [pOb[k]], skip=True)
                for k in range(2):
                    evac_branch(k, 0, True)
                    t3 = tC[:, 0:128].rearrange("p (g j) -> p g j", g=4)
                    tt("dve", t3, pOb[k][:, :, 65:97], bc3(sm8[:, 2, 4 * k:4 * k + 4], 32), ALU.mult,
                       [pOb[k], sm8.b((2, k))], [tC])
                    op("dve", lambda e, k=k: e.tensor_reduce(out=impk[:, 32 * k:32 * k + 32],
                                                             in_=tC[:, 0:128].rearrange("p (g j) -> p j g", g=4),
                                                             axis=AX.X, op=ALU.add), r=[tC], w=[impk.b(k)])
                    tt("dve", impk[:, 32 * k:32 * k + 32], impk[:, 32 * k:32 * k + 32], seladj[:, i, :], ALU.add,
                       [impk.b(k), seladj], [impk.b(k)])
                    op("dve", lambda e, k=k: e.max(out=sm8[:, 6, 0:8], in_=impk[:, 32 * k:32 * k + 32]), r=[impk.b(k)],
                       w=[sm8.b(6)])
                    ts("dve", tD[:, 0:32], impk[:, 32 * k:32 * k + 32], sm8[:, 6, 3:4], None, ALU.is_ge, None,
                       [impk.b(k), sm8.b(6)], [tD])
                    ts("dve", nb[:, k, :], tD[:, 0:32], -1.0, -NEG, ALU.add, ALU.mult, [tD], [nb.b(k)])
                    tr(pT[0:32, 4 + k, :], nb[:, k, :], 128, [nb.b(k)], [pT])
                cp("act", negT[0:32, :, :], pT[0:32, 4:6, :], [pT], [negT])
                attn_branch(i, list(range(0, i + 1)), ksT, Vs1, 1, True)
                attn_branch(i, list(range(max(0, i - 4), i + 1)), kwT, Vw1, 2, False)
                tt("dve", ob16[:, 0:512], acc[:], gat[:, 0:512], ALU.mult, [acc.b(0), acc.b(1), gat.b("a")], [ob16])
                transposes_to_oT(4, 0, 0)
            if doM:
                proj_tok(pZ[:], wqgm, 0, 512, i, [pZ])
                silu_psum(gat[:, 512:768], pZ[:, 256:512], 256, [pZ], [gat.b("m")], tE, tF)
                mem_attn_tile(i, pZ[:, 0:256], [pZ], l, kmT, VM1, gat[:, 512:768], [gat.b("m")], 512)
                for c in range(2):
                    tr(pT[:, 4 + c, :], ob16[:, 512 + c * 128:512 + (c + 1) * 128], 128, [ob16], [pT])
                cp("act", oT[:, 4:6, :], pT[:, 4:6, :], [pT], [oT])
            chunks = ([0, 1, 2, 3] if doD else []) + ([4, 5] if doM else [])
            for half in range(2):
                for n_, c in enumerate(chunks):
                    mm(pYb[half][:], oT[:, c, :], woDM[:, c, half * 512:(half + 1) * 512], n_ == 0, n_ == len(chunks) - 1,
                       [oT, woDM.b("a"), woDM.b("m")], [pYb[half]])
                tt("dve", H[:, i, half * 512:(half + 1) * 512], H[:, i, half * 512:(half + 1) * 512], pYb[half][:],
                   ALU.add, [H.b(i), pYb[half]], [H.b(i)])

    def layer1(s):
        rmsnorm_to_xnT()
        S.phase_reset()
        if "C" in mix1:
            hgrn2_phase(s)
            S.phase_reset()
        doD, doM = "D" in mix1, "M" in mix1
        if doD or doM:
            nsa_phase(s, doD, doM)
            S.phase_reset()

    for s in range(nseq):
        for i in range(NT):
            S.dma(H[:, i, :], D["x"][s, i * 128:(i + 1) * 128, :], w=[H.b(i)])
        if 0 in layers:
            layer0(s)
        if 1 in layers:
            layer1(s)
        for i in range(NT):
            S.dma(Y[s, i * 128:(i + 1) * 128, :], H[:, i, :], r=[H.b(i)])
        S.phase_reset()
    S.emit()
    return nc, S


N_CORES = 8
_PROG = {}


def kernel(**inputs):
    x = np.ascontiguousarray(inputs["x"], dtype=np.float32)
    mem = np.ascontiguousarray(inputs["mem"], dtype=np.float32)
    B = x.shape[0]
    per = B // N_CORES
    if per not in _PROG:
        _PROG[per] = build_program(per)[0]
    nc = _PROG[per]
    consts = host_consts()
    params = {k: np.ascontiguousarray(inputs[k], dtype=np.float32) for k in PARAM_SHAPES}
    in_maps = []
    for c in range(N_CORES):
        m = {"x": x[c * per:(c + 1) * per], "mem": mem[c * per:(c + 1) * per]}
        m.update(params)
        m.update(consts)
        in_maps.append(m)
    res = run_bass_kernel_spmd(nc, in_maps, core_ids=list(range(N_CORES)))
    return np.concatenate([r["y"] for r in res.results], axis=0)
```

```python
from contextlib import ExitStack
import numpy as np
import concourse.bass as bass
import concourse.mybir as mybir
from concourse.bass_utils import run_bass_kernel_spmd

F32 = mybir.dt.float32
BF16 = mybir.dt.bfloat16
AF = mybir.ActivationFunctionType
ALU = mybir.AluOpType
AX = mybir.AxisListType

ENGINES = ["pe", "act", "dve", "pool", "sp"]
EPOCH = 30000
NDMASEM = 8
import os
NO_POOL = os.environ.get("K_NO_POOL", "1") == "1"
DBG = int(os.environ.get("K_DBG", "99"))
FILL_MODE = int(os.environ.get("K_FILL", "1"))

D_MODEL = 1024
SEQ = 2048
NT = SEQ // 128
N_MEM = 256
EVEN_IN = 2816
ODD_IN = 4376
EPS = 1e-6
NEG = -1024.0
N_CMP = 127


class Buf:
    __slots__ = ("name", "lw", "rd", "excl")

    def __init__(self, name, excl=False):
        self.name = name
        self.lw = None
        self.rd = {}
        self.excl = excl


class Op:
    __slots__ = ("eng", "fn", "deps", "is_dma", "needs_inc", "token", "waits", "dsem", "idx")

    def __init__(self, eng, fn, is_dma=False):
        self.eng = eng
        self.fn = fn
        self.deps = []
        self.is_dma = is_dma
        self.needs_inc = is_dma
        self.token = None
        self.waits = []
        self.dsem = None
        self.idx = None


class T:
    def __init__(self, h, name, excl=False):
        self.h = h
        self.name = name
        self.excl = excl
        self.whole = Buf(name, excl)
        self.subs = {}

    def __getitem__(self, idx):
        return self.h[idx]

    def b(self, key=None):
        if key is None:
            return self.whole
        s = self.subs.get(key)
        if s is None:
            s = Buf(f"{self.name}[{key}]", self.excl)
            self.subs[key] = s
        return s


class Sched:
    def __init__(self, nc):
        self.nc = nc
        self.es = ExitStack()
        self.ops = {e: [] for e in ENGINES}
        self.all_dma = []
        self.dma_since_bar = []
        self.pending = {e: [] for e in ENGINES}
        self.arena = None
        self.arena_words = 0
        self.arena_off = 0

    def sb(self, name, shape, dt):
        h = self.es.enter_context(self.nc.sbuf_tensor("sb_" + name, list(shape), dt))
        return T(h, name)

    def ps(self, name, shape, dt):
        h = self.es.enter_context(self.nc.psum_tensor("ps_" + name, list(shape), dt))
        return T(h, name, excl=True)

    def make_arena(self, words):
        self.arena = self.es.enter_context(self.nc.sbuf_tensor("arena", [128, words], F32))
        self.arena_words = words
        self.arena_off = 0

    def carve(self, name, shape, dt):
        n = 1
        for s in shape[1:]:
            n *= s
        words = (n + 1) // 2 if dt == BF16 else n
        words = (words + 7) // 8 * 8
        assert self.arena_off + words <= self.arena_words, (name, self.arena_off, words, self.arena_words)
        ap = self.arena[:, self.arena_off:self.arena_off + words]
        if dt == BF16:
            ap = ap.bitcast(BF16)[:, 0:n]
        else:
            ap = ap[:, 0:n]
        self.arena_off += words
        if len(shape) == 3:
            ap = ap.rearrange("p (a b) -> p a b", a=shape[1])
        elif len(shape) == 4:
            ap = ap.rearrange("p (a b c) -> p a b c", a=shape[1], b=shape[2])
        return T(ap, name)

    def phase_reset(self, to=0):
        self.barrier()
        self.arena_off = to

    def _bufs(self, xs):
        out = []
        for x in xs or []:
            out.append(x.whole if isinstance(x, T) else x)
        return out

    def op(self, eng, fn, r=None, w=None, is_dma=False):
        if eng == "pool" and NO_POOL and not is_dma:
            eng = "dve"
        o = Op(eng, fn, is_dma)
        skey = ("dma", len(self.all_dma)) if is_dma else eng
        deps = []
        rb = self._bufs(r)
        wb = self._bufs(w)
        ex = [b for b in rb if b.excl]
        if ex:
            rb = [b for b in rb if not b.excl]
            wb = wb + [b for b in ex if b not in wb]
        for b in rb:
            if b.lw is not None:
                deps.append(b.lw)
        for b in wb:
            if b.lw is not None:
                deps.append(b.lw)
            deps.extend(b.rd.values())
        if self.pending[eng]:
            deps.extend(self.pending[eng])
            self.pending[eng] = []
        for b in rb:
            b.rd[skey] = o
        for b in wb:
            b.lw = o
            b.rd = {}
        seen = set()
        for d in deps:
            if id(d) in seen or d is o:
                continue
            seen.add(id(d))
            if (not d.is_dma) and (not is_dma) and d.eng == eng and eng == "pe":
                continue
            o.deps.append(d)
        o.idx = len(self.ops[eng])
        self.ops[eng].append(o)
        if is_dma:
            self.all_dma.append(o)
            self.dma_since_bar.append(o)
        return o

    def dma(self, out, in_, r=None, w=None, q="sp", **kw):
        return self.op(q, lambda e: e.dma_start(out=out, in_=in_, **kw), r=r, w=w, is_dma=True)

    def barrier(self):
        lasts = []
        for e in ENGINES:
            for o in reversed(self.ops[e]):
                if not o.is_dma:
                    lasts.append(o)
                    break
        lasts.extend(self.dma_since_bar)
        self.dma_since_bar = []
        for e in ENGINES:
            self.pending[e] = list(self.pending[e]) + lasts

    def emit(self):
        nc = self.nc
        for e in ENGINES:
            for o in self.ops[e]:
                for d in o.deps:
                    d.needs_inc = True
        nsem_eng = {}
        for e in ENGINES:
            c = 0
            k = 0
            for o in self.ops[e]:
                if o.is_dma:
                    o.dsem = (e, k % NDMASEM)
                    k += 1
                elif o.needs_inc:
                    c += 1
                    o.token = (("e", e, (c - 1) // EPOCH), (c - 1) % EPOCH + 1)
            nsem_eng[e] = (c + EPOCH - 1) // EPOCH if c else 0
        dcount = {}
        prev_dma = {}
        for e in ENGINES:
            for o in self.ops[e]:
                if o.is_dma:
                    key = ("d",) + o.dsem
                    v = dcount.get(key, 0) + 16
                    dcount[key] = v
                    o.token = (key, v)
                    if key in prev_dma:
                        o.deps.append(prev_dma[key])
                    prev_dma[key] = o
        sems = {}
        for e in ENGINES:
            for ep in range(nsem_eng[e]):
                sems[("e", e, ep)] = self.es.enter_context(nc.semaphore(f"s_{e}_{ep}"))
        for key in dcount:
            sems[key] = self.es.enter_context(nc.semaphore(f"d_{key[1]}_{key[2]}"))
        for e in ENGINES:
            seen = {}
            for o in self.ops[e]:
                need = {}
                for d in o.deps:
                    k, v = d.token
                    if seen.get(k, 0) >= v:
                        continue
                    if need.get(k, 0) < v:
                        need[k] = v
                for k, v in need.items():
                    seen[k] = v
                o.waits = list(need.items())
        final_waits = list(dcount.items())
        self.nsems = len(sems)
        self.ninst = {e: len(self.ops[e]) for e in ENGINES}
        engmap = {"pe": "tensor", "act": "scalar", "dve": "vector", "pool": "gpsimd", "sp": "sync"}
        with nc.Block() as block:
            for e in ENGINES:
                ops = self.ops[e]

                def body(eng, ops=ops, e=e):
                    for o in ops:
                        for k, v in o.waits:
                            eng.wait_ge(sems[k], v)
                        ins = o.fn(eng)
                        if o.is_dma:
                            ins.then_inc(sems[o.token[0]], 16)
                        elif o.needs_inc:
                            ins.then_inc(sems[o.token[0]], 1)
                    if e == "sp":
                        for k, v in final_waits:
                            eng.wait_ge(sems[k], v)

                getattr(block, engmap[e])(body)
        self.es.close()


def host_consts():
    c = {}
    c["ident"] = np.eye(128, dtype=np.float32)
    half = 32
    inv = 10000.0 ** (-np.arange(half, dtype=np.float32) / half)
    pos = np.arange(SEQ, dtype=np.float32)
    ang = pos[:, None] * inv[None, :]
    c["rope_cs"] = np.concatenate([np.cos(ang), np.sin(ang), -np.sin(ang)], axis=1).astype(np.float32)
    cend = (np.arange(N_CMP) * 16 + 31).astype(np.float32)
    angc = cend[:, None] * inv[None, :]
    c["rope_cmp"] = np.concatenate([np.cos(angc), np.sin(angc), -np.sin(angc)], axis=1).astype(np.float32)
    a = np.arange(128)[:, None]
    b = np.arange(128)[None, :]
    c["mdiag"] = np.where(a <= b, 0.0, NEG).astype(np.float32)
    c["mprev"] = np.where(a > b, 0.0, NEG).astype(np.float32)
    j = np.arange(N_CMP)[:, None, None]
    i = np.arange(NT)[None, :, None]
    bb = np.arange(128)[None, None, :]
    c["cmpmask"] = ((16 * j + 31) <= (128 * i + bb)).astype(np.float32)
    s = np.arange(SEQ)[None, :]
    js = np.arange(32)[:, None]
    c["selE"] = ((s // 64) == js).astype(np.float32)
    n = np.arange(N_CMP)[:, None]
    jj = np.arange(32)[None, :]
    c["ovl"] = ((16 * n < 64 * jj + 64) & (16 * n + 32 > 64 * jj)).astype(np.float32)
    t = np.arange(SEQ)[:, None]
    cur = t // 64
    forced = (jj == 0) | (jj == cur)
    valid = jj <= cur
    c["seladj"] = np.where(forced, 1e4, np.where(valid, 0.0, -1e4)).astype(np.float32)
    tri = (np.arange(64)[:, None] <= np.arange(64)[None, :]).astype(np.float32)
    c["tri64"] = np.concatenate([tri, tri], axis=0)
    return c


CONST_SHAPES = {"ident": [128, 128], "rope_cs": [SEQ, 96], "rope_cmp": [N_CMP, 96], "mdiag": [128, 128],
                "mprev": [128, 128], "cmpmask": [N_CMP, NT, 128], "selE": [32, SEQ], "ovl": [N_CMP, 32],
                "seladj": [SEQ, 32], "tri64": [128, 64]}

PARAM_SHAPES = {
    "norm_g": [2, 1024], "mem_norm_g": [2, 1024], "mem_w_kv": [2, 1024, 512], "mem_qn": [2, 64], "mem_kn": [2, 64],
    "ev_w_in": [1, 1024, 2816], "ev_w_out": [1, 1280, 1024], "a_qn": [1, 64], "a_kn": [1, 64], "a_sinks": [1, 8],
    "b_conv_w": [1, 4, 512], "b_conv_b": [1, 512], "b_w_r": [1, 8, 64, 64], "b_b_r": [1, 512],
    "b_w_i": [1, 8, 64, 64], "b_b_i": [1, 512], "b_lambda": [1, 512], "od_w_in": [1, 1024, 4376],
    "od_w_out": [1, 1280, 1024], "c_lb": [2, 512], "c_onorm": [1, 128], "d_qn": [1, 64], "d_kn_cmp": [1, 64],
    "d_kn_slc": [1, 64], "d_kn_win": [1, 64], "d_pe_k": [1, 32, 64], "d_pe_v": [1, 32, 64],
    "d_w1k": [1, 2048, 128], "d_w2k": [1, 128, 64], "d_w1v": [1, 2048, 128], "d_w2v": [1, 128, 64],
}


def build_program(nseq, layers=(0, 1), mix0=("A", "B", "M"), mix1=("C", "D", "M")):
    nc = bass.Bass("TRN2", target_bir_lowering=False)
    D = {}
    D["x"] = nc.dram_tensor("x", [nseq, SEQ, D_MODEL], F32, kind="ExternalInput").ap()
    D["mem"] = nc.dram_tensor("mem", [nseq, N_MEM, D_MODEL], F32, kind="ExternalInput").ap()
    for k, shp in PARAM_SHAPES.items():
        D[k] = nc.dram_tensor(k, shp, F32, kind="ExternalInput").ap()
    for k, shp in CONST_SHAPES.items():
        D[k] = nc.dram_tensor(k, shp, F32, kind="ExternalInput").ap()
    Y = nc.dram_tensor("y", [nseq, SEQ, D_MODEL], F32, kind="ExternalOutput").ap()
    W_IN = [nc.dram_tensor("w_in0s", [1024, EVEN_IN], BF16, kind="Internal").ap(),
            nc.dram_tensor("w_in1s", [1024, ODD_IN], BF16, kind="Internal").ap()]
    W_OUT = [nc.dram_tensor("w_out0s", [1280, 1024], BF16, kind="Internal").ap(),
             nc.dram_tensor("w_out1s", [1280, 1024], BF16, kind="Internal").ap()]
    W_KV = [nc.dram_tensor("w_kv0s", [1024, 512], BF16, kind="Internal").ap(),
            nc.dram_tensor("w_kv1s", [1024, 512], BF16, kind="Internal").ap()]
    W_1 = {"k": nc.dram_tensor("w1ks", [2048, 128], BF16, kind="Internal").ap(),
           "v": nc.dram_tensor("w1vs", [2048, 128], BF16, kind="Internal").ap()}

    S = Sched(nc)
    op = S.op

    H = S.sb("H", [128, NT, 1024], F32)
    xnT = S.sb("xnT", [128, 8, SEQ], BF16)
    ident = S.sb("ident", [128, 128], BF16)
    ROPE = S.sb("ROPE", [128, NT, 96], F32)
    mdiag = S.sb("mdiag", [128, 128], BF16)
    mprev = S.sb("mprev", [128, 128], BF16)
    normg = S.sb("normg", [128, 2, 8], F32)
    memg = S.sb("memg", [128, 2, 8], F32)
    GN = S.sb("GN", [128, 10, 64], F32)
    GI = {"mem_qn0": 0, "mem_qn1": 1, "mem_kn0": 2, "mem_kn1": 3, "a_qn": 4, "a_kn": 5, "d_qn": 6, "d_kn_slc": 7,
          "d_kn_win": 8, "d_kn_cmp": 9}
    COG = S.sb("COG", [128, 128], F32)
    LB = S.sb("LB", [128, 2, 4], F32)
    ones512 = S.sb("ones512", [128, 512], F32)
    esink = S.sb("esink", [128, 8], F32)
    ss16 = S.sb("ss16", [128, NT], F32)
    rstd16 = S.sb("rstd16", [128, NT], F32)
    tA = S.sb("tA", [128, 512], F32)
    tB = S.sb("tB", [128, 512], F32)
    tC = S.sb("tC", [128, 512], F32)
    tD = S.sb("tD", [128, 512], F32)
    tE = S.sb("tE", [128, 512], F32)
    tF = S.sb("tF", [128, 512], F32)
    xnb = S.sb("xnb", [128, 1024], BF16)
    sm8 = S.sb("sm8", [128, 8, 8], F32)
    Pb = S.sb("Pb", [128, 2, 512], BF16)
    ob16 = S.sb("ob16", [128, 1280], BF16)
    oT = S.sb("oT", [128, 10, 128], BF16)
    qrot = S.sb("qrot", [128, 512], BF16)
    qz = S.sb("qz", [128, 2, 4, 128], BF16)
    qzB = S.sb("qzB", [128, 2, 4, 128], BF16)
    qz2 = [qz, qzB]
    krot = S.sb("krot", [128, 256], BF16)
    pZ = S.ps("pZ", [128, 512], F32)
    pS0 = S.ps("pS0", [128, 512], F32)
    pS1 = S.ps("pS1", [128, 512], F32)
    pSb = [pS0, pS1]
    pO0 = S.ps("pO0", [128, 4, 128], F32)
    pO1 = S.ps("pO1", [128, 4, 128], F32)
    pOb = [pO0, pO1]
    pT = S.ps("pT", [128, 8, 128], BF16)
    pY0 = S.ps("pY0", [128, 512], F32)
    pY1 = S.ps("pY1", [128, 512], F32)
    pYb = [pY0, pY1]

    ARENA_WORDS = 18 * 1024
    S.make_arena(ARENA_WORDS)

    def mm(out, lhsT, rhs, start, stop, r, w, skip=False):
        if skip:
            return op("pe", lambda e: e.matmul(out=out, lhsT=lhsT, rhs=rhs, start=start, stop=stop,
                                               skip_group_check=True), r=r, w=w)
        return op("pe", lambda e: e.matmul(out=out, lhsT=lhsT, rhs=rhs, start=start, stop=stop), r=r, w=w)

    def tr(out, in_, npart, r, w):
        return op("pe", lambda e: e.transpose(out=out, in_=in_, identity=ident[0:npart, 0:npart]), r=list(r) + [ident], w=w)

    def act(out, in_, func, r, w, scale=1.0, bias=0.0, accum=None):
        if accum is None:
            return op("act", lambda e: e.activation(out=out, in_=in_, func=func, scale=scale, bias=bias), r=r, w=w)
        return op("act", lambda e: e.activation(out=out, in_=in_, func=func, scale=scale, bias=bias, accum_out=accum), r=r, w=w)

    def tt(eng, out, in0, in1, o, r, w):
        return op(eng, lambda e: e.tensor_tensor(out=out, in0=in0, in1=in1, op=o), r=r, w=w)

    def ts(eng, out, in0, s1, s2, o0, o1, r, w):
        if s2 is None:
            return op(eng, lambda e: e.tensor_scalar(out=out, in0=in0, scalar1=s1, scalar2=None, op0=o0), r=r, w=w)
        return op(eng, lambda e: e.tensor_scalar(out=out, in0=in0, scalar1=s1, scalar2=s2, op0=o0, op1=o1), r=r, w=w)

    def stt(eng, out, in0, sc, in1, o0, o1, r, w):
        return op(eng, lambda e: e.scalar_tensor_tensor(out=out, in0=in0, scalar=sc, in1=in1, op0=o0, op1=o1), r=r, w=w)

    def cp(eng, out, in_, r, w):
        if eng == "act":
            return act(out, in_, AF.Copy, r, w)
        return op(eng, lambda e: e.tensor_copy(out=out, in_=in_), r=r, w=w)

    def recip(out, in_, r, w):
        return op("dve", lambda e: e.reciprocal(out=out, in_=in_), r=r, w=w)

    def memset(eng, ap, val, w):
        return op(eng, lambda e: e.memset(ap, val), w=w)

    def rstd_from_ss(ap, n_mean, r, w):
        act(ap, ap, AF.Ln, r, w, scale=1.0 / n_mean, bias=EPS)
        act(ap, ap, AF.Exp, w, w, scale=-0.5)

    def bc3(ap2, n):
        return ap2.unsqueeze(2).broadcast_to([ap2.shape[0], ap2.shape[1], n])

    def bch(ap2, nh):
        return ap2.unsqueeze(1).broadcast_to([ap2.shape[0], nh, ap2.shape[1]])

    stage = S.carve("stage0", [128, 2048], F32)
    stage1 = S.carve("stage1", [128, 2048], F32)
    stb0 = S.carve("stb0", [128, 2048], BF16)
    stb1 = S.carve("stb1", [128, 2048], BF16)
    stages = [(stage, stb0), (stage1, stb1)]

    S.dma(stage[:, 0:128], D["ident"], w=[stage])
    cp("dve", ident[:], stage[:, 0:128], [stage], [ident])
    S.dma(stage[:, 0:128], D["mdiag"], w=[stage])
    cp("dve", mdiag[:], stage[:, 0:128], [stage], [mdiag])
    S.dma(stage[:, 0:128], D["mprev"], w=[stage])
    cp("dve", mprev[:], stage[:, 0:128], [stage], [mprev])
    memset("dve", qz[:], 0.0, [qz.b(0), qz.b(1)])
    memset("dve", qzB[:], 0.0, [qzB.b(0), qzB.b(1)])
    S.dma(ROPE[:], D["rope_cs"].rearrange("(i p) f -> p i f", p=128), w=[ROPE])
    S.dma(normg[:], D["norm_g"].rearrange("l (c p) -> p l c", p=128), w=[normg], allow_slow_non_contiguous=True)
    S.dma(memg[:], D["mem_norm_g"].rearrange("l (c p) -> p l c", p=128), w=[memg], allow_slow_non_contiguous=True)
    for nm, gi in GI.items():
        if nm.startswith("mem_"):
            src = D[nm[:-1]][int(nm[-1])]
        else:
            src = D[nm][0]
        S.dma(GN[:, gi, :], src.partition_broadcast(128), w=[GN.b(gi)])
    S.dma(esink[:], D["a_sinks"][0].partition_broadcast(128), w=[esink])
    act(esink[:], esink[:], AF.Exp, [esink], [esink])
    S.dma(COG[:], D["c_onorm"][0].partition_broadcast(128), w=[COG])
    memset("dve", ones512[:], 1.0, [ones512])
    S.dma(LB[:, 0, :], D["c_lb"][0].rearrange("(h p) -> p h", p=128), w=[LB.b(0)], allow_slow_non_contiguous=True)
    S.dma(LB[:, 1, :], D["c_lb"][1].rearrange("(h p) -> p h", p=128), w=[LB.b(1)], allow_slow_non_contiguous=True)
    tt("dve", LB[:, 0, :], LB[:, 0, :], LB[:, 1, :], ALU.subtract, [LB.b(0), LB.b(1)], [LB.b(0)])
    act(LB[:, 0, :], LB[:, 0, :], AF.Exp, [LB.b(0)], [LB.b(0)])
    ts("dve", LB[:, 0, :], LB[:, 0, :], 1.0, None, ALU.add, None, [LB.b(0)], [LB.b(0)])
    recip(LB[:, 0, :], LB[:, 0, :], [LB.b(0)], [LB.b(0)])
    ts("dve", LB[:, 1, :], LB[:, 0, :], -1.0, 1.0, ALU.mult, ALU.add, [LB.b(0)], [LB.b(1)])

    cnt = [0]

    def conv_weight(src, dst, R, C, gt=None, l=0, perm0=None):
        for rc in range(R // 128):
            for c0 in range(0, C, 2048):
                cw = min(2048, C - c0)
                sf, sbf = stages[cnt[0] % 2]
                eng = "dve"
                cnt[0] += 1
                S.dma(sf[:, 0:cw], src[rc * 128:(rc + 1) * 128, c0:c0 + cw], w=[sf])
                if gt is not None:
                    ts(eng, sbf[:, 0:cw], sf[:, 0:cw], gt[:, l, rc:rc + 1], None, ALU.mult, None, [sf, gt], [sbf])
                else:
                    cp(eng, sbf[:, 0:cw], sf[:, 0:cw], [sf], [sbf])
                rows = slice(rc * 128, (rc + 1) * 128)
                if perm0 is not None and c0 <= perm0 < c0 + cw:
                    p0 = perm0 - c0
                    if p0 > 0:
                        S.dma(dst[rows, c0:c0 + p0], sbf[:, 0:p0], r=[sbf])
                    for w_ in range(2):
                        S.dma(dst[rows, perm0:perm0 + 512].rearrange("r (pr w d) -> r w pr d", pr=4, w=2)[:, w_],
                              sbf[:, p0 + w_ * 256:p0 + (w_ + 1) * 256].rearrange("p (pr d) -> p pr d", pr=4), r=[sbf])
                    if p0 + 512 < cw:
                        S.dma(dst[rows, perm0 + 512:c0 + cw], sbf[:, p0 + 512:cw], r=[sbf])
                else:
                    S.dma(dst[rows, c0:c0 + cw], sbf[:, 0:cw], r=[sbf])

    if 0 in layers:
        conv_weight(D["ev_w_in"][0], W_IN[0], 1024, EVEN_IN, normg, 0, perm0=0)
        conv_weight(D["ev_w_out"][0], W_OUT[0], 1280, 1024)
        conv_weight(D["mem_w_kv"][0], W_KV[0], 1024, 512, memg, 0)
    if 1 in layers:
        conv_weight(D["od_w_in"][0], W_IN[1], 1024, ODD_IN, normg, 1, perm0=2048)
        conv_weight(D["od_w_out"][0], W_OUT[1], 1280, 1024)
        conv_weight(D["mem_w_kv"][1], W_KV[1], 1024, 512, memg, 1)
        conv_weight(D["d_w1k"][0], W_1["k"], 2048, 128)
        conv_weight(D["d_w1v"][0], W_1["v"], 2048, 128)
    S.phase_reset()

    def load_slab(dst, src_w, c0, ncols, key=None):
        S.dma(dst[:, :, 0:ncols], src_w.rearrange("(c p) n -> p c n", p=128)[:, :, c0:c0 + ncols],
              w=[dst.b(key)])

    def proj_tok(ps_ap, slab, col0, ncols, i, w, skey=None):
        for c in range(8):
            mm(ps_ap, xnT[:, c, i * 128:(i + 1) * 128], slab[:, c, col0:col0 + ncols], c == 0, c == 7,
               [xnT.b(i), slab.b(skey)], w)

    def proj_feat(ps_ap, slab, col0, g, w, skey=None):
        for c in range(8):
            mm(ps_ap, slab[:, c, col0:col0 + 128], xnT[:, c, g * 512:(g + 1) * 512], c == 0, c == 7,
               [xnT.b(4 * g), xnT.b(4 * g + 1), xnT.b(4 * g + 2), xnT.b(4 * g + 3), slab.b(skey)], w)

    def silu_psum(out_ap, zp, n, rz, wout, t1, t2, np_=128):
        a1 = t1[0:np_, 0:n]
        act(a1, zp, AF.Exp, rz, [t1], scale=-1.0)
        act(a1, a1, AF.Ln, [t1], [t1], bias=1.0)
        act(a1, a1, AF.Exp, [t1], [t1], scale=-1.0)
        tt("dve", out_ap, zp, a1, ALU.mult, list(rz) + [t1], wout)

    def sigmoid_act(ap, r, w):
        act(ap, ap, AF.Ln, r, w, bias=1.0)
        act(ap, ap, AF.Exp, w, w, scale=-1.0)

    def norm_rope_stages(zp, nh, gi, rope_ap, out_ap, rz, wout, slot, np_=128):
        n = nh * 64
        z3 = zp.rearrange("p (h d) -> p h d", h=nh)
        ssq = sm8[0:np_, slot, 0:nh]
        zg = tB[0:np_, 0:n].rearrange("p (h d) -> p h d", h=nh)

        def st0():
            act(tA[0:np_, 0:n], zp, AF.Square, rz, [tA])
            tt("dve", zg, z3, bch(GN[0:np_, gi, :], nh), ALU.mult, list(rz) + [GN.b(gi)], [tB])

        def st1():
            op("dve", lambda e: e.tensor_reduce(out=ssq, in_=tA[0:np_, 0:n].rearrange("p (h d) -> p h d", h=nh),
                                                axis=AX.X, op=ALU.add), r=[tA], w=[sm8.b(slot)])
            rstd_from_ss(ssq, 64.0, [sm8.b(slot)], [sm8.b(slot)])

        if rope_ap is None:
            def st2():
                tt("dve", out_ap, zg, bc3(ssq, 64), ALU.mult, [tB, sm8.b(slot)], wout)
            return [st0, st1, st2]
        zg4 = tB[0:np_, 0:n].rearrange("p (h a f) -> p h a f", h=nh, a=2)
        a4 = tC[0:np_, 0:n].rearrange("p (h a f) -> p h a f", h=nh, a=2)
        b4 = tD[0:np_, 0:n].rearrange("p (h a f) -> p h a f", h=nh, a=2)
        cos4 = rope_ap[:, 0:32].unsqueeze(1).unsqueeze(1).broadcast_to([np_, nh, 2, 32])
        sin3 = bch(rope_ap[:, 32:64], nh)
        nsin3 = bch(rope_ap[:, 64:96], nh)

        def st2():
            tt("dve", a4, zg4, cos4, ALU.mult, [tB], [tC])
            tt("dve", b4[:, :, 0, :], zg4[:, :, 1, :], nsin3, ALU.mult, [tB], [tD])
            tt("dve", b4[:, :, 1, :], zg4[:, :, 0, :], sin3, ALU.mult, [tB], [tD])

        def st3():
            tt("dve", tC[0:np_, 0:n], tC[0:np_, 0:n], tD[0:np_, 0:n], ALU.add, [tC, tD], [tC])
            tt("dve", out_ap, tC[0:np_, 0:n].rearrange("p (h d) -> p h d", h=nh), bc3(ssq, 64), ALU.mult,
               [tC, sm8.b(slot)], wout)
        return [st0, st1, st2, st3]

    def norm_rope(zp, nh, gi, rope_ap, out_ap, rz, wout, slot, np_=128):
        for st in norm_rope_stages(zp, nh, gi, rope_ap, out_ap, rz, wout, slot, np_):
            st()

    def h_update(i, nchunks, wo, wkey=None):
        for half in range(2):
            for c in range(nchunks):
                mm(pYb[half][:], oT[:, c, :], wo[:, c, half * 512:(half + 1) * 512], c == 0, c == nchunks - 1,
                   [oT, wo.b(wkey)], [pYb[half]])
            tt("dve", H[:, i, half * 512:(half + 1) * 512], H[:, i, half * 512:(half + 1) * 512], pYb[half][:], ALU.add,
               [H.b(i), pYb[half]], [H.b(i)])

    def transposes_to_oT(nch, src_cols0=0, dst0=0):
        for c in range(nch):
            tr(pT[:, c, :], ob16[:, src_cols0 + c * 128: src_cols0 + (c + 1) * 128], 128, [ob16], [pT])
        cp("act", oT[:, dst0:dst0 + nch, :], pT[:, 0:nch, :], [pT], [oT])

    def attn_pipeline(branches, qsrc, fillers=None):
        steps = []
        for bi, br in enumerate(branches):
            for n_, kt in enumerate(br["kts"]):
                steps.append((bi, n_, kt))

        def emit_qk(idx):
            bi, n_, kt = steps[idx]
            br = branches[bi]
            k = br["k"]
            ps = pSb[idx % 2]
            psv = ps[:].rearrange("p (g t) -> p g t", g=4)
            extra = br["masks"](kt)
            mm(psv, br["kT"][:, kt * 128:(kt + 1) * 128], qsrc[:, k, :, :], True, len(extra) == 0,
               [br["kT"].b(kt), qsrc.b(k)], [ps])
            for j_, (lt, rh, rd_) in enumerate(extra):
                mm(psv, lt, rh, False, j_ == len(extra) - 1, rd_, [ps])

        emit_qk(0)
        for idx, (bi, n_, kt) in enumerate(steps):
            br = branches[bi]
            k = br["k"]
            if n_ == 0:
                memset("dve", pOb[k][:], 0.0, [pOb[k]])
            if idx + 1 < len(steps):
                emit_qk(idx + 1)
            act(Pb[:, idx % 2, :], pSb[idx % 2][:], AF.Exp, [pSb[idx % 2]], [Pb.b(idx % 2)], scale=0.125)
            V_ = br["V"]
            for g in range(4):
                mm(pOb[k][:, g, 0:65], Pb[:, idx % 2, g * 128:(g + 1) * 128], V_[:, kt, k, :], False, False,
                   [Pb.b(idx % 2), V_.b(kt), V_.b("ones")], [pOb[k]], skip=True)
            if n_ == len(br["kts"]) - 1:
                br["evac"](k)
            if fillers and FILL_MODE == 1:
                fillers.pop(0)()
        while fillers:
            fillers.pop(0)()

    def rmsnorm_to_xnT():
        memset("pool", ss16[:], 0.0, [ss16])
        for i in range(NT):
            act(xnb[:], H[:, i, :], AF.Square, [H.b(i)], [xnb, ss16], accum=ss16[:, i:i + 1])
        cp("dve", rstd16[:], ss16[:], [ss16], [rstd16])
        rstd_from_ss(rstd16[:], 1024.0, [rstd16], [rstd16])
        for i in range(NT):
            ts("dve", xnb[:], H[:, i, :], rstd16[:, i:i + 1], None, ALU.mult, None, [H.b(i), rstd16], [xnb])
            for c in range(8):
                tr(pT[:, c, :], xnb[:, c * 128:(c + 1) * 128], 128, [xnb], [pT])
            cp("act", xnT[:, :, i * 128:(i + 1) * 128], pT[:], [pT], [xnT.b(i)])

    def mem_kv(s, l, kmT, VM1, memf, memT, wkv):
        load_slab(wkv, W_KV[l], 0, 512)
        memset("dve", VM1[:, :, :, 64:65], 1.0, [VM1.b("ones")])
        for nt in range(2):
            S.dma(memf[:], D["mem"][s, nt * 128:(nt + 1) * 128, :], w=[memf])
            memset("dve", sm8[:, 7, 0:1], 0.0, [sm8.b(7)])
            act(xnb[:], memf[:], AF.Square, [memf], [xnb, sm8.b(7)], accum=sm8[:, 7, 0:1])
            rstd_from_ss(sm8[:, 7, 0:1], 1024.0, [sm8.b(7)], [sm8.b(7)])
            ts("dve", xnb[:], memf[:], sm8[:, 7, 0:1], None, ALU.mult, None, [memf, sm8.b(7)], [xnb])
            for c in range(8):
                tr(pT[:, c, :], xnb[:, c * 128:(c + 1) * 128], 128, [xnb], [pT])
            cp("act", memT[:, :, nt * 128:(nt + 1) * 128], pT[:], [pT], [memT.b(nt)])
            for c in range(8):
                mm(pZ[:], memT[:, c, nt * 128:(nt + 1) * 128], wkv[:, c, 0:512], c == 0, c == 7, [memT.b(nt), wkv], [pZ])
            norm_rope(pZ[:, 0:256], 4, GI["mem_kn%d" % l], None, krot[:].rearrange("p (h d) -> p h d", h=4),
                      [pZ], [krot], 6)
            cp("act", VM1[:, nt, :, 0:64], pZ[:, 256:512].rearrange("p (h d) -> p h d", h=4), [pZ], [VM1.b(nt)])
            for pr in range(2):
                tr(pT[:, pr, :], krot[:, pr * 128:(pr + 1) * 128], 128, [krot], [pT])
            cp("act", kmT[:, :, nt * 128:(nt + 1) * 128], pT[:, 0:2, :], [pT], [kmT.b(nt)])

    def mem_attn_tile(i, qz_ap, qz_r, l, kmT, VM1, gate_ap, gate_r, out_cols0, qzt=None):
        qzt = qzt if qzt is not None else qz
        norm_rope(qz_ap, 4, GI["mem_qn%d" % l], None, qrot[:, 0:256].rearrange("p (h d) -> p h d", h=4),
                  qz_r, [qrot], 5)
        for pr in range(2):
            tr(pT[:, pr, :], qrot[:, pr * 128:(pr + 1) * 128], 128, [qrot], [pT])
        cp("act", qzt[0:64, 0, 0:2, :], pT[0:64, 0:2, :], [pT], [qzt.b(0)])
        cp("act", qzt[64:128, 1, 0:2, :], pT[64:128, 0:2, :], [pT], [qzt.b(1)])
        if DBG < 2:
            return
        for h in range(4):
            pr, hf = h // 2, h % 2
            for nt in range(2):
                mm(pSb[nt][:, h * 128:(h + 1) * 128], kmT[:, pr, nt * 128:(nt + 1) * 128],
                   qzt[:, hf, pr, :], True, True, [kmT.b(nt), qzt.b(hf)], [pSb[nt]])
        for nt in range(2):
            act(Pb[:, nt, :], pSb[nt][:], AF.Exp, [pSb[nt]], [Pb.b(nt)], scale=0.125)
        if DBG < 3:
            return
        for h in range(4):
            for nt in range(2):
                mm(pO0[:, h, 0:65], Pb[:, nt, h * 128:(h + 1) * 128], VM1[:, nt, h, :], nt == 0, nt == 1,
                   [Pb.b(nt), VM1.b(nt), VM1.b("ones")], [pO0])
        cp("dve", sm8[:, 4, 0:4], pO0[:, :, 64], [pO0], [sm8.b(4)])
        recip(sm8[:, 4, 0:4], sm8[:, 4, 0:4], [sm8.b(4)], [sm8.b(4)])
        tt("dve", tE[:, 0:256].rearrange("p (h d) -> p h d", h=4), pO0[:, :, 0:64], bc3(sm8[:, 4, 0:4], 64), ALU.mult,
           [pO0, sm8.b(4)], [tE])
        tt("dve", ob16[:, out_cols0:out_cols0 + 256], tE[:, 0:256], gate_ap, ALU.mult, [tE] + list(gate_r), [ob16])

    def mem_front_stages(i, l, wslab, gatT, qzmT):
        st = [lambda: proj_tok(pZ[:], wslab, 0, 512, i, [pZ]),
              lambda: silu_psum(gatT[:, 512:768], pZ[:, 256:512], 256, [pZ], [gatT.b("m")], tF, tF)]
        st += norm_rope_stages(pZ[:, 0:256], 4, GI["mem_qn%d" % l], None, qrot[:, 0:256].rearrange("p (h d) -> p h d", h=4),
                               [pZ], [qrot], 5)

        def t_():
            for pr in range(2):
                tr(pT[:, pr, :], qrot[:, pr * 128:(pr + 1) * 128], 128, [qrot], [pT])
        def c_():
            t_()
            cp("act", qzmT[0:64, 0, :, :], pT[0:64, 0:2, :], [pT], [qzmT.b(0)])
            cp("act", qzmT[64:128, 1, :, :], pT[64:128, 0:2, :], [pT], [qzmT.b(1)])
        st.append(c_)
        return st

    def mem_back(i, l, kmT, VM1, gatT, qzmT, out_cols0=512):
        for h in range(4):
            pr, hf = h // 2, h % 2
            for nt in range(2):
                mm(pSb[nt][:, h * 128:(h + 1) * 128], kmT[:, pr, nt * 128:(nt + 1) * 128],
                   qzmT[:, hf, pr, :], True, True, [kmT.b(nt), qzmT.b(hf)], [pSb[nt]])
        for nt in range(2):
            act(Pb[:, nt, :], pSb[nt][:], AF.Exp, [pSb[nt]], [Pb.b(nt)], scale=0.125)
        for h in range(4):
            for nt in range(2):
                mm(pO0[:, h, 0:65], Pb[:, nt, h * 128:(h + 1) * 128], VM1[:, nt, h, :], nt == 0, nt == 1,
                   [Pb.b(nt), VM1.b(nt), VM1.b("ones")], [pO0])
        cp("dve", sm8[:, 4, 0:4], pO0[:, :, 64], [pO0], [sm8.b(4)])
        recip(sm8[:, 4, 0:4], sm8[:, 4, 0:4], [sm8.b(4)], [sm8.b(4)])
        tt("dve", tE[:, 0:256].rearrange("p (h d) -> p h d", h=4), pO0[:, :, 0:64], bc3(sm8[:, 4, 0:4], 64), ALU.mult,
           [pO0, sm8.b(4)], [tE])
        tt("dve", ob16[:, out_cols0:out_cols0 + 256], tE[:, 0:256], gatT[:, 512:768], ALU.mult, [tE, gatT.b("m")], [ob16])
        for c in range(2):
            tr(pT[:, 4 + c, :], ob16[:, out_cols0 + c * 128:out_cols0 + (c + 1) * 128], 128, [ob16], [pT])
        cp("act", oT[:, 4:6, :], pT[:, 4:6, :], [pT], [oT])

    def layer0(s):
        l = 0
        rmsnorm_to_xnT()
        S.phase_reset()
        if "B" in mix0:
            PB = S.carve("PB", [128, 4, 8], F32)
            PD = S.carve("PD", [128, 4, 4], F32)
            BDf = S.carve("BDf", [128, 2, 4, 128], F32)
            BD = S.carve("BD", [128, 2, 4, 128], BF16)
            XB = S.carve("XB", [128, 3 + SEQ], F32)
            hB = S.carve("hB", [128, SEQ], F32)
            mixB = S.carve("mixB", [128, 4, SEQ], BF16)
            wsl = [S.carve("wslB0", [128, 8, 256], BF16), S.carve("wslB1", [128, 8, 256], BF16)]
            woB = S.carve("woB", [128, 4, 1024], BF16)
            xcb = S.carve("xcb", [128, 512], BF16)
            for j in range(4):
                S.dma(PB[:, :, j], D["b_conv_w"][0, j].rearrange("(c p) -> p c", p=128), w=[PB.b(j)],
                      allow_slow_non_contiguous=True)
            for j, nm in enumerate(["b_conv_b", "b_b_r", "b_b_i", "b_lambda"]):
                S.dma(PB[:, :, 4 + j], D[nm][0].rearrange("(c p) -> p c", p=128), w=[PB.b(4 + j)],
                      allow_slow_non_contiguous=True)
            ts("dve", PD[:, :, 0], PB[:, :, 5], -1.0, None, ALU.mult, None, [PB.b(5)], [PD.b(0)])
            ts("dve", PD[:, :, 1], PB[:, :, 6], -1.0, None, ALU.mult, None, [PB.b(6)], [PD.b(1)])
            act(PD[:, :, 2], PB[:, :, 7], AF.Exp, [PB.b(7)], [PD.b(2)], scale=-1.0)
            act(PD[:, :, 2], PD[:, :, 2], AF.Ln, [PD.b(2)], [PD.b(2)], bias=1.0)
            ts("dve", PD[:, :, 2], PD[:, :, 2], -8.0, None, ALU.mult, None, [PD.b(2)], [PD.b(2)])
            bdkeys = [BDf.b((a_, b_, c_)) for a_ in range(2) for b_ in range(4) for c_ in range(2)]
            memset("pool", BDf[:], 0.0, bdkeys)
            for gi_, nm in enumerate(["b_w_r", "b_w_i"]):
                for cb in range(4):
                    for hb in range(2):
                        S.dma(BDf[64 * hb:64 * hb + 64, gi_, cb, 64 * hb:64 * hb + 64], D[nm][0, 2 * cb + hb],
                              w=[BDf.b((gi_, cb, hb))])
            cp("dve", BD[:], BDf[:], bdkeys, [BD])
            memset("pool", XB[:, 0:3], 0.0, [XB.b("pad")])
            S.dma(woB[:], W_OUT[l].rearrange("(c p) n -> p c n", p=128)[:, 4:8, :], w=[woB])
            for cb in range(4):
                wS = wsl[cb % 2]
                S.dma(wS[:, :, 0:128], W_IN[l].rearrange("(c p) n -> p c n", p=128)[:, :, 1280 + cb * 128:1280 + (cb + 1) * 128],
                      w=[wS.b("x")])
                S.dma(wS[:, :, 128:256], W_IN[l].rearrange("(c p) n -> p c n", p=128)[:, :, 1792 + cb * 128:1792 + (cb + 1) * 128],
                      w=[wS.b("g")])
                for g in range(4):
                    sl = slice(3 + g * 512, 3 + (g + 1) * 512)
                    proj_feat(pZ[:], wS, 0, g, [pZ], skey="x")
                    cp("act", XB[:, sl], pZ[:], [pZ], [XB.b(g)])
                    rd = [XB.b(g), XB.b(g - 1) if g > 0 else XB.b("pad"), PB.b(0), PB.b(1), PB.b(2), PB.b(3), PB.b(4)]
                    xc = tB
                    ts("dve", xc[:], XB[:, g * 512:g * 512 + 512], PB[:, cb, 0:1], PB[:, cb, 4:5], ALU.mult, ALU.add, rd, [tB])
                    for j in (1, 2, 3):
                        stt("dve", xc[:], XB[:, g * 512 + j:g * 512 + j + 512], PB[:, cb, j:j + 1], xc[:], ALU.mult, ALU.add,
                            rd + [tB], [tB])
                    cp("pool", xcb[:], xc[:], [tB], [xcb])
                    mm(pS0[:], BD[:, 0, cb, :], xcb[:], True, True, [BD, xcb], [pS0])
                    mm(pS1[:], BD[:, 1, cb, :], xcb[:], True, True, [BD, xcb], [pS1])
                    act(tC[:], pS0[:], AF.Exp, [pS0, PD.b(0)], [tC], scale=-1.0, bias=PD[:, cb, 0:1])
                    sigmoid_act(tC[:], [tC], [tC])
                    act(tC[:], tC[:], AF.Exp, [tC, PD.b(2)], [tC], scale=PD[:, cb, 2:3])
                    act(tD[:], pS1[:], AF.Exp, [pS1, PD.b(1)], [tD], scale=-1.0, bias=PD[:, cb, 1:2])
                    sigmoid_act(tD[:], [tD], [tD])
                    act(tE[:], tC[:], AF.Square, [tC], [tE])
                    act(tE[:], tE[:], AF.Ln, [tE], [tE], scale=-1.0, bias=1.0)
                    act(tE[:], tE[:], AF.Exp, [tE], [tE], scale=0.5)
                    tt("pool", tD[:], tD[:], xc[:], ALU.mult, [tD, tB], [tD])
                    tt("pool", tD[:], tD[:], tE[:], ALU.mult, [tD, tE], [tD])
                    init = 0.0 if g == 0 else hB[:, g * 512 - 1:g * 512]
                    op("dve", lambda e, g=g, init=init: e.tensor_tensor_scan(
                        out=hB[:, g * 512:(g + 1) * 512], data0=tC[:], data1=tD[:], initial=init,
                        op0=ALU.mult, op1=ALU.add), r=[tC, tD] + ([hB.b(g - 1)] if g > 0 else []), w=[hB.b(g)])
                    proj_feat(pZ[:], wS, 128, g, [pZ], skey="g")
                    silu_psum(tF[:], pZ[:], 512, [pZ], [tF], tE, tF)
                    tt("dve", mixB[:, cb, g * 512:(g + 1) * 512], tF[:], hB[:, g * 512:(g + 1) * 512], ALU.mult,
                       [tF, hB.b(g)], [mixB.b((cb, g))])
            for i in range(NT):
                for half in range(2):
                    for cb in range(4):
                        mm(pYb[half][:], mixB[:, cb, i * 128:(i + 1) * 128], woB[:, cb, half * 512:(half + 1) * 512],
                           cb == 0, cb == 3, [mixB.b((cb, i // 4)), woB], [pYb[half]])
                    tt("dve", H[:, i, half * 512:(half + 1) * 512], H[:, i, half * 512:(half + 1) * 512], pYb[half][:],
                       ALU.add, [H.b(i), pYb[half]], [H.b(i)])
            S.phase_reset()
        doA, doM = "A" in mix0, "M" in mix0
        if doA or doM:
            kT = S.carve("kT", [128, SEQ], BF16)
            V1 = S.carve("V1", [128, NT, 2, 65], BF16)
            wq = S.carve("wq", [128, 8, 512], BF16)
            wga = S.carve("wga", [128, 8, 512], BF16)
            wqgm = S.carve("wqgm", [128, 8, 512], BF16)
            woAM = S.carve("woAM", [128, 6, 1024], BF16)
            kmT = S.carve("kmT", [128, 2, N_MEM], BF16)
            VM1 = S.carve("VM1", [128, 2, 4, 65], BF16)
            memT = S.carve("memT", [128, 8, N_MEM], BF16)
            memf = S.carve("memf", [128, 1024], F32)
            gat = S.carve("gat", [128, 768], BF16)
            if doM:
                mem_kv(s, l, kmT, VM1, memf, memT, wq)
            if doA:
                load_slab(wga, W_IN[l], 512, 256)
                memset("dve", V1[:, :, :, 64:65], 1.0, [V1.b("ones")])
                for i in range(NT):
                    proj_tok(pZ[:, 0:256], wga, 0, 256, i, [pZ])
                    norm_rope(pZ[:, 0:128], 2, GI["a_kn"], ROPE[:, i, :], krot[:, 0:128].rearrange("p (h d) -> p h d", h=2),
                              [pZ], [krot], 0)
                    cp("act", V1[:, i, :, 0:64], pZ[:, 128:256].rearrange("p (h d) -> p h d", h=2), [pZ], [V1.b(i)])
                    tr(pT[:, 0, :], krot[:, 0:128], 128, [krot], [pT])
                    cp("act", kT[:, i * 128:(i + 1) * 128], pT[:, 0, :], [pT], [kT.b(i)])
            if doA:
                load_slab(wq, W_IN[l], 0, 512)
                load_slab(wga, W_IN[l], 768, 512)
                S.dma(woAM[:, 0:4, :], W_OUT[l].rearrange("(c p) n -> p c n", p=128)[:, 0:4, :], w=[woAM.b("a")])
            if doM:
                load_slab(wqgm, W_IN[l], 2304, 512)
                S.dma(woAM[:, 4:6, :], W_OUT[l].rearrange("(c p) n -> p c n", p=128)[:, 8:10, :], w=[woAM.b("m")])
            gatB = S.carve("gatB", [128, 768], BF16)
            gat2 = [gat, gatB]
            qzm2 = [S.carve("qzmA", [128, 2, 2, 128], BF16), S.carve("qzmB", [128, 2, 2, 128], BF16)]
            for q_ in qzm2:
                memset("dve", q_[:], 0.0, [q_.b(0), q_.b(1)])

            def frontA(i):
                par = i % 2
                st = [lambda: proj_tok(pZ[:], wga, 0, 512, i, [pZ]),
                      lambda: silu_psum(gat2[par][:, 0:512], pZ[:], 512, [pZ], [gat2[par].b("a")], tF, tF),
                      lambda: proj_tok(pZ[:], wq, 0, 512, i, [pZ])]
                st += norm_rope_stages(pZ[:], 8, GI["a_qn"], ROPE[:, i, :], qrot[:].rearrange("p (h d) -> p h d", h=8),
                                       [pZ], [qrot], 1)

                def t_():
                    for pr in range(4):
                        tr(pT[:, pr, :], qrot[:, pr * 128:(pr + 1) * 128], 128, [qrot], [pT])
                def c_():
                    t_()
                    cp("act", qz2[par][0:64, 0, :, :], pT[0:64, 0:4, :], [pT], [qz2[par].b(0)])
                    cp("act", qz2[par][64:128, 1, :, :], pT[64:128, 0:4, :], [pT], [qz2[par].b(1)])
                st.append(c_)
                return st

            def front(i):
                st = []
                if doA:
                    st += frontA(i)
                if doM:
                    st += mem_front_stages(i, l, wqgm, gat2[i % 2], qzm2[i % 2])
                return st

            for st_ in front(0):
                st_()
            for i in range(NT):
                par = i % 2
                fill = front(i + 1) if i + 1 < NT else []
                if doA:
                    kts = [i - 1, i] if i > 0 else [i]

                    def masksA(kt, i=i):
                        msk = mdiag if kt == i else mprev
                        return [(ident[:], bch(msk[:], 4), [ident, msk])]

                    def evacA(k):
                        tt("dve", sm8[:, 2, 4 * k:4 * k + 4], pOb[k][:, :, 64], esink[:, 4 * k:4 * k + 4], ALU.add,
                           [pOb[k], esink], [sm8.b((2, k))])
                        recip(sm8[:, 2, 4 * k:4 * k + 4], sm8[:, 2, 4 * k:4 * k + 4], [sm8.b((2, k))], [sm8.b((2, k))])
                        tt("dve", tE[:, 256 * k:256 * k + 256].rearrange("p (h d) -> p h d", h=4), pOb[k][:, :, 0:64],
                           bc3(sm8[:, 2, 4 * k:4 * k + 4], 64), ALU.mult, [pOb[k], sm8.b((2, k))], [tE])

                    nfa = len(fill) // 2 if doM else len(fill)
                    fa = [fill.pop(0) for _ in range(nfa)]
                    attn_pipeline([dict(k=k, kT=kT, V=V1, kts=kts, masks=masksA, evac=evacA) for k in range(2)],
                                  qz2[par], fa)
                    tt("dve", ob16[:, 0:512], tE[:], gat2[par][:, 0:512], ALU.mult, [tE, gat2[par].b("a")], [ob16])
                    transposes_to_oT(4, 0, 0)
                if doM:
                    mem_back(i, l, kmT, VM1, gat2[par], qzm2[par])
                while fill:
                    fill.pop(0)()
                chunks = ([0, 1, 2, 3] if doA else []) + ([4, 5] if doM else [])
                for half in range(2):
                    for n_, c in enumerate(chunks):
                        mm(pYb[half][:], oT[:, c, :], woAM[:, c, half * 512:(half + 1) * 512], n_ == 0, n_ == len(chunks) - 1,
                           [oT, woAM.b("a"), woAM.b("m")], [pYb[half]])
                    tt("dve", H[:, i, half * 512:(half + 1) * 512], H[:, i, half * 512:(half + 1) * 512], pYb[half][:],
                       ALU.add, [H.b(i), pYb[half]], [H.b(i)])
            S.phase_reset()

    def separator():
        mm(pY1[:, 0:1], ident[:], ident[:, 0:1], True, True, [ident], [pY1])

    def hgrn2_phase(s):
        l = 1
        QT = S.carve("QT", [128, 4, 512], BF16)
        KT = S.carve("KT", [128, 4, 512], BF16)
        KH = S.carve("KH", [128, 4, 512], BF16)
        Vc = S.carve("Vc", [128, 4, 512], BF16)
        wsm = [S.carve("wsmC0", [128, 8, 256], BF16), S.carve("wsmC1", [128, 8, 256], BF16)]
        wic = S.carve("wic", [128, 8, 512], BF16)
        wgc = S.carve("wgc", [128, 8, 512], BF16)
        woC = S.carve("woC", [128, 4, 1024], BF16)
        St = S.carve("St", [128, 4, 128], F32)
        Sb = S.carve("Sb", [128, 4, 128], BF16)
        EBL = S.carve("EBL", [128, 4, 32], F32)
        ATb = S.carve("ATb", [128, 4, 128], BF16)
        KHz = S.carve("KHz", [128, 4, 2, 128], BF16)
        tri = S.carve("tri", [128, 64], F32)
        Bp = S.carve("Bp", [128, 8], F32)
        tG = S.carve("tG", [128, 512], F32)
        tH = S.carve("tH", [128, 512], F32)
        gcb = S.carve("gcb", [128, 512], BF16)
        S.dma(tri[:], D["tri64"], w=[tri])
        memset("dve", St[:], 0.0, [St])
        memset("dve", Sb[:], 0.0, [Sb])
        memset("dve", ATb[:], 0.0, [ATb.b(0), ATb.b(1)])
        memset("dve", KHz[:], 0.0, [KHz.b(0), KHz.b(1)])
        memset("dve", Bp[:], 0.0, [Bp])
        load_slab(wic, W_IN[l], 1024, 512)
        load_slab(wgc, W_IN[l], 1536, 512)
        S.dma(woC[:], W_OUT[l].rearrange("(c p) n -> p c n", p=128)[:, 0:4, :], w=[woC])
        pS0v = pS0[:].rearrange("p (h t) -> p h t", h=4)
        pS1v = pS1[:].rearrange("p (h t) -> p h t", h=4)
        for g in range(4):
            for hd in range(4):
                wS = wsm[(g * 4 + hd) % 2]
                wv = W_IN[l].rearrange("(c p) n -> p c n", p=128)
                S.dma(wS[:, :, 0:128], wv[:, :, hd * 128:(hd + 1) * 128], w=[wS.b("q")])
                S.dma(wS[:, :, 128:256], wv[:, :, 512 + hd * 128:512 + (hd + 1) * 128], w=[wS.b("f")])
                proj_feat(pZ[:], wS, 128, g, [pZ], skey="f")
                act(tB[:], pZ[:], AF.Exp, [pZ], [tB], scale=-1.0)
                sigmoid_act(tB[:], [tB], [tB])
                ts("dve", tB[:], tB[:], LB[:, 1, hd:hd + 1], LB[:, 0, hd:hd + 1], ALU.mult, ALU.add,
                   [tB, LB.b(0), LB.b(1)], [tB])
                act(tC[:], tB[:], AF.Ln, [tB], [tC])
                ts("pool", tB[:], tB[:], -1.0, 1.0, ALU.mult, ALU.add, [tB], [tB])
                op("dve", lambda e: e.tensor_tensor_scan(out=tD[:], data0=ones512[:], data1=tC[:], initial=0.0,
                                                         op0=ALU.mult, op1=ALU.add), r=[ones512, tC], w=[tD])
                tD3 = tD[:].rearrange("p (c j) -> p c j", j=64)
                cp("dve", Bp[:, 1:8], tD3[:, 0:7, 63], [tD], [Bp])
                tt("dve", tD3, tD3, bc3(Bp[:, 0:8], 64), ALU.subtract, [tD, Bp], [tD])
                act(tE[:], tD[:], AF.Exp, [tD], [tE])
                cp("dve", EBL[:, hd, 8 * g:8 * g + 8], tE[:].rearrange("p (c j) -> p c j", j=64)[:, :, 63], [tE],
                   [EBL.b((hd, g))])
                act(tF[:], tD[:], AF.Exp, [tD], [tF], scale=-1.0)
                tt("dve", KT[:, hd, :], tB[:], tF[:], ALU.mult, [tB, tF], [KT.b(hd)])
                tt("dve", tG[:].rearrange("p (c j) -> p c j", j=64), tD3, bc3(tD3[:, :, 63], 64), ALU.subtract,
                   [tD], [tG])
                act(tG[:], tG[:], AF.Exp, [tG], [tG], scale=-1.0)
                tt("dve", KH[:, hd, :], tB[:], tG[:], ALU.mult, [tB, tG], [KH.b(hd)])
                proj_feat(pZ[:], wS, 0, g, [pZ], skey="q")
                silu_psum(tH[:], pZ[:], 512, [pZ], [tH], tF, tH)
                tt("dve", QT[:, hd, :], tH[:], tE[:], ALU.mult, [tH, tE], [QT.b(hd)])
            for tl in range(4):
                i = 4 * g + tl
                proj_tok(pZ[:], wic, 0, 512, i, [pZ])
                cp("act", Vc[:, tl, :], pZ[:], [pZ], [Vc.b(tl)])
            for tl in range(4):
                i = 4 * g + tl
                cs = tl * 128
                allh = [QT.b(h_) for h_ in range(4)]
                for hd in range(4):
                    tr(pT[:, hd, :], KH[:, hd, cs:cs + 128], 128, [KH.b(hd)], [pT])
                cp("act", KHz[0:64, :, 0, :], pT[0:64, 0:4, :], [pT], [KHz.b(0)])
                cp("act", KHz[64:128, :, 1, :], pT[64:128, 0:4, :], [pT], [KHz.b(1)])
                for hd in range(4):
                    for c in range(2):
                        mm(pS0v[64 * c:64 * c + 64, hd, 64 * c:64 * c + 64], KT[:, hd, cs + 64 * c:cs + 64 * c + 64],
                           QT[:, hd, cs + 64 * c:cs + 64 * c + 64], True, True, [KT.b(hd), QT.b(hd)], [pS0])
                for c in range(2):
                    tt("dve", ATb[64 * c:64 * c + 64, :, 64 * c:64 * c + 64], pS0v[64 * c:64 * c + 64, :, 64 * c:64 * c + 64],
                       bch(tri[64 * c:64 * c + 64, :], 4), ALU.mult, [pS0, tri], [ATb.b(c)])
                for hd in range(4):
                    mm(pO0[:, hd, :], ATb[:, hd, :], Vc[:, tl, hd * 128:(hd + 1) * 128], True, True,
                       [ATb.b(0), ATb.b(1), Vc.b(tl)], [pO0])
                for c in range(2):
                    ch = 8 * g + 2 * tl + c
                    for hd in range(4):
                        mm(pO1[64 * c:64 * c + 64, hd, :], QT[:, hd, cs + 64 * c:cs + 64 * c + 64], Sb[:, hd, :], True, True,
                           [QT.b(hd), Sb], [pO1])
                    for hd in range(4):
                        mm(pS1v[:, hd, :], KHz[:, hd, c, :], Vc[:, tl, hd * 128:(hd + 1) * 128], True, True,
                           [KHz.b(c), Vc.b(tl)], [pS1])
                    tt("dve", St[:], St[:], bc3(EBL[:, :, ch], 128), ALU.mult, [St] + [EBL.b((h_, g)) for h_ in range(4)], [St])
                    tt("dve", St[:], St[:], pS1v, ALU.add, [St, pS1], [St])
                    cp("act", Sb[:], St[:], [St], [Sb])
                cp("act", tG[:], pO0[:].rearrange("p h v -> p (h v)"), [pO0], [tG])
                tt("dve", tG[:], tG[:], pO1[:].rearrange("p h v -> p (h v)"), ALU.add, [tG, pO1], [tG])
                act(tA[:, 0:512], tG[:], AF.Square, [tG], [tA])
                op("dve", lambda e: e.tensor_reduce(out=sm8[:, 3, 0:4], in_=tA[:, 0:512].rearrange("p (h v) -> p h v", h=4),
                                                    axis=AX.X, op=ALU.add), r=[tA], w=[sm8.b(3)])
                rstd_from_ss(sm8[:, 3, 0:4], 128.0, [sm8.b(3)], [sm8.b(3)])
                tG3 = tG[:].rearrange("p (h v) -> p h v", h=4)
                tt("dve", tG3, tG3, bc3(sm8[:, 3, 0:4], 128), ALU.mult, [tG, sm8.b(3)], [tG])
                tt("dve", tG3, tG3, bch(COG[:], 4), ALU.mult, [tG, COG], [tG])
                proj_tok(pZ[:], wgc, 0, 512, i, [pZ])
                silu_psum(gcb[:], pZ[:], 512, [pZ], [gcb], tH, tF)
                tt("dve", ob16[:, 0:512], tG[:], gcb[:], ALU.mult, [tG, gcb], [ob16])
                transposes_to_oT(4, 0, 0)
                h_update(i, 4, woC)

    def nsa_phase(s, doD, doM):
        l = 1
        ksT = S.carve("ksT", [128, SEQ], BF16)
        kwT = S.carve("kwT", [128, SEQ], BF16)
        Vs1 = S.carve("Vs1", [128, NT, 2, 65], BF16)
        Vw1 = S.carve("Vw1", [128, NT, 2, 65], BF16)
        kcT = S.carve("kcT", [128, 128], BF16)
        VC1 = S.carve("VC1", [128, 2, 97], BF16)
        cmpneg = S.carve("cmpneg", [128, NT, 128], BF16)
        selE = S.carve("selE", [128, SEQ], BF16)
        seladj = S.carve("seladj", [128, NT, 32], F32)
        ROPEC = S.carve("ROPEC", [128, 96], F32)
        kmT = S.carve("kmT1", [128, 2, N_MEM], BF16)
        VM1 = S.carve("VM11", [128, 2, 4, 65], BF16)
        mark = S.arena_off
        if doM:
            wkv = S.carve("wkv1", [128, 8, 512], BF16)
            memT = S.carve("memT1", [128, 8, N_MEM], BF16)
            memf = S.carve("memf1", [128, 1024], F32)
            mem_kv(s, l, kmT, VM1, memf, memT, wkv)
            S.phase_reset(mark)
        if doD:
            tmpf = S.carve("tmpf", [128, 2048], F32)
            w512 = S.carve("w512", [128, 8, 512], BF16)
            wkc = S.carve("wkc", [128, 8, 256], BF16)
            kcdT = S.carve("kcdT", [128, SEQ], BF16)
            vcdT = S.carve("vcdT", [128, SEQ], BF16)
            W1r = S.carve("W1r", [128, 32, 128], BF16)
            w2f = S.carve("w2f", [128, 2, 64], F32)
            w2b = S.carve("w2b", [128, 2, 64], BF16)
            pef = S.carve("pef", [128, 32], F32)
            peT = S.carve("peT", [128, 32], BF16)
            cb = S.carve("cb", [128, 2], F32)
            hid = S.carve("hid", [128, 2, 128], BF16)
            S.dma(tmpf[0:N_CMP, :], D["cmpmask"].rearrange("j i b -> j (i b)"), w=[tmpf])
            ts("dve", cmpneg[0:N_CMP, :, :].rearrange("p i b -> p (i b)"), tmpf[0:N_CMP, :], -1.0, -NEG, ALU.add, ALU.mult,
               [tmpf], [cmpneg])
            S.dma(tmpf[0:32, :], D["selE"], w=[tmpf])
            cp("dve", selE[0:32, :], tmpf[0:32, :], [tmpf], [selE])
            S.dma(seladj[:], D["seladj"].rearrange("(i p) j -> p i j", p=128), w=[seladj])
            S.dma(ROPEC[0:N_CMP, :], D["rope_cmp"], w=[ROPEC])
            S.dma(tmpf[0:N_CMP, 0:32], D["ovl"], w=[tmpf])
            for k in range(2):
                cp("dve", VC1[0:N_CMP, k, 65:97], tmpf[0:N_CMP, 0:32], [tmpf], [VC1.b(("o", k))])
            memset("dve", VC1[:, :, 64:65], 1.0, [VC1.b("ones")])
            memset("dve", Vs1[:, :, :, 64:65], 1.0, [Vs1.b("ones")])
            memset("dve", Vw1[:, :, :, 64:65], 1.0, [Vw1.b("ones")])
            S.dma(w2f[:, 0, :], D["d_w2k"][0], w=[w2f.b(0)])
            S.dma(w2f[:, 1, :], D["d_w2v"][0], w=[w2f.b(1)])
            cp("dve", w2b[:], w2f[:], [w2f.b(0), w2f.b(1)], [w2b])
            load_slab(w512, W_IN[l], 2816, 512)
            load_slab(wkc, W_IN[l], 2560, 256)
            for i in range(NT):
                proj_tok(pZ[:], w512, 0, 512, i, [pZ])
                norm_rope(pZ[:, 0:128], 2, GI["d_kn_slc"], ROPE[:, i, :], krot[:, 0:128].rearrange("p (h d) -> p h d", h=2),
                          [pZ], [krot.b(0)], 0)
                norm_rope(pZ[:, 256:384], 2, GI["d_kn_win"], ROPE[:, i, :], krot[:, 128:256].rearrange("p (h d) -> p h d", h=2),
                          [pZ], [krot.b(1)], 1)
                cp("act", Vs1[:, i, :, 0:64], pZ[:, 128:256].rearrange("p (h d) -> p h d", h=2), [pZ], [Vs1.b(i)])
                cp("act", Vw1[:, i, :, 0:64], pZ[:, 384:512].rearrange("p (h d) -> p h d", h=2), [pZ], [Vw1.b(i)])
                tr(pT[:, 0, :], krot[:, 0:128], 128, [krot.b(0)], [pT])
                tr(pT[:, 1, :], krot[:, 128:256], 128, [krot.b(1)], [pT])
                cp("act", ksT[:, i * 128:(i + 1) * 128], pT[:, 0, :], [pT], [ksT.b(i)])
                cp("act", kwT[:, i * 128:(i + 1) * 128], pT[:, 1, :], [pT], [kwT.b(i)])
            for g in range(4):
                proj_feat(pZ[:], wkc, 0, g, [pZ])
                cp("act", kcdT[:, g * 512:(g + 1) * 512], pZ[:], [pZ], [kcdT.b(g)])
                proj_feat(pZ[:], wkc, 128, g, [pZ])
                cp("act", vcdT[:, g * 512:(g + 1) * 512], pZ[:], [pZ], [vcdT.b(g)])
            for kind_i, kind in enumerate(["k", "v"]):
                srcT = kcdT if kind == "k" else vcdT
                w1v = W_1[kind].rearrange("(l d) m -> d l m", d=64)
                S.dma(W1r[0:64, :, :], w1v, w=[W1r.b(0)])
                S.dma(W1r[64:128, :, :], w1v, w=[W1r.b(1)])
                pesrc = D["d_pe_k" if kind == "k" else "d_pe_v"][0].rearrange("l d -> d l")
                S.dma(pef[0:64, :], pesrc, w=[pef], allow_slow_non_contiguous=True)
                cp("dve", peT[0:64, :], pef[0:64, :], [pef], [peT])
                for l_ in range(32):
                    mm(pY0[:, 0:1], W1r[0:64, l_, :], peT[0:64, l_:l_ + 1], l_ == 0, l_ == 31, [W1r.b(0), peT], [pY0])
                cp("dve", cb[:, 0:1], pY0[:, 0:1], [pY0], [cb])
                ts("dve", cb[:, 1:2], cb[:, 0:1], -1.0, None, ALU.mult, None, [cb], [cb])
                s3 = srcT[:].rearrange("p (j s) -> p j s", s=16)
                srcb = [srcT.b(g_) for g_ in range(4)]
                for k in range(2):
                    separator()
                    for l_ in range(32):
                        rhs = s3[64 * k:64 * k + 64, 0:N_CMP, l_] if l_ < 16 else s3[64 * k:64 * k + 64, 1:N_CMP + 1, l_ - 16]
                        mm(pSb[k][:, 0:N_CMP], W1r[64 * k:64 * k + 64, l_, :], rhs, l_ == 0, l_ == 31,
                           [W1r.b(k)] + srcb, [pSb[k]])
                    separator()
                    act(tB[:, 0:N_CMP], pSb[k][:, 0:N_CMP], AF.Exp, [pSb[k], cb], [tB], scale=-1.0, bias=cb[:, 1:2])
                    act(tC[:, 0:N_CMP], pSb[k][:, 0:N_CMP], AF.Identity, [pSb[k], cb], [tC], bias=cb[:, 0:1])
                    sigmoid_act(tB[:, 0:N_CMP], [tB], [tB])
                    tt("dve", hid[:, k, 0:N_CMP], tC[:, 0:N_CMP], tB[:, 0:N_CMP], ALU.mult, [tB, tC], [hid.b(k)])
                for k in range(2):
                    c0 = kind_i * 128 + k * 64
                    mm(pZ[0:N_CMP, c0:c0 + 64], hid[:, k, 0:N_CMP], w2b[:, kind_i, :], True, True, [hid.b(k), w2b], [pZ])
            norm_rope(pZ[0:N_CMP, 0:128], 2, GI["d_kn_cmp"], ROPEC[0:N_CMP, :],
                      krot[0:N_CMP, 0:128].rearrange("p (h d) -> p h d", h=2), [pZ], [krot.b(0)], 0, np_=N_CMP)
            cp("act", VC1[0:N_CMP, :, 0:64], pZ[0:N_CMP, 128:256].rearrange("p (h d) -> p h d", h=2), [pZ], [VC1.b("v")])
            tr(pT[:, 0, 0:N_CMP], krot[0:N_CMP, 0:128], N_CMP, [krot.b(0)], [pT])
            cp("act", kcT[:, 0:N_CMP], pT[:, 0, 0:N_CMP], [pT], [kcT])
            S.phase_reset(mark)
        wq = S.carve("wq1", [128, 8, 512], BF16)
        wgd = S.carve("wgd", [128, 8, 512], BF16)
        wqgm = S.carve("wqgm1", [128, 8, 512], BF16)
        wgt = S.carve("wgt", [128, 8, 24], BF16)
        woDM = S.carve("woDM", [128, 6, 1024], BF16)
        acc = S.carve("acc", [128, 512], F32)
        negT = S.carve("negT", [128, 2, 128], BF16)
        nb = S.carve("nb", [128, 2, 32], BF16)
        gts = S.carve("gts", [128, 24], F32)
        gat = S.carve("gat1", [128, 768], BF16)
        impk = S.carve("impk", [128, 64], F32)
        wv = W_IN[l].rearrange("(c p) n -> p c n", p=128)
        if doD:
            load_slab(wq, W_IN[l], 2048, 512)
            load_slab(wgd, W_IN[l], 3352, 512)
            S.dma(wgt[:], wv[:, :, 3328:3352], w=[wgt])
            S.dma(woDM[:, 0:4, :], W_OUT[l].rearrange("(c p) n -> p c n", p=128)[:, 4:8, :], w=[woDM.b("a")])
        if doM:
            load_slab(wqgm, W_IN[l], 3864, 512)
            S.dma(woDM[:, 4:6, :], W_OUT[l].rearrange("(c p) n -> p c n", p=128)[:, 8:10, :], w=[woDM.b("m")])
        gtsB = S.carve("gtsB", [128, 24], F32)
        gatB = S.carve("gat1B", [128, 768], BF16)
        gat2 = [gat, gatB]
        gts2 = [gts, gtsB]

        def evac_branch(k, br, first, gtc):
            gts3 = gtc[:].rearrange("p (h b) -> p h b", b=3)
            ts("dve", sm8[:, 2, 4 * k:4 * k + 4], pOb[k][:, :, 64], 1e-30, None, ALU.max, None, [pOb[k]], [sm8.b((2, k))])
            recip(sm8[:, 2, 4 * k:4 * k + 4], sm8[:, 2, 4 * k:4 * k + 4], [sm8.b((2, k))], [sm8.b((2, k))])
            tt("dve", sm8[:, 3, 4 * k:4 * k + 4], sm8[:, 2, 4 * k:4 * k + 4], gts3[:, 4 * k:4 * k + 4, br], ALU.mult,
               [sm8.b((2, k)), gtc], [sm8.b((3, k))])
            a3 = acc[:, 256 * k:256 * k + 256].rearrange("p (h d) -> p h d", h=4)
            if first:
                tt("dve", a3, pOb[k][:, :, 0:64], bc3(sm8[:, 3, 4 * k:4 * k + 4], 64), ALU.mult, [pOb[k], sm8.b((3, k))],
                   [acc.b(k)])
            else:
                t3 = tE[:, 256 * k:256 * k + 256].rearrange("p (h d) -> p h d", h=4)
                tt("dve", t3, pOb[k][:, :, 0:64], bc3(sm8[:, 3, 4 * k:4 * k + 4], 64), ALU.mult, [pOb[k], sm8.b((3, k))],
                   [tE])
                tt("dve", a3, a3, t3, ALU.add, [acc.b(k), tE], [acc.b(k)])

        def frontD(i):
            par = i % 2
            st = []
            st.append(lambda: proj_tok(pZ[:], wgd, 0, 512, i, [pZ]))
            st.append(lambda: silu_psum(gat2[par][:, 0:512], pZ[:], 512, [pZ], [gat2[par].b("a")], tF, tF))
            st.append(lambda: proj_tok(pZ[:, 0:24], wgt, 0, 24, i, [pZ]))

            def g_():
                act(gts2[par][:], pZ[:, 0:24], AF.Exp, [pZ], [gts2[par]], scale=-1.0)
                sigmoid_act(gts2[par][:], [gts2[par]], [gts2[par]])
            st.append(g_)
            st.append(lambda: proj_tok(pZ[:], wq, 0, 512, i, [pZ]))
            st.extend(norm_rope_stages(pZ[:], 8, GI["d_qn"], ROPE[:, i, :], qrot[:].rearrange("p (h d) -> p h d", h=8),
                                       [pZ], [qrot], 1))

            def t_():
                for pr in range(4):
                    tr(pT[:, pr, :], qrot[:, pr * 128:(pr + 1) * 128], 128, [qrot], [pT])
            def c_():
                t_()
                cp("act", qz2[par][0:64, 0, :, :], pT[0:64, 0:4, :], [pT], [qz2[par].b(0)])
                cp("act", qz2[par][64:128, 1, :, :], pT[64:128, 0:4, :], [pT], [qz2[par].b(1)])
            st.append(c_)
            return st

        if doD:
            for st_ in frontD(0):
                st_()
        for i in range(NT):
            par = i % 2
            qzc, gtc, gac = qz2[par], gts2[par], gat2[par]
            if doD:
                for k in range(2):
                    ps = pSb[k]
                    psv = ps[0:N_CMP, :].rearrange("p (g t) -> p g t", g=4)
                    mm(psv, kcT[:, 0:N_CMP], qzc[:, k, :, :], True, False, [kcT, qzc.b(k)], [ps])
                    mm(psv, ident[0:N_CMP, 0:N_CMP], bch(cmpneg[0:N_CMP, i, :], 4), False, True, [ident, cmpneg], [ps])
                    act(Pb[0:N_CMP, k, :], ps[0:N_CMP, :], AF.Exp, [ps], [Pb.b(k)], scale=0.125)
                for k in range(2):
                    memset("dve", pOb[k][:], 0.0, [pOb[k]])
                    for g in range(4):
                        mm(pOb[k][:, g, 0:97], Pb[0:N_CMP, k, g * 128:(g + 1) * 128], VC1[0:N_CMP, k, :], False, False,
                           [Pb.b(k), VC1.b("ones"), VC1.b("v"), VC1.b(("o", k))], [pOb[k]], skip=True)
                for k in range(2):
                    evac_branch(k, 0, True, gtc)
                    t3 = tC[:, 0:128].rearrange("p (g j) -> p g j", g=4)
                    tt("dve", t3, pOb[k][:, :, 65:97], bc3(sm8[:, 2, 4 * k:4 * k + 4], 32), ALU.mult,
                       [pOb[k], sm8.b((2, k))], [tC])
                    op("dve", lambda e, k=k: e.tensor_reduce(out=impk[:, 32 * k:32 * k + 32],
                                                             in_=tC[:, 0:128].rearrange("p (g j) -> p j g", g=4),
                                                             axis=AX.X, op=ALU.add), r=[tC], w=[impk.b(k)])
                    tt("dve", impk[:, 32 * k:32 * k + 32], impk[:, 32 * k:32 * k + 32], seladj[:, i, :], ALU.add,
                       [impk.b(k), seladj], [impk.b(k)])
                    op("dve", lambda e, k=k: e.max(out=sm8[:, 6, 0:8], in_=impk[:, 32 * k:32 * k + 32]), r=[impk.b(k)],
                       w=[sm8.b(6)])
                    ts("dve", tD[:, 0:32], impk[:, 32 * k:32 * k + 32], sm8[:, 6, 3:4], None, ALU.is_ge, None,
                       [impk.b(k), sm8.b(6)], [tD])
                    ts("dve", nb[:, k, :], tD[:, 0:32], -1.0, -NEG, ALU.add, ALU.mult, [tD], [nb.b(k)])
                    tr(pT[0:32, 4 + k, :], nb[:, k, :], 128, [nb.b(k)], [pT])
                cp("act", negT[0:32, :, :], pT[0:32, 4:6, :], [pT], [negT])
                fill = frontD(i + 1) if i + 1 < NT else []

                def masks_sel(kt, k, i=i):
                    ex = [(selE[0:32, kt * 128:(kt + 1) * 128], bch(negT[0:32, k, :], 4), [selE, negT])]
                    if kt == i:
                        ex.append((ident[:], bch(mdiag[:], 4), [ident, mdiag]))
                    return ex

                def masks_win(kt, i=i):
                    if kt == i:
                        return [(ident[:], bch(mdiag[:], 4), [ident, mdiag])]
                    if kt == i - 4:
                        return [(ident[:], bch(mprev[:], 4), [ident, mprev])]
                    return []

                kts_s = list(range(0, i + 1))
                kts_w = list(range(max(0, i - 4), i + 1))
                brs = []
                for k in range(2):
                    brs.append(dict(k=k, kT=ksT, V=Vs1, kts=kts_s, masks=(lambda kt, k=k: masks_sel(kt, k)),
                                    evac=(lambda k_, gtc=gtc: evac_branch(k_, 1, False, gtc))))
                for k in range(2):
                    brs.append(dict(k=k, kT=kwT, V=Vw1, kts=kts_w, masks=masks_win,
                                    evac=(lambda k_, gtc=gtc: evac_branch(k_, 2, False, gtc))))
                attn_pipeline(brs, qzc, fill)
                tt("dve", ob16[:, 0:512], acc[:], gac[:, 0:512], ALU.mult, [acc.b(0), acc.b(1), gac.b("a")], [ob16])
                transposes_to_oT(4, 0, 0)
            if doM:
                proj_tok(pZ[:], wqgm, 0, 512, i, [pZ])
                silu_psum(gac[:, 512:768], pZ[:, 256:512], 256, [pZ], [gac.b("m")], tF, tF)
                mem_attn_tile(i, pZ[:, 0:256], [pZ], l, kmT, VM1, gac[:, 512:768], [gac.b("m")], 512, qzt=qzc)
                for c in range(2):
                    tr(pT[:, 4 + c, :], ob16[:, 512 + c * 128:512 + (c + 1) * 128], 128, [ob16], [pT])
                cp("act", oT[:, 4:6, :], pT[:, 4:6, :], [pT], [oT])
            chunks = ([0, 1, 2, 3] if doD else []) + ([4, 5] if doM else [])
            for half in range(2):
                for n_, c in enumerate(chunks):
                    mm(pYb[half][:], oT[:, c, :], woDM[:, c, half * 512:(half + 1) * 512], n_ == 0, n_ == len(chunks) - 1,
                       [oT, woDM.b("a"), woDM.b("m")], [pYb[half]])
                tt("dve", H[:, i, half * 512:(half + 1) * 512], H[:, i, half * 512:(half + 1) * 512], pYb[half][:],
                   ALU.add, [H.b(i), pYb[half]], [H.b(i)])

    def layer1(s):
        rmsnorm_to_xnT()
        S.phase_reset()
        if "C" in mix1:
            hgrn2_phase(s)
            S.phase_reset()
        doD, doM = "D" in mix1, "M" in mix1
        if doD or doM:
            nsa_phase(s, doD, doM)
            S.phase_reset()

    for s in range(nseq):
        for i in range(NT):
            S.dma(H[:, i, :], D["x"][s, i * 128:(i + 1) * 128, :], w=[H.b(i)])
        if 0 in layers:
            layer0(s)
        if 1 in layers:
            layer1(s)
        for i in range(NT):
            S.dma(Y[s, i * 128:(i + 1) * 128, :], H[:, i, :], r=[H.b(i)])
        S.phase_reset()
    S.emit()
    return nc, S


N_CORES = 8
_PROG = {}


def kernel(**inputs):
    x = np.ascontiguousarray(inputs["x"], dtype=np.float32)
    mem = np.ascontiguousarray(inputs["mem"], dtype=np.float32)
    B = x.shape[0]
    per = B // N_CORES
    if per not in _PROG:
        _PROG[per] = build_program(per)[0]
    nc = _PROG[per]
    consts = host_consts()
    params = {k: np.ascontiguousarray(inputs[k], dtype=np.float32) for k in PARAM_SHAPES}
    in_maps = []
    for c in range(N_CORES):
        m = {"x": x[c * per:(c + 1) * per], "mem": mem[c * per:(c + 1) * per]}
        m.update(params)
        m.update(consts)
        in_maps.append(m)
    res = run_bass_kernel_spmd(nc, in_maps, core_ids=list(range(N_CORES)))
    return np.concatenate([r["y"] for r in res.results], axis=0)
```

```python
from contextlib import ExitStack
import numpy as np
import concourse.bass as bass
import concourse.mybir as mybir
from concourse.bass_utils import run_bass_kernel_spmd

F32 = mybir.dt.float32
BF16 = mybir.dt.bfloat16
AF = mybir.ActivationFunctionType
ALU = mybir.AluOpType
AX = mybir.AxisListType

ENGINES = ["pe", "act", "dve", "pool", "sp"]
EPOCH = 30000
NDMASEM = 8
import os
NO_POOL = os.environ.get("K_NO_POOL", "1") == "1"
DBG = int(os.environ.get("K_DBG", "99"))
FILL_MODE = int(os.environ.get("K_FILL", "1"))

D_MODEL = 1024
SEQ = 2048
NT = SEQ // 128
N_MEM = 256
EVEN_IN = 2816
ODD_IN = 4376
EPS = 1e-6
NEG = -1024.0
N_CMP = 127


class Buf:
    __slots__ = ("name", "lw", "rd", "excl")

    def __init__(self, name, excl=False):
        self.name = name
        self.lw = None
        self.rd = {}
        self.excl = excl


class Op:
    __slots__ = ("eng", "fn", "deps", "is_dma", "needs_inc", "token", "waits", "dsem", "idx")

    def __init__(self, eng, fn, is_dma=False):
        self.eng = eng
        self.fn = fn
        self.deps = []
        self.is_dma = is_dma
        self.needs_inc = is_dma
        self.token = None
        self.waits = []
        self.dsem = None
        self.idx = None


class T:
    def __init__(self, h, name, excl=False):
        self.h = h
        self.name = name
        self.excl = excl
        self.whole = Buf(name, excl)
        self.subs = {}

    def __getitem__(self, idx):
        return self.h[idx]

    def b(self, key=None):
        if key is None:
            return self.whole
        s = self.subs.get(key)
        if s is None:
            s = Buf(f"{self.name}[{key}]", self.excl)
            self.subs[key] = s
        return s


class Sched:
    def __init__(self, nc):
        self.nc = nc
        self.es = ExitStack()
        self.ops = {e: [] for e in ENGINES}
        self.all_dma = []
        self.dma_since_bar = []
        self.pending = {e: [] for e in ENGINES}
        self.arena = None
        self.arena_words = 0
        self.arena_off = 0

    def sb(self, name, shape, dt):
        h = self.es.enter_context(self.nc.sbuf_tensor("sb_" + name, list(shape), dt))
        return T(h, name)

    def ps(self, name, shape, dt):
        h = self.es.enter_context(self.nc.psum_tensor("ps_" + name, list(shape), dt))
        return T(h, name, excl=True)

    def make_arena(self, words):
        self.arena = self.es.enter_context(self.nc.sbuf_tensor("arena", [128, words], F32))
        self.arena_words = words
        self.arena_off = 0

    def carve(self, name, shape, dt):
        n = 1
        for s in shape[1:]:
            n *= s
        words = (n + 1) // 2 if dt == BF16 else n
        words = (words + 7) // 8 * 8
        assert self.arena_off + words <= self.arena_words, (name, self.arena_off, words, self.arena_words)
        ap = self.arena[:, self.arena_off:self.arena_off + words]
        if dt == BF16:
            ap = ap.bitcast(BF16)[:, 0:n]
        else:
            ap = ap[:, 0:n]
        self.arena_off += words
        if len(shape) == 3:
            ap = ap.rearrange("p (a b) -> p a b", a=shape[1])
        elif len(shape) == 4:
            ap = ap.rearrange("p (a b c) -> p a b c", a=shape[1], b=shape[2])
        return T(ap, name)

    def phase_reset(self, to=0):
        self.barrier()
        self.arena_off = to

    def _bufs(self, xs):
        out = []
        for x in xs or []:
            out.append(x.whole if isinstance(x, T) else x)
        return out

    def op(self, eng, fn, r=None, w=None, is_dma=False):
        if eng == "pool" and NO_POOL and not is_dma:
            eng = "dve"
        o = Op(eng, fn, is_dma)
        skey = ("dma", len(self.all_dma)) if is_dma else eng
        deps = []
        rb = self._bufs(r)
        wb = self._bufs(w)
        ex = [b for b in rb if b.excl]
        if ex:
            rb = [b for b in rb if not b.excl]
            wb = wb + [b for b in ex if b not in wb]
        for b in rb:
            if b.lw is not None:
                deps.append(b.lw)
        for b in wb:
            if b.lw is not None:
                deps.append(b.lw)
            deps.extend(b.rd.values())
        if self.pending[eng]:
            deps.extend(self.pending[eng])
            self.pending[eng] = []
        for b in rb:
            b.rd[skey] = o
        for b in wb:
            b.lw = o
            b.rd = {}
        seen = set()
        for d in deps:
            if id(d) in seen or d is o:
                continue
            seen.add(id(d))
            if (not d.is_dma) and (not is_dma) and d.eng == eng and eng == "pe":
                continue
            o.deps.append(d)
        o.idx = len(self.ops[eng])
        self.ops[eng].append(o)
        if is_dma:
            self.all_dma.append(o)
            self.dma_since_bar.append(o)
        return o

    def dma(self, out, in_, r=None, w=None, q="sp", **kw):
        return self.op(q, lambda e: e.dma_start(out=out, in_=in_, **kw), r=r, w=w, is_dma=True)

    def barrier(self):
        lasts = []
        for e in ENGINES:
            for o in reversed(self.ops[e]):
                if not o.is_dma:
                    lasts.append(o)
                    break
        lasts.extend(self.dma_since_bar)
        self.dma_since_bar = []
        for e in ENGINES:
            self.pending[e] = list(self.pending[e]) + lasts

    def emit(self):
        nc = self.nc
        for e in ENGINES:
            for o in self.ops[e]:
                for d in o.deps:
                    d.needs_inc = True
        nsem_eng = {}
        for e in ENGINES:
            c = 0
            k = 0
            for o in self.ops[e]:
                if o.is_dma:
                    o.dsem = (e, k % NDMASEM)
                    k += 1
                elif o.needs_inc:
                    c += 1
                    o.token = (("e", e, (c - 1) // EPOCH), (c - 1) % EPOCH + 1)
            nsem_eng[e] = (c + EPOCH - 1) // EPOCH if c else 0
        dcount = {}
        prev_dma = {}
        for e in ENGINES:
            for o in self.ops[e]:
                if o.is_dma:
                    key = ("d",) + o.dsem
                    v = dcount.get(key, 0) + 16
                    dcount[key] = v
                    o.token = (key, v)
                    if key in prev_dma:
                        o.deps.append(prev_dma[key])
                    prev_dma[key] = o
        sems = {}
        for e in ENGINES:
            for ep in range(nsem_eng[e]):
                sems[("e", e, ep)] = self.es.enter_context(nc.semaphore(f"s_{e}_{ep}"))
        for key in dcount:
            sems[key] = self.es.enter_context(nc.semaphore(f"d_{key[1]}_{key[2]}"))
        for e in ENGINES:
            seen = {}
            for o in self.ops[e]:
                need = {}
                for d in o.deps:
                    k, v = d.token
                    if seen.get(k, 0) >= v:
                        continue
                    if need.get(k, 0) < v:
                        need[k] = v
                for k, v in need.items():
                    seen[k] = v
                o.waits = list(need.items())
        final_waits = list(dcount.items())
        self.nsems = len(sems)
        self.ninst = {e: len(self.ops[e]) for e in ENGINES}
        engmap = {"pe": "tensor", "act": "scalar", "dve": "vector", "pool": "gpsimd", "sp": "sync"}
        with nc.Block() as block:
            for e in ENGINES:
                ops = self.ops[e]

                def body(eng, ops=ops, e=e):
                    for o in ops:
                        for k, v in o.waits:
                            eng.wait_ge(sems[k], v)
                        ins = o.fn(eng)
                        if o.is_dma:
                            ins.then_inc(sems[o.token[0]], 16)
                        elif o.needs_inc:
                            ins.then_inc(sems[o.token[0]], 1)
                    if e == "sp":
                        for k, v in final_waits:
                            eng.wait_ge(sems[k], v)

                getattr(block, engmap[e])(body)
        self.es.close()


def host_consts():
    c = {}
    c["ident"] = np.eye(128, dtype=np.float32)
    half = 32
    inv = 10000.0 ** (-np.arange(half, dtype=np.float32) / half)
    pos = np.arange(SEQ, dtype=np.float32)
    ang = pos[:, None] * inv[None, :]
    c["rope_cs"] = np.concatenate([np.cos(ang), np.sin(ang), -np.sin(ang)], axis=1).astype(np.float32)
    cend = (np.arange(N_CMP) * 16 + 31).astype(np.float32)
    angc = cend[:, None] * inv[None, :]
    c["rope_cmp"] = np.concatenate([np.cos(angc), np.sin(angc), -np.sin(angc)], axis=1).astype(np.float32)
    a = np.arange(128)[:, None]
    b = np.arange(128)[None, :]
    c["mdiag"] = np.where(a <= b, 0.0, NEG).astype(np.float32)
    c["mprev"] = np.where(a > b, 0.0, NEG).astype(np.float32)
    j = np.arange(N_CMP)[:, None, None]
    i = np.arange(NT)[None, :, None]
    bb = np.arange(128)[None, None, :]
    c["cmpmask"] = ((16 * j + 31) <= (128 * i + bb)).astype(np.float32)
    s = np.arange(SEQ)[None, :]
    js = np.arange(32)[:, None]
    c["selE"] = ((s // 64) == js).astype(np.float32)
    n = np.arange(N_CMP)[:, None]
    jj = np.arange(32)[None, :]
    c["ovl"] = ((16 * n < 64 * jj + 64) & (16 * n + 32 > 64 * jj)).astype(np.float32)
    t = np.arange(SEQ)[:, None]
    cur = t // 64
    forced = (jj == 0) | (jj == cur)
    valid = jj <= cur
    c["seladj"] = np.where(forced, 1e4, np.where(valid, 0.0, -1e4)).astype(np.float32)
    tri = (np.arange(64)[:, None] <= np.arange(64)[None, :]).astype(np.float32)
    c["tri64"] = np.concatenate([tri, tri], axis=0)
    return c


CONST_SHAPES = {"ident": [128, 128], "rope_cs": [SEQ, 96], "rope_cmp": [N_CMP, 96], "mdiag": [128, 128],
                "mprev": [128, 128], "cmpmask": [N_CMP, NT, 128], "selE": [32, SEQ], "ovl": [N_CMP, 32],
                "seladj": [SEQ, 32], "tri64": [128, 64]}

PARAM_SHAPES = {
    "norm_g": [2, 1024], "mem_norm_g": [2, 1024], "mem_w_kv": [2, 1024, 512], "mem_qn": [2, 64], "mem_kn": [2, 64],
    "ev_w_in": [1, 1024, 2816], "ev_w_out": [1, 1280, 1024], "a_qn": [1, 64], "a_kn": [1, 64], "a_sinks": [1, 8],
    "b_conv_w": [1, 4, 512], "b_conv_b": [1, 512], "b_w_r": [1, 8, 64, 64], "b_b_r": [1, 512],
    "b_w_i": [1, 8, 64, 64], "b_b_i": [1, 512], "b_lambda": [1, 512], "od_w_in": [1, 1024, 4376],
    "od_w_out": [1, 1280, 1024], "c_lb": [2, 512], "c_onorm": [1, 128], "d_qn": [1, 64], "d_kn_cmp": [1, 64],
    "d_kn_slc": [1, 64], "d_kn_win": [1, 64], "d_pe_k": [1, 32, 64], "d_pe_v": [1, 32, 64],
    "d_w1k": [1, 2048, 128], "d_w2k": [1, 128, 64], "d_w1v": [1, 2048, 128], "d_w2v": [1, 128, 64],
}


def build_program(nseq, layers=(0, 1), mix0=("A", "B", "M"), mix1=("C", "D", "M")):
    nc = bass.Bass("TRN2", target_bir_lowering=False)
    D = {}
    D["x"] = nc.dram_tensor("x", [nseq, SEQ, D_MODEL], F32, kind="ExternalInput").ap()
    D["mem"] = nc.dram_tensor("mem", [nseq, N_MEM, D_MODEL], F32, kind="ExternalInput").ap()
    for k, shp in PARAM_SHAPES.items():
        D[k] = nc.dram_tensor(k, shp, F32, kind="ExternalInput").ap()
    for k, shp in CONST_SHAPES.items():
        D[k] = nc.dram_tensor(k, shp, F32, kind="ExternalInput").ap()
    Y = nc.dram_tensor("y", [nseq, SEQ, D_MODEL], F32, kind="ExternalOutput").ap()
    W_IN = [nc.dram_tensor("w_in0s", [1024, EVEN_IN], BF16, kind="Internal").ap(),
            nc.dram_tensor("w_in1s", [1024, ODD_IN], BF16, kind="Internal").ap()]
    W_OUT = [nc.dram_tensor("w_out0s", [1280, 1024], BF16, kind="Internal").ap(),
             nc.dram_tensor("w_out1s", [1280, 1024], BF16, kind="Internal").ap()]
    W_KV = [nc.dram_tensor("w_kv0s", [1024, 512], BF16, kind="Internal").ap(),
            nc.dram_tensor("w_kv1s", [1024, 512], BF16, kind="Internal").ap()]
    W_1 = {"k": nc.dram_tensor("w1ks", [2048, 128], BF16, kind="Internal").ap(),
           "v": nc.dram_tensor("w1vs", [2048, 128], BF16, kind="Internal").ap()}

    S = Sched(nc)
    op = S.op

    H = S.sb("H", [128, NT, 1024], F32)
    xnT = S.sb("xnT", [128, 8, SEQ], BF16)
    ident = S.sb("ident", [128, 128], BF16)
    ROPE = S.sb("ROPE", [128, NT, 96], F32)
    mdiag = S.sb("mdiag", [128, 128], BF16)
    mprev = S.sb("mprev", [128, 128], BF16)
    normg = S.sb("normg", [128, 2, 8], F32)
    memg = S.sb("memg", [128, 2, 8], F32)
    GN = S.sb("GN", [128, 14, 64], F32)
    GI = {"mem_qn0": 0, "mem_qn1": 1, "mem_kn0": 2, "mem_kn1": 3, "a_qn": 4, "a_kn": 5, "d_qn": 6, "d_kn_slc": 7,
          "d_kn_win": 8, "d_kn_cmp": 9}
    COG = S.sb("COG", [128, 128], F32)
    LB = S.sb("LB", [128, 2, 4], F32)
    ones512 = S.sb("ones512", [128, 512], F32)
    esink = S.sb("esink", [128, 8], F32)
    ss16 = S.sb("ss16", [128, NT], F32)
    rstd16 = S.sb("rstd16", [128, NT], F32)
    tA = S.sb("tA", [128, 512], F32)
    tB = S.sb("tB", [128, 512], F32)
    tC = S.sb("tC", [128, 512], F32)
    tD = S.sb("tD", [128, 512], F32)
    tE = S.sb("tE", [128, 512], F32)
    tF = S.sb("tF", [128, 512], F32)
    xnb = S.sb("xnb", [128, 1024], BF16)
    sm8 = S.sb("sm8", [128, 8, 8], F32)
    Pb = S.sb("Pb", [128, 2, 512], BF16)
    ob16 = S.sb("ob16", [128, 1280], BF16)
    oT = S.sb("oT", [128, 10, 128], BF16)
    qrot = S.sb("qrot", [128, 512], BF16)
    qz = S.sb("qz", [128, 2, 4, 128], BF16)
    qzB = S.sb("qzB", [128, 2, 4, 128], BF16)
    qz2 = [qz, qzB]
    krot = S.sb("krot", [128, 256], BF16)
    pZ = S.ps("pZ", [128, 512], F32)
    pS0 = S.ps("pS0", [128, 512], F32)
    pS1 = S.ps("pS1", [128, 512], F32)
    pSb = [pS0, pS1]
    pO0 = S.ps("pO0", [128, 4, 128], F32)
    pO1 = S.ps("pO1", [128, 4, 128], F32)
    pOb = [pO0, pO1]
    pT = S.ps("pT", [128, 8, 128], BF16)
    pY0 = S.ps("pY0", [128, 512], F32)
    pY1 = S.ps("pY1", [128, 512], F32)
    pYb = [pY0, pY1]

    ARENA_WORDS = 18 * 1024
    S.make_arena(ARENA_WORDS)

    def mm(out, lhsT, rhs, start, stop, r, w, skip=False):
        if skip:
            return op("pe", lambda e: e.matmul(out=out, lhsT=lhsT, rhs=rhs, start=start, stop=stop,
                                               skip_group_check=True), r=r, w=w)
        return op("pe", lambda e: e.matmul(out=out, lhsT=lhsT, rhs=rhs, start=start, stop=stop), r=r, w=w)

    def tr(out, in_, npart, r, w):
        return op("pe", lambda e: e.transpose(out=out, in_=in_, identity=ident[0:npart, 0:npart]), r=list(r) + [ident], w=w)

    def act(out, in_, func, r, w, scale=1.0, bias=0.0, accum=None):
        if accum is None:
            return op("act", lambda e: e.activation(out=out, in_=in_, func=func, scale=scale, bias=bias), r=r, w=w)
        return op("act", lambda e: e.activation(out=out, in_=in_, func=func, scale=scale, bias=bias, accum_out=accum), r=r, w=w)

    def tt(eng, out, in0, in1, o, r, w):
        return op(eng, lambda e: e.tensor_tensor(out=out, in0=in0, in1=in1, op=o), r=r, w=w)

    def ts(eng, out, in0, s1, s2, o0, o1, r, w):
        if s2 is None:
            return op(eng, lambda e: e.tensor_scalar(out=out, in0=in0, scalar1=s1, scalar2=None, op0=o0), r=r, w=w)
        return op(eng, lambda e: e.tensor_scalar(out=out, in0=in0, scalar1=s1, scalar2=s2, op0=o0, op1=o1), r=r, w=w)

    def stt(eng, out, in0, sc, in1, o0, o1, r, w):
        return op(eng, lambda e: e.scalar_tensor_tensor(out=out, in0=in0, scalar=sc, in1=in1, op0=o0, op1=o1), r=r, w=w)

    def cp(eng, out, in_, r, w):
        if eng == "act":
            return act(out, in_, AF.Copy, r, w)
        return op(eng, lambda e: e.tensor_copy(out=out, in_=in_), r=r, w=w)

    def recip(out, in_, r, w):
        return op("dve", lambda e: e.reciprocal(out=out, in_=in_), r=r, w=w)

    def memset(eng, ap, val, w):
        return op(eng, lambda e: e.memset(ap, val), w=w)

    def rstd_from_ss(ap, n_mean, r, w):
        act(ap, ap, AF.Ln, r, w, scale=1.0 / n_mean, bias=EPS)
        act(ap, ap, AF.Exp, w, w, scale=-0.5)

    def bc3(ap2, n):
        return ap2.unsqueeze(2).broadcast_to([ap2.shape[0], ap2.shape[1], n])

    def bch(ap2, nh):
        return ap2.unsqueeze(1).broadcast_to([ap2.shape[0], nh, ap2.shape[1]])

    stage = S.carve("stage0", [128, 2048], F32)
    stage1 = S.carve("stage1", [128, 2048], F32)
    stb0 = S.carve("stb0", [128, 2048], BF16)
    stb1 = S.carve("stb1", [128, 2048], BF16)
    stages = [(stage, stb0), (stage1, stb1)]

    S.dma(stage[:, 0:128], D["ident"], w=[stage])
    cp("dve", ident[:], stage[:, 0:128], [stage], [ident])
    S.dma(stage[:, 0:128], D["mdiag"], w=[stage])
    cp("dve", mdiag[:], stage[:, 0:128], [stage], [mdiag])
    S.dma(stage[:, 0:128], D["mprev"], w=[stage])
    cp("dve", mprev[:], stage[:, 0:128], [stage], [mprev])
    memset("dve", qz[:], 0.0, [qz.b(0), qz.b(1)])
    memset("dve", qzB[:], 0.0, [qzB.b(0), qzB.b(1)])
    S.dma(ROPE[:], D["rope_cs"].rearrange("(i p) f -> p i f", p=128), w=[ROPE])
    S.dma(normg[:], D["norm_g"].rearrange("l (c p) -> p l c", p=128), w=[normg], allow_slow_non_contiguous=True)
    S.dma(memg[:], D["mem_norm_g"].rearrange("l (c p) -> p l c", p=128), w=[memg], allow_slow_non_contiguous=True)
    for nm, gi in GI.items():
        if nm.startswith("mem_"):
            src = D[nm[:-1]][int(nm[-1])]
        else:
            src = D[nm][0]
        S.dma(GN[:, gi, :], src.partition_broadcast(128), w=[GN.b(gi)])
    for j_, nm_ in enumerate(["d_kn_slc", "d_kn_slc", "d_kn_win", "d_kn_win"]):
        S.dma(GN[:, 10 + j_, :], D[nm_][0].partition_broadcast(128), w=[GN.b(10)])
    S.dma(esink[:], D["a_sinks"][0].partition_broadcast(128), w=[esink])
    act(esink[:], esink[:], AF.Exp, [esink], [esink])
    S.dma(COG[:], D["c_onorm"][0].partition_broadcast(128), w=[COG])
    memset("dve", ones512[:], 1.0, [ones512])
    S.dma(LB[:, 0, :], D["c_lb"][0].rearrange("(h p) -> p h", p=128), w=[LB.b(0)], allow_slow_non_contiguous=True)
    S.dma(LB[:, 1, :], D["c_lb"][1].rearrange("(h p) -> p h", p=128), w=[LB.b(1)], allow_slow_non_contiguous=True)
    tt("dve", LB[:, 0, :], LB[:, 0, :], LB[:, 1, :], ALU.subtract, [LB.b(0), LB.b(1)], [LB.b(0)])
    act(LB[:, 0, :], LB[:, 0, :], AF.Exp, [LB.b(0)], [LB.b(0)])
    ts("dve", LB[:, 0, :], LB[:, 0, :], 1.0, None, ALU.add, None, [LB.b(0)], [LB.b(0)])
    recip(LB[:, 0, :], LB[:, 0, :], [LB.b(0)], [LB.b(0)])
    ts("dve", LB[:, 1, :], LB[:, 0, :], -1.0, 1.0, ALU.mult, ALU.add, [LB.b(0)], [LB.b(1)])

    cnt = [0]

    def conv_weight(src, dst, R, C, gt=None, l=0, perm0=None):
        for rc in range(R // 128):
            for c0 in range(0, C, 2048):
                cw = min(2048, C - c0)
                sf, sbf = stages[cnt[0] % 2]
                eng = "dve"
                cnt[0] += 1
                S.dma(sf[:, 0:cw], src[rc * 128:(rc + 1) * 128, c0:c0 + cw], w=[sf])
                if gt is not None:
                    ts(eng, sbf[:, 0:cw], sf[:, 0:cw], gt[:, l, rc:rc + 1], None, ALU.mult, None, [sf, gt], [sbf])
                else:
                    cp(eng, sbf[:, 0:cw], sf[:, 0:cw], [sf], [sbf])
                rows = slice(rc * 128, (rc + 1) * 128)
                if perm0 is not None and c0 <= perm0 < c0 + cw:
                    p0 = perm0 - c0
                    if p0 > 0:
                        S.dma(dst[rows, c0:c0 + p0], sbf[:, 0:p0], r=[sbf])
                    for w_ in range(2):
                        S.dma(dst[rows, perm0:perm0 + 512].rearrange("r (pr w d) -> r w pr d", pr=4, w=2)[:, w_],
                              sbf[:, p0 + w_ * 256:p0 + (w_ + 1) * 256].rearrange("p (pr d) -> p pr d", pr=4), r=[sbf])
                    if p0 + 512 < cw:
                        S.dma(dst[rows, perm0 + 512:c0 + cw], sbf[:, p0 + 512:cw], r=[sbf])
                else:
                    S.dma(dst[rows, c0:c0 + cw], sbf[:, 0:cw], r=[sbf])

    if 0 in layers:
        conv_weight(D["ev_w_in"][0], W_IN[0], 1024, EVEN_IN, normg, 0, perm0=0)
        conv_weight(D["ev_w_out"][0], W_OUT[0], 1280, 1024)
        conv_weight(D["mem_w_kv"][0], W_KV[0], 1024, 512, memg, 0)
    if 1 in layers:
        conv_weight(D["od_w_in"][0], W_IN[1], 1024, ODD_IN, normg, 1, perm0=2048)
        conv_weight(D["od_w_out"][0], W_OUT[1], 1280, 1024)
        conv_weight(D["mem_w_kv"][1], W_KV[1], 1024, 512, memg, 1)
        conv_weight(D["d_w1k"][0], W_1["k"], 2048, 128)
        conv_weight(D["d_w1v"][0], W_1["v"], 2048, 128)
    S.phase_reset()

    def load_slab(dst, src_w, c0, ncols, key=None):
        S.dma(dst[:, :, 0:ncols], src_w.rearrange("(c p) n -> p c n", p=128)[:, :, c0:c0 + ncols],
              w=[dst.b(key)])

    def proj_tok(ps_ap, slab, col0, ncols, i, w, skey=None):
        for c in range(8):
            mm(ps_ap, xnT[:, c, i * 128:(i + 1) * 128], slab[:, c, col0:col0 + ncols], c == 0, c == 7,
               [xnT.b(i), slab.b(skey)], w)

    def proj_feat(ps_ap, slab, col0, g, w, skey=None):
        for c in range(8):
            mm(ps_ap, slab[:, c, col0:col0 + 128], xnT[:, c, g * 512:(g + 1) * 512], c == 0, c == 7,
               [xnT.b(4 * g), xnT.b(4 * g + 1), xnT.b(4 * g + 2), xnT.b(4 * g + 3), slab.b(skey)], w)

    def silu_psum(out_ap, zp, n, rz, wout, t1, t2, np_=128):
        a1 = t1[0:np_, 0:n]
        act(a1, zp, AF.Exp, rz, [t1], scale=-1.0)
        act(a1, a1, AF.Ln, [t1], [t1], bias=1.0)
        act(a1, a1, AF.Exp, [t1], [t1], scale=-1.0)
        tt("dve", out_ap, zp, a1, ALU.mult, list(rz) + [t1], wout)

    def silu_stages(out_ap, zp, n, rz, wout, t1, np_=128):
        a1 = t1[0:np_, 0:n]
        return [lambda: act(a1, zp, AF.Exp, rz, [t1], scale=-1.0),
                lambda: act(a1, a1, AF.Ln, [t1], [t1], bias=1.0),
                lambda: act(a1, a1, AF.Exp, [t1], [t1], scale=-1.0),
                lambda: tt("dve", out_ap, zp, a1, ALU.mult, list(rz) + [t1], wout)]

    def sigmoid_act(ap, r, w):
        act(ap, ap, AF.Ln, r, w, bias=1.0)
        act(ap, ap, AF.Exp, w, w, scale=-1.0)

    def norm_rope_stages(zp, nh, gi, rope_ap, out_ap, rz, wout, slot, np_=128):
        n = nh * 64
        z3 = zp.rearrange("p (h d) -> p h d", h=nh)
        ssq = sm8[0:np_, slot, 0:nh]
        zg = tB[0:np_, 0:n].rearrange("p (h d) -> p h d", h=nh)

        def st0():
            act(tA[0:np_, 0:n], zp, AF.Square, rz, [tA])
            if isinstance(gi, tuple):
                tt("dve", zg, z3, GN[0:np_, gi[0]:gi[0] + nh, :], ALU.mult, list(rz) + [GN.b(gi[0])], [tB])
            else:
                tt("dve", zg, z3, bch(GN[0:np_, gi, :], nh), ALU.mult, list(rz) + [GN.b(gi)], [tB])

        def st1():
            op("dve", lambda e: e.tensor_reduce(out=ssq, in_=tA[0:np_, 0:n].rearrange("p (h d) -> p h d", h=nh),
                                                axis=AX.X, op=ALU.add), r=[tA], w=[sm8.b(slot)])

        def st1b():
            act(ssq, ssq, AF.Ln, [sm8.b(slot)], [sm8.b(slot)], scale=1.0 / 64.0, bias=EPS)

        def st1c():
            act(ssq, ssq, AF.Exp, [sm8.b(slot)], [sm8.b(slot)], scale=-0.5)

        if rope_ap is None:
            def st2():
                tt("dve", out_ap, zg, bc3(ssq, 64), ALU.mult, [tB, sm8.b(slot)], wout)
            return [st0, st1, st1b, st1c, st2]
        zg4 = tB[0:np_, 0:n].rearrange("p (h a f) -> p h a f", h=nh, a=2)
        a4 = tC[0:np_, 0:n].rearrange("p (h a f) -> p h a f", h=nh, a=2)
        b4 = tD[0:np_, 0:n].rearrange("p (h a f) -> p h a f", h=nh, a=2)
        cos4 = rope_ap[:, 0:32].unsqueeze(1).unsqueeze(1).broadcast_to([np_, nh, 2, 32])
        sin3 = bch(rope_ap[:, 32:64], nh)
        nsin3 = bch(rope_ap[:, 64:96], nh)

        def st2():
            tt("dve", a4, zg4, cos4, ALU.mult, [tB], [tC])
            tt("dve", b4[:, :, 0, :], zg4[:, :, 1, :], nsin3, ALU.mult, [tB], [tD])
            tt("dve", b4[:, :, 1, :], zg4[:, :, 0, :], sin3, ALU.mult, [tB], [tD])

        def st3():
            tt("dve", tC[0:np_, 0:n], tC[0:np_, 0:n], tD[0:np_, 0:n], ALU.add, [tC, tD], [tC])
            tt("dve", out_ap, tC[0:np_, 0:n].rearrange("p (h d) -> p h d", h=nh), bc3(ssq, 64), ALU.mult,
               [tC, sm8.b(slot)], wout)
        return [st0, st1, st1b, st1c, st2, st3]

    def norm_rope(zp, nh, gi, rope_ap, out_ap, rz, wout, slot, np_=128):
        for st in norm_rope_stages(zp, nh, gi, rope_ap, out_ap, rz, wout, slot, np_):
            st()

    def h_update(i, nchunks, wo, wkey=None):
        for half in range(2):
            for c in range(nchunks):
                mm(pYb[half][:], oT[:, c, :], wo[:, c, half * 512:(half + 1) * 512], c == 0, c == nchunks - 1,
                   [oT, wo.b(wkey)], [pYb[half]])
            tt("dve", H[:, i, half * 512:(half + 1) * 512], H[:, i, half * 512:(half + 1) * 512], pYb[half][:], ALU.add,
               [H.b(i), pYb[half]], [H.b(i)])

    def transposes_to_oT(nch, src_cols0=0, dst0=0):
        for c in range(nch):
            tr(pT[:, c, :], ob16[:, src_cols0 + c * 128: src_cols0 + (c + 1) * 128], 128, [ob16], [pT])
        cp("act", oT[:, dst0:dst0 + nch, :], pT[:, 0:nch, :], [pT], [oT])

    def attn_pipeline(branches, qsrc, fillers=None):
        steps = []
        for bi, br in enumerate(branches):
            for n_, kt in enumerate(br["kts"]):
                steps.append((bi, n_, kt))

        def emit_qk(idx):
            bi, n_, kt = steps[idx]
            br = branches[bi]
            k = br["k"]
            ps = pSb[idx % 2]
            psv = ps[:].rearrange("p (g t) -> p g t", g=4)
            extra = br["masks"](kt)
            mm(psv, br["kT"][:, kt * 128:(kt + 1) * 128], qsrc[:, k, :, :], True, len(extra) == 0,
               [br["kT"].b(kt), qsrc.b(k)], [ps])
            for j_, (lt, rh, rd_) in enumerate(extra):
                mm(psv, lt, rh, False, j_ == len(extra) - 1, rd_, [ps])

        first_use = {}
        for bi, br in enumerate(branches):
            first_use.setdefault(br["k"], bi)
        for k_, bi in first_use.items():
            memset("dve", pOb[k_][:], 0.0, [pOb[k_]])
        emit_qk(0)
        for idx, (bi, n_, kt) in enumerate(steps):
            br = branches[bi]
            k = br["k"]
            if idx + 1 < len(steps):
                emit_qk(idx + 1)
            act(Pb[:, idx % 2, :], pSb[idx % 2][:], AF.Exp, [pSb[idx % 2]], [Pb.b(idx % 2)], scale=0.125)
            V_ = br["V"]
            for g in range(4):
                mm(pOb[k][:, g, 0:65], Pb[:, idx % 2, g * 128:(g + 1) * 128], V_[:, kt, k, :], False, False,
                   [Pb.b(idx % 2), V_.b(kt), V_.b("ones")], [pOb[k]], skip=True)
            if n_ == len(br["kts"]) - 1:
                br["evac"](k)
                if any(b2["k"] == k for b2 in branches[bi + 1:]):
                    memset("dve", pOb[k][:], 0.0, [pOb[k]])
            if fillers and FILL_MODE == 1:
                fillers.pop(0)()
        while fillers:
            fillers.pop(0)()

    def rmsnorm_to_xnT():
        memset("pool", ss16[:], 0.0, [ss16])
        for i in range(NT):
            act(xnb[:], H[:, i, :], AF.Square, [H.b(i)], [xnb, ss16], accum=ss16[:, i:i + 1])
        cp("dve", rstd16[:], ss16[:], [ss16], [rstd16])
        rstd_from_ss(rstd16[:], 1024.0, [rstd16], [rstd16])
        for i in range(NT):
            ts("dve", xnb[:], H[:, i, :], rstd16[:, i:i + 1], None, ALU.mult, None, [H.b(i), rstd16], [xnb])
            for c in range(8):
                tr(pT[:, c, :], xnb[:, c * 128:(c + 1) * 128], 128, [xnb], [pT])
            cp("act", xnT[:, :, i * 128:(i + 1) * 128], pT[:], [pT], [xnT.b(i)])

    def mem_kv(s, l, kmT, VM1, memf, memT, wkv):
        load_slab(wkv, W_KV[l], 0, 512)
        memset("dve", VM1[:, :, :, 64:65], 1.0, [VM1.b("ones")])
        for nt in range(2):
            S.dma(memf[:], D["mem"][s, nt * 128:(nt + 1) * 128, :], w=[memf])
            memset("dve", sm8[:, 7, 0:1], 0.0, [sm8.b(7)])
            act(xnb[:], memf[:], AF.Square, [memf], [xnb, sm8.b(7)], accum=sm8[:, 7, 0:1])
            rstd_from_ss(sm8[:, 7, 0:1], 1024.0, [sm8.b(7)], [sm8.b(7)])
            ts("dve", xnb[:], memf[:], sm8[:, 7, 0:1], None, ALU.mult, None, [memf, sm8.b(7)], [xnb])
            for c in range(8):
                tr(pT[:, c, :], xnb[:, c * 128:(c + 1) * 128], 128, [xnb], [pT])
            cp("act", memT[:, :, nt * 128:(nt + 1) * 128], pT[:], [pT], [memT.b(nt)])
            for c in range(8):
                mm(pZ[:], memT[:, c, nt * 128:(nt + 1) * 128], wkv[:, c, 0:512], c == 0, c == 7, [memT.b(nt), wkv], [pZ])
            norm_rope(pZ[:, 0:256], 4, GI["mem_kn%d" % l], None, krot[:].rearrange("p (h d) -> p h d", h=4),
                      [pZ], [krot], 6)
            cp("act", VM1[:, nt, :, 0:64], pZ[:, 256:512].rearrange("p (h d) -> p h d", h=4), [pZ], [VM1.b(nt)])
            for pr in range(2):
                tr(pT[:, pr, :], krot[:, pr * 128:(pr + 1) * 128], 128, [krot], [pT])
            cp("act", kmT[:, :, nt * 128:(nt + 1) * 128], pT[:, 0:2, :], [pT], [kmT.b(nt)])

    def mem_attn_tile(i, qz_ap, qz_r, l, kmT, VM1, gate_ap, gate_r, out_cols0, qzt=None):
        qzt = qzt if qzt is not None else qz
        norm_rope(qz_ap, 4, GI["mem_qn%d" % l], None, qrot[:, 0:256].rearrange("p (h d) -> p h d", h=4),
                  qz_r, [qrot], 5)
        for pr in range(2):
            tr(pT[:, pr, :], qrot[:, pr * 128:(pr + 1) * 128], 128, [qrot], [pT])
        cp("act", qzt[0:64, 0, 0:2, :], pT[0:64, 0:2, :], [pT], [qzt.b(0)])
        cp("act", qzt[64:128, 1, 0:2, :], pT[64:128, 0:2, :], [pT], [qzt.b(1)])
        if DBG < 2:
            return
        for h in range(4):
            pr, hf = h // 2, h % 2
            for nt in range(2):
                mm(pSb[nt][:, h * 128:(h + 1) * 128], kmT[:, pr, nt * 128:(nt + 1) * 128],
                   qzt[:, hf, pr, :], True, True, [kmT.b(nt), qzt.b(hf)], [pSb[nt]])
        for nt in range(2):
            act(Pb[:, nt, :], pSb[nt][:], AF.Exp, [pSb[nt]], [Pb.b(nt)], scale=0.125)
        if DBG < 3:
            return
        for h in range(4):
            for nt in range(2):
                mm(pO0[:, h, 0:65], Pb[:, nt, h * 128:(h + 1) * 128], VM1[:, nt, h, :], nt == 0, nt == 1,
                   [Pb.b(nt), VM1.b(nt), VM1.b("ones")], [pO0])
        cp("dve", sm8[:, 4, 0:4], pO0[:, :, 64], [pO0], [sm8.b(4)])
        recip(sm8[:, 4, 0:4], sm8[:, 4, 0:4], [sm8.b(4)], [sm8.b(4)])
        tt("dve", tE[:, 0:256].rearrange("p (h d) -> p h d", h=4), pO0[:, :, 0:64], bc3(sm8[:, 4, 0:4], 64), ALU.mult,
           [pO0, sm8.b(4)], [tE])
        tt("dve", ob16[:, out_cols0:out_cols0 + 256], tE[:, 0:256], gate_ap, ALU.mult, [tE] + list(gate_r), [ob16])

    def mem_front_stages(i, l, wslab, gatT, qzmT):
        st = [lambda: proj_tok(pZ[:], wslab, 0, 512, i, [pZ])]
        st += silu_stages(gatT[:, 512:768], pZ[:, 256:512], 256, [pZ], [gatT.b("m")], tF)
        st += norm_rope_stages(pZ[:, 0:256], 4, GI["mem_qn%d" % l], None, qrot[:, 0:256].rearrange("p (h d) -> p h d", h=4),
                               [pZ], [qrot], 5)

        def t_():
            for pr in range(2):
                tr(pT[:, pr, :], qrot[:, pr * 128:(pr + 1) * 128], 128, [qrot], [pT])
        def c_():
            t_()
            cp("act", qzmT[0:64, 0, :, :], pT[0:64, 0:2, :], [pT], [qzmT.b(0)])
            cp("act", qzmT[64:128, 1, :, :], pT[64:128, 0:2, :], [pT], [qzmT.b(1)])
        st.append(c_)
        return st

    def mem_back(i, l, kmT, VM1, gatT, qzmT, out_cols0=512):
        for h in range(4):
            pr, hf = h // 2, h % 2
            for nt in range(2):
                mm(pSb[nt][:, h * 128:(h + 1) * 128], kmT[:, pr, nt * 128:(nt + 1) * 128],
                   qzmT[:, hf, pr, :], True, True, [kmT.b(nt), qzmT.b(hf)], [pSb[nt]])
        for nt in range(2):
            act(Pb[:, nt, :], pSb[nt][:], AF.Exp, [pSb[nt]], [Pb.b(nt)], scale=0.125)
        for h in range(4):
            for nt in range(2):
                mm(pO0[:, h, 0:65], Pb[:, nt, h * 128:(h + 1) * 128], VM1[:, nt, h, :], nt == 0, nt == 1,
                   [Pb.b(nt), VM1.b(nt), VM1.b("ones")], [pO0])
        cp("dve", sm8[:, 4, 0:4], pO0[:, :, 64], [pO0], [sm8.b(4)])
        recip(sm8[:, 4, 0:4], sm8[:, 4, 0:4], [sm8.b(4)], [sm8.b(4)])
        tt("dve", tE[:, 0:256].rearrange("p (h d) -> p h d", h=4), pO0[:, :, 0:64], bc3(sm8[:, 4, 0:4], 64), ALU.mult,
           [pO0, sm8.b(4)], [tE])
        tt("dve", ob16[:, out_cols0:out_cols0 + 256], tE[:, 0:256], gatT[:, 512:768], ALU.mult, [tE, gatT.b("m")], [ob16])
        for c in range(2):
            tr(pT[:, 4 + c, :], ob16[:, out_cols0 + c * 128:out_cols0 + (c + 1) * 128], 128, [ob16], [pT])
        cp("act", oT[:, 4:6, :], pT[:, 4:6, :], [pT], [oT])

    def layer0(s):
        l = 0
        rmsnorm_to_xnT()
        S.phase_reset()
        if "B" in mix0:
            PB = S.carve("PB", [128, 4, 8], F32)
            PD = S.carve("PD", [128, 4, 4], F32)
            BDf = S.carve("BDf", [128, 2, 4, 128], F32)
            BD = S.carve("BD", [128, 2, 4, 128], BF16)
            XB = S.carve("XB", [128, 3 + SEQ], F32)
            hB = S.carve("hB", [128, SEQ], F32)
            mixB = S.carve("mixB", [128, 4, SEQ], BF16)
            wsl = [S.carve("wslB0", [128, 8, 256], BF16), S.carve("wslB1", [128, 8, 256], BF16)]
            woB = S.carve("woB", [128, 4, 1024], BF16)
            xcb = S.carve("xcb", [128, 512], BF16)
            for j in range(4):
                S.dma(PB[:, :, j], D["b_conv_w"][0, j].rearrange("(c p) -> p c", p=128), w=[PB.b(j)],
                      allow_slow_non_contiguous=True)
            for j, nm in enumerate(["b_conv_b", "b_b_r", "b_b_i", "b_lambda"]):
                S.dma(PB[:, :, 4 + j], D[nm][0].rearrange("(c p) -> p c", p=128), w=[PB.b(4 + j)],
                      allow_slow_non_contiguous=True)
            ts("dve", PD[:, :, 0], PB[:, :, 5], -1.0, None, ALU.mult, None, [PB.b(5)], [PD.b(0)])
            ts("dve", PD[:, :, 1], PB[:, :, 6], -1.0, None, ALU.mult, None, [PB.b(6)], [PD.b(1)])
            act(PD[:, :, 2], PB[:, :, 7], AF.Exp, [PB.b(7)], [PD.b(2)], scale=-1.0)
            act(PD[:, :, 2], PD[:, :, 2], AF.Ln, [PD.b(2)], [PD.b(2)], bias=1.0)
            ts("dve", PD[:, :, 2], PD[:, :, 2], -8.0, None, ALU.mult, None, [PD.b(2)], [PD.b(2)])
            bdkeys = [BDf.b((a_, b_, c_)) for a_ in range(2) for b_ in range(4) for c_ in range(2)]
            memset("pool", BDf[:], 0.0, bdkeys)
            for gi_, nm in enumerate(["b_w_r", "b_w_i"]):
                for cb in range(4):
                    for hb in range(2):
                        S.dma(BDf[64 * hb:64 * hb + 64, gi_, cb, 64 * hb:64 * hb + 64], D[nm][0, 2 * cb + hb],
                              w=[BDf.b((gi_, cb, hb))])
            cp("dve", BD[:], BDf[:], bdkeys, [BD])
            memset("pool", XB[:, 0:3], 0.0, [XB.b("pad")])
            S.dma(woB[:], W_OUT[l].rearrange("(c p) n -> p c n", p=128)[:, 4:8, :], w=[woB])
            for cb in range(4):
                wS = wsl[cb % 2]
                S.dma(wS[:, :, 0:128], W_IN[l].rearrange("(c p) n -> p c n", p=128)[:, :, 1280 + cb * 128:1280 + (cb + 1) * 128],
                      w=[wS.b("x")])
                S.dma(wS[:, :, 128:256], W_IN[l].rearrange("(c p) n -> p c n", p=128)[:, :, 1792 + cb * 128:1792 + (cb + 1) * 128],
                      w=[wS.b("g")])
                for g in range(4):
                    sl = slice(3 + g * 512, 3 + (g + 1) * 512)
                    proj_feat(pZ[:], wS, 0, g, [pZ], skey="x")
                    cp("act", XB[:, sl], pZ[:], [pZ], [XB.b(g)])
                    rd = [XB.b(g), XB.b(g - 1) if g > 0 else XB.b("pad"), PB.b(0), PB.b(1), PB.b(2), PB.b(3), PB.b(4)]
                    xc = tB
                    ts("dve", xc[:], XB[:, g * 512:g * 512 + 512], PB[:, cb, 0:1], PB[:, cb, 4:5], ALU.mult, ALU.add, rd, [tB])
                    for j in (1, 2, 3):
                        stt("dve", xc[:], XB[:, g * 512 + j:g * 512 + j + 512], PB[:, cb, j:j + 1], xc[:], ALU.mult, ALU.add,
                            rd + [tB], [tB])
                    cp("pool", xcb[:], xc[:], [tB], [xcb])
                    mm(pS0[:], BD[:, 0, cb, :], xcb[:], True, True, [BD, xcb], [pS0])
                    mm(pS1[:], BD[:, 1, cb, :], xcb[:], True, True, [BD, xcb], [pS1])
                    act(tC[:], pS0[:], AF.Exp, [pS0, PD.b(0)], [tC], scale=-1.0, bias=PD[:, cb, 0:1])
                    sigmoid_act(tC[:], [tC], [tC])
                    act(tC[:], tC[:], AF.Exp, [tC, PD.b(2)], [tC], scale=PD[:, cb, 2:3])
                    act(tD[:], pS1[:], AF.Exp, [pS1, PD.b(1)], [tD], scale=-1.0, bias=PD[:, cb, 1:2])
                    sigmoid_act(tD[:], [tD], [tD])
                    act(tE[:], tC[:], AF.Square, [tC], [tE])
                    act(tE[:], tE[:], AF.Ln, [tE], [tE], scale=-1.0, bias=1.0)
                    act(tE[:], tE[:], AF.Exp, [tE], [tE], scale=0.5)
                    tt("pool", tD[:], tD[:], xc[:], ALU.mult, [tD, tB], [tD])
                    tt("pool", tD[:], tD[:], tE[:], ALU.mult, [tD, tE], [tD])
                    init = 0.0 if g == 0 else hB[:, g * 512 - 1:g * 512]
                    op("dve", lambda e, g=g, init=init: e.tensor_tensor_scan(
                        out=hB[:, g * 512:(g + 1) * 512], data0=tC[:], data1=tD[:], initial=init,
                        op0=ALU.mult, op1=ALU.add), r=[tC, tD] + ([hB.b(g - 1)] if g > 0 else []), w=[hB.b(g)])
                    proj_feat(pZ[:], wS, 128, g, [pZ], skey="g")
                    silu_psum(tF[:], pZ[:], 512, [pZ], [tF], tE, tF)
                    tt("dve", mixB[:, cb, g * 512:(g + 1) * 512], tF[:], hB[:, g * 512:(g + 1) * 512], ALU.mult,
                       [tF, hB.b(g)], [mixB.b((cb, g))])
            for i in range(NT):
                for half in range(2):
                    for cb in range(4):
                        mm(pYb[half][:], mixB[:, cb, i * 128:(i + 1) * 128], woB[:, cb, half * 512:(half + 1) * 512],
                           cb == 0, cb == 3, [mixB.b((cb, i // 4)), woB], [pYb[half]])
                    tt("dve", H[:, i, half * 512:(half + 1) * 512], H[:, i, half * 512:(half + 1) * 512], pYb[half][:],
                       ALU.add, [H.b(i), pYb[half]], [H.b(i)])
            S.phase_reset()
        doA, doM = "A" in mix0, "M" in mix0
        if doA or doM:
            kT = S.carve("kT", [128, SEQ], BF16)
            V1 = S.carve("V1", [128, NT, 2, 65], BF16)
            wq = S.carve("wq", [128, 8, 512], BF16)
            wga = S.carve("wga", [128, 8, 512], BF16)
            wqgm = S.carve("wqgm", [128, 8, 512], BF16)
            woAM = S.carve("woAM", [128, 6, 1024], BF16)
            kmT = S.carve("kmT", [128, 2, N_MEM], BF16)
            VM1 = S.carve("VM1", [128, 2, 4, 65], BF16)
            memT = S.carve("memT", [128, 8, N_MEM], BF16)
            memf = S.carve("memf", [128, 1024], F32)
            gat = S.carve("gat", [128, 768], BF16)
            if doM:
                mem_kv(s, l, kmT, VM1, memf, memT, wq)
            if doA:
                load_slab(wga, W_IN[l], 512, 256)
                memset("dve", V1[:, :, :, 64:65], 1.0, [V1.b("ones")])
                for i in range(NT):
                    proj_tok(pZ[:, 0:256], wga, 0, 256, i, [pZ])
                    norm_rope(pZ[:, 0:128], 2, GI["a_kn"], ROPE[:, i, :], krot[:, 0:128].rearrange("p (h d) -> p h d", h=2),
                              [pZ], [krot], 0)
                    cp("act", V1[:, i, :, 0:64], pZ[:, 128:256].rearrange("p (h d) -> p h d", h=2), [pZ], [V1.b(i)])
                    tr(pT[:, 0, :], krot[:, 0:128], 128, [krot], [pT])
                    cp("act", kT[:, i * 128:(i + 1) * 128], pT[:, 0, :], [pT], [kT.b(i)])
            if doA:
                load_slab(wq, W_IN[l], 0, 512)
                load_slab(wga, W_IN[l], 768, 512)
                S.dma(woAM[:, 0:4, :], W_OUT[l].rearrange("(c p) n -> p c n", p=128)[:, 0:4, :], w=[woAM.b("a")])
            if doM:
                load_slab(wqgm, W_IN[l], 2304, 512)
                S.dma(woAM[:, 4:6, :], W_OUT[l].rearrange("(c p) n -> p c n", p=128)[:, 8:10, :], w=[woAM.b("m")])
            gatB = S.carve("gatB", [128, 768], BF16)
            gat2 = [gat, gatB]
            qzm2 = [S.carve("qzmA", [128, 2, 2, 128], BF16), S.carve("qzmB", [128, 2, 2, 128], BF16)]
            for q_ in qzm2:
                memset("dve", q_[:], 0.0, [q_.b(0), q_.b(1)])

            def frontA(i):
                par = i % 2
                st = [lambda: proj_tok(pZ[:], wga, 0, 512, i, [pZ]),
                      ] + silu_stages(gat2[par][:, 0:512], pZ[:], 512, [pZ], [gat2[par].b("a")], tF) + [
                      lambda: proj_tok(pZ[:], wq, 0, 512, i, [pZ])]
                st += norm_rope_stages(pZ[:], 8, GI["a_qn"], ROPE[:, i, :], qrot[:].rearrange("p (h d) -> p h d", h=8),
                                       [pZ], [qrot], 1)

                def t_():
                    for pr in range(4):
                        tr(pT[:, pr, :], qrot[:, pr * 128:(pr + 1) * 128], 128, [qrot], [pT])
                def c_():
                    t_()
                    cp("act", qz2[par][0:64, 0, :, :], pT[0:64, 0:4, :], [pT], [qz2[par].b(0)])
                    cp("act", qz2[par][64:128, 1, :, :], pT[64:128, 0:4, :], [pT], [qz2[par].b(1)])
                st.append(c_)
                return st

            def front(i):
                st = []
                if doA:
                    st += frontA(i)
                if doM:
                    st += mem_front_stages(i, l, wqgm, gat2[i % 2], qzm2[i % 2])
                return st

            for st_ in front(0):
                st_()
            for i in range(NT):
                par = i % 2
                fill = front(i + 1) if i + 1 < NT else []
                if doA:
                    kts = [i - 1, i] if i > 0 else [i]

                    def masksA(kt, i=i):
                        msk = mdiag if kt == i else mprev
                        return [(ident[:], bch(msk[:], 4), [ident, msk])]

                    def evacA(k):
                        tt("dve", sm8[:, 2, 4 * k:4 * k + 4], pOb[k][:, :, 64], esink[:, 4 * k:4 * k + 4], ALU.add,
                           [pOb[k], esink], [sm8.b((2, k))])
                        recip(sm8[:, 2, 4 * k:4 * k + 4], sm8[:, 2, 4 * k:4 * k + 4], [sm8.b((2, k))], [sm8.b((2, k))])
                        tt("dve", tE[:, 256 * k:256 * k + 256].rearrange("p (h d) -> p h d", h=4), pOb[k][:, :, 0:64],
                           bc3(sm8[:, 2, 4 * k:4 * k + 4], 64), ALU.mult, [pOb[k], sm8.b((2, k))], [tE])

                    nfa = len(fill) // 2 if doM else len(fill)
                    fa = [fill.pop(0) for _ in range(nfa)]
                    attn_pipeline([dict(k=k, kT=kT, V=V1, kts=kts, masks=masksA, evac=evacA) for k in range(2)],
                                  qz2[par], fa)
                    tt("dve", ob16[:, 0:512], tE[:], gat2[par][:, 0:512], ALU.mult, [tE, gat2[par].b("a")], [ob16])
                    transposes_to_oT(4, 0, 0)
                if doM:
                    mem_back(i, l, kmT, VM1, gat2[par], qzm2[par])
                while fill:
                    fill.pop(0)()
                chunks = ([0, 1, 2, 3] if doA else []) + ([4, 5] if doM else [])
                for half in range(2):
                    for n_, c in enumerate(chunks):
                        mm(pYb[half][:], oT[:, c, :], woAM[:, c, half * 512:(half + 1) * 512], n_ == 0, n_ == len(chunks) - 1,
                           [oT, woAM.b("a"), woAM.b("m")], [pYb[half]])
                    tt("dve", H[:, i, half * 512:(half + 1) * 512], H[:, i, half * 512:(half + 1) * 512], pYb[half][:],
                       ALU.add, [H.b(i), pYb[half]], [H.b(i)])
            S.phase_reset()

    def separator():
        mm(pY1[:, 0:1], ident[:], ident[:, 0:1], True, True, [ident], [pY1])

    def hgrn2_phase(s):
        l = 1
        QT = S.carve("QT", [128, 4, 512], BF16)
        KT = S.carve("KT", [128, 4, 512], BF16)
        KH = S.carve("KH", [128, 4, 512], BF16)
        Vc = S.carve("Vc", [128, 4, 512], BF16)
        wsm = [S.carve("wsmC0", [128, 8, 256], BF16), S.carve("wsmC1", [128, 8, 256], BF16)]
        wic = S.carve("wic", [128, 8, 512], BF16)
        wgc = S.carve("wgc", [128, 8, 512], BF16)
        woC = S.carve("woC", [128, 4, 1024], BF16)
        St = S.carve("St", [128, 4, 128], F32)
        Sb = S.carve("Sb", [128, 4, 128], BF16)
        EBL = S.carve("EBL", [128, 4, 32], F32)
        ATb = S.carve("ATb", [128, 4, 128], BF16)
        KHz = S.carve("KHz", [128, 4, 2, 128], BF16)
        tri = S.carve("tri", [128, 64], F32)
        Bp = S.carve("Bp", [128, 8], F32)
        tG = S.carve("tG", [128, 512], F32)
        tH = S.carve("tH", [128, 512], F32)
        gcb = S.carve("gcb", [128, 512], BF16)
        S.dma(tri[:], D["tri64"], w=[tri])
        memset("dve", St[:], 0.0, [St])
        memset("dve", Sb[:], 0.0, [Sb])
        memset("dve", ATb[:], 0.0, [ATb.b(0), ATb.b(1)])
        memset("dve", KHz[:], 0.0, [KHz.b(0), KHz.b(1)])
        memset("dve", Bp[:], 0.0, [Bp])
        load_slab(wic, W_IN[l], 1024, 512)
        load_slab(wgc, W_IN[l], 1536, 512)
        S.dma(woC[:], W_OUT[l].rearrange("(c p) n -> p c n", p=128)[:, 0:4, :], w=[woC])
        pS0v = pS0[:].rearrange("p (h t) -> p h t", h=4)
        pS1v = pS1[:].rearrange("p (h t) -> p h t", h=4)
        for g in range(4):
            for hd in range(4):
                wS = wsm[(g * 4 + hd) % 2]
                wv = W_IN[l].rearrange("(c p) n -> p c n", p=128)
                S.dma(wS[:, :, 0:128], wv[:, :, hd * 128:(hd + 1) * 128], w=[wS.b("q")])
                S.dma(wS[:, :, 128:256], wv[:, :, 512 + hd * 128:512 + (hd + 1) * 128], w=[wS.b("f")])
                proj_feat(pZ[:], wS, 128, g, [pZ], skey="f")
                act(tB[:], pZ[:], AF.Exp, [pZ], [tB], scale=-1.0)
                sigmoid_act(tB[:], [tB], [tB])
                ts("dve", tB[:], tB[:], LB[:, 1, hd:hd + 1], LB[:, 0, hd:hd + 1], ALU.mult, ALU.add,
                   [tB, LB.b(0), LB.b(1)], [tB])
                act(tC[:], tB[:], AF.Ln, [tB], [tC])
                ts("pool", tB[:], tB[:], -1.0, 1.0, ALU.mult, ALU.add, [tB], [tB])
                op("dve", lambda e: e.tensor_tensor_scan(out=tD[:], data0=ones512[:], data1=tC[:], initial=0.0,
                                                         op0=ALU.mult, op1=ALU.add), r=[ones512, tC], w=[tD])
                tD3 = tD[:].rearrange("p (c j) -> p c j", j=64)
                cp("dve", Bp[:, 1:8], tD3[:, 0:7, 63], [tD], [Bp])
                tt("dve", tD3, tD3, bc3(Bp[:, 0:8], 64), ALU.subtract, [tD, Bp], [tD])
                act(tE[:], tD[:], AF.Exp, [tD], [tE])
                cp("dve", EBL[:, hd, 8 * g:8 * g + 8], tE[:].rearrange("p (c j) -> p c j", j=64)[:, :, 63], [tE],
                   [EBL.b((hd, g))])
                act(tF[:], tD[:], AF.Exp, [tD], [tF], scale=-1.0)
                tt("dve", KT[:, hd, :], tB[:], tF[:], ALU.mult, [tB, tF], [KT.b(hd)])
                tt("dve", tG[:].rearrange("p (c j) -> p c j", j=64), tD3, bc3(tD3[:, :, 63], 64), ALU.subtract,
                   [tD], [tG])
                act(tG[:], tG[:], AF.Exp, [tG], [tG], scale=-1.0)
                tt("dve", KH[:, hd, :], tB[:], tG[:], ALU.mult, [tB, tG], [KH.b(hd)])
                proj_feat(pZ[:], wS, 0, g, [pZ], skey="q")
                silu_psum(tH[:], pZ[:], 512, [pZ], [tH], tF, tH)
                tt("dve", QT[:, hd, :], tH[:], tE[:], ALU.mult, [tH, tE], [QT.b(hd)])
            for tl in range(4):
                i = 4 * g + tl
                proj_tok(pZ[:], wic, 0, 512, i, [pZ])
                cp("act", Vc[:, tl, :], pZ[:], [pZ], [Vc.b(tl)])
            for tl in range(4):
                i = 4 * g + tl
                cs = tl * 128
                allh = [QT.b(h_) for h_ in range(4)]
                for hd in range(4):
                    tr(pT[:, hd, :], KH[:, hd, cs:cs + 128], 128, [KH.b(hd)], [pT])
                cp("act", KHz[0:64, :, 0, :], pT[0:64, 0:4, :], [pT], [KHz.b(0)])
                cp("act", KHz[64:128, :, 1, :], pT[64:128, 0:4, :], [pT], [KHz.b(1)])
                for hd in range(4):
                    for c in range(2):
                        mm(pS0v[64 * c:64 * c + 64, hd, 64 * c:64 * c + 64], KT[:, hd, cs + 64 * c:cs + 64 * c + 64],
                           QT[:, hd, cs + 64 * c:cs + 64 * c + 64], True, True, [KT.b(hd), QT.b(hd)], [pS0])
                for c in range(2):
                    tt("dve", ATb[64 * c:64 * c + 64, :, 64 * c:64 * c + 64], pS0v[64 * c:64 * c + 64, :, 64 * c:64 * c + 64],
                       bch(tri[64 * c:64 * c + 64, :], 4), ALU.mult, [pS0, tri], [ATb.b(c)])
                for hd in range(4):
                    mm(pO0[:, hd, :], ATb[:, hd, :], Vc[:, tl, hd * 128:(hd + 1) * 128], True, True,
                       [ATb.b(0), ATb.b(1), Vc.b(tl)], [pO0])
                for c in range(2):
                    ch = 8 * g + 2 * tl + c
                    for hd in range(4):
                        mm(pO1[64 * c:64 * c + 64, hd, :], QT[:, hd, cs + 64 * c:cs + 64 * c + 64], Sb[:, hd, :], True, True,
                           [QT.b(hd), Sb], [pO1])
                    for hd in range(4):
                        mm(pS1v[:, hd, :], KHz[:, hd, c, :], Vc[:, tl, hd * 128:(hd + 1) * 128], True, True,
                           [KHz.b(c), Vc.b(tl)], [pS1])
                    tt("dve", St[:], St[:], bc3(EBL[:, :, ch], 128), ALU.mult, [St] + [EBL.b((h_, g)) for h_ in range(4)], [St])
                    tt("dve", St[:], St[:], pS1v, ALU.add, [St, pS1], [St])
                    cp("act", Sb[:], St[:], [St], [Sb])
                cp("act", tG[:], pO0[:].rearrange("p h v -> p (h v)"), [pO0], [tG])
                tt("dve", tG[:], tG[:], pO1[:].rearrange("p h v -> p (h v)"), ALU.add, [tG, pO1], [tG])
                act(tA[:, 0:512], tG[:], AF.Square, [tG], [tA])
                op("dve", lambda e: e.tensor_reduce(out=sm8[:, 3, 0:4], in_=tA[:, 0:512].rearrange("p (h v) -> p h v", h=4),
                                                    axis=AX.X, op=ALU.add), r=[tA], w=[sm8.b(3)])
                rstd_from_ss(sm8[:, 3, 0:4], 128.0, [sm8.b(3)], [sm8.b(3)])
                tG3 = tG[:].rearrange("p (h v) -> p h v", h=4)
                tt("dve", tG3, tG3, bc3(sm8[:, 3, 0:4], 128), ALU.mult, [tG, sm8.b(3)], [tG])
                tt("dve", tG3, tG3, bch(COG[:], 4), ALU.mult, [tG, COG], [tG])
                proj_tok(pZ[:], wgc, 0, 512, i, [pZ])
                silu_psum(gcb[:], pZ[:], 512, [pZ], [gcb], tH, tF)
                tt("dve", ob16[:, 0:512], tG[:], gcb[:], ALU.mult, [tG, gcb], [ob16])
                transposes_to_oT(4, 0, 0)
                h_update(i, 4, woC)

    def nsa_phase(s, doD, doM):
        l = 1
        ksT = S.carve("ksT", [128, SEQ], BF16)
        kwT = S.carve("kwT", [128, SEQ], BF16)
        Vs1 = S.carve("Vs1", [128, NT, 2, 65], BF16)
        Vw1 = S.carve("Vw1", [128, NT, 2, 65], BF16)
        kcT = S.carve("kcT", [128, 128], BF16)
        VC1 = S.carve("VC1", [128, 2, 97], BF16)
        cmpneg = S.carve("cmpneg", [128, NT, 128], BF16)
        selE = S.carve("selE", [128, SEQ], BF16)
        seladj = S.carve("seladj", [128, NT, 32], BF16)
        kmT = S.carve("kmT1", [128, 2, N_MEM], BF16)
        VM1 = S.carve("VM11", [128, 2, 4, 65], BF16)
        mark = S.arena_off
        if doM:
            wkv = S.carve("wkv1", [128, 8, 512], BF16)
            memT = S.carve("memT1", [128, 8, N_MEM], BF16)
            memf = S.carve("memf1", [128, 1024], F32)
            mem_kv(s, l, kmT, VM1, memf, memT, wkv)
            S.phase_reset(mark)
        if doD:
            tmpf = S.carve("tmpf", [128, 2048], F32)
            ROPEC = S.carve("ROPEC", [128, 96], F32)
            w512 = S.carve("w512", [128, 8, 512], BF16)
            wkc = S.carve("wkc", [128, 8, 256], BF16)
            kcdT = S.carve("kcdT", [128, SEQ], BF16)
            vcdT = S.carve("vcdT", [128, SEQ], BF16)
            W1r = S.carve("W1r", [128, 32, 128], BF16)
            w2f = S.carve("w2f", [128, 2, 64], F32)
            w2b = S.carve("w2b", [128, 2, 64], BF16)
            pef = S.carve("pef", [128, 32], F32)
            peT = S.carve("peT", [128, 32], BF16)
            cb = S.carve("cb", [128, 2], F32)
            hid = S.carve("hid", [128, 2, 128], BF16)
            S.dma(tmpf[0:N_CMP, :], D["cmpmask"].rearrange("j i b -> j (i b)"), w=[tmpf])
            ts("dve", cmpneg[0:N_CMP, :, :].rearrange("p i b -> p (i b)"), tmpf[0:N_CMP, :], -1.0, -NEG, ALU.add, ALU.mult,
               [tmpf], [cmpneg])
            S.dma(tmpf[0:32, :], D["selE"], w=[tmpf])
            cp("dve", selE[0:32, :], tmpf[0:32, :], [tmpf], [selE])
            S.dma(tmpf[:, 0:512].rearrange("p (i j) -> p i j", i=NT), D["seladj"].rearrange("(i p) j -> p i j", p=128), w=[tmpf])
            cp("dve", seladj[:].rearrange("p i j -> p (i j)"), tmpf[:, 0:512], [tmpf], [seladj])
            S.dma(ROPEC[0:N_CMP, :], D["rope_cmp"], w=[ROPEC])
            S.dma(tmpf[0:N_CMP, 0:32], D["ovl"], w=[tmpf])
            for k in range(2):
                cp("dve", VC1[0:N_CMP, k, 65:97], tmpf[0:N_CMP, 0:32], [tmpf], [VC1.b(("o", k))])
            memset("dve", VC1[:, :, 64:65], 1.0, [VC1.b("ones")])
            memset("dve", Vs1[:, :, :, 64:65], 1.0, [Vs1.b("ones")])
            memset("dve", Vw1[:, :, :, 64:65], 1.0, [Vw1.b("ones")])
            S.dma(w2f[:, 0, :], D["d_w2k"][0], w=[w2f.b(0)])
            S.dma(w2f[:, 1, :], D["d_w2v"][0], w=[w2f.b(1)])
            cp("dve", w2b[:], w2f[:], [w2f.b(0), w2f.b(1)], [w2b])
            for j_, c0_ in enumerate([2816, 3072, 2944, 3200]):
                S.dma(w512[:, :, j_ * 128:(j_ + 1) * 128], W_IN[l].rearrange("(c p) n -> p c n", p=128)[:, :, c0_:c0_ + 128],
                      w=[w512.b(j_)])
            load_slab(wkc, W_IN[l], 2560, 256)
            for i in range(NT):
                for c in range(8):
                    mm(pZ[:], xnT[:, c, i * 128:(i + 1) * 128], w512[:, c, 0:512], c == 0, c == 7,
                       [xnT.b(i)] + [w512.b(j_) for j_ in range(4)], [pZ])
                norm_rope(pZ[:, 0:256], 4, (10,), ROPE[:, i, :], krot[:, 0:256].rearrange("p (h d) -> p h d", h=4),
                          [pZ], [krot.b(0), krot.b(1)], 0)
                cp("act", Vs1[:, i, :, 0:64], pZ[:, 256:384].rearrange("p (h d) -> p h d", h=2), [pZ], [Vs1.b(i)])
                cp("act", Vw1[:, i, :, 0:64], pZ[:, 384:512].rearrange("p (h d) -> p h d", h=2), [pZ], [Vw1.b(i)])
                tr(pT[:, 0, :], krot[:, 0:128], 128, [krot.b(0)], [pT])
                tr(pT[:, 1, :], krot[:, 128:256], 128, [krot.b(1)], [pT])
                cp("act", ksT[:, i * 128:(i + 1) * 128], pT[:, 0, :], [pT], [ksT.b(i)])
                cp("act", kwT[:, i * 128:(i + 1) * 128], pT[:, 1, :], [pT], [kwT.b(i)])
            for g in range(4):
                proj_feat(pZ[:], wkc, 0, g, [pZ])
                cp("act", kcdT[:, g * 512:(g + 1) * 512], pZ[:], [pZ], [kcdT.b(g)])
                proj_feat(pZ[:], wkc, 128, g, [pZ])
                cp("act", vcdT[:, g * 512:(g + 1) * 512], pZ[:], [pZ], [vcdT.b(g)])
            for kind_i, kind in enumerate(["k", "v"]):
                srcT = kcdT if kind == "k" else vcdT
                w1v = W_1[kind].rearrange("(l d) m -> d l m", d=64)
                S.dma(W1r[0:64, :, :], w1v, w=[W1r.b(0)])
                S.dma(W1r[64:128, :, :], w1v, w=[W1r.b(1)])
                pesrc = D["d_pe_k" if kind == "k" else "d_pe_v"][0].rearrange("l d -> d l")
                S.dma(pef[0:64, :], pesrc, w=[pef], allow_slow_non_contiguous=True)
                cp("dve", peT[0:64, :], pef[0:64, :], [pef], [peT])
                for l_ in range(32):
                    mm(pY0[:, 0:1], W1r[0:64, l_, :], peT[0:64, l_:l_ + 1], l_ == 0, l_ == 31, [W1r.b(0), peT], [pY0])
                cp("dve", cb[:, 0:1], pY0[:, 0:1], [pY0], [cb])
                ts("dve", cb[:, 1:2], cb[:, 0:1], -1.0, None, ALU.mult, None, [cb], [cb])
                s3 = srcT[:].rearrange("p (j s) -> p j s", s=16)
                srcb = [srcT.b(g_) for g_ in range(4)]
                for k in range(2):
                    separator()
                    for l_ in range(32):
                        rhs = s3[64 * k:64 * k + 64, 0:N_CMP, l_] if l_ < 16 else s3[64 * k:64 * k + 64, 1:N_CMP + 1, l_ - 16]
                        mm(pSb[k][:, 0:N_CMP], W1r[64 * k:64 * k + 64, l_, :], rhs, l_ == 0, l_ == 31,
                           [W1r.b(k)] + srcb, [pSb[k]])
                    separator()
                    act(tB[:, 0:N_CMP], pSb[k][:, 0:N_CMP], AF.Exp, [pSb[k], cb], [tB], scale=-1.0, bias=cb[:, 1:2])
                    act(tC[:, 0:N_CMP], pSb[k][:, 0:N_CMP], AF.Identity, [pSb[k], cb], [tC], bias=cb[:, 0:1])
                    sigmoid_act(tB[:, 0:N_CMP], [tB], [tB])
                    tt("dve", hid[:, k, 0:N_CMP], tC[:, 0:N_CMP], tB[:, 0:N_CMP], ALU.mult, [tB, tC], [hid.b(k)])
                for k in range(2):
                    c0 = kind_i * 128 + k * 64
                    mm(pZ[0:N_CMP, c0:c0 + 64], hid[:, k, 0:N_CMP], w2b[:, kind_i, :], True, True, [hid.b(k), w2b], [pZ])
            norm_rope(pZ[0:N_CMP, 0:128], 2, GI["d_kn_cmp"], ROPEC[0:N_CMP, :],
                      krot[0:N_CMP, 0:128].rearrange("p (h d) -> p h d", h=2), [pZ], [krot.b(0)], 0, np_=N_CMP)
            cp("act", VC1[0:N_CMP, :, 0:64], pZ[0:N_CMP, 128:256].rearrange("p (h d) -> p h d", h=2), [pZ], [VC1.b("v")])
            tr(pT[:, 0, 0:N_CMP], krot[0:N_CMP, 0:128], N_CMP, [krot.b(0)], [pT])
            cp("act", kcT[:, 0:N_CMP], pT[:, 0, 0:N_CMP], [pT], [kcT])
            S.phase_reset(mark)
        wq = S.carve("wq1", [128, 8, 512], BF16)
        wgd = S.carve("wgd", [128, 8, 512], BF16)
        wqgm = S.carve("wqgm1", [128, 8, 512], BF16)
        wgt = S.carve("wgt", [128, 8, 24], BF16)
        woDM = S.carve("woDM", [128, 6, 1024], BF16)
        acc = S.carve("acc", [128, 512], F32)
        negT = S.carve("negT", [128, 2, 128], BF16)
        nb = S.carve("nb", [128, 2, 32], BF16)
        gts = S.carve("gts", [128, 24], F32)
        gat = S.carve("gat1", [128, 768], BF16)
        wv = W_IN[l].rearrange("(c p) n -> p c n", p=128)
        if doD:
            load_slab(wq, W_IN[l], 2048, 512)
            load_slab(wgd, W_IN[l], 3352, 512)
            S.dma(wgt[:], wv[:, :, 3328:3352], w=[wgt])
            S.dma(woDM[:, 0:4, :], W_OUT[l].rearrange("(c p) n -> p c n", p=128)[:, 4:8, :], w=[woDM.b("a")])
        if doM:
            load_slab(wqgm, W_IN[l], 3864, 512)
            S.dma(woDM[:, 4:6, :], W_OUT[l].rearrange("(c p) n -> p c n", p=128)[:, 8:10, :], w=[woDM.b("m")])
        gtsB = S.carve("gtsB", [128, 24], F32)
        gatB = S.carve("gat1B", [128, 768], BF16)
        gat2 = [gat, gatB]
        gts2 = [gts, gtsB]
        qzm2 = [S.carve("qzm1A", [128, 2, 2, 128], BF16), S.carve("qzm1B", [128, 2, 2, 128], BF16)]
        for q_ in qzm2:
            memset("dve", q_[:], 0.0, [q_.b(0), q_.b(1)])

        def evac_branch(k, br, first, gtc):
            gts3 = gtc[:].rearrange("p (h b) -> p h b", b=3)
            ts("dve", sm8[:, 2, 4 * k:4 * k + 4], pOb[k][:, :, 64], 1e-30, None, ALU.max, None, [pOb[k]], [sm8.b((2, k))])
            recip(sm8[:, 2, 4 * k:4 * k + 4], sm8[:, 2, 4 * k:4 * k + 4], [sm8.b((2, k))], [sm8.b((2, k))])
            tt("dve", sm8[:, 3, 4 * k:4 * k + 4], sm8[:, 2, 4 * k:4 * k + 4], gts3[:, 4 * k:4 * k + 4, br], ALU.mult,
               [sm8.b((2, k)), gtc], [sm8.b((3, k))])
            a3 = acc[:, 256 * k:256 * k + 256].rearrange("p (h d) -> p h d", h=4)
            if first:
                tt("dve", a3, pOb[k][:, :, 0:64], bc3(sm8[:, 3, 4 * k:4 * k + 4], 64), ALU.mult, [pOb[k], sm8.b((3, k))],
                   [acc.b(k)])
            else:
                t3 = tE[:, 256 * k:256 * k + 256].rearrange("p (h d) -> p h d", h=4)
                tt("dve", t3, pOb[k][:, :, 0:64], bc3(sm8[:, 3, 4 * k:4 * k + 4], 64), ALU.mult, [pOb[k], sm8.b((3, k))],
                   [tE])
                tt("dve", a3, a3, t3, ALU.add, [acc.b(k), tE], [acc.b(k)])

        def frontD(i):
            par = i % 2
            st = []
            st.append(lambda: proj_tok(pZ[:], wgd, 0, 512, i, [pZ]))
            st.extend(silu_stages(gat2[par][:, 0:512], pZ[:], 512, [pZ], [gat2[par].b("a")], tF))
            st.append(lambda: proj_tok(pZ[:, 0:24], wgt, 0, 24, i, [pZ]))
            st.append(lambda: act(gts2[par][:], pZ[:, 0:24], AF.Exp, [pZ], [gts2[par]], scale=-1.0))
            st.append(lambda: act(gts2[par][:], gts2[par][:], AF.Ln, [gts2[par]], [gts2[par]], bias=1.0))
            st.append(lambda: act(gts2[par][:], gts2[par][:], AF.Exp, [gts2[par]], [gts2[par]], scale=-1.0))
            st.append(lambda: proj_tok(pZ[:], wq, 0, 512, i, [pZ]))
            st.extend(norm_rope_stages(pZ[:], 8, GI["d_qn"], ROPE[:, i, :], qrot[:].rearrange("p (h d) -> p h d", h=8),
                                       [pZ], [qrot], 1))

            def t_():
                for pr in range(4):
                    tr(pT[:, pr, :], qrot[:, pr * 128:(pr + 1) * 128], 128, [qrot], [pT])
            def c_():
                t_()
                cp("act", qz2[par][0:64, 0, :, :], pT[0:64, 0:4, :], [pT], [qz2[par].b(0)])
                cp("act", qz2[par][64:128, 1, :, :], pT[64:128, 0:4, :], [pT], [qz2[par].b(1)])
            st.append(c_)
            return st

        def front1(i):
            st = frontD(i) if doD else []
            if doM:
                st += mem_front_stages(i, l, wqgm, gat2[i % 2], qzm2[i % 2])
            return st

        for st_ in front1(0):
            st_()
        for i in range(NT):
            par = i % 2
            qzc, gtc, gac = qz2[par], gts2[par], gat2[par]
            fill = front1(i + 1) if i + 1 < NT else []
            if doD:
                for k in range(2):
                    ps = pSb[k]
                    psv = ps[0:N_CMP, :].rearrange("p (g t) -> p g t", g=4)
                    mm(psv, kcT[:, 0:N_CMP], qzc[:, k, :, :], True, False, [kcT, qzc.b(k)], [ps])
                    mm(psv, ident[0:N_CMP, 0:N_CMP], bch(cmpneg[0:N_CMP, i, :], 4), False, True, [ident, cmpneg], [ps])
                    act(Pb[0:N_CMP, k, :], ps[0:N_CMP, :], AF.Exp, [ps], [Pb.b(k)], scale=0.125)
                for k in range(2):
                    memset("dve", pOb[k][:], 0.0, [pOb[k]])
                    for g in range(4):
                        mm(pOb[k][:, g, 0:97], Pb[0:N_CMP, k, g * 128:(g + 1) * 128], VC1[0:N_CMP, k, :], False, False,
                           [Pb.b(k), VC1.b("ones"), VC1.b("v"), VC1.b(("o", k))], [pOb[k]], skip=True)
                for k in range(2):
                    evac_branch(k, 0, True, gtc)
                    t3 = tC[:, 0:128].rearrange("p (g j) -> p g j", g=4)
                    tt("dve", t3, pOb[k][:, :, 65:97], bc3(sm8[:, 2, 4 * k:4 * k + 4], 32), ALU.mult,
                       [pOb[k], sm8.b((2, k))], [tC])
                    op("dve", lambda e, k=k: e.tensor_reduce(out=tE[:, 32 * k:32 * k + 32],
                                                             in_=tC[:, 0:128].rearrange("p (g j) -> p j g", g=4),
                                                             axis=AX.X, op=ALU.add), r=[tC], w=[tE])
                    tt("dve", tE[:, 32 * k:32 * k + 32], tE[:, 32 * k:32 * k + 32], seladj[:, i, :], ALU.add,
                       [tE, seladj], [tE])
                    op("dve", lambda e, k=k: e.max(out=sm8[:, 6, 0:8], in_=tE[:, 32 * k:32 * k + 32]), r=[tE],
                       w=[sm8.b(6)])
                    ts("dve", tD[:, 0:32], tE[:, 32 * k:32 * k + 32], sm8[:, 6, 3:4], None, ALU.is_ge, None,
                       [tE, sm8.b(6)], [tD])
                    ts("dve", nb[:, k, :], tD[:, 0:32], -1.0, -NEG, ALU.add, ALU.mult, [tD], [nb.b(k)])
                    tr(pT[0:32, 4 + k, :], nb[:, k, :], 128, [nb.b(k)], [pT])
                cp("act", negT[0:32, :, :], pT[0:32, 4:6, :], [pT], [negT])
                def masks_sel(kt, k, i=i):
                    ex = [(selE[0:32, kt * 128:(kt + 1) * 128], bch(negT[0:32, k, :], 4), [selE, negT])]
                    if kt == i:
                        ex.append((ident[:], bch(mdiag[:], 4), [ident, mdiag]))
                    return ex

                def masks_win(kt, i=i):
                    if kt == i:
                        return [(ident[:], bch(mdiag[:], 4), [ident, mdiag])]
                    if kt == i - 4:
                        return [(ident[:], bch(mprev[:], 4), [ident, mprev])]
                    return []

                kts_s = list(range(0, i + 1))
                kts_w = list(range(max(0, i - 4), i + 1))
                brs = []
                for k in range(2):
                    brs.append(dict(k=k, kT=ksT, V=Vs1, kts=kts_s, masks=(lambda kt, k=k: masks_sel(kt, k)),
                                    evac=(lambda k_, gtc=gtc: evac_branch(k_, 1, False, gtc))))
                for k in range(2):
                    brs.append(dict(k=k, kT=kwT, V=Vw1, kts=kts_w, masks=masks_win,
                                    evac=(lambda k_, gtc=gtc: evac_branch(k_, 2, False, gtc))))
                attn_pipeline(brs, qzc, fill)
                tt("dve", ob16[:, 0:512], acc[:], gac[:, 0:512], ALU.mult, [acc.b(0), acc.b(1), gac.b("a")], [ob16])
                transposes_to_oT(4, 0, 0)
            while fill:
                fill.pop(0)()
            if doM:
                mem_back(i, l, kmT, VM1, gac, qzm2[par])
            chunks = ([0, 1, 2, 3] if doD else []) + ([4, 5] if doM else [])
            for half in range(2):
                for n_, c in enumerate(chunks):
                    mm(pYb[half][:], oT[:, c, :], woDM[:, c, half * 512:(half + 1) * 512], n_ == 0, n_ == len(chunks) - 1,
                       [oT, woDM.b("a"), woDM.b("m")], [pYb[half]])
                tt("dve", H[:, i, half * 512:(half + 1) * 512], H[:, i, half * 512:(half + 1) * 512], pYb[half][:],
                   ALU.add, [H.b(i), pYb[half]], [H.b(i)])

    def layer1(s):
        rmsnorm_to_xnT()
        S.phase_reset()
        if "C" in mix1:
            hgrn2_phase(s)
            S.phase_reset()
        doD, doM = "D" in mix1, "M" in mix1
        if doD or doM:
            nsa_phase(s, doD, doM)
            S.phase_reset()

    for s in range(nseq):
        for i in range(NT):
            S.dma(H[:, i, :], D["x"][s, i * 128:(i + 1) * 128, :], w=[H.b(i)])
        if 0 in layers:
            layer0(s)
        if 1 in layers:
            layer1(s)
        for i in range(NT):
            S.dma(Y[s, i * 128:(i + 1) * 128, :], H[:, i, :], r=[H.b(i)])
        S.phase_reset()
    S.emit()
    return nc, S


N_CORES = 8
_PROG = {}


def kernel(**inputs):
    x = np.ascontiguousarray(inputs["x"], dtype=np.float32)
    mem = np.ascontiguousarray(inputs["mem"], dtype=np.float32)
    B = x.shape[0]
    per = B // N_CORES
    if per not in _PROG:
        _PROG[per] = build_program(per)[0]
    nc = _PROG[per]
    consts = host_consts()
    params = {k: np.ascontiguousarray(inputs[k], dtype=np.float32) for k in PARAM_SHAPES}
    in_maps = []
    for c in range(N_CORES):
        m = {"x": x[c * per:(c + 1) * per], "mem": mem[c * per:(c + 1) * per]}
        m.update(params)
        m.update(consts)
        in_maps.append(m)
    res = run_bass_kernel_spmd(nc, in_maps, core_ids=list(range(N_CORES)))
    return np.concatenate([r["y"] for r in res.results], axis=0)
```

```python
from contextlib import ExitStack
import numpy as np
import concourse.bass as bass
import concourse.mybir as mybir
from concourse.bass_utils import run_bass_kernel_spmd

F32 = mybir.dt.float32
BF16 = mybir.dt.bfloat16
AF = mybir.ActivationFunctionType
ALU = mybir.AluOpType
AX = mybir.AxisListType

ENGINES = ["pe", "act", "dve", "pool", "sp"]
EPOCH = 30000
NDMASEM = 8
import os
NO_POOL = os.environ.get("K_NO_POOL", "1") == "1"
DBG = int(os.environ.get("K_DBG", "99"))
FILL_MODE = int(os.environ.get("K_FILL", "1"))

D_MODEL = 1024
SEQ = 2048
NT = SEQ // 128
N_MEM = 256
EVEN_IN = 2816
ODD_IN = 4376
EPS = 1e-6
NEG = -1024.0
N_CMP = 127


class Buf:
    __slots__ = ("name", "lw", "rd", "excl")

    def __init__(self, name, excl=False):
        self.name = name
        self.lw = None
        self.rd = {}
        self.excl = excl


class Op:
    __slots__ = ("eng", "fn", "deps", "is_dma", "needs_inc", "token", "waits", "dsem", "idx")

    def __init__(self, eng, fn, is_dma=False):
        self.eng = eng
        self.fn = fn
        self.deps = []
        self.is_dma = is_dma
        self.needs_inc = is_dma
        self.token = None
        self.waits = []
        self.dsem = None
        self.idx = None


class T:
    def __init__(self, h, name, excl=False):
        self.h = h
        self.name = name
        self.excl = excl
        self.whole = Buf(name, excl)
        self.subs = {}

    def __getitem__(self, idx):
        return self.h[idx]

    def b(self, key=None):
        if key is None:
            return self.whole
        s = self.subs.get(key)
        if s is None:
            s = Buf(f"{self.name}[{key}]", self.excl)
            self.subs[key] = s
        return s


class Sched:
    def __init__(self, nc):
        self.nc = nc
        self.es = ExitStack()
        self.ops = {e: [] for e in ENGINES}
        self.all_dma = []
        self.dma_since_bar = []
        self.pending = {e: [] for e in ENGINES}
        self.arena = None
        self.arena_words = 0
        self.arena_off = 0

    def sb(self, name, shape, dt):
        h = self.es.enter_context(self.nc.sbuf_tensor("sb_" + name, list(shape), dt))
        return T(h, name)

    def ps(self, name, shape, dt):
        h = self.es.enter_context(self.nc.psum_tensor("ps_" + name, list(shape), dt))
        return T(h, name, excl=True)

    def make_arena(self, words):
        self.arena = self.es.enter_context(self.nc.sbuf_tensor("arena", [128, words], F32))
        self.arena_words = words
        self.arena_off = 0

    def carve(self, name, shape, dt):
        n = 1
        for s in shape[1:]:
            n *= s
        words = (n + 1) // 2 if dt == BF16 else n
        words = (words + 7) // 8 * 8
        assert self.arena_off + words <= self.arena_words, (name, self.arena_off, words, self.arena_words)
        ap = self.arena[:, self.arena_off:self.arena_off + words]
        if dt == BF16:
            ap = ap.bitcast(BF16)[:, 0:n]
        else:
            ap = ap[:, 0:n]
        self.arena_off += words
        if len(shape) == 3:
            ap = ap.rearrange("p (a b) -> p a b", a=shape[1])
        elif len(shape) == 4:
            ap = ap.rearrange("p (a b c) -> p a b c", a=shape[1], b=shape[2])
        return T(ap, name)

    def phase_reset(self, to=0):
        self.barrier()
        self.arena_off = to

    def _bufs(self, xs):
        out = []
        for x in xs or []:
            out.append(x.whole if isinstance(x, T) else x)
        return out

    def op(self, eng, fn, r=None, w=None, is_dma=False):
        if eng == "pool" and NO_POOL and not is_dma:
            eng = "dve"
        o = Op(eng, fn, is_dma)
        skey = ("dma", len(self.all_dma)) if is_dma else eng
        deps = []
        rb = self._bufs(r)
        wb = self._bufs(w)
        ex = [b for b in rb if b.excl]
        if ex:
            rb = [b for b in rb if not b.excl]
            wb = wb + [b for b in ex if b not in wb]
        for b in rb:
            if b.lw is not None:
                deps.append(b.lw)
        for b in wb:
            if b.lw is not None:
                deps.append(b.lw)
            deps.extend(b.rd.values())
        if self.pending[eng]:
            deps.extend(self.pending[eng])
            self.pending[eng] = []
        for b in rb:
            b.rd[skey] = o
        for b in wb:
            b.lw = o
            b.rd = {}
        seen = set()
        for d in deps:
            if id(d) in seen or d is o:
                continue
            seen.add(id(d))
            if (not d.is_dma) and (not is_dma) and d.eng == eng and eng == "pe":
                continue
            o.deps.append(d)
        o.idx = len(self.ops[eng])
        self.ops[eng].append(o)
        if is_dma:
            self.all_dma.append(o)
            self.dma_since_bar.append(o)
        return o

    def dma(self, out, in_, r=None, w=None, q="sp", **kw):
        return self.op(q, lambda e: e.dma_start(out=out, in_=in_, **kw), r=r, w=w, is_dma=True)

    def barrier(self):
        lasts = []
        for e in ENGINES:
            for o in reversed(self.ops[e]):
                if not o.is_dma:
                    lasts.append(o)
                    break
        lasts.extend(self.dma_since_bar)
        self.dma_since_bar = []
        for e in ENGINES:
            self.pending[e] = list(self.pending[e]) + lasts

    def emit(self):
        nc = self.nc
        for e in ENGINES:
            for o in self.ops[e]:
                for d in o.deps:
                    d.needs_inc = True
        nsem_eng = {}
        for e in ENGINES:
            c = 0
            k = 0
            for o in self.ops[e]:
                if o.is_dma:
                    o.dsem = (e, k % NDMASEM)
                    k += 1
                elif o.needs_inc:
                    c += 1
                    o.token = (("e", e, (c - 1) // EPOCH), (c - 1) % EPOCH + 1)
            nsem_eng[e] = (c + EPOCH - 1) // EPOCH if c else 0
        dcount = {}
        prev_dma = {}
        for e in ENGINES:
            for o in self.ops[e]:
                if o.is_dma:
                    key = ("d",) + o.dsem
                    v = dcount.get(key, 0) + 16
                    dcount[key] = v
                    o.token = (key, v)
                    if key in prev_dma:
                        o.deps.append(prev_dma[key])
                    prev_dma[key] = o
        sems = {}
        for e in ENGINES:
            for ep in range(nsem_eng[e]):
                sems[("e", e, ep)] = self.es.enter_context(nc.semaphore(f"s_{e}_{ep}"))
        for key in dcount:
            sems[key] = self.es.enter_context(nc.semaphore(f"d_{key[1]}_{key[2]}"))
        for e in ENGINES:
            seen = {}
            for o in self.ops[e]:
                need = {}
                for d in o.deps:
                    k, v = d.token
                    if seen.get(k, 0) >= v:
                        continue
                    if need.get(k, 0) < v:
                        need[k] = v
                for k, v in need.items():
                    seen[k] = v
                o.waits = list(need.items())
        final_waits = list(dcount.items())
        self.nsems = len(sems)
        self.ninst = {e: len(self.ops[e]) for e in ENGINES}
        engmap = {"pe": "tensor", "act": "scalar", "dve": "vector", "pool": "gpsimd", "sp": "sync"}
        with nc.Block() as block:
            for e in ENGINES:
                ops = self.ops[e]

                def body(eng, ops=ops, e=e):
                    for o in ops:
                        for k, v in o.waits:
                            eng.wait_ge(sems[k], v)
                        ins = o.fn(eng)
                        if o.is_dma:
                            ins.then_inc(sems[o.token[0]], 16)
                        elif o.needs_inc:
                            ins.then_inc(sems[o.token[0]], 1)
                    if e == "sp":
                        for k, v in final_waits:
                            eng.wait_ge(sems[k], v)

                getattr(block, engmap[e])(body)
        self.es.close()


def host_consts():
    c = {}
    c["ident"] = np.eye(128, dtype=np.float32)
    half = 32
    inv = 10000.0 ** (-np.arange(half, dtype=np.float32) / half)
    pos = np.arange(SEQ, dtype=np.float32)
    ang = pos[:, None] * inv[None, :]
    c["rope_cs"] = np.concatenate([np.cos(ang), np.sin(ang), -np.sin(ang)], axis=1).astype(np.float32)
    cend = (np.arange(N_CMP) * 16 + 31).astype(np.float32)
    angc = cend[:, None] * inv[None, :]
    c["rope_cmp"] = np.concatenate([np.cos(angc), np.sin(angc), -np.sin(angc)], axis=1).astype(np.float32)
    a = np.arange(128)[:, None]
    b = np.arange(128)[None, :]
    c["mdiag"] = np.where(a <= b, 0.0, NEG).astype(np.float32)
    c["mprev"] = np.where(a > b, 0.0, NEG).astype(np.float32)
    j = np.arange(N_CMP)[:, None, None]
    i = np.arange(NT)[None, :, None]
    bb = np.arange(128)[None, None, :]
    c["cmpmask"] = ((16 * j + 31) <= (128 * i + bb)).astype(np.float32)
    s = np.arange(SEQ)[None, :]
    js = np.arange(32)[:, None]
    c["selE"] = ((s // 64) == js).astype(np.float32)
    n = np.arange(N_CMP)[:, None]
    jj = np.arange(32)[None, :]
    c["ovl"] = ((16 * n < 64 * jj + 64) & (16 * n + 32 > 64 * jj)).astype(np.float32)
    t = np.arange(SEQ)[:, None]
    cur = t // 64
    forced = (jj == 0) | (jj == cur)
    valid = jj <= cur
    c["seladj"] = np.where(forced, 1e4, np.where(valid, 0.0, -1e4)).astype(np.float32)
    tri = (np.arange(64)[:, None] <= np.arange(64)[None, :]).astype(np.float32)
    c["tri64"] = np.concatenate([tri, tri], axis=0)
    return c


CONST_SHAPES = {"ident": [128, 128], "rope_cs": [SEQ, 96], "rope_cmp": [N_CMP, 96], "mdiag": [128, 128],
                "mprev": [128, 128], "cmpmask": [N_CMP, NT, 128], "selE": [32, SEQ], "ovl": [N_CMP, 32],
                "seladj": [SEQ, 32], "tri64": [128, 64]}

PARAM_SHAPES = {
    "norm_g": [2, 1024], "mem_norm_g": [2, 1024], "mem_w_kv": [2, 1024, 512], "mem_qn": [2, 64], "mem_kn": [2, 64],
    "ev_w_in": [1, 1024, 2816], "ev_w_out": [1, 1280, 1024], "a_qn": [1, 64], "a_kn": [1, 64], "a_sinks": [1, 8],
    "b_conv_w": [1, 4, 512], "b_conv_b": [1, 512], "b_w_r": [1, 8, 64, 64], "b_b_r": [1, 512],
    "b_w_i": [1, 8, 64, 64], "b_b_i": [1, 512], "b_lambda": [1, 512], "od_w_in": [1, 1024, 4376],
    "od_w_out": [1, 1280, 1024], "c_lb": [2, 512], "c_onorm": [1, 128], "d_qn": [1, 64], "d_kn_cmp": [1, 64],
    "d_kn_slc": [1, 64], "d_kn_win": [1, 64], "d_pe_k": [1, 32, 64], "d_pe_v": [1, 32, 64],
    "d_w1k": [1, 2048, 128], "d_w2k": [1, 128, 64], "d_w1v": [1, 2048, 128], "d_w2v": [1, 128, 64],
}


def build_program(nseq, layers=(0, 1), mix0=("A", "B", "M"), mix1=("C", "D", "M")):
    nc = bass.Bass("TRN2", target_bir_lowering=False)
    D = {}
    D["x"] = nc.dram_tensor("x", [nseq, SEQ, D_MODEL], F32, kind="ExternalInput").ap()
    D["mem"] = nc.dram_tensor("mem", [nseq, N_MEM, D_MODEL], F32, kind="ExternalInput").ap()
    for k, shp in PARAM_SHAPES.items():
        D[k] = nc.dram_tensor(k, shp, F32, kind="ExternalInput").ap()
    for k, shp in CONST_SHAPES.items():
        D[k] = nc.dram_tensor(k, shp, F32, kind="ExternalInput").ap()
    Y = nc.dram_tensor("y", [nseq, SEQ, D_MODEL], F32, kind="ExternalOutput").ap()
    W_IN = [nc.dram_tensor("w_in0s", [1024, EVEN_IN], BF16, kind="Internal").ap(),
            nc.dram_tensor("w_in1s", [1024, ODD_IN], BF16, kind="Internal").ap()]
    W_OUT = [nc.dram_tensor("w_out0s", [1280, 1024], BF16, kind="Internal").ap(),
             nc.dram_tensor("w_out1s", [1280, 1024], BF16, kind="Internal").ap()]
    W_KV = [nc.dram_tensor("w_kv0s", [1024, 512], BF16, kind="Internal").ap(),
            nc.dram_tensor("w_kv1s", [1024, 512], BF16, kind="Internal").ap()]
    W_1 = {"k": nc.dram_tensor("w1ks", [2048, 128], BF16, kind="Internal").ap(),
           "v": nc.dram_tensor("w1vs", [2048, 128], BF16, kind="Internal").ap()}

    S = Sched(nc)
    op = S.op

    H = S.sb("H", [128, NT, 1024], F32)
    xnT = S.sb("xnT", [128, 8, SEQ], BF16)
    ident = S.sb("ident", [128, 128], BF16)
    ROPE = S.sb("ROPE", [128, NT, 96], F32)
    mdiag = S.sb("mdiag", [128, 128], BF16)
    mprev = S.sb("mprev", [128, 128], BF16)
    normg = S.sb("normg", [128, 2, 8], F32)
    memg = S.sb("memg", [128, 2, 8], F32)
    GN = S.sb("GN", [128, 14, 64], F32)
    GI = {"mem_qn0": 0, "mem_qn1": 1, "mem_kn0": 2, "mem_kn1": 3, "a_qn": 4, "a_kn": 5, "d_qn": 6, "d_kn_slc": 7,
          "d_kn_win": 8, "d_kn_cmp": 9}
    COG = S.sb("COG", [128, 128], F32)
    LB = S.sb("LB", [128, 2, 4], F32)
    ones512 = S.sb("ones512", [128, 512], F32)
    esink = S.sb("esink", [128, 8], F32)
    ss16 = S.sb("ss16", [128, NT], F32)
    rstd16 = S.sb("rstd16", [128, NT], F32)
    tA = S.sb("tA", [128, 512], F32)
    tB = S.sb("tB", [128, 512], F32)
    tC = S.sb("tC", [128, 512], F32)
    tD = S.sb("tD", [128, 512], F32)
    tE = S.sb("tE", [128, 512], F32)
    tF = S.sb("tF", [128, 512], F32)
    xnb = S.sb("xnb", [128, 1024], BF16)
    sm8 = S.sb("sm8", [128, 8, 8], F32)
    Pb = S.sb("Pb", [128, 2, 512], BF16)
    ob16 = S.sb("ob16", [128, 1280], BF16)
    oT = S.sb("oT", [128, 10, 128], BF16)
    qrot = S.sb("qrot", [128, 512], BF16)
    qz = S.sb("qz", [128, 2, 4, 128], BF16)
    qzB = S.sb("qzB", [128, 2, 4, 128], BF16)
    qz2 = [qz, qzB]
    krot = S.sb("krot", [128, 256], BF16)
    pZ = S.ps("pZ", [128, 512], F32)
    pS0 = S.ps("pS0", [128, 512], F32)
    pS1 = S.ps("pS1", [128, 512], F32)
    pSb = [pS0, pS1]
    pO0 = S.ps("pO0", [128, 4, 128], F32)
    pO1 = S.ps("pO1", [128, 4, 128], F32)
    pOb = [pO0, pO1]
    pT = S.ps("pT", [128, 8, 128], BF16)
    pY0 = S.ps("pY0", [128, 512], F32)
    pY1 = S.ps("pY1", [128, 512], F32)
    pYb = [pY0, pY1]

    ARENA_WORDS = 18 * 1024
    S.make_arena(ARENA_WORDS)

    def mm(out, lhsT, rhs, start, stop, r, w, skip=False):
        if skip:
            return op("pe", lambda e: e.matmul(out=out, lhsT=lhsT, rhs=rhs, start=start, stop=stop,
                                               skip_group_check=True), r=r, w=w)
        return op("pe", lambda e: e.matmul(out=out, lhsT=lhsT, rhs=rhs, start=start, stop=stop), r=r, w=w)

    def tr(out, in_, npart, r, w):
        return op("pe", lambda e: e.transpose(out=out, in_=in_, identity=ident[0:npart, 0:npart]), r=list(r) + [ident], w=w)

    def act(out, in_, func, r, w, scale=1.0, bias=0.0, accum=None):
        if accum is None:
            return op("act", lambda e: e.activation(out=out, in_=in_, func=func, scale=scale, bias=bias), r=r, w=w)
        return op("act", lambda e: e.activation(out=out, in_=in_, func=func, scale=scale, bias=bias, accum_out=accum), r=r, w=w)

    def tt(eng, out, in0, in1, o, r, w):
        return op(eng, lambda e: e.tensor_tensor(out=out, in0=in0, in1=in1, op=o), r=r, w=w)

    def ts(eng, out, in0, s1, s2, o0, o1, r, w):
        if s2 is None:
            return op(eng, lambda e: e.tensor_scalar(out=out, in0=in0, scalar1=s1, scalar2=None, op0=o0), r=r, w=w)
        return op(eng, lambda e: e.tensor_scalar(out=out, in0=in0, scalar1=s1, scalar2=s2, op0=o0, op1=o1), r=r, w=w)

    def stt(eng, out, in0, sc, in1, o0, o1, r, w):
        return op(eng, lambda e: e.scalar_tensor_tensor(out=out, in0=in0, scalar=sc, in1=in1, op0=o0, op1=o1), r=r, w=w)

    def cp(eng, out, in_, r, w):
        if eng == "act":
            return act(out, in_, AF.Copy, r, w)
        return op(eng, lambda e: e.tensor_copy(out=out, in_=in_), r=r, w=w)

    def recip(out, in_, r, w):
        return op("dve", lambda e: e.reciprocal(out=out, in_=in_), r=r, w=w)

    def memset(eng, ap, val, w):
        return op(eng, lambda e: e.memset(ap, val), w=w)

    def rstd_from_ss(ap, n_mean, r, w):
        act(ap, ap, AF.Ln, r, w, scale=1.0 / n_mean, bias=EPS)
        act(ap, ap, AF.Exp, w, w, scale=-0.5)

    def bc3(ap2, n):
        return ap2.unsqueeze(2).broadcast_to([ap2.shape[0], ap2.shape[1], n])

    def bch(ap2, nh):
        return ap2.unsqueeze(1).broadcast_to([ap2.shape[0], nh, ap2.shape[1]])

    stage = S.carve("stage0", [128, 2048], F32)
    stage1 = S.carve("stage1", [128, 2048], F32)
    stb0 = S.carve("stb0", [128, 2048], BF16)
    stb1 = S.carve("stb1", [128, 2048], BF16)
    stages = [(stage, stb0), (stage1, stb1)]
    for j_ in range(2, 5):
        stages.append((S.carve("stage%d" % j_, [128, 2048], F32), S.carve("stb%d" % j_, [128, 2048], BF16)))

    S.dma(stage[:, 0:128], D["ident"], w=[stage])
    cp("dve", ident[:], stage[:, 0:128], [stage], [ident])
    S.dma(stage[:, 0:128], D["mdiag"], w=[stage])
    cp("dve", mdiag[:], stage[:, 0:128], [stage], [mdiag])
    S.dma(stage[:, 0:128], D["mprev"], w=[stage])
    cp("dve", mprev[:], stage[:, 0:128], [stage], [mprev])
    memset("dve", qz[:], 0.0, [qz.b(0), qz.b(1)])
    memset("dve", qzB[:], 0.0, [qzB.b(0), qzB.b(1)])
    S.dma(ROPE[:], D["rope_cs"].rearrange("(i p) f -> p i f", p=128), w=[ROPE])
    S.dma(normg[:], D["norm_g"].rearrange("l (c p) -> p l c", p=128), w=[normg], allow_slow_non_contiguous=True)
    S.dma(memg[:], D["mem_norm_g"].rearrange("l (c p) -> p l c", p=128), w=[memg], allow_slow_non_contiguous=True)
    for nm, gi in GI.items():
        if nm.startswith("mem_"):
            src = D[nm[:-1]][int(nm[-1])]
        else:
            src = D[nm][0]
        S.dma(GN[:, gi, :], src.partition_broadcast(128), w=[GN.b(gi)])
    for j_, nm_ in enumerate(["d_kn_slc", "d_kn_slc", "d_kn_win", "d_kn_win"]):
        S.dma(GN[:, 10 + j_, :], D[nm_][0].partition_broadcast(128), w=[GN.b(10)])
    S.dma(esink[:], D["a_sinks"][0].partition_broadcast(128), w=[esink])
    act(esink[:], esink[:], AF.Exp, [esink], [esink])
    S.dma(COG[:], D["c_onorm"][0].partition_broadcast(128), w=[COG])
    memset("dve", ones512[:], 1.0, [ones512])
    S.dma(LB[:, 0, :], D["c_lb"][0].rearrange("(h p) -> p h", p=128), w=[LB.b(0)], allow_slow_non_contiguous=True)
    S.dma(LB[:, 1, :], D["c_lb"][1].rearrange("(h p) -> p h", p=128), w=[LB.b(1)], allow_slow_non_contiguous=True)
    tt("dve", LB[:, 0, :], LB[:, 0, :], LB[:, 1, :], ALU.subtract, [LB.b(0), LB.b(1)], [LB.b(0)])
    act(LB[:, 0, :], LB[:, 0, :], AF.Exp, [LB.b(0)], [LB.b(0)])
    ts("dve", LB[:, 0, :], LB[:, 0, :], 1.0, None, ALU.add, None, [LB.b(0)], [LB.b(0)])
    recip(LB[:, 0, :], LB[:, 0, :], [LB.b(0)], [LB.b(0)])
    ts("dve", LB[:, 1, :], LB[:, 0, :], -1.0, 1.0, ALU.mult, ALU.add, [LB.b(0)], [LB.b(1)])

    cnt = [0]

    def conv_weight(src, dst, R, C, gt=None, l=0, perm0=None):
        for rc in range(R // 128):
            for c0 in range(0, C, 2048):
                cw = min(2048, C - c0)
                sf, sbf = stages[cnt[0] % len(stages)]
                eng = "dve"
                cnt[0] += 1
                S.dma(sf[:, 0:cw], src[rc * 128:(rc + 1) * 128, c0:c0 + cw], w=[sf])
                if gt is not None:
                    ts(eng, sbf[:, 0:cw], sf[:, 0:cw], gt[:, l, rc:rc + 1], None, ALU.mult, None, [sf, gt], [sbf])
                else:
                    cp(eng, sbf[:, 0:cw], sf[:, 0:cw], [sf], [sbf])
                rows = slice(rc * 128, (rc + 1) * 128)
                if perm0 is not None and c0 <= perm0 < c0 + cw:
                    p0 = perm0 - c0
                    if p0 > 0:
                        S.dma(dst[rows, c0:c0 + p0], sbf[:, 0:p0], r=[sbf])
                    for w_ in range(2):
                        S.dma(dst[rows, perm0:perm0 + 512].rearrange("r (pr w d) -> r w pr d", pr=4, w=2)[:, w_],
                              sbf[:, p0 + w_ * 256:p0 + (w_ + 1) * 256].rearrange("p (pr d) -> p pr d", pr=4), r=[sbf])
                    if p0 + 512 < cw:
                        S.dma(dst[rows, perm0 + 512:c0 + cw], sbf[:, p0 + 512:cw], r=[sbf])
                else:
                    S.dma(dst[rows, c0:c0 + cw], sbf[:, 0:cw], r=[sbf])

    if 0 in layers:
        conv_weight(D["ev_w_in"][0], W_IN[0], 1024, EVEN_IN, normg, 0, perm0=0)
        conv_weight(D["ev_w_out"][0], W_OUT[0], 1280, 1024)
        conv_weight(D["mem_w_kv"][0], W_KV[0], 1024, 512, memg, 0)
    if 1 in layers:
        conv_weight(D["od_w_in"][0], W_IN[1], 1024, ODD_IN, normg, 1, perm0=2048)
        conv_weight(D["od_w_out"][0], W_OUT[1], 1280, 1024)
        conv_weight(D["mem_w_kv"][1], W_KV[1], 1024, 512, memg, 1)
        conv_weight(D["d_w1k"][0], W_1["k"], 2048, 128)
        conv_weight(D["d_w1v"][0], W_1["v"], 2048, 128)
    S.phase_reset()

    def load_slab(dst, src_w, c0, ncols, key=None):
        S.dma(dst[:, :, 0:ncols], src_w.rearrange("(c p) n -> p c n", p=128)[:, :, c0:c0 + ncols],
              w=[dst.b(key)])

    def proj_tok(ps_ap, slab, col0, ncols, i, w, skey=None):
        for c in range(8):
            mm(ps_ap, xnT[:, c, i * 128:(i + 1) * 128], slab[:, c, col0:col0 + ncols], c == 0, c == 7,
               [xnT.b(i), slab.b(skey)], w)

    def proj_feat(ps_ap, slab, col0, g, w, skey=None):
        for c in range(8):
            mm(ps_ap, slab[:, c, col0:col0 + 128], xnT[:, c, g * 512:(g + 1) * 512], c == 0, c == 7,
               [xnT.b(4 * g), xnT.b(4 * g + 1), xnT.b(4 * g + 2), xnT.b(4 * g + 3), slab.b(skey)], w)

    def silu_psum(out_ap, zp, n, rz, wout, t1, t2, np_=128):
        a1 = t1[0:np_, 0:n]
        act(a1, zp, AF.Exp, rz, [t1], scale=-1.0)
        act(a1, a1, AF.Ln, [t1], [t1], bias=1.0)
        act(a1, a1, AF.Exp, [t1], [t1], scale=-1.0)
        tt("dve", out_ap, zp, a1, ALU.mult, list(rz) + [t1], wout)

    def silu_stages(out_ap, zp, n, rz, wout, t1, np_=128):
        a1 = t1[0:np_, 0:n]
        return [lambda: act(a1, zp, AF.Exp, rz, [t1], scale=-1.0),
                lambda: act(a1, a1, AF.Ln, [t1], [t1], bias=1.0),
                lambda: act(a1, a1, AF.Exp, [t1], [t1], scale=-1.0),
                lambda: tt("dve", out_ap, zp, a1, ALU.mult, list(rz) + [t1], wout)]

    def sigmoid_act(ap, r, w):
        act(ap, ap, AF.Ln, r, w, bias=1.0)
        act(ap, ap, AF.Exp, w, w, scale=-1.0)

    def norm_rope_stages(zp, nh, gi, rope_ap, out_ap, rz, wout, slot, np_=128):
        n = nh * 64
        z3 = zp.rearrange("p (h d) -> p h d", h=nh)
        ssq = sm8[0:np_, slot, 0:nh]
        zg = tB[0:np_, 0:n].rearrange("p (h d) -> p h d", h=nh)

        def st0():
            act(tA[0:np_, 0:n], zp, AF.Square, rz, [tA])
            if isinstance(gi, tuple):
                tt("dve", zg, z3, GN[0:np_, gi[0]:gi[0] + nh, :], ALU.mult, list(rz) + [GN.b(gi[0])], [tB])
            else:
                tt("dve", zg, z3, bch(GN[0:np_, gi, :], nh), ALU.mult, list(rz) + [GN.b(gi)], [tB])

        def st1():
            op("dve", lambda e: e.tensor_reduce(out=ssq, in_=tA[0:np_, 0:n].rearrange("p (h d) -> p h d", h=nh),
                                                axis=AX.X, op=ALU.add), r=[tA], w=[sm8.b(slot)])

        def st1b():
            act(ssq, ssq, AF.Ln, [sm8.b(slot)], [sm8.b(slot)], scale=1.0 / 64.0, bias=EPS)

        def st1c():
            act(ssq, ssq, AF.Exp, [sm8.b(slot)], [sm8.b(slot)], scale=-0.5)

        if rope_ap is None:
            def st2():
                tt("dve", out_ap, zg, bc3(ssq, 64), ALU.mult, [tB, sm8.b(slot)], wout)
            return [st0, st1, st1b, st1c, st2]
        zg4 = tB[0:np_, 0:n].rearrange("p (h a f) -> p h a f", h=nh, a=2)
        a4 = tC[0:np_, 0:n].rearrange("p (h a f) -> p h a f", h=nh, a=2)
        b4 = tD[0:np_, 0:n].rearrange("p (h a f) -> p h a f", h=nh, a=2)
        cos4 = rope_ap[:, 0:32].unsqueeze(1).unsqueeze(1).broadcast_to([np_, nh, 2, 32])
        sin3 = bch(rope_ap[:, 32:64], nh)
        nsin3 = bch(rope_ap[:, 64:96], nh)

        def st2():
            tt("dve", a4, zg4, cos4, ALU.mult, [tB], [tC])
            tt("dve", b4[:, :, 0, :], zg4[:, :, 1, :], nsin3, ALU.mult, [tB], [tD])
            tt("dve", b4[:, :, 1, :], zg4[:, :, 0, :], sin3, ALU.mult, [tB], [tD])

        def st3():
            tt("dve", tC[0:np_, 0:n], tC[0:np_, 0:n], tD[0:np_, 0:n], ALU.add, [tC, tD], [tC])
            tt("dve", out_ap, tC[0:np_, 0:n].rearrange("p (h d) -> p h d", h=nh), bc3(ssq, 64), ALU.mult,
               [tC, sm8.b(slot)], wout)
        return [st0, st1, st1b, st1c, st2, st3]

    def norm_rope(zp, nh, gi, rope_ap, out_ap, rz, wout, slot, np_=128):
        for st in norm_rope_stages(zp, nh, gi, rope_ap, out_ap, rz, wout, slot, np_):
            st()

    def h_update(i, nchunks, wo, wkey=None):
        for half in range(2):
            for c in range(nchunks):
                mm(pYb[half][:], oT[:, c, :], wo[:, c, half * 512:(half + 1) * 512], c == 0, c == nchunks - 1,
                   [oT, wo.b(wkey)], [pYb[half]])
            tt("dve", H[:, i, half * 512:(half + 1) * 512], H[:, i, half * 512:(half + 1) * 512], pYb[half][:], ALU.add,
               [H.b(i), pYb[half]], [H.b(i)])

    def transposes_to_oT(nch, src_cols0=0, dst0=0):
        for c in range(nch):
            tr(pT[:, c, :], ob16[:, src_cols0 + c * 128: src_cols0 + (c + 1) * 128], 128, [ob16], [pT])
        cp("act", oT[:, dst0:dst0 + nch, :], pT[:, 0:nch, :], [pT], [oT])

    def attn_pipeline(branches, qsrc, fillers=None):
        steps = []
        for bi, br in enumerate(branches):
            for n_, kt in enumerate(br["kts"]):
                steps.append((bi, n_, kt))

        def emit_qk(idx):
            bi, n_, kt = steps[idx]
            br = branches[bi]
            k = br["k"]
            ps = pSb[idx % 2]
            psv = ps[:].rearrange("p (g t) -> p g t", g=4)
            extra = br["masks"](kt)
            mm(psv, br["kT"][:, kt * 128:(kt + 1) * 128], qsrc[:, k, :, :], True, len(extra) == 0,
               [br["kT"].b(kt), qsrc.b(k)], [ps])
            for j_, (lt, rh, rd_) in enumerate(extra):
                mm(psv, lt, rh, False, j_ == len(extra) - 1, rd_, [ps])

        first_use = {}
        for bi, br in enumerate(branches):
            first_use.setdefault(br["k"], bi)
        for k_, bi in first_use.items():
            memset("dve", pOb[k_][:], 0.0, [pOb[k_]])
        emit_qk(0)
        for idx, (bi, n_, kt) in enumerate(steps):
            br = branches[bi]
            k = br["k"]
            if idx + 1 < len(steps):
                emit_qk(idx + 1)
            act(Pb[:, idx % 2, :], pSb[idx % 2][:], AF.Exp, [pSb[idx % 2]], [Pb.b(idx % 2)], scale=0.125)
            V_ = br["V"]
            for g in range(4):
                mm(pOb[k][:, g, 0:65], Pb[:, idx % 2, g * 128:(g + 1) * 128], V_[:, kt, k, :], False, False,
                   [Pb.b(idx % 2), V_.b(kt), V_.b("ones")], [pOb[k]], skip=True)
            if n_ == len(br["kts"]) - 1:
                br["evac"](k)
                if any(b2["k"] == k for b2 in branches[bi + 1:]):
                    memset("dve", pOb[k][:], 0.0, [pOb[k]])
            if fillers and FILL_MODE == 1:
                fillers.pop(0)()
        while fillers:
            fillers.pop(0)()

    def rmsnorm_to_xnT():
        memset("pool", ss16[:], 0.0, [ss16])
        for i in range(NT):
            act(xnb[:], H[:, i, :], AF.Square, [H.b(i)], [xnb, ss16], accum=ss16[:, i:i + 1])
        cp("dve", rstd16[:], ss16[:], [ss16], [rstd16])
        rstd_from_ss(rstd16[:], 1024.0, [rstd16], [rstd16])
        for i in range(NT):
            ts("dve", xnb[:], H[:, i, :], rstd16[:, i:i + 1], None, ALU.mult, None, [H.b(i), rstd16], [xnb])
            for c in range(8):
                tr(pT[:, c, :], xnb[:, c * 128:(c + 1) * 128], 128, [xnb], [pT])
            cp("act", xnT[:, :, i * 128:(i + 1) * 128], pT[:], [pT], [xnT.b(i)])

    def mem_kv(s, l, kmT, VM1, memf, memT, wkv):
        load_slab(wkv, W_KV[l], 0, 512)
        memset("dve", VM1[:, :, :, 64:65], 1.0, [VM1.b("ones")])
        for nt in range(2):
            S.dma(memf[:], D["mem"][s, nt * 128:(nt + 1) * 128, :], w=[memf])
            memset("dve", sm8[:, 7, 0:1], 0.0, [sm8.b(7)])
            act(xnb[:], memf[:], AF.Square, [memf], [xnb, sm8.b(7)], accum=sm8[:, 7, 0:1])
            rstd_from_ss(sm8[:, 7, 0:1], 1024.0, [sm8.b(7)], [sm8.b(7)])
            ts("dve", xnb[:], memf[:], sm8[:, 7, 0:1], None, ALU.mult, None, [memf, sm8.b(7)], [xnb])
            for c in range(8):
                tr(pT[:, c, :], xnb[:, c * 128:(c + 1) * 128], 128, [xnb], [pT])
            cp("act", memT[:, :, nt * 128:(nt + 1) * 128], pT[:], [pT], [memT.b(nt)])
            for c in range(8):
                mm(pZ[:], memT[:, c, nt * 128:(nt + 1) * 128], wkv[:, c, 0:512], c == 0, c == 7, [memT.b(nt), wkv], [pZ])
            norm_rope(pZ[:, 0:256], 4, GI["mem_kn%d" % l], None, krot[:].rearrange("p (h d) -> p h d", h=4),
                      [pZ], [krot], 6)
            cp("act", VM1[:, nt, :, 0:64], pZ[:, 256:512].rearrange("p (h d) -> p h d", h=4), [pZ], [VM1.b(nt)])
            for pr in range(2):
                tr(pT[:, pr, :], krot[:, pr * 128:(pr + 1) * 128], 128, [krot], [pT])
            cp("act", kmT[:, :, nt * 128:(nt + 1) * 128], pT[:, 0:2, :], [pT], [kmT.b(nt)])

    def mem_attn_tile(i, qz_ap, qz_r, l, kmT, VM1, gate_ap, gate_r, out_cols0, qzt=None):
        qzt = qzt if qzt is not None else qz
        norm_rope(qz_ap, 4, GI["mem_qn%d" % l], None, qrot[:, 0:256].rearrange("p (h d) -> p h d", h=4),
                  qz_r, [qrot], 5)
        for pr in range(2):
            tr(pT[:, pr, :], qrot[:, pr * 128:(pr + 1) * 128], 128, [qrot], [pT])
        cp("act", qzt[0:64, 0, 0:2, :], pT[0:64, 0:2, :], [pT], [qzt.b(0)])
        cp("act", qzt[64:128, 1, 0:2, :], pT[64:128, 0:2, :], [pT], [qzt.b(1)])
        if DBG < 2:
            return
        for h in range(4):
            pr, hf = h // 2, h % 2
            for nt in range(2):
                mm(pSb[nt][:, h * 128:(h + 1) * 128], kmT[:, pr, nt * 128:(nt + 1) * 128],
                   qzt[:, hf, pr, :], True, True, [kmT.b(nt), qzt.b(hf)], [pSb[nt]])
        for nt in range(2):
            act(Pb[:, nt, :], pSb[nt][:], AF.Exp, [pSb[nt]], [Pb.b(nt)], scale=0.125)
        if DBG < 3:
            return
        for h in range(4):
            for nt in range(2):
                mm(pO0[:, h, 0:65], Pb[:, nt, h * 128:(h + 1) * 128], VM1[:, nt, h, :], nt == 0, nt == 1,
                   [Pb.b(nt), VM1.b(nt), VM1.b("ones")], [pO0])
        cp("dve", sm8[:, 4, 0:4], pO0[:, :, 64], [pO0], [sm8.b(4)])
        recip(sm8[:, 4, 0:4], sm8[:, 4, 0:4], [sm8.b(4)], [sm8.b(4)])
        tt("dve", tE[:, 0:256].rearrange("p (h d) -> p h d", h=4), pO0[:, :, 0:64], bc3(sm8[:, 4, 0:4], 64), ALU.mult,
           [pO0, sm8.b(4)], [tE])
        tt("dve", ob16[:, out_cols0:out_cols0 + 256], tE[:, 0:256], gate_ap, ALU.mult, [tE] + list(gate_r), [ob16])

    def mem_front_stages(i, l, wslab, gatT, qzmT):
        st = [lambda: proj_tok(pZ[:], wslab, 0, 512, i, [pZ])]
        st += silu_stages(gatT[:, 512:768], pZ[:, 256:512], 256, [pZ], [gatT.b("m")], tF)
        st += norm_rope_stages(pZ[:, 0:256], 4, GI["mem_qn%d" % l], None, qrot[:, 0:256].rearrange("p (h d) -> p h d", h=4),
                               [pZ], [qrot], 5)

        def t_():
            for pr in range(2):
                tr(pT[:, pr, :], qrot[:, pr * 128:(pr + 1) * 128], 128, [qrot], [pT])
        def c_():
            t_()
            cp("act", qzmT[0:64, 0, :, :], pT[0:64, 0:2, :], [pT], [qzmT.b(0)])
            cp("act", qzmT[64:128, 1, :, :], pT[64:128, 0:2, :], [pT], [qzmT.b(1)])
        st.append(c_)
        return st

    def mem_back(i, l, kmT, VM1, gatT, qzmT, out_cols0=512):
        for h in range(4):
            pr, hf = h // 2, h % 2
            for nt in range(2):
                mm(pSb[nt][:, h * 128:(h + 1) * 128], kmT[:, pr, nt * 128:(nt + 1) * 128],
                   qzmT[:, hf, pr, :], True, True, [kmT.b(nt), qzmT.b(hf)], [pSb[nt]])
        for nt in range(2):
            act(Pb[:, nt, :], pSb[nt][:], AF.Exp, [pSb[nt]], [Pb.b(nt)], scale=0.125)
        for h in range(4):
            for nt in range(2):
                mm(pO0[:, h, 0:65], Pb[:, nt, h * 128:(h + 1) * 128], VM1[:, nt, h, :], nt == 0, nt == 1,
                   [Pb.b(nt), VM1.b(nt), VM1.b("ones")], [pO0])
        cp("dve", sm8[:, 4, 0:4], pO0[:, :, 64], [pO0], [sm8.b(4)])
        recip(sm8[:, 4, 0:4], sm8[:, 4, 0:4], [sm8.b(4)], [sm8.b(4)])
        tt("dve", tE[:, 0:256].rearrange("p (h d) -> p h d", h=4), pO0[:, :, 0:64], bc3(sm8[:, 4, 0:4], 64), ALU.mult,
           [pO0, sm8.b(4)], [tE])
        tt("dve", ob16[:, out_cols0:out_cols0 + 256], tE[:, 0:256], gatT[:, 512:768], ALU.mult, [tE, gatT.b("m")], [ob16])
        for c in range(2):
            tr(pT[:, 4 + c, :], ob16[:, out_cols0 + c * 128:out_cols0 + (c + 1) * 128], 128, [ob16], [pT])
        cp("act", oT[:, 4:6, :], pT[:, 4:6, :], [pT], [oT])

    def layer0(s):
        l = 0
        rmsnorm_to_xnT()
        if "B" in mix0:
            PB = S.carve("PB", [128, 4, 8], F32)
            PD = S.carve("PD", [128, 4, 4], F32)
            BDf = S.carve("BDf", [128, 2, 4, 128], F32)
            BD = S.carve("BD", [128, 2, 4, 128], BF16)
            XB = S.carve("XB", [128, 3 + SEQ], F32)
            hB = S.carve("hB", [128, SEQ], F32)
            mixB = S.carve("mixB", [128, 4, SEQ], BF16)
            wsl = [S.carve("wslB0", [128, 8, 256], BF16), S.carve("wslB1", [128, 8, 256], BF16)]
            woB = S.carve("woB", [128, 4, 1024], BF16)
            xcb = S.carve("xcb", [128, 512], BF16)
            for j in range(4):
                S.dma(PB[:, :, j], D["b_conv_w"][0, j].rearrange("(c p) -> p c", p=128), w=[PB.b(j)],
                      allow_slow_non_contiguous=True)
            for j, nm in enumerate(["b_conv_b", "b_b_r", "b_b_i", "b_lambda"]):
                S.dma(PB[:, :, 4 + j], D[nm][0].rearrange("(c p) -> p c", p=128), w=[PB.b(4 + j)],
                      allow_slow_non_contiguous=True)
            ts("dve", PD[:, :, 0], PB[:, :, 5], -1.0, None, ALU.mult, None, [PB.b(5)], [PD.b(0)])
            ts("dve", PD[:, :, 1], PB[:, :, 6], -1.0, None, ALU.mult, None, [PB.b(6)], [PD.b(1)])
            act(PD[:, :, 2], PB[:, :, 7], AF.Exp, [PB.b(7)], [PD.b(2)], scale=-1.0)
            act(PD[:, :, 2], PD[:, :, 2], AF.Ln, [PD.b(2)], [PD.b(2)], bias=1.0)
            ts("dve", PD[:, :, 2], PD[:, :, 2], -8.0, None, ALU.mult, None, [PD.b(2)], [PD.b(2)])
            bdkeys = [BDf.b((a_, b_, c_)) for a_ in range(2) for b_ in range(4) for c_ in range(2)]
            memset("pool", BDf[:], 0.0, bdkeys)
            for gi_, nm in enumerate(["b_w_r", "b_w_i"]):
                for cb in range(4):
                    for hb in range(2):
                        S.dma(BDf[64 * hb:64 * hb + 64, gi_, cb, 64 * hb:64 * hb + 64], D[nm][0, 2 * cb + hb],
                              w=[BDf.b((gi_, cb, hb))])
            cp("dve", BD[:], BDf[:], bdkeys, [BD])
            memset("pool", XB[:, 0:3], 0.0, [XB.b("pad")])
            S.dma(woB[:], W_OUT[l].rearrange("(c p) n -> p c n", p=128)[:, 4:8, :], w=[woB])
            for cb in range(4):
                wS = wsl[cb % 2]
                S.dma(wS[:, :, 0:128], W_IN[l].rearrange("(c p) n -> p c n", p=128)[:, :, 1280 + cb * 128:1280 + (cb + 1) * 128],
                      w=[wS.b("x")])
                S.dma(wS[:, :, 128:256], W_IN[l].rearrange("(c p) n -> p c n", p=128)[:, :, 1792 + cb * 128:1792 + (cb + 1) * 128],
                      w=[wS.b("g")])
                for g in range(4):
                    sl = slice(3 + g * 512, 3 + (g + 1) * 512)
                    proj_feat(pZ[:], wS, 0, g, [pZ], skey="x")
                    cp("act", XB[:, sl], pZ[:], [pZ], [XB.b(g)])
                    rd = [XB.b(g), XB.b(g - 1) if g > 0 else XB.b("pad"), PB.b(0), PB.b(1), PB.b(2), PB.b(3), PB.b(4)]
                    xc = tB
                    ts("dve", xc[:], XB[:, g * 512:g * 512 + 512], PB[:, cb, 0:1], PB[:, cb, 4:5], ALU.mult, ALU.add, rd, [tB])
                    for j in (1, 2, 3):
                        stt("dve", xc[:], XB[:, g * 512 + j:g * 512 + j + 512], PB[:, cb, j:j + 1], xc[:], ALU.mult, ALU.add,
                            rd + [tB], [tB])
                    cp("pool", xcb[:], xc[:], [tB], [xcb])
                    mm(pS0[:], BD[:, 0, cb, :], xcb[:], True, True, [BD, xcb], [pS0])
                    mm(pS1[:], BD[:, 1, cb, :], xcb[:], True, True, [BD, xcb], [pS1])
                    act(tC[:], pS0[:], AF.Exp, [pS0, PD.b(0)], [tC], scale=-1.0, bias=PD[:, cb, 0:1])
                    sigmoid_act(tC[:], [tC], [tC])
                    act(tC[:], tC[:], AF.Exp, [tC, PD.b(2)], [tC], scale=PD[:, cb, 2:3])
                    act(tD[:], pS1[:], AF.Exp, [pS1, PD.b(1)], [tD], scale=-1.0, bias=PD[:, cb, 1:2])
                    sigmoid_act(tD[:], [tD], [tD])
                    act(tE[:], tC[:], AF.Square, [tC], [tE])
                    act(tE[:], tE[:], AF.Ln, [tE], [tE], scale=-1.0, bias=1.0)
                    act(tE[:], tE[:], AF.Exp, [tE], [tE], scale=0.5)
                    tt("pool", tD[:], tD[:], xc[:], ALU.mult, [tD, tB], [tD])
                    tt("pool", tD[:], tD[:], tE[:], ALU.mult, [tD, tE], [tD])
                    init = 0.0 if g == 0 else hB[:, g * 512 - 1:g * 512]
                    op("dve", lambda e, g=g, init=init: e.tensor_tensor_scan(
                        out=hB[:, g * 512:(g + 1) * 512], data0=tC[:], data1=tD[:], initial=init,
                        op0=ALU.mult, op1=ALU.add), r=[tC, tD] + ([hB.b(g - 1)] if g > 0 else []), w=[hB.b(g)])
                    proj_feat(pZ[:], wS, 128, g, [pZ], skey="g")
                    silu_psum(tF[:], pZ[:], 512, [pZ], [tF], tE, tF)
                    tt("dve", mixB[:, cb, g * 512:(g + 1) * 512], tF[:], hB[:, g * 512:(g + 1) * 512], ALU.mult,
                       [tF, hB.b(g)], [mixB.b((cb, g))])
            for i in range(NT):
                for half in range(2):
                    for cb in range(4):
                        mm(pYb[half][:], mixB[:, cb, i * 128:(i + 1) * 128], woB[:, cb, half * 512:(half + 1) * 512],
                           cb == 0, cb == 3, [mixB.b((cb, i // 4)), woB], [pYb[half]])
                    tt("dve", H[:, i, half * 512:(half + 1) * 512], H[:, i, half * 512:(half + 1) * 512], pYb[half][:],
                       ALU.add, [H.b(i), pYb[half]], [H.b(i)])
            S.phase_reset()
        doA, doM = "A" in mix0, "M" in mix0
        if doA or doM:
            kT = S.carve("kT", [128, SEQ], BF16)
            V1 = S.carve("V1", [128, NT, 2, 65], BF16)
            wq = S.carve("wq", [128, 8, 512], BF16)
            wga = S.carve("wga", [128, 8, 512], BF16)
            wqgm = S.carve("wqgm", [128, 8, 512], BF16)
            woAM = S.carve("woAM", [128, 6, 1024], BF16)
            kmT = S.carve("kmT", [128, 2, N_MEM], BF16)
            VM1 = S.carve("VM1", [128, 2, 4, 65], BF16)
            memT = S.carve("memT", [128, 8, N_MEM], BF16)
            memf = S.carve("memf", [128, 1024], F32)
            gat = S.carve("gat", [128, 768], BF16)
            if doM:
                mem_kv(s, l, kmT, VM1, memf, memT, wq)
            if doA:
                load_slab(wga, W_IN[l], 512, 256)
                memset("dve", V1[:, :, :, 64:65], 1.0, [V1.b("ones")])
                for i in range(NT):
                    proj_tok(pZ[:, 0:256], wga, 0, 256, i, [pZ])
                    norm_rope(pZ[:, 0:128], 2, GI["a_kn"], ROPE[:, i, :], krot[:, 0:128].rearrange("p (h d) -> p h d", h=2),
                              [pZ], [krot], 0)
                    cp("act", V1[:, i, :, 0:64], pZ[:, 128:256].rearrange("p (h d) -> p h d", h=2), [pZ], [V1.b(i)])
                    tr(pT[:, 0, :], krot[:, 0:128], 128, [krot], [pT])
                    cp("act", kT[:, i * 128:(i + 1) * 128], pT[:, 0, :], [pT], [kT.b(i)])
            if doA:
                load_slab(wq, W_IN[l], 0, 512)
                load_slab(wga, W_IN[l], 768, 512)
                S.dma(woAM[:, 0:4, :], W_OUT[l].rearrange("(c p) n -> p c n", p=128)[:, 0:4, :], w=[woAM.b("a")])
            if doM:
                load_slab(wqgm, W_IN[l], 2304, 512)
                S.dma(woAM[:, 4:6, :], W_OUT[l].rearrange("(c p) n -> p c n", p=128)[:, 8:10, :], w=[woAM.b("m")])
            gatB = S.carve("gatB", [128, 768], BF16)
            gat2 = [gat, gatB]
            qzm2 = [S.carve("qzmA", [128, 2, 2, 128], BF16), S.carve("qzmB", [128, 2, 2, 128], BF16)]
            for q_ in qzm2:
                memset("dve", q_[:], 0.0, [q_.b(0), q_.b(1)])

            def frontA(i):
                par = i % 2
                st = [lambda: proj_tok(pZ[:], wga, 0, 512, i, [pZ]),
                      ] + silu_stages(gat2[par][:, 0:512], pZ[:], 512, [pZ], [gat2[par].b("a")], tF) + [
                      lambda: proj_tok(pZ[:], wq, 0, 512, i, [pZ])]
                st += norm_rope_stages(pZ[:], 8, GI["a_qn"], ROPE[:, i, :], qrot[:].rearrange("p (h d) -> p h d", h=8),
                                       [pZ], [qrot], 1)

                def t_():
                    for pr in range(4):
                        tr(pT[:, pr, :], qrot[:, pr * 128:(pr + 1) * 128], 128, [qrot], [pT])
                def c_():
                    t_()
                    cp("act", qz2[par][0:64, 0, :, :], pT[0:64, 0:4, :], [pT], [qz2[par].b(0)])
                    cp("act", qz2[par][64:128, 1, :, :], pT[64:128, 0:4, :], [pT], [qz2[par].b(1)])
                st.append(c_)
                return st

            def front(i):
                st = []
                if doA:
                    st += frontA(i)
                if doM:
                    st += mem_front_stages(i, l, wqgm, gat2[i % 2], qzm2[i % 2])
                return st

            for st_ in front(0):
                st_()
            for i in range(NT):
                par = i % 2
                fill = front(i + 1) if i + 1 < NT else []
                if doA:
                    kts = [i - 1, i] if i > 0 else [i]

                    def masksA(kt, i=i):
                        msk = mdiag if kt == i else mprev
                        return [(ident[:], bch(msk[:], 4), [ident, msk])]

                    def evacA(k):
                        tt("dve", sm8[:, 2, 4 * k:4 * k + 4], pOb[k][:, :, 64], esink[:, 4 * k:4 * k + 4], ALU.add,
                           [pOb[k], esink], [sm8.b((2, k))])
                        recip(sm8[:, 2, 4 * k:4 * k + 4], sm8[:, 2, 4 * k:4 * k + 4], [sm8.b((2, k))], [sm8.b((2, k))])
                        tt("dve", tE[:, 256 * k:256 * k + 256].rearrange("p (h d) -> p h d", h=4), pOb[k][:, :, 0:64],
                           bc3(sm8[:, 2, 4 * k:4 * k + 4], 64), ALU.mult, [pOb[k], sm8.b((2, k))], [tE])

                    nfa = len(fill) // 2 if doM else len(fill)
                    fa = [fill.pop(0) for _ in range(nfa)]
                    attn_pipeline([dict(k=k, kT=kT, V=V1, kts=kts, masks=masksA, evac=evacA) for k in range(2)],
                                  qz2[par], fa)
                    tt("dve", ob16[:, 0:512], tE[:], gat2[par][:, 0:512], ALU.mult, [tE, gat2[par].b("a")], [ob16])
                    transposes_to_oT(4, 0, 0)
                if doM:
                    mem_back(i, l, kmT, VM1, gat2[par], qzm2[par])
                while fill:
                    fill.pop(0)()
                chunks = ([0, 1, 2, 3] if doA else []) + ([4, 5] if doM else [])
                for half in range(2):
                    for n_, c in enumerate(chunks):
                        mm(pYb[half][:], oT[:, c, :], woAM[:, c, half * 512:(half + 1) * 512], n_ == 0, n_ == len(chunks) - 1,
                           [oT, woAM.b("a"), woAM.b("m")], [pYb[half]])
                    tt("dve", H[:, i, half * 512:(half + 1) * 512], H[:, i, half * 512:(half + 1) * 512], pYb[half][:],
                       ALU.add, [H.b(i), pYb[half]], [H.b(i)])
            S.phase_reset()

    def separator():
        mm(pY1[:, 0:1], ident[:], ident[:, 0:1], True, True, [ident], [pY1])

    def hgrn2_phase(s):
        l = 1
        QT = S.carve("QT", [128, 4, 512], BF16)
        KT = S.carve("KT", [128, 4, 512], BF16)
        KH = S.carve("KH", [128, 4, 512], BF16)
        Vc = S.carve("Vc", [128, 4, 512], BF16)
        wsm = [S.carve("wsmC0", [128, 8, 256], BF16), S.carve("wsmC1", [128, 8, 256], BF16)]
        wic = S.carve("wic", [128, 8, 512], BF16)
        wgc = S.carve("wgc", [128, 8, 512], BF16)
        woC = S.carve("woC", [128, 4, 1024], BF16)
        St = S.carve("St", [128, 4, 128], F32)
        Sb = S.carve("Sb", [128, 4, 128], BF16)
        EBL = S.carve("EBL", [128, 4, 32], F32)
        ATb = S.carve("ATb", [128, 4, 128], BF16)
        KHz = S.carve("KHz", [128, 4, 2, 128], BF16)
        tri = S.carve("tri", [128, 64], F32)
        Bp = S.carve("Bp", [128, 8], F32)
        tG = S.carve("tG", [128, 512], F32)
        tH = S.carve("tH", [128, 512], F32)
        gcb = S.carve("gcb", [128, 512], BF16)
        S.dma(tri[:], D["tri64"], w=[tri])
        memset("dve", St[:], 0.0, [St])
        memset("dve", Sb[:], 0.0, [Sb])
        memset("dve", ATb[:], 0.0, [ATb.b(0), ATb.b(1)])
        memset("dve", KHz[:], 0.0, [KHz.b(0), KHz.b(1)])
        memset("dve", Bp[:], 0.0, [Bp])
        load_slab(wic, W_IN[l], 1024, 512)
        load_slab(wgc, W_IN[l], 1536, 512)
        S.dma(woC[:], W_OUT[l].rearrange("(c p) n -> p c n", p=128)[:, 0:4, :], w=[woC])
        pS0v = pS0[:].rearrange("p (h t) -> p h t", h=4)
        pS1v = pS1[:].rearrange("p (h t) -> p h t", h=4)
        for g in range(4):
            for hd in range(4):
                wS = wsm[(g * 4 + hd) % 2]
                wv = W_IN[l].rearrange("(c p) n -> p c n", p=128)
                S.dma(wS[:, :, 0:128], wv[:, :, hd * 128:(hd + 1) * 128], w=[wS.b("q")])
                S.dma(wS[:, :, 128:256], wv[:, :, 512 + hd * 128:512 + (hd + 1) * 128], w=[wS.b("f")])
                proj_feat(pZ[:], wS, 128, g, [pZ], skey="f")
                act(tB[:], pZ[:], AF.Exp, [pZ], [tB], scale=-1.0)
                sigmoid_act(tB[:], [tB], [tB])
                ts("dve", tB[:], tB[:], LB[:, 1, hd:hd + 1], LB[:, 0, hd:hd + 1], ALU.mult, ALU.add,
                   [tB, LB.b(0), LB.b(1)], [tB])
                act(tC[:], tB[:], AF.Ln, [tB], [tC])
                ts("pool", tB[:], tB[:], -1.0, 1.0, ALU.mult, ALU.add, [tB], [tB])
                op("dve", lambda e: e.tensor_tensor_scan(out=tD[:], data0=ones512[:], data1=tC[:], initial=0.0,
                                                         op0=ALU.mult, op1=ALU.add), r=[ones512, tC], w=[tD])
                tD3 = tD[:].rearrange("p (c j) -> p c j", j=64)
                cp("dve", Bp[:, 1:8], tD3[:, 0:7, 63], [tD], [Bp])
                tt("dve", tD3, tD3, bc3(Bp[:, 0:8], 64), ALU.subtract, [tD, Bp], [tD])
                act(tE[:], tD[:], AF.Exp, [tD], [tE])
                cp("dve", EBL[:, hd, 8 * g:8 * g + 8], tE[:].rearrange("p (c j) -> p c j", j=64)[:, :, 63], [tE],
                   [EBL.b((hd, g))])
                act(tF[:], tD[:], AF.Exp, [tD], [tF], scale=-1.0)
                tt("dve", KT[:, hd, :], tB[:], tF[:], ALU.mult, [tB, tF], [KT.b(hd)])
                tt("dve", tG[:].rearrange("p (c j) -> p c j", j=64), tD3, bc3(tD3[:, :, 63], 64), ALU.subtract,
                   [tD], [tG])
                act(tG[:], tG[:], AF.Exp, [tG], [tG], scale=-1.0)
                tt("dve", KH[:, hd, :], tB[:], tG[:], ALU.mult, [tB, tG], [KH.b(hd)])
                proj_feat(pZ[:], wS, 0, g, [pZ], skey="q")
                silu_psum(tH[:], pZ[:], 512, [pZ], [tH], tF, tH)
                tt("dve", QT[:, hd, :], tH[:], tE[:], ALU.mult, [tH, tE], [QT.b(hd)])
            for tl in range(4):
                i = 4 * g + tl
                proj_tok(pZ[:], wic, 0, 512, i, [pZ])
                cp("act", Vc[:, tl, :], pZ[:], [pZ], [Vc.b(tl)])
            for tl in range(4):
                i = 4 * g + tl
                cs = tl * 128
                allh = [QT.b(h_) for h_ in range(4)]
                for hd in range(4):
                    tr(pT[:, hd, :], KH[:, hd, cs:cs + 128], 128, [KH.b(hd)], [pT])
                cp("act", KHz[0:64, :, 0, :], pT[0:64, 0:4, :], [pT], [KHz.b(0)])
                cp("act", KHz[64:128, :, 1, :], pT[64:128, 0:4, :], [pT], [KHz.b(1)])
                for hd in range(4):
                    for c in range(2):
                        mm(pS0v[64 * c:64 * c + 64, hd, 64 * c:64 * c + 64], KT[:, hd, cs + 64 * c:cs + 64 * c + 64],
                           QT[:, hd, cs + 64 * c:cs + 64 * c + 64], True, True, [KT.b(hd), QT.b(hd)], [pS0])
                for c in range(2):
                    tt("dve", ATb[64 * c:64 * c + 64, :, 64 * c:64 * c + 64], pS0v[64 * c:64 * c + 64, :, 64 * c:64 * c + 64],
                       bch(tri[64 * c:64 * c + 64, :], 4), ALU.mult, [pS0, tri], [ATb.b(c)])
                for hd in range(4):
                    mm(pO0[:, hd, :], ATb[:, hd, :], Vc[:, tl, hd * 128:(hd + 1) * 128], True, True,
                       [ATb.b(0), ATb.b(1), Vc.b(tl)], [pO0])
                for c in range(2):
                    ch = 8 * g + 2 * tl + c
                    for hd in range(4):
                        mm(pO1[64 * c:64 * c + 64, hd, :], QT[:, hd, cs + 64 * c:cs + 64 * c + 64], Sb[:, hd, :], True, True,
                           [QT.b(hd), Sb], [pO1])
                    for hd in range(4):
                        mm(pS1v[:, hd, :], KHz[:, hd, c, :], Vc[:, tl, hd * 128:(hd + 1) * 128], True, True,
                           [KHz.b(c), Vc.b(tl)], [pS1])
                    tt("dve", St[:], St[:], bc3(EBL[:, :, ch], 128), ALU.mult, [St] + [EBL.b((h_, g)) for h_ in range(4)], [St])
                    tt("dve", St[:], St[:], pS1v, ALU.add, [St, pS1], [St])
                    cp("act", Sb[:], St[:], [St], [Sb])
                cp("act", tG[:], pO0[:].rearrange("p h v -> p (h v)"), [pO0], [tG])
                tt("dve", tG[:], tG[:], pO1[:].rearrange("p h v -> p (h v)"), ALU.add, [tG, pO1], [tG])
                act(tA[:, 0:512], tG[:], AF.Square, [tG], [tA])
                op("dve", lambda e: e.tensor_reduce(out=sm8[:, 3, 0:4], in_=tA[:, 0:512].rearrange("p (h v) -> p h v", h=4),
                                                    axis=AX.X, op=ALU.add), r=[tA], w=[sm8.b(3)])
                rstd_from_ss(sm8[:, 3, 0:4], 128.0, [sm8.b(3)], [sm8.b(3)])
                tG3 = tG[:].rearrange("p (h v) -> p h v", h=4)
                tt("dve", tG3, tG3, bc3(sm8[:, 3, 0:4], 128), ALU.mult, [tG, sm8.b(3)], [tG])
                tt("dve", tG3, tG3, bch(COG[:], 4), ALU.mult, [tG, COG], [tG])
                proj_tok(pZ[:], wgc, 0, 512, i, [pZ])
                silu_psum(gcb[:], pZ[:], 512, [pZ], [gcb], tH, tF)
                tt("dve", ob16[:, 0:512], tG[:], gcb[:], ALU.mult, [tG, gcb], [ob16])
                transposes_to_oT(4, 0, 0)
                h_update(i, 4, woC)

    def nsa_phase(s, doD, doM):
        l = 1
        ksT = S.carve("ksT", [128, SEQ], BF16)
        kwT = S.carve("kwT", [128, SEQ], BF16)
        Vs1 = S.carve("Vs1", [128, NT, 2, 65], BF16)
        Vw1 = S.carve("Vw1", [128, NT, 2, 65], BF16)
        kcT = S.carve("kcT", [128, 128], BF16)
        VC1 = S.carve("VC1", [128, 2, 97], BF16)
        cmpneg = S.carve("cmpneg", [128, NT, 128], BF16)
        selE = S.carve("selE", [128, SEQ], BF16)
        seladj = S.carve("seladj", [128, NT, 32], BF16)
        kmT = S.carve("kmT1", [128, 2, N_MEM], BF16)
        VM1 = S.carve("VM11", [128, 2, 4, 65], BF16)
        mark = S.arena_off
        if doM:
            wkv = S.carve("wkv1", [128, 8, 512], BF16)
            memT = S.carve("memT1", [128, 8, N_MEM], BF16)
            memf = S.carve("memf1", [128, 1024], F32)
            mem_kv(s, l, kmT, VM1, memf, memT, wkv)
            S.phase_reset(mark)
        if doD:
            tmpf = S.carve("tmpf", [128, 2048], F32)
            ROPEC = S.carve("ROPEC", [128, 96], F32)
            w512 = S.carve("w512", [128, 8, 512], BF16)
            wkc = S.carve("wkc", [128, 8, 256], BF16)
            kcdT = S.carve("kcdT", [128, SEQ], BF16)
            vcdT = S.carve("vcdT", [128, SEQ], BF16)
            W1r = S.carve("W1r", [128, 32, 128], BF16)
            w2f = S.carve("w2f", [128, 2, 64], F32)
            w2b = S.carve("w2b", [128, 2, 64], BF16)
            pef = S.carve("pef", [128, 32], F32)
            peT = S.carve("peT", [128, 32], BF16)
            cb = S.carve("cb", [128, 2], F32)
            hid = S.carve("hid", [128, 2, 128], BF16)
            S.dma(tmpf[0:N_CMP, :], D["cmpmask"].rearrange("j i b -> j (i b)"), w=[tmpf])
            ts("dve", cmpneg[0:N_CMP, :, :].rearrange("p i b -> p (i b)"), tmpf[0:N_CMP, :], -1.0, -NEG, ALU.add, ALU.mult,
               [tmpf], [cmpneg])
            S.dma(tmpf[0:32, :], D["selE"], w=[tmpf])
            cp("dve", selE[0:32, :], tmpf[0:32, :], [tmpf], [selE])
            S.dma(tmpf[:, 0:512].rearrange("p (i j) -> p i j", i=NT), D["seladj"].rearrange("(i p) j -> p i j", p=128), w=[tmpf])
            cp("dve", seladj[:].rearrange("p i j -> p (i j)"), tmpf[:, 0:512], [tmpf], [seladj])
            S.dma(ROPEC[0:N_CMP, :], D["rope_cmp"], w=[ROPEC])
            S.dma(tmpf[0:N_CMP, 0:32], D["ovl"], w=[tmpf])
            for k in range(2):
                cp("dve", VC1[0:N_CMP, k, 65:97], tmpf[0:N_CMP, 0:32], [tmpf], [VC1.b(("o", k))])
            memset("dve", VC1[:, :, 64:65], 1.0, [VC1.b("ones")])
            memset("dve", Vs1[:, :, :, 64:65], 1.0, [Vs1.b("ones")])
            memset("dve", Vw1[:, :, :, 64:65], 1.0, [Vw1.b("ones")])
            S.dma(w2f[:, 0, :], D["d_w2k"][0], w=[w2f.b(0)])
            S.dma(w2f[:, 1, :], D["d_w2v"][0], w=[w2f.b(1)])
            cp("dve", w2b[:], w2f[:], [w2f.b(0), w2f.b(1)], [w2b])
            for j_, c0_ in enumerate([2816, 3072, 2944, 3200]):
                S.dma(w512[:, :, j_ * 128:(j_ + 1) * 128], W_IN[l].rearrange("(c p) n -> p c n", p=128)[:, :, c0_:c0_ + 128],
                      w=[w512.b(j_)])
            load_slab(wkc, W_IN[l], 2560, 256)
            for i in range(NT):
                for c in range(8):
                    mm(pZ[:], xnT[:, c, i * 128:(i + 1) * 128], w512[:, c, 0:512], c == 0, c == 7,
                       [xnT.b(i)] + [w512.b(j_) for j_ in range(4)], [pZ])
                norm_rope(pZ[:, 0:256], 4, (10,), ROPE[:, i, :], krot[:, 0:256].rearrange("p (h d) -> p h d", h=4),
                          [pZ], [krot.b(0), krot.b(1)], 0)
                cp("act", Vs1[:, i, :, 0:64], pZ[:, 256:384].rearrange("p (h d) -> p h d", h=2), [pZ], [Vs1.b(i)])
                cp("act", Vw1[:, i, :, 0:64], pZ[:, 384:512].rearrange("p (h d) -> p h d", h=2), [pZ], [Vw1.b(i)])
                tr(pT[:, 0, :], krot[:, 0:128], 128, [krot.b(0)], [pT])
                tr(pT[:, 1, :], krot[:, 128:256], 128, [krot.b(1)], [pT])
                cp("act", ksT[:, i * 128:(i + 1) * 128], pT[:, 0, :], [pT], [ksT.b(i)])
                cp("act", kwT[:, i * 128:(i + 1) * 128], pT[:, 1, :], [pT], [kwT.b(i)])
            for g in range(4):
                proj_feat(pZ[:], wkc, 0, g, [pZ])
                cp("act", kcdT[:, g * 512:(g + 1) * 512], pZ[:], [pZ], [kcdT.b(g)])
                proj_feat(pZ[:], wkc, 128, g, [pZ])
                cp("act", vcdT[:, g * 512:(g + 1) * 512], pZ[:], [pZ], [vcdT.b(g)])
            for kind_i, kind in enumerate(["k", "v"]):
                srcT = kcdT if kind == "k" else vcdT
                w1v = W_1[kind].rearrange("(l d) m -> d l m", d=64)
                S.dma(W1r[0:64, :, :], w1v, w=[W1r.b(0)])
                S.dma(W1r[64:128, :, :], w1v, w=[W1r.b(1)])
                pesrc = D["d_pe_k" if kind == "k" else "d_pe_v"][0].rearrange("l d -> d l")
                S.dma(pef[0:64, :], pesrc, w=[pef], allow_slow_non_contiguous=True)
                cp("dve", peT[0:64, :], pef[0:64, :], [pef], [peT])
                for l_ in range(32):
                    mm(pY0[:, 0:1], W1r[0:64, l_, :], peT[0:64, l_:l_ + 1], l_ == 0, l_ == 31, [W1r.b(0), peT], [pY0])
                cp("dve", cb[:, 0:1], pY0[:, 0:1], [pY0], [cb])
                ts("dve", cb[:, 1:2], cb[:, 0:1], -1.0, None, ALU.mult, None, [cb], [cb])
                s3 = srcT[:].rearrange("p (j s) -> p j s", s=16)
                srcb = [srcT.b(g_) for g_ in range(4)]
                for k in range(2):
                    separator()
                    for l_ in range(32):
                        rhs = s3[64 * k:64 * k + 64, 0:N_CMP, l_] if l_ < 16 else s3[64 * k:64 * k + 64, 1:N_CMP + 1, l_ - 16]
                        mm(pSb[k][:, 0:N_CMP], W1r[64 * k:64 * k + 64, l_, :], rhs, l_ == 0, l_ == 31,
                           [W1r.b(k)] + srcb, [pSb[k]])
                    separator()
                    act(tB[:, 0:N_CMP], pSb[k][:, 0:N_CMP], AF.Exp, [pSb[k], cb], [tB], scale=-1.0, bias=cb[:, 1:2])
                    act(tC[:, 0:N_CMP], pSb[k][:, 0:N_CMP], AF.Identity, [pSb[k], cb], [tC], bias=cb[:, 0:1])
                    sigmoid_act(tB[:, 0:N_CMP], [tB], [tB])
                    tt("dve", hid[:, k, 0:N_CMP], tC[:, 0:N_CMP], tB[:, 0:N_CMP], ALU.mult, [tB, tC], [hid.b(k)])
                for k in range(2):
                    c0 = kind_i * 128 + k * 64
                    mm(pZ[0:N_CMP, c0:c0 + 64], hid[:, k, 0:N_CMP], w2b[:, kind_i, :], True, True, [hid.b(k), w2b], [pZ])
            norm_rope(pZ[0:N_CMP, 0:128], 2, GI["d_kn_cmp"], ROPEC[0:N_CMP, :],
                      krot[0:N_CMP, 0:128].rearrange("p (h d) -> p h d", h=2), [pZ], [krot.b(0)], 0, np_=N_CMP)
            cp("act", VC1[0:N_CMP, :, 0:64], pZ[0:N_CMP, 128:256].rearrange("p (h d) -> p h d", h=2), [pZ], [VC1.b("v")])
            tr(pT[:, 0, 0:N_CMP], krot[0:N_CMP, 0:128], N_CMP, [krot.b(0)], [pT])
            cp("act", kcT[:, 0:N_CMP], pT[:, 0, 0:N_CMP], [pT], [kcT])
            S.phase_reset(mark)
        wq = S.carve("wq1", [128, 8, 512], BF16)
        wgd = S.carve("wgd", [128, 8, 512], BF16)
        wqgm = S.carve("wqgm1", [128, 8, 512], BF16)
        wgt = S.carve("wgt", [128, 8, 24], BF16)
        woDM = S.carve("woDM", [128, 6, 1024], BF16)
        acc = S.carve("acc", [128, 512], F32)
        negT = S.carve("negT", [128, 2, 128], BF16)
        nb = S.carve("nb", [128, 2, 32], BF16)
        gts = S.carve("gts", [128, 24], F32)
        gat = S.carve("gat1", [128, 768], BF16)
        wv = W_IN[l].rearrange("(c p) n -> p c n", p=128)
        if doD:
            load_slab(wq, W_IN[l], 2048, 512)
            load_slab(wgd, W_IN[l], 3352, 512)
            S.dma(wgt[:], wv[:, :, 3328:3352], w=[wgt])
            S.dma(woDM[:, 0:4, :], W_OUT[l].rearrange("(c p) n -> p c n", p=128)[:, 4:8, :], w=[woDM.b("a")])
        if doM:
            load_slab(wqgm, W_IN[l], 3864, 512)
            S.dma(woDM[:, 4:6, :], W_OUT[l].rearrange("(c p) n -> p c n", p=128)[:, 8:10, :], w=[woDM.b("m")])
        gtsB = S.carve("gtsB", [128, 24], F32)
        gatB = S.carve("gat1B", [128, 768], BF16)
        gat2 = [gat, gatB]
        gts2 = [gts, gtsB]
        qzm2 = [S.carve("qzm1A", [128, 2, 2, 128], BF16), S.carve("qzm1B", [128, 2, 2, 128], BF16)]
        for q_ in qzm2:
            memset("dve", q_[:], 0.0, [q_.b(0), q_.b(1)])

        def evac_branch(k, br, first, gtc):
            gts3 = gtc[:].rearrange("p (h b) -> p h b", b=3)
            ts("dve", sm8[:, 2, 4 * k:4 * k + 4], pOb[k][:, :, 64], 1e-30, None, ALU.max, None, [pOb[k]], [sm8.b((2, k))])
            recip(sm8[:, 2, 4 * k:4 * k + 4], sm8[:, 2, 4 * k:4 * k + 4], [sm8.b((2, k))], [sm8.b((2, k))])
            tt("dve", sm8[:, 3, 4 * k:4 * k + 4], sm8[:, 2, 4 * k:4 * k + 4], gts3[:, 4 * k:4 * k + 4, br], ALU.mult,
               [sm8.b((2, k)), gtc], [sm8.b((3, k))])
            a3 = acc[:, 256 * k:256 * k + 256].rearrange("p (h d) -> p h d", h=4)
            if first:
                tt("dve", a3, pOb[k][:, :, 0:64], bc3(sm8[:, 3, 4 * k:4 * k + 4], 64), ALU.mult, [pOb[k], sm8.b((3, k))],
                   [acc.b(k)])
            else:
                t3 = tE[:, 256 * k:256 * k + 256].rearrange("p (h d) -> p h d", h=4)
                tt("dve", t3, pOb[k][:, :, 0:64], bc3(sm8[:, 3, 4 * k:4 * k + 4], 64), ALU.mult, [pOb[k], sm8.b((3, k))],
                   [tE])
                tt("dve", a3, a3, t3, ALU.add, [acc.b(k), tE], [acc.b(k)])

        def frontD(i):
            par = i % 2
            st = []
            st.append(lambda: proj_tok(pZ[:], wgd, 0, 512, i, [pZ]))
            st.extend(silu_stages(gat2[par][:, 0:512], pZ[:], 512, [pZ], [gat2[par].b("a")], tF))
            st.append(lambda: proj_tok(pZ[:, 0:24], wgt, 0, 24, i, [pZ]))
            st.append(lambda: act(gts2[par][:], pZ[:, 0:24], AF.Exp, [pZ], [gts2[par]], scale=-1.0))
            st.append(lambda: act(gts2[par][:], gts2[par][:], AF.Ln, [gts2[par]], [gts2[par]], bias=1.0))
            st.append(lambda: act(gts2[par][:], gts2[par][:], AF.Exp, [gts2[par]], [gts2[par]], scale=-1.0))
            st.append(lambda: proj_tok(pZ[:], wq, 0, 512, i, [pZ]))
            st.extend(norm_rope_stages(pZ[:], 8, GI["d_qn"], ROPE[:, i, :], qrot[:].rearrange("p (h d) -> p h d", h=8),
                                       [pZ], [qrot], 1))

            def t_():
                for pr in range(4):
                    tr(pT[:, pr, :], qrot[:, pr * 128:(pr + 1) * 128], 128, [qrot], [pT])
            def c_():
                t_()
                cp("act", qz2[par][0:64, 0, :, :], pT[0:64, 0:4, :], [pT], [qz2[par].b(0)])
                cp("act", qz2[par][64:128, 1, :, :], pT[64:128, 0:4, :], [pT], [qz2[par].b(1)])
            st.append(c_)
            return st

        def front1(i):
            st = frontD(i) if doD else []
            if doM:
                st += mem_front_stages(i, l, wqgm, gat2[i % 2], qzm2[i % 2])
            return st

        def cmp_pe(i):
            par = i % 2
            qzc, gtc = qz2[par], gts2[par]
            for k in range(2):
                ps = pSb[k]
                psv = ps[0:N_CMP, :].rearrange("p (g t) -> p g t", g=4)
                mm(psv, kcT[:, 0:N_CMP], qzc[:, k, :, :], True, False, [kcT, qzc.b(k)], [ps])
                mm(psv, ident[0:N_CMP, 0:N_CMP], bch(cmpneg[0:N_CMP, i, :], 4), False, True, [ident, cmpneg], [ps])
                act(Pb[0:N_CMP, k, :], ps[0:N_CMP, :], AF.Exp, [ps], [Pb.b(k)], scale=0.125)
            for k in range(2):
                memset("dve", pOb[k][:], 0.0, [pOb[k]])
                for g in range(4):
                    mm(pOb[k][:, g, 0:97], Pb[0:N_CMP, k, g * 128:(g + 1) * 128], VC1[0:N_CMP, k, :], False, False,
                       [Pb.b(k), VC1.b("ones"), VC1.b("v"), VC1.b(("o", k))], [pOb[k]], skip=True)

        def cmp_sel(i):
            par = i % 2
            qzc, gtc = qz2[par], gts2[par]
            for k in range(2):
                evac_branch(k, 0, True, gtc)
                t3 = tC[:, 0:128].rearrange("p (g j) -> p g j", g=4)
                tt("dve", t3, pOb[k][:, :, 65:97], bc3(sm8[:, 2, 4 * k:4 * k + 4], 32), ALU.mult,
                   [pOb[k], sm8.b((2, k))], [tC])
                op("dve", lambda e, k=k: e.tensor_reduce(out=tE[:, 32 * k:32 * k + 32],
                                                         in_=tC[:, 0:128].rearrange("p (g j) -> p j g", g=4),
                                                         axis=AX.X, op=ALU.add), r=[tC], w=[tE])
                tt("dve", tE[:, 32 * k:32 * k + 32], tE[:, 32 * k:32 * k + 32], seladj[:, i, :], ALU.add,
                   [tE, seladj], [tE])
                op("dve", lambda e, k=k: e.max(out=sm8[:, 6, 0:8], in_=tE[:, 32 * k:32 * k + 32]), r=[tE],
                   w=[sm8.b(6)])
                ts("dve", tD[:, 0:32], tE[:, 32 * k:32 * k + 32], sm8[:, 6, 3:4], None, ALU.is_ge, None,
                   [tE, sm8.b(6)], [tD])
                ts("dve", nb[:, k, :], tD[:, 0:32], -1.0, -NEG, ALU.add, ALU.mult, [tD], [nb.b(k)])
                tr(pT[0:32, 4 + k, :], nb[:, k, :], 128, [nb.b(k)], [pT])
            cp("act", negT[0:32, :, :], pT[0:32, 4:6, :], [pT], [negT])

        for st_ in front1(0):
            st_()
        if doD:
            cmp_pe(0)
            cmp_sel(0)
        for i in range(NT):
            par = i % 2
            qzc, gtc, gac = qz2[par], gts2[par], gat2[par]
            fill = front1(i + 1) if i + 1 < NT else []
            if doD:
                def masks_sel(kt, k, i=i):
                    ex = [(selE[0:32, kt * 128:(kt + 1) * 128], bch(negT[0:32, k, :], 4), [selE, negT])]
                    if kt == i:
                        ex.append((ident[:], bch(mdiag[:], 4), [ident, mdiag]))
                    return ex

                def masks_win(kt, i=i):
                    if kt == i:
                        return [(ident[:], bch(mdiag[:], 4), [ident, mdiag])]
                    if kt == i - 4:
                        return [(ident[:], bch(mprev[:], 4), [ident, mprev])]
                    return []

                kts_s = list(range(0, i + 1))
                kts_w = list(range(max(0, i - 4), i + 1))
                brs = []
                for k in range(2):
                    brs.append(dict(k=k, kT=ksT, V=Vs1, kts=kts_s, masks=(lambda kt, k=k: masks_sel(kt, k)),
                                    evac=(lambda k_, gtc=gtc: evac_branch(k_, 1, False, gtc))))
                for k in range(2):
                    brs.append(dict(k=k, kT=kwT, V=Vw1, kts=kts_w, masks=masks_win,
                                    evac=(lambda k_, gtc=gtc: evac_branch(k_, 2, False, gtc))))
                attn_pipeline(brs, qzc, fill)
            while fill:
                fill.pop(0)()
            if doM:
                mem_back(i, l, kmT, VM1, gac, qzm2[par])
            if doD:
                tt("dve", ob16[:, 0:512], acc[:], gac[:, 0:512], ALU.mult, [acc.b(0), acc.b(1), gac.b("a")], [ob16])
                if i + 1 < NT:
                    cmp_pe(i + 1)
                transposes_to_oT(4, 0, 0)
            chunks = ([0, 1, 2, 3] if doD else []) + ([4, 5] if doM else [])
            for half in range(2):
                for n_, c in enumerate(chunks):
                    mm(pYb[half][:], oT[:, c, :], woDM[:, c, half * 512:(half + 1) * 512], n_ == 0, n_ == len(chunks) - 1,
                       [oT, woDM.b("a"), woDM.b("m")], [pYb[half]])
            if doD and i + 1 < NT:
                cmp_sel(i + 1)
            for half in range(2):
                tt("dve", H[:, i, half * 512:(half + 1) * 512], H[:, i, half * 512:(half + 1) * 512], pYb[half][:],
                   ALU.add, [H.b(i), pYb[half]], [H.b(i)])

    def layer1(s):
        rmsnorm_to_xnT()
        if "C" in mix1:
            hgrn2_phase(s)
            S.phase_reset()
        doD, doM = "D" in mix1, "M" in mix1
        if doD or doM:
            nsa_phase(s, doD, doM)
            S.phase_reset()

    for s in range(nseq):
        for i in range(NT):
            S.dma(H[:, i, :], D["x"][s, i * 128:(i + 1) * 128, :], w=[H.b(i)])
        if 0 in layers:
            layer0(s)
        if 1 in layers:
            layer1(s)
        for i in range(NT):
            S.dma(Y[s, i * 128:(i + 1) * 128, :], H[:, i, :], r=[H.b(i)])
    S.emit()
    return nc, S


N_CORES = 8
_PROG = {}


def kernel(**inputs):
    x = np.ascontiguousarray(inputs["x"], dtype=np.float32)
    mem = np.ascontiguousarray(inputs["mem"], dtype=np.float32)
    B = x.shape[0]
    per = B // N_CORES
    if per not in _PROG:
        _PROG[per] = build_program(per)[0]
    nc = _PROG[per]
    consts = host_consts()
    params = {k: np.ascontiguousarray(inputs[k], dtype=np.float32) for k in PARAM_SHAPES}
    in_maps = []
    for c in range(N_CORES):
        m = {"x": x[c * per:(c + 1) * per], "mem": mem[c * per:(c + 1) * per]}
        m.update(params)
        m.update(consts)
        in_maps.append(m)
    res = run_bass_kernel_spmd(nc, in_maps, core_ids=list(range(N_CORES)))
    return np.concatenate([r["y"] for r in res.results], axis=0)
```

```python
from contextlib import ExitStack
import numpy as np
import concourse.bass as bass
import concourse.mybir as mybir
from concourse.bass_utils import run_bass_kernel_spmd

F32 = mybir.dt.float32
BF16 = mybir.dt.bfloat16
AF = mybir.ActivationFunctionType
ALU = mybir.AluOpType
AX = mybir.AxisListType

ENGINES = ["pe", "act", "dve", "pool", "sp"]
EPOCH = 30000
NDMASEM = 8
import os
NO_POOL = os.environ.get("K_NO_POOL", "1") == "1"
DBG = int(os.environ.get("K_DBG", "99"))
FILL_MODE = int(os.environ.get("K_FILL", "1"))

D_MODEL = 1024
SEQ = 2048
NT = SEQ // 128
N_MEM = 256
EVEN_IN = 2816
ODD_IN = 4376
EPS = 1e-6
NEG = -1024.0
N_CMP = 127


class Buf:
    __slots__ = ("name", "lw", "rd", "excl")

    def __init__(self, name, excl=False):
        self.name = name
        self.lw = None
        self.rd = {}
        self.excl = excl


class Op:
    __slots__ = ("eng", "fn", "deps", "is_dma", "needs_inc", "token", "waits", "dsem", "idx")

    def __init__(self, eng, fn, is_dma=False):
        self.eng = eng
        self.fn = fn
        self.deps = []
        self.is_dma = is_dma
        self.needs_inc = is_dma
        self.token = None
        self.waits = []
        self.dsem = None
        self.idx = None


class T:
    def __init__(self, h, name, excl=False):
        self.h = h
        self.name = name
        self.excl = excl
        self.whole = Buf(name, excl)
        self.subs = {}

    def __getitem__(self, idx):
        return self.h[idx]

    def b(self, key=None):
        if key is None:
            return self.whole
        s = self.subs.get(key)
        if s is None:
            s = Buf(f"{self.name}[{key}]", self.excl)
            self.subs[key] = s
        return s


class Sched:
    def __init__(self, nc):
        self.nc = nc
        self.es = ExitStack()
        self.ops = {e: [] for e in ENGINES}
        self.all_dma = []
        self.dma_since_bar = []
        self.pending = {e: [] for e in ENGINES}
        self.arena = None
        self.arena_words = 0
        self.arena_off = 0

    def sb(self, name, shape, dt):
        h = self.es.enter_context(self.nc.sbuf_tensor("sb_" + name, list(shape), dt))
        return T(h, name)

    def ps(self, name, shape, dt):
        h = self.es.enter_context(self.nc.psum_tensor("ps_" + name, list(shape), dt))
        return T(h, name, excl=True)

    def make_arena(self, words):
        self.arena = self.es.enter_context(self.nc.sbuf_tensor("arena", [128, words], F32))
        self.arena_words = words
        self.arena_off = 0

    def carve(self, name, shape, dt):
        n = 1
        for s in shape[1:]:
            n *= s
        words = (n + 1) // 2 if dt == BF16 else n
        words = (words + 7) // 8 * 8
        assert self.arena_off + words <= self.arena_words, (name, self.arena_off, words, self.arena_words)
        ap = self.arena[:, self.arena_off:self.arena_off + words]
        if dt == BF16:
            ap = ap.bitcast(BF16)[:, 0:n]
        else:
            ap = ap[:, 0:n]
        self.arena_off += words
        if len(shape) == 3:
            ap = ap.rearrange("p (a b) -> p a b", a=shape[1])
        elif len(shape) == 4:
            ap = ap.rearrange("p (a b c) -> p a b c", a=shape[1], b=shape[2])
        return T(ap, name)

    def phase_reset(self, to=0):
        self.barrier()
        self.arena_off = to

    def _bufs(self, xs):
        out = []
        for x in xs or []:
            out.append(x.whole if isinstance(x, T) else x)
        return out

    def op(self, eng, fn, r=None, w=None, is_dma=False):
        if eng == "pool" and NO_POOL and not is_dma:
            eng = "dve"
        o = Op(eng, fn, is_dma)
        skey = ("dma", len(self.all_dma)) if is_dma else eng
        deps = []
        rb = self._bufs(r)
        wb = self._bufs(w)
        ex = [b for b in rb if b.excl]
        if ex:
            rb = [b for b in rb if not b.excl]
            wb = wb + [b for b in ex if b not in wb]
        for b in rb:
            if b.lw is not None:
                deps.append(b.lw)
        for b in wb:
            if b.lw is not None:
                deps.append(b.lw)
            deps.extend(b.rd.values())
        if self.pending[eng]:
            deps.extend(self.pending[eng])
            self.pending[eng] = []
        for b in rb:
            b.rd[skey] = o
        for b in wb:
            b.lw = o
            b.rd = {}
        seen = set()
        for d in deps:
            if id(d) in seen or d is o:
                continue
            seen.add(id(d))
            if (not d.is_dma) and (not is_dma) and d.eng == eng and eng == "pe":
                continue
            o.deps.append(d)
        o.idx = len(self.ops[eng])
        self.ops[eng].append(o)
        if is_dma:
            self.all_dma.append(o)
            self.dma_since_bar.append(o)
        return o

    def dma(self, out, in_, r=None, w=None, q="sp", **kw):
        return self.op(q, lambda e: e.dma_start(out=out, in_=in_, **kw), r=r, w=w, is_dma=True)

    def barrier(self):
        lasts = []
        for e in ENGINES:
            for o in reversed(self.ops[e]):
                if not o.is_dma:
                    lasts.append(o)
                    break
        lasts.extend(self.dma_since_bar)
        self.dma_since_bar = []
        for e in ENGINES:
            self.pending[e] = list(self.pending[e]) + lasts

    def emit(self):
        nc = self.nc
        for e in ENGINES:
            for o in self.ops[e]:
                for d in o.deps:
                    d.needs_inc = True
        nsem_eng = {}
        for e in ENGINES:
            c = 0
            k = 0
            for o in self.ops[e]:
                if o.is_dma:
                    o.dsem = (e, k % NDMASEM)
                    k += 1
                elif o.needs_inc:
                    c += 1
                    o.token = (("e", e, (c - 1) // EPOCH), (c - 1) % EPOCH + 1)
            nsem_eng[e] = (c + EPOCH - 1) // EPOCH if c else 0
        dcount = {}
        prev_dma = {}
        for e in ENGINES:
            for o in self.ops[e]:
                if o.is_dma:
                    key = ("d",) + o.dsem
                    v = dcount.get(key, 0) + 16
                    dcount[key] = v
                    o.token = (key, v)
                    if key in prev_dma:
                        o.deps.append(prev_dma[key])
                    prev_dma[key] = o
        sems = {}
        for e in ENGINES:
            for ep in range(nsem_eng[e]):
                sems[("e", e, ep)] = self.es.enter_context(nc.semaphore(f"s_{e}_{ep}"))
        for key in dcount:
            sems[key] = self.es.enter_context(nc.semaphore(f"d_{key[1]}_{key[2]}"))
        for e in ENGINES:
            seen = {}
            for o in self.ops[e]:
                need = {}
                for d in o.deps:
                    k, v = d.token
                    if seen.get(k, 0) >= v:
                        continue
                    if need.get(k, 0) < v:
                        need[k] = v
                for k, v in need.items():
                    seen[k] = v
                o.waits = list(need.items())
        final_waits = list(dcount.items())
        self.nsems = len(sems)
        self.ninst = {e: len(self.ops[e]) for e in ENGINES}
        engmap = {"pe": "tensor", "act": "scalar", "dve": "vector", "pool": "gpsimd", "sp": "sync"}
        with nc.Block() as block:
            for e in ENGINES:
                ops = self.ops[e]

                def body(eng, ops=ops, e=e):
                    for o in ops:
                        for k, v in o.waits:
                            eng.wait_ge(sems[k], v)
                        ins = o.fn(eng)
                        if o.is_dma:
                            ins.then_inc(sems[o.token[0]], 16)
                        elif o.needs_inc:
                            ins.then_inc(sems[o.token[0]], 1)
                    if e == "sp":
                        for k, v in final_waits:
                            eng.wait_ge(sems[k], v)

                getattr(block, engmap[e])(body)
        self.es.close()


def host_consts():
    c = {}
    c["ident"] = np.eye(128, dtype=np.float32)
    half = 32
    inv = 10000.0 ** (-np.arange(half, dtype=np.float32) / half)
    pos = np.arange(SEQ, dtype=np.float32)
    ang = pos[:, None] * inv[None, :]
    c["rope_cs"] = np.concatenate([np.cos(ang), np.sin(ang), -np.sin(ang)], axis=1).astype(np.float32)
    cend = (np.arange(N_CMP) * 16 + 31).astype(np.float32)
    angc = cend[:, None] * inv[None, :]
    c["rope_cmp"] = np.concatenate([np.cos(angc), np.sin(angc), -np.sin(angc)], axis=1).astype(np.float32)
    a = np.arange(128)[:, None]
    b = np.arange(128)[None, :]
    c["mdiag"] = np.where(a <= b, 0.0, NEG).astype(np.float32)
    c["mprev"] = np.where(a > b, 0.0, NEG).astype(np.float32)
    j = np.arange(N_CMP)[:, None, None]
    i = np.arange(NT)[None, :, None]
    bb = np.arange(128)[None, None, :]
    c["cmpmask"] = ((16 * j + 31) <= (128 * i + bb)).astype(np.float32)
    s = np.arange(SEQ)[None, :]
    js = np.arange(32)[:, None]
    c["selE"] = ((s // 64) == js).astype(np.float32)
    n = np.arange(N_CMP)[:, None]
    jj = np.arange(32)[None, :]
    c["ovl"] = ((16 * n < 64 * jj + 64) & (16 * n + 32 > 64 * jj)).astype(np.float32)
    t = np.arange(SEQ)[:, None]
    cur = t // 64
    forced = (jj == 0) | (jj == cur)
    valid = jj <= cur
    c["seladj"] = np.where(forced, 1e4, np.where(valid, 0.0, -1e4)).astype(np.float32)
    tri = (np.arange(64)[:, None] <= np.arange(64)[None, :]).astype(np.float32)
    c["tri64"] = np.concatenate([tri, tri], axis=0)
    return c


CONST_SHAPES = {"ident": [128, 128], "rope_cs": [SEQ, 96], "rope_cmp": [N_CMP, 96], "mdiag": [128, 128],
                "mprev": [128, 128], "cmpmask": [N_CMP, NT, 128], "selE": [32, SEQ], "ovl": [N_CMP, 32],
                "seladj": [SEQ, 32], "tri64": [128, 64]}

PARAM_SHAPES = {
    "norm_g": [2, 1024], "mem_norm_g": [2, 1024], "mem_w_kv": [2, 1024, 512], "mem_qn": [2, 64], "mem_kn": [2, 64],
    "ev_w_in": [1, 1024, 2816], "ev_w_out": [1, 1280, 1024], "a_qn": [1, 64], "a_kn": [1, 64], "a_sinks": [1, 8],
    "b_conv_w": [1, 4, 512], "b_conv_b": [1, 512], "b_w_r": [1, 8, 64, 64], "b_b_r": [1, 512],
    "b_w_i": [1, 8, 64, 64], "b_b_i": [1, 512], "b_lambda": [1, 512], "od_w_in": [1, 1024, 4376],
    "od_w_out": [1, 1280, 1024], "c_lb": [2, 512], "c_onorm": [1, 128], "d_qn": [1, 64], "d_kn_cmp": [1, 64],
    "d_kn_slc": [1, 64], "d_kn_win": [1, 64], "d_pe_k": [1, 32, 64], "d_pe_v": [1, 32, 64],
    "d_w1k": [1, 2048, 128], "d_w2k": [1, 128, 64], "d_w1v": [1, 2048, 128], "d_w2v": [1, 128, 64],
}


def build_program(nseq, layers=(0, 1), mix0=("A", "B", "M"), mix1=("C", "D", "M")):
    nc = bass.Bass("TRN2", target_bir_lowering=False)
    D = {}
    D["x"] = nc.dram_tensor("x", [nseq, SEQ, D_MODEL], F32, kind="ExternalInput").ap()
    D["mem"] = nc.dram_tensor("mem", [nseq, N_MEM, D_MODEL], F32, kind="ExternalInput").ap()
    for k, shp in PARAM_SHAPES.items():
        D[k] = nc.dram_tensor(k, shp, F32, kind="ExternalInput").ap()
    for k, shp in CONST_SHAPES.items():
        D[k] = nc.dram_tensor(k, shp, F32, kind="ExternalInput").ap()
    Y = nc.dram_tensor("y", [nseq, SEQ, D_MODEL], F32, kind="ExternalOutput").ap()
    W_IN = [nc.dram_tensor("w_in0s", [1024, EVEN_IN], BF16, kind="Internal").ap(),
            nc.dram_tensor("w_in1s", [1024, ODD_IN], BF16, kind="Internal").ap()]
    W_OUT = [nc.dram_tensor("w_out0s", [1280, 1024], BF16, kind="Internal").ap(),
             nc.dram_tensor("w_out1s", [1280, 1024], BF16, kind="Internal").ap()]
    W_KV = [nc.dram_tensor("w_kv0s", [1024, 512], BF16, kind="Internal").ap(),
            nc.dram_tensor("w_kv1s", [1024, 512], BF16, kind="Internal").ap()]
    W_1 = {"k": nc.dram_tensor("w1ks", [2048, 128], BF16, kind="Internal").ap(),
           "v": nc.dram_tensor("w1vs", [2048, 128], BF16, kind="Internal").ap()}

    S = Sched(nc)
    op = S.op

    H = S.sb("H", [128, NT, 1024], F32)
    xnT = S.sb("xnT", [128, 8, SEQ], BF16)
    ident = S.sb("ident", [128, 128], BF16)
    ROPE = S.sb("ROPE", [128, NT, 96], F32)
    mdiag = S.sb("mdiag", [128, 128], BF16)
    mprev = S.sb("mprev", [128, 128], BF16)
    normg = S.sb("normg", [128, 2, 8], F32)
    memg = S.sb("memg", [128, 2, 8], F32)
    GN = S.sb("GN", [128, 14, 64], F32)
    GI = {"mem_qn0": 0, "mem_qn1": 1, "mem_kn0": 2, "mem_kn1": 3, "a_qn": 4, "a_kn": 5, "d_qn": 6, "d_kn_slc": 7,
          "d_kn_win": 8, "d_kn_cmp": 9}
    COG = S.sb("COG", [128, 128], F32)
    LB = S.sb("LB", [128, 2, 4], F32)
    ones512 = S.sb("ones512", [128, 512], F32)
    esink = S.sb("esink", [128, 8], F32)
    ss16 = S.sb("ss16", [128, NT], F32)
    rstd16 = S.sb("rstd16", [128, NT], F32)
    tA = S.sb("tA", [128, 512], F32)
    tB = S.sb("tB", [128, 512], F32)
    tC = S.sb("tC", [128, 512], F32)
    tD = S.sb("tD", [128, 512], F32)
    tE = S.sb("tE", [128, 512], F32)
    tF = S.sb("tF", [128, 512], F32)
    xnb = S.sb("xnb", [128, 1024], BF16)
    sm8 = S.sb("sm8", [128, 8, 8], F32)
    Pb = S.sb("Pb", [128, 2, 512], BF16)
    ob16 = S.sb("ob16", [128, 1280], BF16)
    oT = S.sb("oT", [128, 10, 128], BF16)
    qrot = S.sb("qrot", [128, 512], BF16)
    qz = S.sb("qz", [128, 2, 4, 128], BF16)
    qzB = S.sb("qzB", [128, 2, 4, 128], BF16)
    qz2 = [qz, qzB]
    krot = S.sb("krot", [128, 256], BF16)
    pZ = S.ps("pZ", [128, 512], F32)
    pS0 = S.ps("pS0", [128, 512], F32)
    pS1 = S.ps("pS1", [128, 512], F32)
    pSb = [pS0, pS1]
    pO0 = S.ps("pO0", [128, 4, 128], F32)
    pO1 = S.ps("pO1", [128, 4, 128], F32)
    pOb = [pO0, pO1]
    pT = S.ps("pT", [128, 8, 128], BF16)
    pY0 = S.ps("pY0", [128, 512], F32)
    pY1 = S.ps("pY1", [128, 512], F32)
    pYb = [pY0, pY1]

    ARENA_WORDS = 18 * 1024
    S.make_arena(ARENA_WORDS)

    def mm(out, lhsT, rhs, start, stop, r, w, skip=False):
        if skip:
            return op("pe", lambda e: e.matmul(out=out, lhsT=lhsT, rhs=rhs, start=start, stop=stop,
                                               skip_group_check=True), r=r, w=w)
        return op("pe", lambda e: e.matmul(out=out, lhsT=lhsT, rhs=rhs, start=start, stop=stop), r=r, w=w)

    def tr(out, in_, npart, r, w):
        return op("pe", lambda e: e.transpose(out=out, in_=in_, identity=ident[0:npart, 0:npart]), r=list(r) + [ident], w=w)

    def act(out, in_, func, r, w, scale=1.0, bias=0.0, accum=None):
        if accum is None:
            return op("act", lambda e: e.activation(out=out, in_=in_, func=func, scale=scale, bias=bias), r=r, w=w)
        return op("act", lambda e: e.activation(out=out, in_=in_, func=func, scale=scale, bias=bias, accum_out=accum), r=r, w=w)

    def tt(eng, out, in0, in1, o, r, w):
        return op(eng, lambda e: e.tensor_tensor(out=out, in0=in0, in1=in1, op=o), r=r, w=w)

    def ts(eng, out, in0, s1, s2, o0, o1, r, w):
        if s2 is None:
            return op(eng, lambda e: e.tensor_scalar(out=out, in0=in0, scalar1=s1, scalar2=None, op0=o0), r=r, w=w)
        return op(eng, lambda e: e.tensor_scalar(out=out, in0=in0, scalar1=s1, scalar2=s2, op0=o0, op1=o1), r=r, w=w)

    def stt(eng, out, in0, sc, in1, o0, o1, r, w):
        return op(eng, lambda e: e.scalar_tensor_tensor(out=out, in0=in0, scalar=sc, in1=in1, op0=o0, op1=o1), r=r, w=w)

    def cp(eng, out, in_, r, w):
        if eng == "act":
            return act(out, in_, AF.Copy, r, w)
        return op(eng, lambda e: e.tensor_copy(out=out, in_=in_), r=r, w=w)

    def recip(out, in_, r, w):
        return op("dve", lambda e: e.reciprocal(out=out, in_=in_), r=r, w=w)

    def memset(eng, ap, val, w):
        return op(eng, lambda e: e.memset(ap, val), w=w)

    def rstd_from_ss(ap, n_mean, r, w):
        act(ap, ap, AF.Ln, r, w, scale=1.0 / n_mean, bias=EPS)
        act(ap, ap, AF.Exp, w, w, scale=-0.5)

    def bc3(ap2, n):
        return ap2.unsqueeze(2).broadcast_to([ap2.shape[0], ap2.shape[1], n])

    def bch(ap2, nh):
        return ap2.unsqueeze(1).broadcast_to([ap2.shape[0], nh, ap2.shape[1]])

    stage = S.carve("stage0", [128, 2048], F32)
    stage1 = S.carve("stage1", [128, 2048], F32)
    stb0 = S.carve("stb0", [128, 2048], BF16)
    stb1 = S.carve("stb1", [128, 2048], BF16)
    stages = [(stage, stb0), (stage1, stb1)]
    for j_ in range(2, 5):
        stages.append((S.carve("stage%d" % j_, [128, 2048], F32), S.carve("stb%d" % j_, [128, 2048], BF16)))

    S.dma(stage[:, 0:128], D["ident"], w=[stage])
    cp("dve", ident[:], stage[:, 0:128], [stage], [ident])
    S.dma(stage[:, 0:128], D["mdiag"], w=[stage])
    cp("dve", mdiag[:], stage[:, 0:128], [stage], [mdiag])
    S.dma(stage[:, 0:128], D["mprev"], w=[stage])
    cp("dve", mprev[:], stage[:, 0:128], [stage], [mprev])
    memset("dve", qz[:], 0.0, [qz.b(0), qz.b(1)])
    memset("dve", qzB[:], 0.0, [qzB.b(0), qzB.b(1)])
    S.dma(ROPE[:], D["rope_cs"].rearrange("(i p) f -> p i f", p=128), w=[ROPE])
    S.dma(normg[:], D["norm_g"].rearrange("l (c p) -> p l c", p=128), w=[normg], allow_slow_non_contiguous=True)
    S.dma(memg[:], D["mem_norm_g"].rearrange("l (c p) -> p l c", p=128), w=[memg], allow_slow_non_contiguous=True)
    for nm, gi in GI.items():
        if nm.startswith("mem_"):
            src = D[nm[:-1]][int(nm[-1])]
        else:
            src = D[nm][0]
        S.dma(GN[:, gi, :], src.partition_broadcast(128), w=[GN.b(gi)])
    for j_, nm_ in enumerate(["d_kn_slc", "d_kn_slc", "d_kn_win", "d_kn_win"]):
        S.dma(GN[:, 10 + j_, :], D[nm_][0].partition_broadcast(128), w=[GN.b(10)])
    S.dma(esink[:], D["a_sinks"][0].partition_broadcast(128), w=[esink])
    act(esink[:], esink[:], AF.Exp, [esink], [esink])
    S.dma(COG[:], D["c_onorm"][0].partition_broadcast(128), w=[COG])
    memset("dve", ones512[:], 1.0, [ones512])
    S.dma(LB[:, 0, :], D["c_lb"][0].rearrange("(h p) -> p h", p=128), w=[LB.b(0)], allow_slow_non_contiguous=True)
    S.dma(LB[:, 1, :], D["c_lb"][1].rearrange("(h p) -> p h", p=128), w=[LB.b(1)], allow_slow_non_contiguous=True)
    tt("dve", LB[:, 0, :], LB[:, 0, :], LB[:, 1, :], ALU.subtract, [LB.b(0), LB.b(1)], [LB.b(0)])
    act(LB[:, 0, :], LB[:, 0, :], AF.Exp, [LB.b(0)], [LB.b(0)])
    ts("dve", LB[:, 0, :], LB[:, 0, :], 1.0, None, ALU.add, None, [LB.b(0)], [LB.b(0)])
    recip(LB[:, 0, :], LB[:, 0, :], [LB.b(0)], [LB.b(0)])
    ts("dve", LB[:, 1, :], LB[:, 0, :], -1.0, 1.0, ALU.mult, ALU.add, [LB.b(0)], [LB.b(1)])

    cnt = [0]

    def conv_weight(src, dst, R, C, gt=None, l=0, perm0=None):
        for rc in range(R // 128):
            for c0 in range(0, C, 2048):
                cw = min(2048, C - c0)
                sf, sbf = stages[cnt[0] % len(stages)]
                eng = "dve"
                cnt[0] += 1
                S.dma(sf[:, 0:cw], src[rc * 128:(rc + 1) * 128, c0:c0 + cw], w=[sf])
                if gt is not None:
                    ts(eng, sbf[:, 0:cw], sf[:, 0:cw], gt[:, l, rc:rc + 1], None, ALU.mult, None, [sf, gt], [sbf])
                else:
                    cp(eng, sbf[:, 0:cw], sf[:, 0:cw], [sf], [sbf])
                rows = slice(rc * 128, (rc + 1) * 128)
                if perm0 is not None and c0 <= perm0 < c0 + cw:
                    p0 = perm0 - c0
                    if p0 > 0:
                        S.dma(dst[rows, c0:c0 + p0], sbf[:, 0:p0], r=[sbf])
                    for w_ in range(2):
                        S.dma(dst[rows, perm0:perm0 + 512].rearrange("r (pr w d) -> r w pr d", pr=4, w=2)[:, w_],
                              sbf[:, p0 + w_ * 256:p0 + (w_ + 1) * 256].rearrange("p (pr d) -> p pr d", pr=4), r=[sbf])
                    if p0 + 512 < cw:
                        S.dma(dst[rows, perm0 + 512:c0 + cw], sbf[:, p0 + 512:cw], r=[sbf])
                else:
                    S.dma(dst[rows, c0:c0 + cw], sbf[:, 0:cw], r=[sbf])

    if 0 in layers:
        conv_weight(D["ev_w_in"][0], W_IN[0], 1024, EVEN_IN, normg, 0, perm0=0)
        conv_weight(D["ev_w_out"][0], W_OUT[0], 1280, 1024)
        conv_weight(D["mem_w_kv"][0], W_KV[0], 1024, 512, memg, 0)
    if 1 in layers:
        conv_weight(D["od_w_in"][0], W_IN[1], 1024, ODD_IN, normg, 1, perm0=2048)
        conv_weight(D["od_w_out"][0], W_OUT[1], 1280, 1024)
        conv_weight(D["mem_w_kv"][1], W_KV[1], 1024, 512, memg, 1)
        conv_weight(D["d_w1k"][0], W_1["k"], 2048, 128)
        conv_weight(D["d_w1v"][0], W_1["v"], 2048, 128)
    S.phase_reset()

    def load_slab(dst, src_w, c0, ncols, key=None):
        S.dma(dst[:, :, 0:ncols], src_w.rearrange("(c p) n -> p c n", p=128)[:, :, c0:c0 + ncols],
              w=[dst.b(key)])

    def proj_tok(ps_ap, slab, col0, ncols, i, w, skey=None):
        for c in range(8):
            mm(ps_ap, xnT[:, c, i * 128:(i + 1) * 128], slab[:, c, col0:col0 + ncols], c == 0, c == 7,
               [xnT.b(i), slab.b(skey)], w)

    def proj_feat(ps_ap, slab, col0, g, w, skey=None):
        for c in range(8):
            mm(ps_ap, slab[:, c, col0:col0 + 128], xnT[:, c, g * 512:(g + 1) * 512], c == 0, c == 7,
               [xnT.b(4 * g), xnT.b(4 * g + 1), xnT.b(4 * g + 2), xnT.b(4 * g + 3), slab.b(skey)], w)

    def silu_psum(out_ap, zp, n, rz, wout, t1, t2, np_=128):
        a1 = t1[0:np_, 0:n]
        act(a1, zp, AF.Exp, rz, [t1], scale=-1.0)
        act(a1, a1, AF.Ln, [t1], [t1], bias=1.0)
        act(a1, a1, AF.Exp, [t1], [t1], scale=-1.0)
        tt("dve", out_ap, zp, a1, ALU.mult, list(rz) + [t1], wout)

    def silu_stages(out_ap, zp, n, rz, wout, t1, np_=128):
        a1 = t1[0:np_, 0:n]
        return [lambda: act(a1, zp, AF.Exp, rz, [t1], scale=-1.0),
                lambda: act(a1, a1, AF.Ln, [t1], [t1], bias=1.0),
                lambda: act(a1, a1, AF.Exp, [t1], [t1], scale=-1.0),
                lambda: tt("dve", out_ap, zp, a1, ALU.mult, list(rz) + [t1], wout)]

    def sigmoid_act(ap, r, w):
        act(ap, ap, AF.Ln, r, w, bias=1.0)
        act(ap, ap, AF.Exp, w, w, scale=-1.0)

    def norm_rope_stages(zp, nh, gi, rope_ap, out_ap, rz, wout, slot, np_=128):
        n = nh * 64
        z3 = zp.rearrange("p (h d) -> p h d", h=nh)
        ssq = sm8[0:np_, slot, 0:nh]
        zg = tB[0:np_, 0:n].rearrange("p (h d) -> p h d", h=nh)

        def st0():
            act(tA[0:np_, 0:n], zp, AF.Square, rz, [tA])
            if isinstance(gi, tuple):
                tt("dve", zg, z3, GN[0:np_, gi[0]:gi[0] + nh, :], ALU.mult, list(rz) + [GN.b(gi[0])], [tB])
            else:
                tt("dve", zg, z3, bch(GN[0:np_, gi, :], nh), ALU.mult, list(rz) + [GN.b(gi)], [tB])

        def st1():
            op("dve", lambda e: e.tensor_reduce(out=ssq, in_=tA[0:np_, 0:n].rearrange("p (h d) -> p h d", h=nh),
                                                axis=AX.X, op=ALU.add), r=[tA], w=[sm8.b(slot)])

        def st1b():
            act(ssq, ssq, AF.Ln, [sm8.b(slot)], [sm8.b(slot)], scale=1.0 / 64.0, bias=EPS)

        def st1c():
            act(ssq, ssq, AF.Exp, [sm8.b(slot)], [sm8.b(slot)], scale=-0.5)

        if rope_ap is None:
            def st2():
                tt("dve", out_ap, zg, bc3(ssq, 64), ALU.mult, [tB, sm8.b(slot)], wout)
            return [st0, st1, st1b, st1c, st2]
        zg4 = tB[0:np_, 0:n].rearrange("p (h a f) -> p h a f", h=nh, a=2)
        a4 = tC[0:np_, 0:n].rearrange("p (h a f) -> p h a f", h=nh, a=2)
        b4 = tD[0:np_, 0:n].rearrange("p (h a f) -> p h a f", h=nh, a=2)
        cos4 = rope_ap[:, 0:32].unsqueeze(1).unsqueeze(1).broadcast_to([np_, nh, 2, 32])
        sin3 = bch(rope_ap[:, 32:64], nh)
        nsin3 = bch(rope_ap[:, 64:96], nh)

        def st2():
            tt("dve", a4, zg4, cos4, ALU.mult, [tB], [tC])
            tt("dve", b4[:, :, 0, :], zg4[:, :, 1, :], nsin3, ALU.mult, [tB], [tD])
            tt("dve", b4[:, :, 1, :], zg4[:, :, 0, :], sin3, ALU.mult, [tB], [tD])

        def st3():
            tt("dve", tC[0:np_, 0:n], tC[0:np_, 0:n], tD[0:np_, 0:n], ALU.add, [tC, tD], [tC])
            tt("dve", out_ap, tC[0:np_, 0:n].rearrange("p (h d) -> p h d", h=nh), bc3(ssq, 64), ALU.mult,
               [tC, sm8.b(slot)], wout)
        return [st0, st1, st1b, st1c, st2, st3]

    def norm_rope(zp, nh, gi, rope_ap, out_ap, rz, wout, slot, np_=128):
        for st in norm_rope_stages(zp, nh, gi, rope_ap, out_ap, rz, wout, slot, np_):
            st()

    def h_update(i, nchunks, wo, wkey=None):
        for half in range(2):
            for c in range(nchunks):
                mm(pYb[half][:], oT[:, c, :], wo[:, c, half * 512:(half + 1) * 512], c == 0, c == nchunks - 1,
                   [oT, wo.b(wkey)], [pYb[half]])
            tt("dve", H[:, i, half * 512:(half + 1) * 512], H[:, i, half * 512:(half + 1) * 512], pYb[half][:], ALU.add,
               [H.b(i), pYb[half]], [H.b(i)])

    def transposes_to_oT(nch, src_cols0=0, dst0=0):
        for c in range(nch):
            tr(pT[:, c, :], ob16[:, src_cols0 + c * 128: src_cols0 + (c + 1) * 128], 128, [ob16], [pT])
        cp("act", oT[:, dst0:dst0 + nch, :], pT[:, 0:nch, :], [pT], [oT])

    def attn_pipeline(branches, qsrc, fillers=None):
        steps = []
        for bi, br in enumerate(branches):
            for n_, kt in enumerate(br["kts"]):
                steps.append((bi, n_, kt))

        def emit_qk(idx):
            bi, n_, kt = steps[idx]
            br = branches[bi]
            k = br["k"]
            ps = pSb[idx % 2]
            psv = ps[:].rearrange("p (g t) -> p g t", g=4)
            extra = br["masks"](kt)
            mm(psv, br["kT"][:, kt * 128:(kt + 1) * 128], qsrc[:, k, :, :], True, len(extra) == 0,
               [br["kT"].b(kt), qsrc.b(k)], [ps])
            for j_, (lt, rh, rd_) in enumerate(extra):
                mm(psv, lt, rh, False, j_ == len(extra) - 1, rd_, [ps])

        first_use = {}
        for bi, br in enumerate(branches):
            first_use.setdefault(br["k"], bi)
        for k_, bi in first_use.items():
            memset("dve", pOb[k_][:], 0.0, [pOb[k_]])
        emit_qk(0)
        for idx, (bi, n_, kt) in enumerate(steps):
            br = branches[bi]
            k = br["k"]
            if idx + 1 < len(steps):
                emit_qk(idx + 1)
            act(Pb[:, idx % 2, :], pSb[idx % 2][:], AF.Exp, [pSb[idx % 2]], [Pb.b(idx % 2)], scale=0.125)
            V_ = br["V"]
            for g in range(4):
                mm(pOb[k][:, g, 0:65], Pb[:, idx % 2, g * 128:(g + 1) * 128], V_[:, kt, k, :], False, False,
                   [Pb.b(idx % 2), V_.b(kt), V_.b("ones")], [pOb[k]], skip=True)
            if n_ == len(br["kts"]) - 1:
                br["evac"](k)
                if any(b2["k"] == k for b2 in branches[bi + 1:]):
                    memset("dve", pOb[k][:], 0.0, [pOb[k]])
            if fillers and FILL_MODE == 1:
                fillers.pop(0)()
        while fillers:
            fillers.pop(0)()

    def rmsnorm_to_xnT():
        memset("pool", ss16[:], 0.0, [ss16])
        for i in range(NT):
            act(xnb[:], H[:, i, :], AF.Square, [H.b(i)], [xnb, ss16], accum=ss16[:, i:i + 1])
        cp("dve", rstd16[:], ss16[:], [ss16], [rstd16])
        rstd_from_ss(rstd16[:], 1024.0, [rstd16], [rstd16])
        for i in range(NT):
            ts("dve", xnb[:], H[:, i, :], rstd16[:, i:i + 1], None, ALU.mult, None, [H.b(i), rstd16], [xnb])
            for c in range(8):
                tr(pT[:, c, :], xnb[:, c * 128:(c + 1) * 128], 128, [xnb], [pT])
            cp("act", xnT[:, :, i * 128:(i + 1) * 128], pT[:], [pT], [xnT.b(i)])

    def mem_kv(s, l, kmT, VM1, memf, memT, wkv):
        load_slab(wkv, W_KV[l], 0, 512)
        memset("dve", VM1[:, :, :, 64:65], 1.0, [VM1.b("ones")])
        for nt in range(2):
            S.dma(memf[:], D["mem"][s, nt * 128:(nt + 1) * 128, :], w=[memf])
            memset("dve", sm8[:, 7, 0:1], 0.0, [sm8.b(7)])
            act(xnb[:], memf[:], AF.Square, [memf], [xnb, sm8.b(7)], accum=sm8[:, 7, 0:1])
            rstd_from_ss(sm8[:, 7, 0:1], 1024.0, [sm8.b(7)], [sm8.b(7)])
            ts("dve", xnb[:], memf[:], sm8[:, 7, 0:1], None, ALU.mult, None, [memf, sm8.b(7)], [xnb])
            for c in range(8):
                tr(pT[:, c, :], xnb[:, c * 128:(c + 1) * 128], 128, [xnb], [pT])
            cp("act", memT[:, :, nt * 128:(nt + 1) * 128], pT[:], [pT], [memT.b(nt)])
            for c in range(8):
                mm(pZ[:], memT[:, c, nt * 128:(nt + 1) * 128], wkv[:, c, 0:512], c == 0, c == 7, [memT.b(nt), wkv], [pZ])
            norm_rope(pZ[:, 0:256], 4, GI["mem_kn%d" % l], None, krot[:].rearrange("p (h d) -> p h d", h=4),
                      [pZ], [krot], 6)
            cp("act", VM1[:, nt, :, 0:64], pZ[:, 256:512].rearrange("p (h d) -> p h d", h=4), [pZ], [VM1.b(nt)])
            for pr in range(2):
                tr(pT[:, pr, :], krot[:, pr * 128:(pr + 1) * 128], 128, [krot], [pT])
            cp("act", kmT[:, :, nt * 128:(nt + 1) * 128], pT[:, 0:2, :], [pT], [kmT.b(nt)])

    def mem_attn_tile(i, qz_ap, qz_r, l, kmT, VM1, gate_ap, gate_r, out_cols0, qzt=None):
        qzt = qzt if qzt is not None else qz
        norm_rope(qz_ap, 4, GI["mem_qn%d" % l], None, qrot[:, 0:256].rearrange("p (h d) -> p h d", h=4),
                  qz_r, [qrot], 5)
        for pr in range(2):
            tr(pT[:, pr, :], qrot[:, pr * 128:(pr + 1) * 128], 128, [qrot], [pT])
        cp("act", qzt[0:64, 0, 0:2, :], pT[0:64, 0:2, :], [pT], [qzt.b(0)])
        cp("act", qzt[64:128, 1, 0:2, :], pT[64:128, 0:2, :], [pT], [qzt.b(1)])
        if DBG < 2:
            return
        for h in range(4):
            pr, hf = h // 2, h % 2
            for nt in range(2):
                mm(pSb[nt][:, h * 128:(h + 1) * 128], kmT[:, pr, nt * 128:(nt + 1) * 128],
                   qzt[:, hf, pr, :], True, True, [kmT.b(nt), qzt.b(hf)], [pSb[nt]])
        for nt in range(2):
            act(Pb[:, nt, :], pSb[nt][:], AF.Exp, [pSb[nt]], [Pb.b(nt)], scale=0.125)
        if DBG < 3:
            return
        for h in range(4):
            for nt in range(2):
                mm(pO0[:, h, 0:65], Pb[:, nt, h * 128:(h + 1) * 128], VM1[:, nt, h, :], nt == 0, nt == 1,
                   [Pb.b(nt), VM1.b(nt), VM1.b("ones")], [pO0])
        cp("dve", sm8[:, 4, 0:4], pO0[:, :, 64], [pO0], [sm8.b(4)])
        recip(sm8[:, 4, 0:4], sm8[:, 4, 0:4], [sm8.b(4)], [sm8.b(4)])
        tt("dve", tE[:, 0:256].rearrange("p (h d) -> p h d", h=4), pO0[:, :, 0:64], bc3(sm8[:, 4, 0:4], 64), ALU.mult,
           [pO0, sm8.b(4)], [tE])
        tt("dve", ob16[:, out_cols0:out_cols0 + 256], tE[:, 0:256], gate_ap, ALU.mult, [tE] + list(gate_r), [ob16])

    def mem_front_stages(i, l, wslab, gatT, qzmT):
        st = [lambda: proj_tok(pZ[:], wslab, 0, 512, i, [pZ])]
        st += silu_stages(gatT[:, 512:768], pZ[:, 256:512], 256, [pZ], [gatT.b("m")], tF)
        st += norm_rope_stages(pZ[:, 0:256], 4, GI["mem_qn%d" % l], None, qrot[:, 0:256].rearrange("p (h d) -> p h d", h=4),
                               [pZ], [qrot], 5)

        def t_():
            for pr in range(2):
                tr(pT[:, pr, :], qrot[:, pr * 128:(pr + 1) * 128], 128, [qrot], [pT])
        def c_():
            t_()
            cp("act", qzmT[0:64, 0, :, :], pT[0:64, 0:2, :], [pT], [qzmT.b(0)])
            cp("act", qzmT[64:128, 1, :, :], pT[64:128, 0:2, :], [pT], [qzmT.b(1)])
        st.append(c_)
        return st

    def mem_back(i, l, kmT, VM1, gatT, qzmT, out_cols0=512):
        for h in range(4):
            pr, hf = h // 2, h % 2
            for nt in range(2):
                mm(pSb[nt][:, h * 128:(h + 1) * 128], kmT[:, pr, nt * 128:(nt + 1) * 128],
                   qzmT[:, hf, pr, :], True, True, [kmT.b(nt), qzmT.b(hf)], [pSb[nt]])
        for nt in range(2):
            act(Pb[:, nt, :], pSb[nt][:], AF.Exp, [pSb[nt]], [Pb.b(nt)], scale=0.125)
        for h in range(4):
            for nt in range(2):
                mm(pO0[:, h, 0:65], Pb[:, nt, h * 128:(h + 1) * 128], VM1[:, nt, h, :], nt == 0, nt == 1,
                   [Pb.b(nt), VM1.b(nt), VM1.b("ones")], [pO0])
        cp("dve", sm8[:, 4, 0:4], pO0[:, :, 64], [pO0], [sm8.b(4)])
        recip(sm8[:, 4, 0:4], sm8[:, 4, 0:4], [sm8.b(4)], [sm8.b(4)])
        tt("dve", tE[:, 0:256].rearrange("p (h d) -> p h d", h=4), pO0[:, :, 0:64], bc3(sm8[:, 4, 0:4], 64), ALU.mult,
           [pO0, sm8.b(4)], [tE])
        tt("dve", ob16[:, out_cols0:out_cols0 + 256], tE[:, 0:256], gatT[:, 512:768], ALU.mult, [tE, gatT.b("m")], [ob16])
        for c in range(2):
            tr(pT[:, 4 + c, :], ob16[:, out_cols0 + c * 128:out_cols0 + (c + 1) * 128], 128, [ob16], [pT])
        cp("act", oT[:, 4:6, :], pT[:, 4:6, :], [pT], [oT])

    def layer0(s):
        l = 0
        rmsnorm_to_xnT()
        if "B" in mix0:
            PB = S.carve("PB", [128, 4, 8], F32)
            PD = S.carve("PD", [128, 4, 4], F32)
            BDf = S.carve("BDf", [128, 2, 4, 128], F32)
            BD = S.carve("BD", [128, 2, 4, 128], BF16)
            XB = S.carve("XB", [128, 3 + SEQ], F32)
            hB = S.carve("hB", [128, SEQ], F32)
            mixB = S.carve("mixB", [128, 4, SEQ], BF16)
            wsl = [S.carve("wslB0", [128, 8, 256], BF16), S.carve("wslB1", [128, 8, 256], BF16)]
            woB = S.carve("woB", [128, 4, 1024], BF16)
            xcb = S.carve("xcb", [128, 512], BF16)
            for j in range(4):
                S.dma(PB[:, :, j], D["b_conv_w"][0, j].rearrange("(c p) -> p c", p=128), w=[PB.b(j)],
                      allow_slow_non_contiguous=True)
            for j, nm in enumerate(["b_conv_b", "b_b_r", "b_b_i", "b_lambda"]):
                S.dma(PB[:, :, 4 + j], D[nm][0].rearrange("(c p) -> p c", p=128), w=[PB.b(4 + j)],
                      allow_slow_non_contiguous=True)
            ts("dve", PD[:, :, 0], PB[:, :, 5], -1.0, None, ALU.mult, None, [PB.b(5)], [PD.b(0)])
            ts("dve", PD[:, :, 1], PB[:, :, 6], -1.0, None, ALU.mult, None, [PB.b(6)], [PD.b(1)])
            act(PD[:, :, 2], PB[:, :, 7], AF.Exp, [PB.b(7)], [PD.b(2)], scale=-1.0)
            act(PD[:, :, 2], PD[:, :, 2], AF.Ln, [PD.b(2)], [PD.b(2)], bias=1.0)
            ts("dve", PD[:, :, 2], PD[:, :, 2], -8.0, None, ALU.mult, None, [PD.b(2)], [PD.b(2)])
            bdkeys = [BDf.b((a_, b_, c_)) for a_ in range(2) for b_ in range(4) for c_ in range(2)]
            memset("pool", BDf[:], 0.0, bdkeys)
            for gi_, nm in enumerate(["b_w_r", "b_w_i"]):
                for cb in range(4):
                    for hb in range(2):
                        S.dma(BDf[64 * hb:64 * hb + 64, gi_, cb, 64 * hb:64 * hb + 64], D[nm][0, 2 * cb + hb],
                              w=[BDf.b((gi_, cb, hb))])
            cp("dve", BD[:], BDf[:], bdkeys, [BD])
            memset("pool", XB[:, 0:3], 0.0, [XB.b("pad")])
            S.dma(woB[:], W_OUT[l].rearrange("(c p) n -> p c n", p=128)[:, 4:8, :], w=[woB])
            for cb in range(4):
                wS = wsl[cb % 2]
                S.dma(wS[:, :, 0:128], W_IN[l].rearrange("(c p) n -> p c n", p=128)[:, :, 1280 + cb * 128:1280 + (cb + 1) * 128],
                      w=[wS.b("x")])
                S.dma(wS[:, :, 128:256], W_IN[l].rearrange("(c p) n -> p c n", p=128)[:, :, 1792 + cb * 128:1792 + (cb + 1) * 128],
                      w=[wS.b("g")])
                for g in range(4):
                    sl = slice(3 + g * 512, 3 + (g + 1) * 512)
                    proj_feat(pZ[:], wS, 0, g, [pZ], skey="x")
                    cp("act", XB[:, sl], pZ[:], [pZ], [XB.b(g)])
                    rd = [XB.b(g), XB.b(g - 1) if g > 0 else XB.b("pad"), PB.b(0), PB.b(1), PB.b(2), PB.b(3), PB.b(4)]
                    xc = tB
                    ts("dve", xc[:], XB[:, g * 512:g * 512 + 512], PB[:, cb, 0:1], PB[:, cb, 4:5], ALU.mult, ALU.add, rd, [tB])
                    for j in (1, 2, 3):
                        stt("dve", xc[:], XB[:, g * 512 + j:g * 512 + j + 512], PB[:, cb, j:j + 1], xc[:], ALU.mult, ALU.add,
                            rd + [tB], [tB])
                    cp("pool", xcb[:], xc[:], [tB], [xcb])
                    mm(pS0[:], BD[:, 0, cb, :], xcb[:], True, True, [BD, xcb], [pS0])
                    mm(pS1[:], BD[:, 1, cb, :], xcb[:], True, True, [BD, xcb], [pS1])
                    act(tC[:], pS0[:], AF.Exp, [pS0, PD.b(0)], [tC], scale=-1.0, bias=PD[:, cb, 0:1])
                    sigmoid_act(tC[:], [tC], [tC])
                    act(tC[:], tC[:], AF.Exp, [tC, PD.b(2)], [tC], scale=PD[:, cb, 2:3])
                    act(tD[:], pS1[:], AF.Exp, [pS1, PD.b(1)], [tD], scale=-1.0, bias=PD[:, cb, 1:2])
                    sigmoid_act(tD[:], [tD], [tD])
                    act(tE[:], tC[:], AF.Square, [tC], [tE])
                    act(tE[:], tE[:], AF.Ln, [tE], [tE], scale=-1.0, bias=1.0)
                    act(tE[:], tE[:], AF.Exp, [tE], [tE], scale=0.5)
                    tt("pool", tD[:], tD[:], xc[:], ALU.mult, [tD, tB], [tD])
                    tt("pool", tD[:], tD[:], tE[:], ALU.mult, [tD, tE], [tD])
                    init = 0.0 if g == 0 else hB[:, g * 512 - 1:g * 512]
                    op("dve", lambda e, g=g, init=init: e.tensor_tensor_scan(
                        out=hB[:, g * 512:(g + 1) * 512], data0=tC[:], data1=tD[:], initial=init,
                        op0=ALU.mult, op1=ALU.add), r=[tC, tD] + ([hB.b(g - 1)] if g > 0 else []), w=[hB.b(g)])
                    proj_feat(pZ[:], wS, 128, g, [pZ], skey="g")
                    silu_psum(tF[:], pZ[:], 512, [pZ], [tF], tE, tF)
                    tt("dve", mixB[:, cb, g * 512:(g + 1) * 512], tF[:], hB[:, g * 512:(g + 1) * 512], ALU.mult,
                       [tF, hB.b(g)], [mixB.b((cb, g))])
            for i in range(NT):
                for half in range(2):
                    for cb in range(4):
                        mm(pYb[half][:], mixB[:, cb, i * 128:(i + 1) * 128], woB[:, cb, half * 512:(half + 1) * 512],
                           cb == 0, cb == 3, [mixB.b((cb, i // 4)), woB], [pYb[half]])
                    tt("dve", H[:, i, half * 512:(half + 1) * 512], H[:, i, half * 512:(half + 1) * 512], pYb[half][:],
                       ALU.add, [H.b(i), pYb[half]], [H.b(i)])
            S.phase_reset()
        doA, doM = "A" in mix0, "M" in mix0
        if doA or doM:
            kT = S.carve("kT", [128, SEQ], BF16)
            V1 = S.carve("V1", [128, NT, 2, 65], BF16)
            wq = S.carve("wq", [128, 8, 512], BF16)
            wga = S.carve("wga", [128, 8, 512], BF16)
            wqgm = S.carve("wqgm", [128, 8, 512], BF16)
            woAM = S.carve("woAM", [128, 6, 1024], BF16)
            kmT = S.carve("kmT", [128, 2, N_MEM], BF16)
            VM1 = S.carve("VM1", [128, 2, 4, 65], BF16)
            memT = S.carve("memT", [128, 8, N_MEM], BF16)
            memf = S.carve("memf", [128, 1024], F32)
            gat = S.carve("gat", [128, 768], BF16)
            if doM:
                mem_kv(s, l, kmT, VM1, memf, memT, wq)
            if doA:
                load_slab(wga, W_IN[l], 512, 256)
                memset("dve", V1[:, :, :, 64:65], 1.0, [V1.b("ones")])
                for i in range(NT):
                    proj_tok(pZ[:, 0:256], wga, 0, 256, i, [pZ])
                    norm_rope(pZ[:, 0:128], 2, GI["a_kn"], ROPE[:, i, :], krot[:, 0:128].rearrange("p (h d) -> p h d", h=2),
                              [pZ], [krot], 0)
                    cp("act", V1[:, i, :, 0:64], pZ[:, 128:256].rearrange("p (h d) -> p h d", h=2), [pZ], [V1.b(i)])
                    tr(pT[:, 0, :], krot[:, 0:128], 128, [krot], [pT])
                    cp("act", kT[:, i * 128:(i + 1) * 128], pT[:, 0, :], [pT], [kT.b(i)])
            if doA:
                load_slab(wq, W_IN[l], 0, 512)
                load_slab(wga, W_IN[l], 768, 512)
                S.dma(woAM[:, 0:4, :], W_OUT[l].rearrange("(c p) n -> p c n", p=128)[:, 0:4, :], w=[woAM.b("a")])
            if doM:
                load_slab(wqgm, W_IN[l], 2304, 512)
                S.dma(woAM[:, 4:6, :], W_OUT[l].rearrange("(c p) n -> p c n", p=128)[:, 8:10, :], w=[woAM.b("m")])
            gatB = S.carve("gatB", [128, 768], BF16)
            gat2 = [gat, gatB]
            qzm2 = [S.carve("qzmA", [128, 2, 2, 128], BF16), S.carve("qzmB", [128, 2, 2, 128], BF16)]
            for q_ in qzm2:
                memset("dve", q_[:], 0.0, [q_.b(0), q_.b(1)])

            def frontA(i):
                par = i % 2
                st = [lambda: proj_tok(pZ[:], wga, 0, 512, i, [pZ]),
                      ] + silu_stages(gat2[par][:, 0:512], pZ[:], 512, [pZ], [gat2[par].b("a")], tF) + [
                      lambda: proj_tok(pZ[:], wq, 0, 512, i, [pZ])]
                st += norm_rope_stages(pZ[:], 8, GI["a_qn"], ROPE[:, i, :], qrot[:].rearrange("p (h d) -> p h d", h=8),
                                       [pZ], [qrot], 1)

                def t_():
                    for pr in range(4):
                        tr(pT[:, pr, :], qrot[:, pr * 128:(pr + 1) * 128], 128, [qrot], [pT])
                def c_():
                    t_()
                    cp("act", qz2[par][0:64, 0, :, :], pT[0:64, 0:4, :], [pT], [qz2[par].b(0)])
                    cp("act", qz2[par][64:128, 1, :, :], pT[64:128, 0:4, :], [pT], [qz2[par].b(1)])
                st.append(c_)
                return st

            def front(i):
                st = []
                if doA:
                    st += frontA(i)
                if doM:
                    st += mem_front_stages(i, l, wqgm, gat2[i % 2], qzm2[i % 2])
                return st

            for st_ in front(0):
                st_()
            for i in range(NT):
                par = i % 2
                fill = front(i + 1) if i + 1 < NT else []
                if doA:
                    kts = [i - 1, i] if i > 0 else [i]

                    def masksA(kt, i=i):
                        msk = mdiag if kt == i else mprev
                        return [(ident[:], bch(msk[:], 4), [ident, msk])]

                    def evacA(k):
                        tt("dve", sm8[:, 2, 4 * k:4 * k + 4], pOb[k][:, :, 64], esink[:, 4 * k:4 * k + 4], ALU.add,
                           [pOb[k], esink], [sm8.b((2, k))])
                        recip(sm8[:, 2, 4 * k:4 * k + 4], sm8[:, 2, 4 * k:4 * k + 4], [sm8.b((2, k))], [sm8.b((2, k))])
                        tt("dve", tE[:, 256 * k:256 * k + 256].rearrange("p (h d) -> p h d", h=4), pOb[k][:, :, 0:64],
                           bc3(sm8[:, 2, 4 * k:4 * k + 4], 64), ALU.mult, [pOb[k], sm8.b((2, k))], [tE])

                    nfa = len(fill) // 2 if doM else len(fill)
                    fa = [fill.pop(0) for _ in range(nfa)]
                    attn_pipeline([dict(k=k, kT=kT, V=V1, kts=kts, masks=masksA, evac=evacA) for k in range(2)],
                                  qz2[par], fa)
                    tt("dve", ob16[:, 0:512], tE[:], gat2[par][:, 0:512], ALU.mult, [tE, gat2[par].b("a")], [ob16])
                    transposes_to_oT(4, 0, 0)
                if doM:
                    mem_back(i, l, kmT, VM1, gat2[par], qzm2[par])
                while fill:
                    fill.pop(0)()
                chunks = ([0, 1, 2, 3] if doA else []) + ([4, 5] if doM else [])
                for half in range(2):
                    for n_, c in enumerate(chunks):
                        mm(pYb[half][:], oT[:, c, :], woAM[:, c, half * 512:(half + 1) * 512], n_ == 0, n_ == len(chunks) - 1,
                           [oT, woAM.b("a"), woAM.b("m")], [pYb[half]])
                    tt("dve", H[:, i, half * 512:(half + 1) * 512], H[:, i, half * 512:(half + 1) * 512], pYb[half][:],
                       ALU.add, [H.b(i), pYb[half]], [H.b(i)])
            S.phase_reset()

    def separator():
        mm(pY1[:, 0:1], ident[:], ident[:, 0:1], True, True, [ident], [pY1])

    def hgrn2_phase(s):
        l = 1
        QT = S.carve("QT", [128, 4, 512], BF16)
        KT = S.carve("KT", [128, 4, 512], BF16)
        KH = S.carve("KH", [128, 4, 512], BF16)
        Vc = S.carve("Vc", [128, 4, 512], BF16)
        wsm = [S.carve("wsmC0", [128, 8, 256], BF16), S.carve("wsmC1", [128, 8, 256], BF16)]
        wic = S.carve("wic", [128, 8, 512], BF16)
        wgc = S.carve("wgc", [128, 8, 512], BF16)
        woC = S.carve("woC", [128, 4, 1024], BF16)
        St = S.carve("St", [128, 4, 128], F32)
        Sb = S.carve("Sb", [128, 4, 128], BF16)
        EBL = S.carve("EBL", [128, 4, 32], F32)
        ATb = S.carve("ATb", [128, 4, 128], BF16)
        KHz = S.carve("KHz", [128, 4, 2, 128], BF16)
        tri = S.carve("tri", [128, 64], F32)
        Bp = S.carve("Bp", [128, 8], F32)
        tG = S.carve("tG", [128, 512], F32)
        tH = S.carve("tH", [128, 512], F32)
        gcb = S.carve("gcb", [128, 512], BF16)
        S.dma(tri[:], D["tri64"], w=[tri])
        memset("dve", St[:], 0.0, [St])
        memset("dve", Sb[:], 0.0, [Sb])
        memset("dve", ATb[:], 0.0, [ATb.b(0), ATb.b(1)])
        memset("dve", KHz[:], 0.0, [KHz.b(0), KHz.b(1)])
        memset("dve", Bp[:], 0.0, [Bp])
        load_slab(wic, W_IN[l], 1024, 512)
        load_slab(wgc, W_IN[l], 1536, 512)
        S.dma(woC[:], W_OUT[l].rearrange("(c p) n -> p c n", p=128)[:, 0:4, :], w=[woC])
        pS0v = pS0[:].rearrange("p (h t) -> p h t", h=4)
        pS1v = pS1[:].rearrange("p (h t) -> p h t", h=4)
        cfill = []

        def cpop():
            if cfill:
                cfill.pop(0)()

        for g in range(4):
            for hd in range(4):
                wS = wsm[(g * 4 + hd) % 2]
                wv = W_IN[l].rearrange("(c p) n -> p c n", p=128)
                S.dma(wS[:, :, 0:128], wv[:, :, hd * 128:(hd + 1) * 128], w=[wS.b("q")])
                S.dma(wS[:, :, 128:256], wv[:, :, 512 + hd * 128:512 + (hd + 1) * 128], w=[wS.b("f")])
                proj_feat(pZ[:], wS, 128, g, [pZ], skey="f")
                act(tB[:], pZ[:], AF.Exp, [pZ], [tB], scale=-1.0)
                sigmoid_act(tB[:], [tB], [tB])
                ts("dve", tB[:], tB[:], LB[:, 1, hd:hd + 1], LB[:, 0, hd:hd + 1], ALU.mult, ALU.add,
                   [tB, LB.b(0), LB.b(1)], [tB])
                act(tC[:], tB[:], AF.Ln, [tB], [tC])
                ts("pool", tB[:], tB[:], -1.0, 1.0, ALU.mult, ALU.add, [tB], [tB])
                op("dve", lambda e: e.tensor_tensor_scan(out=tD[:], data0=ones512[:], data1=tC[:], initial=0.0,
                                                         op0=ALU.mult, op1=ALU.add), r=[ones512, tC], w=[tD])
                tD3 = tD[:].rearrange("p (c j) -> p c j", j=64)
                cp("dve", Bp[:, 1:8], tD3[:, 0:7, 63], [tD], [Bp])
                tt("dve", tD3, tD3, bc3(Bp[:, 0:8], 64), ALU.subtract, [tD, Bp], [tD])
                act(tE[:], tD[:], AF.Exp, [tD], [tE])
                cp("dve", EBL[:, hd, 8 * g:8 * g + 8], tE[:].rearrange("p (c j) -> p c j", j=64)[:, :, 63], [tE],
                   [EBL.b((hd, g))])
                act(tF[:], tD[:], AF.Exp, [tD], [tF], scale=-1.0)
                tt("dve", KT[:, hd, :], tB[:], tF[:], ALU.mult, [tB, tF], [KT.b(hd)])
                tt("dve", tG[:].rearrange("p (c j) -> p c j", j=64), tD3, bc3(tD3[:, :, 63], 64), ALU.subtract,
                   [tD], [tG])
                act(tG[:], tG[:], AF.Exp, [tG], [tG], scale=-1.0)
                tt("dve", KH[:, hd, :], tB[:], tG[:], ALU.mult, [tB, tG], [KH.b(hd)])
                proj_feat(pZ[:], wS, 0, g, [pZ], skey="q")
                silu_psum(tH[:], pZ[:], 512, [pZ], [tH], tF, tH)
                tt("dve", QT[:, hd, :], tH[:], tE[:], ALU.mult, [tH, tE], [QT.b(hd)])
            for tl in range(4):
                i = 4 * g + tl
                proj_tok(pZ[:], wic, 0, 512, i, [pZ])
                cp("act", Vc[:, tl, :], pZ[:], [pZ], [Vc.b(tl)])
            for tl in range(4):
                i = 4 * g + tl
                cs = tl * 128
                allh = [QT.b(h_) for h_ in range(4)]
                for hd in range(4):
                    tr(pT[:, hd, :], KH[:, hd, cs:cs + 128], 128, [KH.b(hd)], [pT])
                cp("act", KHz[0:64, :, 0, :], pT[0:64, 0:4, :], [pT], [KHz.b(0)])
                cp("act", KHz[64:128, :, 1, :], pT[64:128, 0:4, :], [pT], [KHz.b(1)])
                cpop()
                for hd in range(4):
                    for c in range(2):
                        mm(pS0v[64 * c:64 * c + 64, hd, 64 * c:64 * c + 64], KT[:, hd, cs + 64 * c:cs + 64 * c + 64],
                           QT[:, hd, cs + 64 * c:cs + 64 * c + 64], True, True, [KT.b(hd), QT.b(hd)], [pS0])
                for c in range(2):
                    tt("dve", ATb[64 * c:64 * c + 64, :, 64 * c:64 * c + 64], pS0v[64 * c:64 * c + 64, :, 64 * c:64 * c + 64],
                       bch(tri[64 * c:64 * c + 64, :], 4), ALU.mult, [pS0, tri], [ATb.b(c)])
                cpop()
                for hd in range(4):
                    mm(pO0[:, hd, :], ATb[:, hd, :], Vc[:, tl, hd * 128:(hd + 1) * 128], True, True,
                       [ATb.b(0), ATb.b(1), Vc.b(tl)], [pO0])
                for c in range(2):
                    ch = 8 * g + 2 * tl + c
                    for hd in range(4):
                        mm(pO1[64 * c:64 * c + 64, hd, :], QT[:, hd, cs + 64 * c:cs + 64 * c + 64], Sb[:, hd, :], True, True,
                           [QT.b(hd), Sb], [pO1])
                    cpop()
                    for hd in range(4):
                        mm(pS1v[:, hd, :], KHz[:, hd, c, :], Vc[:, tl, hd * 128:(hd + 1) * 128], True, True,
                           [KHz.b(c), Vc.b(tl)], [pS1])
                    tt("dve", St[:], St[:], bc3(EBL[:, :, ch], 128), ALU.mult, [St] + [EBL.b((h_, g)) for h_ in range(4)], [St])
                    cpop()
                    tt("dve", St[:], St[:], pS1v, ALU.add, [St, pS1], [St])
                    cpop()
                    cp("act", Sb[:], St[:], [St], [Sb])
                    cpop()
                while cfill:
                    cfill.pop(0)()
                cp("act", tG[:], pO0[:].rearrange("p h v -> p (h v)"), [pO0], [tG])
                tt("dve", tG[:], tG[:], pO1[:].rearrange("p h v -> p (h v)"), ALU.add, [tG, pO1], [tG])
                tG3 = tG[:].rearrange("p (h v) -> p h v", h=4)

                def o_red():
                    op("dve", lambda e: e.tensor_reduce(out=sm8[:, 3, 0:4], in_=tA[:, 0:512].rearrange("p (h v) -> p h v", h=4),
                                                        axis=AX.X, op=ALU.add), r=[tA], w=[sm8.b(3)])

                def o_scale():
                    tt("dve", tG3, tG3, bc3(sm8[:, 3, 0:4], 128), ALU.mult, [tG, sm8.b(3)], [tG])
                    tt("dve", tG3, tG3, bch(COG[:], 4), ALU.mult, [tG, COG], [tG])

                cfill.extend([
                    lambda: act(tA[:, 0:512], tG[:], AF.Square, [tG], [tA]),
                    o_red,
                    lambda: act(sm8[:, 3, 0:4], sm8[:, 3, 0:4], AF.Ln, [sm8.b(3)], [sm8.b(3)], scale=1.0 / 128.0, bias=EPS),
                    lambda: act(sm8[:, 3, 0:4], sm8[:, 3, 0:4], AF.Exp, [sm8.b(3)], [sm8.b(3)], scale=-0.5),
                    o_scale,
                    lambda i=i: proj_tok(pZ[:], wgc, 0, 512, i, [pZ])])
                cfill.extend(silu_stages(gcb[:], pZ[:], 512, [pZ], [gcb], tH))
                cfill.extend([
                    lambda: tt("dve", ob16[:, 0:512], tG[:], gcb[:], ALU.mult, [tG, gcb], [ob16]),
                    lambda: transposes_to_oT(4, 0, 0),
                    lambda i=i: h_update(i, 4, woC)])
            while cfill:
                cfill.pop(0)()

    def nsa_phase(s, doD, doM):
        l = 1
        ksT = S.carve("ksT", [128, SEQ], BF16)
        kwT = S.carve("kwT", [128, SEQ], BF16)
        Vs1 = S.carve("Vs1", [128, NT, 2, 65], BF16)
        Vw1 = S.carve("Vw1", [128, NT, 2, 65], BF16)
        kcT = S.carve("kcT", [128, 128], BF16)
        VC1 = S.carve("VC1", [128, 2, 97], BF16)
        cmpneg = S.carve("cmpneg", [128, NT, 128], BF16)
        selE = S.carve("selE", [128, SEQ], BF16)
        seladj = S.carve("seladj", [128, NT, 32], BF16)
        kmT = S.carve("kmT1", [128, 2, N_MEM], BF16)
        VM1 = S.carve("VM11", [128, 2, 4, 65], BF16)
        mark = S.arena_off
        if doM:
            wkv = S.carve("wkv1", [128, 8, 512], BF16)
            memT = S.carve("memT1", [128, 8, N_MEM], BF16)
            memf = S.carve("memf1", [128, 1024], F32)
            mem_kv(s, l, kmT, VM1, memf, memT, wkv)
            S.phase_reset(mark)
        if doD:
            tmpf = S.carve("tmpf", [128, 2048], F32)
            ROPEC = S.carve("ROPEC", [128, 96], F32)
            w512 = S.carve("w512", [128, 8, 512], BF16)
            wkc = S.carve("wkc", [128, 8, 256], BF16)
            kcdT = S.carve("kcdT", [128, SEQ], BF16)
            vcdT = S.carve("vcdT", [128, SEQ], BF16)
            W1r = S.carve("W1r", [128, 32, 128], BF16)
            w2f = S.carve("w2f", [128, 2, 64], F32)
            w2b = S.carve("w2b", [128, 2, 64], BF16)
            pef = S.carve("pef", [128, 32], F32)
            peT = S.carve("peT", [128, 32], BF16)
            cb = S.carve("cb", [128, 2], F32)
            hid = S.carve("hid", [128, 2, 128], BF16)
            S.dma(tmpf[0:N_CMP, :], D["cmpmask"].rearrange("j i b -> j (i b)"), w=[tmpf])
            ts("dve", cmpneg[0:N_CMP, :, :].rearrange("p i b -> p (i b)"), tmpf[0:N_CMP, :], -1.0, -NEG, ALU.add, ALU.mult,
               [tmpf], [cmpneg])
            S.dma(tmpf[0:32, :], D["selE"], w=[tmpf])
            cp("dve", selE[0:32, :], tmpf[0:32, :], [tmpf], [selE])
            S.dma(tmpf[:, 0:512].rearrange("p (i j) -> p i j", i=NT), D["seladj"].rearrange("(i p) j -> p i j", p=128), w=[tmpf])
            cp("dve", seladj[:].rearrange("p i j -> p (i j)"), tmpf[:, 0:512], [tmpf], [seladj])
            S.dma(ROPEC[0:N_CMP, :], D["rope_cmp"], w=[ROPEC])
            S.dma(tmpf[0:N_CMP, 0:32], D["ovl"], w=[tmpf])
            for k in range(2):
                cp("dve", VC1[0:N_CMP, k, 65:97], tmpf[0:N_CMP, 0:32], [tmpf], [VC1.b(("o", k))])
            memset("dve", VC1[:, :, 64:65], 1.0, [VC1.b("ones")])
            memset("dve", Vs1[:, :, :, 64:65], 1.0, [Vs1.b("ones")])
            memset("dve", Vw1[:, :, :, 64:65], 1.0, [Vw1.b("ones")])
            S.dma(w2f[:, 0, :], D["d_w2k"][0], w=[w2f.b(0)])
            S.dma(w2f[:, 1, :], D["d_w2v"][0], w=[w2f.b(1)])
            cp("dve", w2b[:], w2f[:], [w2f.b(0), w2f.b(1)], [w2b])
            for j_, c0_ in enumerate([2816, 3072, 2944, 3200]):
                S.dma(w512[:, :, j_ * 128:(j_ + 1) * 128], W_IN[l].rearrange("(c p) n -> p c n", p=128)[:, :, c0_:c0_ + 128],
                      w=[w512.b(j_)])
            load_slab(wkc, W_IN[l], 2560, 256)
            for i in range(NT):
                for c in range(8):
                    mm(pZ[:], xnT[:, c, i * 128:(i + 1) * 128], w512[:, c, 0:512], c == 0, c == 7,
                       [xnT.b(i)] + [w512.b(j_) for j_ in range(4)], [pZ])
                norm_rope(pZ[:, 0:256], 4, (10,), ROPE[:, i, :], krot[:, 0:256].rearrange("p (h d) -> p h d", h=4),
                          [pZ], [krot.b(0), krot.b(1)], 0)
                cp("act", Vs1[:, i, :, 0:64], pZ[:, 256:384].rearrange("p (h d) -> p h d", h=2), [pZ], [Vs1.b(i)])
                cp("act", Vw1[:, i, :, 0:64], pZ[:, 384:512].rearrange("p (h d) -> p h d", h=2), [pZ], [Vw1.b(i)])
                tr(pT[:, 0, :], krot[:, 0:128], 128, [krot.b(0)], [pT])
                tr(pT[:, 1, :], krot[:, 128:256], 128, [krot.b(1)], [pT])
                cp("act", ksT[:, i * 128:(i + 1) * 128], pT[:, 0, :], [pT], [ksT.b(i)])
                cp("act", kwT[:, i * 128:(i + 1) * 128], pT[:, 1, :], [pT], [kwT.b(i)])
            for g in range(4):
                proj_feat(pZ[:], wkc, 0, g, [pZ])
                cp("act", kcdT[:, g * 512:(g + 1) * 512], pZ[:], [pZ], [kcdT.b(g)])
                proj_feat(pZ[:], wkc, 128, g, [pZ])
                cp("act", vcdT[:, g * 512:(g + 1) * 512], pZ[:], [pZ], [vcdT.b(g)])
            for kind_i, kind in enumerate(["k", "v"]):
                srcT = kcdT if kind == "k" else vcdT
                w1v = W_1[kind].rearrange("(l d) m -> d l m", d=64)
                S.dma(W1r[0:64, :, :], w1v, w=[W1r.b(0)])
                S.dma(W1r[64:128, :, :], w1v, w=[W1r.b(1)])
                pesrc = D["d_pe_k" if kind == "k" else "d_pe_v"][0].rearrange("l d -> d l")
                S.dma(pef[0:64, :], pesrc, w=[pef], allow_slow_non_contiguous=True)
                cp("dve", peT[0:64, :], pef[0:64, :], [pef], [peT])
                for l_ in range(32):
                    mm(pY0[:, 0:1], W1r[0:64, l_, :], peT[0:64, l_:l_ + 1], l_ == 0, l_ == 31, [W1r.b(0), peT], [pY0])
                cp("dve", cb[:, 0:1], pY0[:, 0:1], [pY0], [cb])
                ts("dve", cb[:, 1:2], cb[:, 0:1], -1.0, None, ALU.mult, None, [cb], [cb])
                s3 = srcT[:].rearrange("p (j s) -> p j s", s=16)
                srcb = [srcT.b(g_) for g_ in range(4)]
                for k in range(2):
                    separator()
                    for l_ in range(32):
                        rhs = s3[64 * k:64 * k + 64, 0:N_CMP, l_] if l_ < 16 else s3[64 * k:64 * k + 64, 1:N_CMP + 1, l_ - 16]
                        mm(pSb[k][:, 0:N_CMP], W1r[64 * k:64 * k + 64, l_, :], rhs, l_ == 0, l_ == 31,
                           [W1r.b(k)] + srcb, [pSb[k]])
                    separator()
                    act(tB[:, 0:N_CMP], pSb[k][:, 0:N_CMP], AF.Exp, [pSb[k], cb], [tB], scale=-1.0, bias=cb[:, 1:2])
                    act(tC[:, 0:N_CMP], pSb[k][:, 0:N_CMP], AF.Identity, [pSb[k], cb], [tC], bias=cb[:, 0:1])
                    sigmoid_act(tB[:, 0:N_CMP], [tB], [tB])
                    tt("dve", hid[:, k, 0:N_CMP], tC[:, 0:N_CMP], tB[:, 0:N_CMP], ALU.mult, [tB, tC], [hid.b(k)])
                for k in range(2):
                    c0 = kind_i * 128 + k * 64
                    mm(pZ[0:N_CMP, c0:c0 + 64], hid[:, k, 0:N_CMP], w2b[:, kind_i, :], True, True, [hid.b(k), w2b], [pZ])
            norm_rope(pZ[0:N_CMP, 0:128], 2, GI["d_kn_cmp"], ROPEC[0:N_CMP, :],
                      krot[0:N_CMP, 0:128].rearrange("p (h d) -> p h d", h=2), [pZ], [krot.b(0)], 0, np_=N_CMP)
            cp("act", VC1[0:N_CMP, :, 0:64], pZ[0:N_CMP, 128:256].rearrange("p (h d) -> p h d", h=2), [pZ], [VC1.b("v")])
            tr(pT[:, 0, 0:N_CMP], krot[0:N_CMP, 0:128], N_CMP, [krot.b(0)], [pT])
            cp("act", kcT[:, 0:N_CMP], pT[:, 0, 0:N_CMP], [pT], [kcT])
            S.phase_reset(mark)
        wq = S.carve("wq1", [128, 8, 512], BF16)
        wgd = S.carve("wgd", [128, 8, 512], BF16)
        wqgm = S.carve("wqgm1", [128, 8, 512], BF16)
        wgt = S.carve("wgt", [128, 8, 24], BF16)
        woDM = S.carve("woDM", [128, 6, 1024], BF16)
        acc = S.carve("acc", [128, 512], F32)
        negT = S.carve("negT", [128, 2, 128], BF16)
        nb = S.carve("nb", [128, 2, 32], BF16)
        gts = S.carve("gts", [128, 24], F32)
        gat = S.carve("gat1", [128, 768], BF16)
        wv = W_IN[l].rearrange("(c p) n -> p c n", p=128)
        if doD:
            load_slab(wq, W_IN[l], 2048, 512)
            load_slab(wgd, W_IN[l], 3352, 512)
            S.dma(wgt[:], wv[:, :, 3328:3352], w=[wgt])
            S.dma(woDM[:, 0:4, :], W_OUT[l].rearrange("(c p) n -> p c n", p=128)[:, 4:8, :], w=[woDM.b("a")])
        if doM:
            load_slab(wqgm, W_IN[l], 3864, 512)
            S.dma(woDM[:, 4:6, :], W_OUT[l].rearrange("(c p) n -> p c n", p=128)[:, 8:10, :], w=[woDM.b("m")])
        gtsB = S.carve("gtsB", [128, 24], F32)
        gatB = S.carve("gat1B", [128, 768], BF16)
        gat2 = [gat, gatB]
        gts2 = [gts, gtsB]
        qzm2 = [S.carve("qzm1A", [128, 2, 2, 128], BF16), S.carve("qzm1B", [128, 2, 2, 128], BF16)]
        for q_ in qzm2:
            memset("dve", q_[:], 0.0, [q_.b(0), q_.b(1)])

        def evac_branch(k, br, first, gtc):
            gts3 = gtc[:].rearrange("p (h b) -> p h b", b=3)
            ts("dve", sm8[:, 2, 4 * k:4 * k + 4], pOb[k][:, :, 64], 1e-30, None, ALU.max, None, [pOb[k]], [sm8.b((2, k))])
            recip(sm8[:, 2, 4 * k:4 * k + 4], sm8[:, 2, 4 * k:4 * k + 4], [sm8.b((2, k))], [sm8.b((2, k))])
            tt("dve", sm8[:, 3, 4 * k:4 * k + 4], sm8[:, 2, 4 * k:4 * k + 4], gts3[:, 4 * k:4 * k + 4, br], ALU.mult,
               [sm8.b((2, k)), gtc], [sm8.b((3, k))])
            a3 = acc[:, 256 * k:256 * k + 256].rearrange("p (h d) -> p h d", h=4)
            if first:
                tt("dve", a3, pOb[k][:, :, 0:64], bc3(sm8[:, 3, 4 * k:4 * k + 4], 64), ALU.mult, [pOb[k], sm8.b((3, k))],
                   [acc.b(k)])
            else:
                t3 = tE[:, 256 * k:256 * k + 256].rearrange("p (h d) -> p h d", h=4)
                tt("dve", t3, pOb[k][:, :, 0:64], bc3(sm8[:, 3, 4 * k:4 * k + 4], 64), ALU.mult, [pOb[k], sm8.b((3, k))],
                   [tE])
                tt("dve", a3, a3, t3, ALU.add, [acc.b(k), tE], [acc.b(k)])

        def frontD(i):
            par = i % 2
            st = []
            st.append(lambda: proj_tok(pZ[:], wgd, 0, 512, i, [pZ]))
            st.extend(silu_stages(gat2[par][:, 0:512], pZ[:], 512, [pZ], [gat2[par].b("a")], tF))
            st.append(lambda: proj_tok(pZ[:, 0:24], wgt, 0, 24, i, [pZ]))
            st.append(lambda: act(gts2[par][:], pZ[:, 0:24], AF.Exp, [pZ], [gts2[par]], scale=-1.0))
            st.append(lambda: act(gts2[par][:], gts2[par][:], AF.Ln, [gts2[par]], [gts2[par]], bias=1.0))
            st.append(lambda: act(gts2[par][:], gts2[par][:], AF.Exp, [gts2[par]], [gts2[par]], scale=-1.0))
            st.append(lambda: proj_tok(pZ[:], wq, 0, 512, i, [pZ]))
            st.extend(norm_rope_stages(pZ[:], 8, GI["d_qn"], ROPE[:, i, :], qrot[:].rearrange("p (h d) -> p h d", h=8),
                                       [pZ], [qrot], 1))

            def t_():
                for pr in range(4):
                    tr(pT[:, pr, :], qrot[:, pr * 128:(pr + 1) * 128], 128, [qrot], [pT])
            def c_():
                t_()
                cp("act", qz2[par][0:64, 0, :, :], pT[0:64, 0:4, :], [pT], [qz2[par].b(0)])
                cp("act", qz2[par][64:128, 1, :, :], pT[64:128, 0:4, :], [pT], [qz2[par].b(1)])
            st.append(c_)
            return st

        def front1(i):
            st = frontD(i) if doD else []
            if doM:
                st += mem_front_stages(i, l, wqgm, gat2[i % 2], qzm2[i % 2])
            return st

        def cmp_pe(i):
            par = i % 2
            qzc, gtc = qz2[par], gts2[par]
            for k in range(2):
                ps = pSb[k]
                psv = ps[0:N_CMP, :].rearrange("p (g t) -> p g t", g=4)
                mm(psv, kcT[:, 0:N_CMP], qzc[:, k, :, :], True, False, [kcT, qzc.b(k)], [ps])
                mm(psv, ident[0:N_CMP, 0:N_CMP], bch(cmpneg[0:N_CMP, i, :], 4), False, True, [ident, cmpneg], [ps])
                act(Pb[0:N_CMP, k, :], ps[0:N_CMP, :], AF.Exp, [ps], [Pb.b(k)], scale=0.125)
            for k in range(2):
                memset("dve", pOb[k][:], 0.0, [pOb[k]])
                for g in range(4):
                    mm(pOb[k][:, g, 0:97], Pb[0:N_CMP, k, g * 128:(g + 1) * 128], VC1[0:N_CMP, k, :], False, False,
                       [Pb.b(k), VC1.b("ones"), VC1.b("v"), VC1.b(("o", k))], [pOb[k]], skip=True)

        def cmp_sel(i):
            par = i % 2
            qzc, gtc = qz2[par], gts2[par]
            for k in range(2):
                evac_branch(k, 0, True, gtc)
                t3 = tC[:, 0:128].rearrange("p (g j) -> p g j", g=4)
                tt("dve", t3, pOb[k][:, :, 65:97], bc3(sm8[:, 2, 4 * k:4 * k + 4], 32), ALU.mult,
                   [pOb[k], sm8.b((2, k))], [tC])
                op("dve", lambda e, k=k: e.tensor_reduce(out=tE[:, 32 * k:32 * k + 32],
                                                         in_=tC[:, 0:128].rearrange("p (g j) -> p j g", g=4),
                                                         axis=AX.X, op=ALU.add), r=[tC], w=[tE])
                tt("dve", tE[:, 32 * k:32 * k + 32], tE[:, 32 * k:32 * k + 32], seladj[:, i, :], ALU.add,
                   [tE, seladj], [tE])
                op("dve", lambda e, k=k: e.max(out=sm8[:, 6, 0:8], in_=tE[:, 32 * k:32 * k + 32]), r=[tE],
                   w=[sm8.b(6)])
                ts("dve", tD[:, 0:32], tE[:, 32 * k:32 * k + 32], sm8[:, 6, 3:4], None, ALU.is_ge, None,
                   [tE, sm8.b(6)], [tD])
                ts("dve", nb[:, k, :], tD[:, 0:32], -1.0, -NEG, ALU.add, ALU.mult, [tD], [nb.b(k)])
                tr(pT[0:32, 4 + k, :], nb[:, k, :], 128, [nb.b(k)], [pT])
            cp("act", negT[0:32, :, :], pT[0:32, 4:6, :], [pT], [negT])

        for st_ in front1(0):
            st_()
        if doD:
            cmp_pe(0)
            cmp_sel(0)
        for i in range(NT):
            par = i % 2
            qzc, gtc, gac = qz2[par], gts2[par], gat2[par]
            fill = front1(i + 1) if i + 1 < NT else []
            if doD:
                def masks_sel(kt, k, i=i):
                    ex = [(selE[0:32, kt * 128:(kt + 1) * 128], bch(negT[0:32, k, :], 4), [selE, negT])]
                    if kt == i:
                        ex.append((ident[:], bch(mdiag[:], 4), [ident, mdiag]))
                    return ex

                def masks_win(kt, i=i):
                    if kt == i:
                        return [(ident[:], bch(mdiag[:], 4), [ident, mdiag])]
                    if kt == i - 4:
                        return [(ident[:], bch(mprev[:], 4), [ident, mprev])]
                    return []

                kts_s = list(range(0, i + 1))
                kts_w = list(range(max(0, i - 4), i + 1))
                brs = []
                for k in range(2):
                    brs.append(dict(k=k, kT=ksT, V=Vs1, kts=kts_s, masks=(lambda kt, k=k: masks_sel(kt, k)),
                                    evac=(lambda k_, gtc=gtc: evac_branch(k_, 1, False, gtc))))
                for k in range(2):
                    brs.append(dict(k=k, kT=kwT, V=Vw1, kts=kts_w, masks=masks_win,
                                    evac=(lambda k_, gtc=gtc: evac_branch(k_, 2, False, gtc))))
                attn_pipeline(brs, qzc, fill)
            while fill:
                fill.pop(0)()
            if doM:
                mem_back(i, l, kmT, VM1, gac, qzm2[par])
            if doD:
                tt("dve", ob16[:, 0:512], acc[:], gac[:, 0:512], ALU.mult, [acc.b(0), acc.b(1), gac.b("a")], [ob16])
                if i + 1 < NT:
                    cmp_pe(i + 1)
                transposes_to_oT(4, 0, 0)
            chunks = ([0, 1, 2, 3] if doD else []) + ([4, 5] if doM else [])
            for half in range(2):
                for n_, c in enumerate(chunks):
                    mm(pYb[half][:], oT[:, c, :], woDM[:, c, half * 512:(half + 1) * 512], n_ == 0, n_ == len(chunks) - 1,
                       [oT, woDM.b("a"), woDM.b("m")], [pYb[half]])
            if doD and i + 1 < NT:
                cmp_sel(i + 1)
            for half in range(2):
                tt("dve", H[:, i, half * 512:(half + 1) * 512], H[:, i, half * 512:(half + 1) * 512], pYb[half][:],
                   ALU.add, [H.b(i), pYb[half]], [H.b(i)])

    def layer1(s):
        rmsnorm_to_xnT()
        if "C" in mix1:
            hgrn2_phase(s)
            S.phase_reset()
        doD, doM = "D" in mix1, "M" in mix1
        if doD or doM:
            nsa_phase(s, doD, doM)
            S.phase_reset()

    for s in range(nseq):
        for i in range(NT):
            S.dma(H[:, i, :], D["x"][s, i * 128:(i + 1) * 128, :], w=[H.b(i)])
        if 0 in layers:
            layer0(s)
        if 1 in layers:
            layer1(s)
        for i in range(NT):
            S.dma(Y[s, i * 128:(i + 1) * 128, :], H[:, i, :], r=[H.b(i)])
    S.emit()
    return nc, S


N_CORES = 8
_PROG = {}


def kernel(**inputs):
    x = np.ascontiguousarray(inputs["x"], dtype=np.float32)
    mem = np.ascontiguousarray(inputs["mem"], dtype=np.float32)
    B = x.shape[0]
    per = B // N_CORES
    if per not in _PROG:
        _PROG[per] = build_program(per)[0]
    nc = _PROG[per]
    consts = host_consts()
    params = {k: np.ascontiguousarray(inputs[k], dtype=np.float32) for k in PARAM_SHAPES}
    in_maps = []
    for c in range(N_CORES):
        m = {"x": x[c * per:(c + 1) * per], "mem": mem[c * per:(c + 1) * per]}
        m.update(params)
        m.update(consts)
        in_maps.append(m)
    res = run_bass_kernel_spmd(nc, in_maps, core_ids=list(range(N_CORES)))
    return np.concatenate([r["y"] for r in res.results], axis=0)
```
